# Optimizing a Trainium2 kernel written in Bass

```python
import jax, jax.numpy as jnp
from jax import lax
import numpy as np

D_MODEL = 2048
BATCH = 4
SEQ = 2048
DEPTH = 1
DEC_BATCH = 32
DEC_SEQ = 4
PAST_LEN = 8192
PAGE_SIZE = 128

GLA_HEADS = 4
GLA_DK = 128
GLA_DV = 256
GLA_LOWRANK = 16
GLA_GATE_NORM = 16.0
GLA_CHUNK = 64
NSA_HEADS = 8
NSA_KV_HEADS = 2
HEAD_DIM = 128
L_CMP = 32
D_CMP = 16
CMP_HID = 128
L_SLC = 64
N_SEL = 16
WINDOW = 512
SLC_QBLK = 64
SWA_QBLK = 128
ROPE_THETA = 10000.0
D_FF = 5632
CONV_W = 3
EPS = 1e-6

KV_W = NSA_KV_HEADS * HEAD_DIM
IN_SPLITS = (GLA_HEADS * GLA_DK, GLA_HEADS * GLA_DK, GLA_HEADS * GLA_DV, GLA_HEADS * GLA_DV, GLA_LOWRANK,
             NSA_HEADS * HEAD_DIM, KV_W, KV_W, KV_W, KV_W, KV_W, KV_W, 3 * NSA_HEADS, 2 * D_MODEL)
IN_COLS = sum(IN_SPLITS)

kernel_name = 'gla_nsa_parallel_hybrid_step'


def _rms(x, g):
    xf = x.astype(jnp.float32)
    y = xf * lax.rsqrt(jnp.mean(xf * xf, axis=-1, keepdims=True) + EPS)
    return (y * g.astype(jnp.float32)).astype(x.dtype)


def _rope(x, pos):
    half = HEAD_DIM // 2
    inv = ROPE_THETA ** (-jnp.arange(half, dtype=jnp.float32) / half)
    ang = pos.astype(jnp.float32)[:, None] * inv[None, :]
    cos = jnp.cos(ang)[None, :, None, :]
    sin = jnp.sin(ang)[None, :, None, :]
    xf = x.astype(jnp.float32)
    x1, x2 = xf[..., :half], xf[..., half:]
    return jnp.concatenate([x1 * cos - x2 * sin, x2 * cos + x1 * sin], axis=-1).astype(x.dtype)


def _gla_scan(q, k, v, logf, s0):
    b_, t_, h_, _ = q.shape
    dv = v.shape[-1]
    c = GLA_CHUNK if t_ % GLA_CHUNK == 0 else t_
    n = t_ // c

    def chunks(a):
        return a.astype(jnp.float32).reshape(b_, n, c, h_, a.shape[-1]).transpose(1, 0, 3, 2, 4)

    causal = jnp.tril(jnp.ones((c, c), dtype=bool))[:, :, None]

    def step(s, inp):
        qc, kc, vc, gc = inp
        bcum = jnp.cumsum(gc, axis=2)
        blast = bcum[:, :, -1:, :]
        o = jnp.einsum('bhtd,bhde->bhte', qc * jnp.exp(bcum), s)
        decay = jnp.exp(jnp.where(causal, bcum[:, :, :, None, :] - bcum[:, :, None, :, :], -jnp.inf))
        att = jnp.einsum('bhtd,bhtsd,bhsd->bhts', qc, decay, kc)
        o = o + jnp.einsum('bhts,bhse->bhte', att, vc)
        s = s * jnp.exp(blast)[:, :, 0, :, None] + jnp.einsum('bhsd,bhse->bhde', kc * jnp.exp(blast - bcum), vc)
        return s, o

    s_t, o = lax.scan(step, s0.astype(jnp.float32), (chunks(q), chunks(k), chunks(v), chunks(logf)))
    o = o.transpose(1, 0, 3, 2, 4).reshape(b_, t_, h_, dv)
    return o.astype(v.dtype), s_t.astype(s0.dtype)


def _compress(rows, pe, w1, b1, w2, b2):
    b_, tk, g_, dh = rows.shape
    n_cmp = (tk - L_CMP) // D_CMP + 1
    r_pieces = L_CMP // D_CMP
    ch = rows[:, : (n_cmp - 1 + r_pieces) * D_CMP].reshape(b_, n_cmp - 1 + r_pieces, D_CMP, g_, dh)
    h = jnp.einsum('rd,rdf->f', pe, w1) + b1
    for r in range(r_pieces):
        h = h + jnp.einsum('bnrgd,rdf->bngf', ch[:, r:r + n_cmp], w1[r * D_CMP:(r + 1) * D_CMP])
    return jax.nn.gelu(h) @ w2 + b2


def _cmp_to_slc_map(n_cmp, n_slc):
    i = np.arange(n_cmp)[:, None]
    j = np.arange(n_slc)[None, :]
    m = (D_CMP * i < L_SLC * (j + 1)) & (D_CMP * i + L_CMP > L_SLC * j)
    return jnp.asarray(m, dtype=jnp.float32)


def _nsa_compressed(qn, k_rows, v_rows, q_pos, w):
    b_, tq, h_, dh = qn.shape
    g_ = k_rows.shape[2]
    kc = _rms(_compress(k_rows, w['cmp_pe_k'], w['cmp_w1_k'], w['cmp_b1_k'], w['cmp_w2_k'], w['cmp_b2_k']), w['k_cmp_norm_g'])
    vc = _compress(v_rows, w['cmp_pe_v'], w['cmp_w1_v'], w['cmp_b1_v'], w['cmp_w2_v'], w['cmp_b2_v'])
    n_cmp = kc.shape[1]
    qg = qn.reshape(b_, tq, g_, h_ // g_, dh)
    s = jnp.einsum('btghd,bngd->bghtn', qg, kc).astype(jnp.float32) * (dh ** -0.5)
    ends = D_CMP * jnp.arange(n_cmp) + (L_CMP - 1)
    valid = ends[None, :] <= q_pos[:, None]
    p = jnp.where(valid, jax.nn.softmax(jnp.where(valid, s, -1e30), axis=-1), 0.0)
    o = jnp.einsum('bghtn,bngd->btghd', p.astype(vc.dtype), vc)
    return o, p.sum(axis=2)


def _nsa_selected(q, k, v, imp, q_pos):
    b_, tq, g_, hg, dh = q.shape
    tk = k.shape[1]
    n_slc = -(-tk // L_SLC)
    n_top = min(N_SEL, n_slc)
    score = jnp.einsum('bgtn,nj->bgtj', imp, _cmp_to_slc_map(imp.shape[-1], n_slc))
    j = jnp.arange(n_slc)[None, :]
    cur = (q_pos // L_SLC)[:, None]
    forced = (j == 0) | (j == cur) | (j == cur - 1)
    score = jnp.where(forced, jnp.inf, score)
    score = jnp.where(j * L_SLC <= q_pos[:, None], score, -jnp.inf)
    _, idx = lax.top_k(score, n_top)
    pad = n_slc * L_SLC - tk
    kb = jnp.pad(k, ((0, 0), (0, pad), (0, 0), (0, 0))).reshape(b_, n_slc, L_SLC, g_, dh).transpose(0, 3, 1, 2, 4)
    vb = jnp.pad(v, ((0, 0), (0, pad), (0, 0), (0, 0))).reshape(b_, n_slc, L_SLC, g_, dh).transpose(0, 3, 1, 2, 4)
    qb = SLC_QBLK if tq % SLC_QBLK == 0 else tq
    nq = tq // qb
    bi = jnp.arange(b_)[:, None, None, None]
    gi = jnp.arange(g_)[None, :, None, None]
    scale = dh ** -0.5

    def one(args):
        qblk, iblk, pblk = args
        kg = kb[bi, gi, iblk]
        vg = vb[bi, gi, iblk]
        kpos = iblk[..., None] * L_SLC + jnp.arange(L_SLC)
        mask = kpos <= pblk[None, None, :, None, None]
        s = jnp.einsum('bqghd,bgqnkd->bghqnk', qblk, kg).astype(jnp.float32) * scale
        s = jnp.where(mask[:, :, None], s, -jnp.inf)
        p = jax.nn.softmax(s.reshape(s.shape[:-2] + (-1,)), axis=-1).reshape(s.shape)
        return jnp.einsum('bghqnk,bgqnkd->bqghd', p.astype(vg.dtype), vg)

    o = lax.map(one, (q.reshape(b_, nq, qb, g_, hg, dh).swapaxes(0, 1),
                      idx.reshape(b_, g_, nq, qb, n_top).transpose(2, 0, 1, 3, 4),
                      q_pos.reshape(nq, qb)))
    return o.swapaxes(0, 1).reshape(b_, tq, g_, hg, dh)


def _swa_dense(q, k, v, q_pos, k_pos):
    s = jnp.einsum('bqghd,bkgd->bghqk', q, k).astype(jnp.float32) * (q.shape[-1] ** -0.5)
    rel = q_pos[:, None] - k_pos[None, :]
    mask = (rel >= 0) & (rel < WINDOW) & (k_pos[None, :] >= 0)
    p = jax.nn.softmax(jnp.where(mask, s, -jnp.inf), axis=-1)
    return jnp.einsum('bghqk,bkgd->bqghd', p.astype(v.dtype), v)


def _swa_prompt(q, k, v, q_pos):
    b_, t_, g_, hg, dh = q.shape
    blk = SWA_QBLK
    nb = t_ // blk
    nback = -(-WINDOW // blk)

    def band(a):
        ap = jnp.pad(a, ((0, 0), (nback * blk, 0), (0, 0), (0, 0))).reshape(b_, nb + nback, blk, g_, dh)
        return jnp.concatenate([ap[:, j:j + nb] for j in range(nback + 1)], axis=2)

    kpos = (jnp.arange(nb)[:, None] - nback) * blk + jnp.arange((nback + 1) * blk)[None, :]
    o = jax.vmap(_swa_dense, in_axes=(1, 1, 1, 0, 0), out_axes=1)(
        q.reshape(b_, nb, blk, g_, hg, dh), band(k), band(v), q_pos.reshape(nb, blk), kpos)
    return o.reshape(b_, t_, g_, hg, dh)


def _gather_pages(cache, page_table):
    d, n = page_table.shape
    return cache[page_table].reshape((d, n * cache.shape[1]) + cache.shape[2:])


def _layer(x, q_pos, past, w):
    b_, t_, _ = x.shape
    g_, hg = NSA_KV_HEADS, NSA_HEADS // NSA_KV_HEADS
    xn = _rms(x, w['norm1_g'])
    cuts = [int(c) for c in np.cumsum(IN_SPLITS)[:-1]]
    (gla_q, gla_k, gla_v, gla_r, gla_a, nsa_q, k_cmp, v_cmp, k_slc, v_slc, k_swa, v_swa,
     nsa_g, merge_g) = jnp.split(xn @ w['w_in'], cuts, axis=-1)

    q = gla_q.reshape(b_, t_, GLA_HEADS, GLA_DK) * (GLA_DK ** -0.5)
    k = gla_k.reshape(b_, t_, GLA_HEADS, GLA_DK)
    v = gla_v.reshape(b_, t_, GLA_HEADS, GLA_DV)
    logf = jax.nn.log_sigmoid((gla_a @ w['gla_w_a2'] + w['gla_b_a2']).astype(jnp.float32)) / GLA_GATE_NORM
    s0 = jnp.zeros((b_, GLA_HEADS, GLA_DK, GLA_DV), x.dtype) if past is None else past['gla']
    o_gla, gla_new = _gla_scan(q, k, v, logf.reshape(b_, t_, GLA_HEADS, GLA_DK), s0)
    o_gla = _rms(o_gla, w['gla_onorm_g']) * jax.nn.silu(gla_r.reshape(b_, t_, GLA_HEADS, GLA_DV))
    gla_out = o_gla.reshape(b_, t_, GLA_HEADS * GLA_DV) @ w['w_br_gla']

    def kvh(a):
        return a.reshape(b_, t_, g_, HEAD_DIM)

    qn = _rms(nsa_q.reshape(b_, t_, NSA_HEADS, HEAD_DIM), w['q_norm_g'])
    qr = _rope(qn, q_pos).reshape(b_, t_, g_, hg, HEAD_DIM)
    k_cmp, v_cmp = kvh(k_cmp), kvh(v_cmp)
    k_slc, v_slc = _rope(_rms(kvh(k_slc), w['k_slc_norm_g']), q_pos), kvh(v_slc)
    k_swa, v_swa = _rope(_rms(kvh(k_swa), w['k_swa_norm_g']), q_pos), kvh(v_swa)
    if past is None:
        kc_all, vc_all, ks_all, vs_all = k_cmp, v_cmp, k_slc, v_slc
    else:
        kc_all = jnp.concatenate([past['k_cmp'], k_cmp], axis=1)
        vc_all = jnp.concatenate([past['v_cmp'], v_cmp], axis=1)
        ks_all = jnp.concatenate([past['k_slc'], k_slc], axis=1)
        vs_all = jnp.concatenate([past['v_slc'], v_slc], axis=1)
    o_cmp, imp = _nsa_compressed(qn, kc_all, vc_all, q_pos, w)
    o_slc = _nsa_selected(qr, ks_all, vs_all, imp, q_pos)
    if past is None:
        o_swa = _swa_prompt(qr, k_swa, v_swa, q_pos)
        n_keep = min(WINDOW, t_)
        swa_k_new, swa_v_new = k_swa[:, t_ - n_keep:], v_swa[:, t_ - n_keep:]
    else:
        wb = past['swa_k'].shape[1]
        k_win = jnp.concatenate([past['swa_k'], k_swa], axis=1)
        v_win = jnp.concatenate([past['swa_v'], v_swa], axis=1)
        k_pos = q_pos[0] - wb + jnp.arange(wb + t_)
        o_swa = _swa_dense(qr, k_win, v_win, q_pos, k_pos)
        swa_k_new, swa_v_new = k_win[:, t_:], v_win[:, t_:]
    g = jax.nn.sigmoid(nsa_g).reshape(b_, t_, 3, g_, hg, 1)
    o_nsa = g[:, :, 0] * o_cmp + g[:, :, 1] * o_slc + g[:, :, 2] * o_swa
    nsa_out = o_nsa.reshape(b_, t_, NSA_HEADS * HEAD_DIM) @ w['w_br_nsa']

    g_gla, g_nsa = jnp.split(jax.nn.sigmoid(merge_g), 2, axis=-1)
    h = x + (g_gla * gla_out + g_nsa * nsa_out) @ w['w_o']

    hn = _rms(h, w['norm2_g'])
    a, bgate = jnp.split(hn @ w['w_up'], 2, axis=-1)
    buf = jnp.zeros((b_, CONV_W - 1, D_FF), a.dtype) if past is None else past['conv']
    a_full = jnp.concatenate([buf, a], axis=1)
    a_c = w['conv_b'] + sum(w['conv_w'][j] * a_full[:, j:j + t_] for j in range(CONV_W))
    y = h + (jax.nn.gelu(a_c) * bgate) @ w['w_down']
    conv_new = a_full[:, a_full.shape[1] - (CONV_W - 1):]
    return y, (k_cmp, v_cmp, k_slc, v_slc, swa_k_new, swa_v_new, gla_new, conv_new)


def setup_inputs(seed: int = 0) -> dict:
    key = jax.random.key(seed)
    ks = iter(jax.random.split(key, 64))

    def nrm(shape, scale):
        return jax.random.normal(next(ks), shape, jnp.float32) * scale

    def gain(n):
        return 1.0 + nrm((n,), 0.05)

    n_pages = PAST_LEN // PAGE_SIZE
    n_used = DEC_BATCH * n_pages
    n_pool = n_used + max(1, n_used // 4)
    swa_buf = min(WINDOW, PAST_LEN)
    pool_shape = (n_pool, PAGE_SIZE, NSA_KV_HEADS, HEAD_DIM)
    return {
        'x_prompt': nrm((BATCH, SEQ, D_MODEL), 1.0),
        'x_sample': nrm((DEC_BATCH, DEC_SEQ, D_MODEL), 1.0),
        'cache_k_cmp': nrm(pool_shape, 1.0),
        'cache_v_cmp': nrm(pool_shape, 1.0),
        'cache_k_slc': nrm(pool_shape, 1.0),
        'cache_v_slc': nrm(pool_shape, 1.0),
        'state_swa_k': nrm((DEC_BATCH, swa_buf, NSA_KV_HEADS, HEAD_DIM), 1.0),
        'state_swa_v': nrm((DEC_BATCH, swa_buf, NSA_KV_HEADS, HEAD_DIM), 1.0),
        'state_gla': nrm((DEC_BATCH, GLA_HEADS, GLA_DK, GLA_DV), 1.0),
        'state_conv': nrm((DEC_BATCH, CONV_W - 1, D_FF), 1.0),
        'page_table': jax.random.permutation(next(ks), n_pool)[:n_used].reshape(DEC_BATCH, n_pages).astype(jnp.int32),
        'norm1_g': gain(D_MODEL),
        'w_in': nrm((D_MODEL, IN_COLS), D_MODEL ** -0.5),
        'gla_w_a2': nrm((GLA_LOWRANK, GLA_HEADS * GLA_DK), GLA_LOWRANK ** -0.5),
        'gla_b_a2': nrm((GLA_HEADS * GLA_DK,), 0.1),
        'gla_onorm_g': gain(GLA_DV),
        'q_norm_g': gain(HEAD_DIM),
        'k_cmp_norm_g': gain(HEAD_DIM),
        'k_slc_norm_g': gain(HEAD_DIM),
        'k_swa_norm_g': gain(HEAD_DIM),
        'cmp_pe_k': nrm((L_CMP, HEAD_DIM), 0.1),
        'cmp_w1_k': nrm((L_CMP, HEAD_DIM, CMP_HID), (L_CMP * HEAD_DIM) ** -0.5),
        'cmp_b1_k': nrm((CMP_HID,), 0.01),
        'cmp_w2_k': nrm((CMP_HID, HEAD_DIM), CMP_HID ** -0.5),
        'cmp_b2_k': nrm((HEAD_DIM,), 0.01),
        'cmp_pe_v': nrm((L_CMP, HEAD_DIM), 0.1),
        'cmp_w1_v': nrm((L_CMP, HEAD_DIM, CMP_HID), (L_CMP * HEAD_DIM) ** -0.5),
        'cmp_b1_v': nrm((CMP_HID,), 0.01),
        'cmp_w2_v': nrm((CMP_HID, HEAD_DIM), CMP_HID ** -0.5),
        'cmp_b2_v': nrm((HEAD_DIM,), 0.01),
        'w_br_gla': nrm((GLA_HEADS * GLA_DV, D_MODEL), (GLA_HEADS * GLA_DV) ** -0.5),
        'w_br_nsa': nrm((NSA_HEADS * HEAD_DIM, D_MODEL), (NSA_HEADS * HEAD_DIM) ** -0.5),
        'w_o': nrm((D_MODEL, D_MODEL), D_MODEL ** -0.5),
        'norm2_g': gain(D_MODEL),
        'w_up': nrm((D_MODEL, 2 * D_FF), D_MODEL ** -0.5),
        'conv_w': nrm((CONV_W, D_FF), CONV_W ** -0.5),
        'conv_b': nrm((D_FF,), 0.01),
        'w_down': nrm((D_FF, D_MODEL), D_FF ** -0.5),
    }


def reference(x_prompt, x_sample, cache_k_cmp, cache_v_cmp, cache_k_slc, cache_v_slc, state_swa_k, state_swa_v,
              state_gla, state_conv, page_table, norm1_g, w_in, gla_w_a2, gla_b_a2, gla_onorm_g, q_norm_g,
              k_cmp_norm_g, k_slc_norm_g, k_swa_norm_g, cmp_pe_k, cmp_w1_k, cmp_b1_k, cmp_w2_k, cmp_b2_k,
              cmp_pe_v, cmp_w1_v, cmp_b1_v, cmp_w2_v, cmp_b2_v, w_br_gla, w_br_nsa, w_o, norm2_g, w_up,
              conv_w, conv_b, w_down):
    w = dict(norm1_g=norm1_g, w_in=w_in, gla_w_a2=gla_w_a2, gla_b_a2=gla_b_a2, gla_onorm_g=gla_onorm_g,
             q_norm_g=q_norm_g, k_cmp_norm_g=k_cmp_norm_g, k_slc_norm_g=k_slc_norm_g, k_swa_norm_g=k_swa_norm_g,
             cmp_pe_k=cmp_pe_k, cmp_w1_k=cmp_w1_k, cmp_b1_k=cmp_b1_k, cmp_w2_k=cmp_w2_k, cmp_b2_k=cmp_b2_k,
             cmp_pe_v=cmp_pe_v, cmp_w1_v=cmp_w1_v, cmp_b1_v=cmp_b1_v, cmp_w2_v=cmp_w2_v, cmp_b2_v=cmp_b2_v,
             w_br_gla=w_br_gla, w_br_nsa=w_br_nsa, w_o=w_o, norm2_g=norm2_g, w_up=w_up,
             conv_w=conv_w, conv_b=conv_b, w_down=w_down)
    past_len = page_table.shape[1] * cache_k_cmp.shape[1]
    past = dict(k_cmp=_gather_pages(cache_k_cmp, page_table), v_cmp=_gather_pages(cache_v_cmp, page_table),
                k_slc=_gather_pages(cache_k_slc, page_table), v_slc=_gather_pages(cache_v_slc, page_table),
                swa_k=state_swa_k, swa_v=state_swa_v, gla=state_gla, conv=state_conv)
    pos_p = jnp.arange(x_prompt.shape[1], dtype=jnp.int32)
    pos_s = past_len + jnp.arange(x_sample.shape[1], dtype=jnp.int32)
    y_p, y_s = x_prompt, x_sample
    for _ in range(DEPTH):
        y_p, st_p = _layer(y_p, pos_p, None, w)
        y_s, st_s = _layer(y_s, pos_s, past, w)
    k_cmp_p, v_cmp_p, k_slc_p, v_slc_p, swa_k_p, swa_v_p, gla_p, conv_p = st_p
    k_cmp_s, v_cmp_s, k_slc_s, v_slc_s, swa_k_s, swa_v_s, gla_s, conv_s = st_s
    return (y_p, y_s, k_cmp_p, v_cmp_p, k_slc_p, v_slc_p, swa_k_p, swa_v_p, gla_p, conv_p,
            k_cmp_s, v_cmp_s, k_slc_s, v_slc_s, swa_k_s, swa_v_s, gla_s, conv_s)
```

```python
import numpy as np
import concourse.bass as bass
import concourse.mybir as mybir
from concourse.bass_utils import run_bass_kernel_spmd

F32 = mybir.dt.float32
BF16 = mybir.dt.bfloat16
I32 = mybir.dt.int32
AF = mybir.ActivationFunctionType
ALU = mybir.AluOpType
AX = mybir.AxisListType

D = 2048
NCOLS = 9768
NT = 17
QT0 = 7
EPS = 1e-6
DFF = 5632
NFB = 44
C_GQ, C_GK, C_GV, C_GR, C_GA, C_NQ, C_KV, C_NG, C_MG = 0, 512, 1024, 2048, 3072, 3088, 4112, 5648, 5672
BIG = 1.0e5
WITH_SAMPLE = True


class MK:
    def __init__(self, nc, n_dma_sems=40):
        self.nc = nc
        self.eng = {"pe": nc.tensor, "act": nc.scalar, "dve": nc.vector,
                    "pool": nc.gpsimd, "sp": nc.sync}
        self._stack = []
        self.esem = {}
        for e in ("pe", "act", "dve", "pool"):
            self.esem[e] = self._sem("es_" + e)
        self.ecount = {e: 0 for e in self.esem}
        self.dsem = [self._sem("ds%d" % i) for i in range(n_dma_sems)]
        self.dtot = [0] * n_dma_sems
        self.drr = 0
        self.known = {e: {} for e in self.eng}
        self.last_w = {}
        self.readers = {}
        self.n_wait = 0
        self.n_ins = 0
        self.n_dma = 0

    def _sem(self, name):
        cm = self.nc.semaphore(name)
        s = cm.__enter__()
        self._stack.append(cm)
        return s

    def _need(self, E, reads, writes):
        need = {}

        def add(tok, same_ok):
            if tok is None:
                return
            sem, val, src = tok
            if src == E and not same_ok:
                return
            k = id(sem)
            if k not in need or need[k][1] < val:
                need[k] = (sem, val)

        for k in reads:
            add(self.last_w.get(k), E != "pe")
        for k in writes:
            add(self.last_w.get(k), False)
            for tok in self.readers.get(k, {}).values():
                add(tok, False)
        kn = self.known[E]
        eng = self.eng[E]
        for k, (sem, val) in need.items():
            if kn.get(k, 0) >= val:
                continue
            eng.wait_ge(sem, val)
            self.n_wait += 1
            kn[k] = val

    def _commit(self, tok, reads, writes):
        for k in reads:
            d = self.readers.setdefault(k, {})
            d[id(tok[0])] = tok
        for k in writes:
            self.last_w[k] = tok
            self.readers[k] = {}

    def op(self, E, fn, reads=(), writes=()):
        self._need(E, reads, writes)
        ins = fn(self.eng[E])
        self.ecount[E] += 1
        ins.then_inc(self.esem[E], 1)
        self.n_ins += 1
        tok = (self.esem[E], self.ecount[E], E)
        self._commit(tok, reads, writes)
        return tok

    def dma(self, E, out, in_, reads=(), writes=(), **kw):
        i = self.drr
        self.drr = (self.drr + 1) % len(self.dsem)
        sem = self.dsem[i]
        kn = self.known[E]
        if self.dtot[i] > 0 and kn.get(id(sem), 0) < self.dtot[i]:
            self.eng[E].wait_ge(sem, self.dtot[i])
            kn[id(sem)] = self.dtot[i]
        self._need(E, reads, writes)
        self.eng[E].dma_start(out=out, in_=in_, **kw).then_inc(sem, 16)
        self.n_dma += 1
        self.dtot[i] += 16
        tok = (sem, self.dtot[i], "dma")
        self._commit(tok, reads, writes)
        return tok

    def barrier(self):
        for E, eng in self.eng.items():
            kn = self.known[E]
            for X, sem in self.esem.items():
                if X == E or self.ecount[X] == 0:
                    continue
                if kn.get(id(sem), 0) < self.ecount[X]:
                    eng.wait_ge(sem, self.ecount[X])
                    kn[id(sem)] = self.ecount[X]
            for i, sem in enumerate(self.dsem):
                if self.dtot[i] and kn.get(id(sem), 0) < self.dtot[i]:
                    eng.wait_ge(sem, self.dtot[i])
                    kn[id(sem)] = self.dtot[i]
        self.last_w = {}
        self.readers = {}


class Rot:
    def __init__(self, aps, name):
        self.aps = aps
        self.name = name
        self.i = 0

    def next(self):
        j = self.i % len(self.aps)
        self.i += 1
        return self.aps[j], (self.name, j)


class Ctx:
    pass


def build_program(stage=99, n_pool=2560):
    N_POOL = n_pool
    nc = bass.Bass("TRN2", target_bir_lowering=False)
    mk = MK(nc)
    X = Ctx()
    X.nc, X.mk = nc, mk
    live = []

    def din(name, shape, dt=F32):
        return nc.dram_tensor(name, list(shape), dt, kind="ExternalInput").ap()

    def dout(name, shape, dt=F32):
        return nc.dram_tensor(name, list(shape), dt, kind="ExternalOutput").ap()

    def dscr(name, shape, dt=F32):
        return nc.dram_tensor(name, list(shape), dt).ap()

    def sb(name, shape, dt=F32):
        cm = nc.sbuf_tensor(name, list(shape), dt)
        t = cm.__enter__()
        live.append(cm)
        return t[:] if not hasattr(t, "shape") or True else t

    def sb_mark():
        return len(live)

    def sb_release(mark):
        while len(live) > mark:
            live.pop().__exit__(None, None, None)

    def dbg(name, ap, keys):
        if stage != 2:
            return
        d_ = dout("dbg_" + name, list(ap.shape), F32 if ap.dtype == F32 else ap.dtype)
        mk.dma("sp", d_, ap, reads=keys, writes=["dbg_" + name])

    def rot(name, shape, dt, n):
        return Rot([sb("%s%d" % (name, i), shape, dt) for i in range(n)], name)

    xbuf = din("xbuf", [NT * 128, D])
    w_in = din("w_in", [D, NCOLS])
    norm1_g = din("norm1_g", [D])
    ident_d = din("ident", [128, 128])
    rope_cos = din("rope_cos", [NT * 128, 64])
    rope_sin = din("rope_sin", [NT * 128, 64])
    k_slc_norm_g = din("k_slc_norm_g", [128])
    k_swa_norm_g = din("k_swa_norm_g", [128])
    kv_out = dout("kv_out", [NT * 128, 1536])
    proj = dscr("proj", [NT * 128, NCOLS])
    h_scr = dscr("h_scr", [10 * 128, D])
    gla_out = dout("gla_out", [5, 4, 128, 256])
    conv_out = dout("conv_out", [10, DFF])
    y_out = dout("y_out", [1040, D])
    swa_out = dout("swa_out", [2, 4, 512, 256])
    gla_w_a2 = din("gla_w_a2", [16, 512]); gla_b_a2 = din("gla_b_a2", [512]); gla_onorm_g = din("gla_onorm_g", [256])
    q_norm_g = din("q_norm_g", [128]); k_cmp_norm_g = din("k_cmp_norm_g", [128])
    CMPW = {}
    for nm in ("k", "v"):
        CMPW[nm] = dict(pe=din("cmp_pe_" + nm, [32, 128]), w1=din("cmp_w1_" + nm, [32, 128, 128]),
                        b1=din("cmp_b1_" + nm, [128]), w2=din("cmp_w2_" + nm, [128, 128]), b2=din("cmp_b2_" + nm, [128]))
    ucum_p = din("ucum_p", [128, 128]); ucum_s = din("ucum_s", [128, 128])
    caus_p = din("caus_p", [128, 128]); caus_s = din("caus_s", [128, 128])
    uend_p = din("uend_p", [128, 1]); uend_s = din("uend_s", [128, 4]); seqmask = din("seqmask", [128, 4])
    state_gla_c = din("state_gla_c", [4, 4, 128, 256])
    state_conv_c = din("state_conv_c", [4, 2, DFF])
    state_swa_c = din("state_swa_c", [2, 4, 512, 256])
    mmap_p = din("mmap_p", [128, 32]); tri_mask = din("tri_mask", [128, 128])
    cmp_add = din("cmp_add", [9 * 128, 127]); cmp_mul = din("cmp_mul", [9 * 128, 127])
    t_sel = din("t_sel", [9 * 128, 32]); t_inv = din("t_inv", [9 * 128, 32])
    swa_mask = din("swa_mask", [9 * 128, 640])
    page_tab = din("page_tab", [1, 256], I32)
    if WITH_SAMPLE:
        cache_k_cmp = din("cache_k_cmp", [N_POOL, 128, 256]); cache_v_cmp = din("cache_v_cmp", [N_POOL, 128, 256])
        cache_k_slc = din("cache_k_slc", [N_POOL, 128, 256]); cache_v_slc = din("cache_v_slc", [N_POOL, 128, 256])
    mmap_s = din("mmap_s", [512, 129]); t_sel_s = din("t_sel_s", [128, 129]); newmask = din("newmask", [4, 128, 128])
    swa_past_mask = din("swa_past_mask", [128, 512])
    o_scr = dscr("o_scr", [4, 2, 3, 128, 128])
    w_br_gla = din("w_br_gla", [1024, D]); w_br_nsa = din("w_br_nsa", [1024, D]); w_o = din("w_o", [D, D])
    norm2_g = din("norm2_g", [D]); w_up = din("w_up", [D, 2 * DFF]); conv_w = din("conv_w", [3, DFF])
    conv_b = din("conv_b", [DFF]); w_down = din("w_down", [DFF, D])

    Q1 = "act" if WITH_SAMPLE else "sp"
    if WITH_SAMPLE:
        gath = dscr("gath", [4, 256, 128, 256])
        caches = [cache_k_cmp, cache_v_cmp, cache_k_slc, cache_v_slc]
        gsem = [mk._sem("gs%d" % i) for i in range(4)]
        sp_ = nc.sync
        g_cnt = sp_.alloc_register("g_cnt")
        g_pid = sp_.alloc_register("g_pid")
        sp_.reg_mov(g_cnt, 256)
        with sp_.While(g_cnt):
            sp_.reg_sub(g_cnt, g_cnt, 1)
            g_idx = sp_.snap(g_cnt, min_val=0, max_val=255)
            sp_.reg_load(g_pid, page_tab[0:1, bass.ds(g_idx, 1)])
            g_pv = sp_.snap(g_pid, min_val=0, max_val=N_POOL - 1)
            for ci in range(4):
                sp_.dma_start(out=gath[ci, bass.ds(g_idx, 1)], in_=caches[ci][bass.ds(g_pv, 1)]).then_inc(gsem[ci], 16)
        for ci in range(4):
            sp_.wait_ge(gsem[ci], 16 * 256)
    psf = Rot([nc.alloc_psum_tensor("psf%d" % i, [128, 512], F32).ap() for i in range(6)], "psf")
    psb = Rot([nc.alloc_psum_tensor("psb%d" % i, [128, 1024], BF16).ap() for i in range(2)], "psb")

    ident = sb("identf", [128, 128], F32)
    mk.dma(Q1, ident, ident_d, writes=["ident"])
    epsc = sb("epsc", [128, 1], F32)
    mk.op("dve", lambda e: e.memset(epsc, EPS), writes=["epsc"])
    onec = sb("onec", [128, 1], F32)
    mk.op("dve", lambda e: e.memset(onec, 1.0), writes=["onec"])
    identb = sb("identb", [128, 128], BF16)
    mk.op("dve", lambda e: e.tensor_copy(identb, ident), reads=["ident"], writes=["identb"])
    cpy_i = [0]
    m_keep = sb_mark()
    o_glaT = sb("o_glaT", [128, 8, 10 * 128], BF16)
    o_nsaT = sb("o_nsaT", [128, 8, 10 * 128], BF16)

    def evac(out, in_, ps_key, writes, reads=()):
        cpy_i[0] += 1
        if cpy_i[0] % 2:
            mk.op("act", lambda e: e.copy(out, in_), reads=reads, writes=[ps_key] + list(writes))
        else:
            mk.op("dve", lambda e: e.tensor_copy(out, in_), reads=reads, writes=[ps_key] + list(writes))

    SC = 128.0 ** -0.5

    def ld(name, dram_ap, shape, dt=F32, eng="sp"):
        t_ = sb(name, shape, dt)
        mk.dma(eng, t_, dram_ap, writes=[name])
        return t_

    def transpose_into(dst, src_list, reads, wkey, idn=None, n_in=128):
        bank, kb = psf.next()

        def tr(e):
            for j, s_ in enumerate(src_list):
                i = e.transpose(bank[:, j * 128:(j + 1) * 128], s_, ident)
            return i
        mk.op("pe", tr, reads=list(reads) + ["ident"], writes=[kb])
        evac(dst, bank[:, 0:128 * len(src_list)].rearrange("p (j n) -> p j n", j=len(src_list)), kb, [wkey])


    m_ph12 = sb_mark()
    xnT = sb("xnT", [128, 16, NT * 128], BF16)
    g1b = sb("g1b", [128, D], F32)
    mk.dma(Q1, g1b, norm1_g.partition_broadcast(128), writes=["g1b"])
    xr = rot("xt", [128, D], F32, 2)
    xnr = rot("xn", [128, D], F32, 2)
    junk = sb("junk", [128, D], F32)
    ssr = rot("ss", [128, 2], F32, 2)
    for t in range(NT):
        xt, kx = xr.next()
        xn, kn_ = xnr.next()
        ss, ks = ssr.next()
        mk.dma(Q1, xt, xbuf[t * 128:(t + 1) * 128, :], writes=[kx])
        mk.op("act", lambda e: e.activation(junk, xt, AF.Square), reads=[kx], writes=["junk"])
        mk.op("dve", lambda e: e.reduce_sum(ss[:, 0:1], junk, axis=AX.X), reads=["junk"], writes=[ks])
        mk.op("act", lambda e: e.activation(ss[:, 1:2], ss[:, 0:1], AF.Sqrt, bias=epsc[:, 0:1], scale=1.0 / D),
              reads=[ks, "epsc"], writes=[ks])
        mk.op("dve", lambda e: e.reciprocal(ss[:, 1:2], ss[:, 1:2]), reads=[ks], writes=[ks])
        mk.op("dve", lambda e: e.scalar_tensor_tensor(xn, xt, ss[:, 1:2], g1b, op0=ALU.mult, op1=ALU.mult),
              reads=[kx, ks, "g1b"], writes=[kn_])
        for c4 in range(4):
            bank, kb = psf.next()

            def tr(e, c4=c4, bank=bank, xn=xn):
                for j in range(4):
                    c = c4 * 4 + j
                    i = e.transpose(bank[:, j * 128:(j + 1) * 128], xn[:, c * 128:(c + 1) * 128], ident)
                return i
            mk.op("pe", tr, reads=[kn_, "ident"], writes=[kb])
            evac(xnT[:, c4 * 4:(c4 + 1) * 4, t * 128:(t + 1) * 128],
                 bank.rearrange("p (j n) -> p j n", j=4), kb, [("xnT", t)])

    wr = rot("wb", [128, 16, 512], BF16, 2)
    str_ = rot("stg", [128, 512], F32, 4)
    ncb = (NCOLS + 511) // 512
    for cb in range(ncb):
        c0 = cb * 512
        n = min(512, NCOLS - c0)
        c1 = c0 + n
        prev_need = (c0 < C_GR and c1 > C_GK) or (c0 < C_NQ and c1 > C_GA) or (c0 < C_NG and c1 > C_KV)
        wb, kw = wr.next()
        for c in range(16):
            mk.dma("pool", wb[:, c, :n], w_in[c * 128:(c + 1) * 128, c0:c1], writes=[(kw, c)])
        for t in (range(NT) if prev_need else range(QT0, NT)):
            bank, kb = psf.next()

            def mm(e, bank=bank, wb=wb, t=t, n=n):
                for c in range(16):
                    i = e.matmul(bank[:, :n], xnT[:, c, t * 128:(t + 1) * 128], wb[:, c, :n],
                                 start=(c == 0), stop=(c == 15))
                return i
            mk.op("pe", mm, reads=[(kw, c) for c in range(16)] + [("xnT", t)], writes=[kb])
            st, kst = str_.next()
            evac(st[:, :n], bank[:, :n], kb, [kst])
            mk.dma(Q1, proj[t * 128:(t + 1) * 128, c0:c1], st[:, :n], reads=[kst], writes=[("proj", t, cb)])
    mk.barrier()
    sb_release(m_ph12)
    if stage == 0:
        while live:
            live.pop().__exit__(None, None, None)
        return nc, mk

    m_gla = sb_mark()
    wa2 = ld("wa2", gla_w_a2, [16, 512])
    ba2b = ld("ba2b", gla_b_a2.partition_broadcast(128), [128, 512])
    gob = ld("gob", gla_onorm_g.partition_broadcast(128), [128, 256])
    Ucp = ld("Ucp", ucum_p, [128, 128])
    Ucs = ld("Ucs", ucum_s, [128, 128])
    c01p = ld("c01p", caus_p, [128, 128])
    c01s = ld("c01s", caus_s, [128, 128])
    uendp = ld("uendp", uend_p, [128, 1])
    uends = ld("uends", uend_s, [128, 4])
    seqm = ld("seqm", seqmask, [128, 4])
    Sp = sb("Sp", [128, 4, 256], F32)
    Spb = sb("Spb", [128, 4, 256], BF16)
    mk.op("dve", lambda e: e.memset(Sp, 0.0), writes=[("S", 0, h) for h in range(4)])
    mk.op("pool", lambda e: e.memset(Spb, 0.0), writes=[("Sb", 0, h) for h in range(4)])
    Ss = sb("Ss", [128, 4, 4, 256], F32)
    Ssb = sb("Ssb", [128, 4, 4, 256], BF16)
    for s in range(4):
        mk.dma("sp", Ss[:, s], state_gla_c[s].rearrange("h d e -> d h e"), writes=[("S", 1 + s, h) for h in range(4)])
        mk.op("pool", lambda e: e.tensor_copy(Ssb[:, s], Ss[:, s]), reads=[("S", 1 + s, h) for h in range(4)],
              writes=[("Sb", 1 + s, h) for h in range(4)])
    qeTm = sb("qeTm", [128, 4, 4, 128], BF16)
    mk.op("pool", lambda e: e.memset(qeTm, 0.0), writes=["qeTm"])
    a_r = rot("ga", [128, 16], F32, 2)
    aT_r = rot("gaT", [16, 128], F32, 2)
    k_r = rot("gk", [128, 512], F32, 2)
    v_r = rot("gv", [128, 1024], F32, 2)
    q_r = rot("gq", [128, 512], F32, 2)
    r_r = rot("gr", [128, 1024], F32, 1)
    ln_r = rot("gln", [128, 512], F32, 2)
    e1_r = rot("ge1", [128, 512], F32, 1)
    e2_r = rot("ge2", [128, 512], F32, 2)
    eb_r = rot("geb", [128, 16], F32, 2)
    qe_r = rot("gqe", [128, 512], F32, 1)
    ke_r = rot("gke", [128, 512], F32, 2)
    keb_r = rot("gkeb", [128, 512], BF16, 2)
    kem = sb("gkem", [128, 4, 512], BF16)
    vb_r = rot("gvb", [128, 1024], BF16, 2)
    qeT_r = rot("gqeT", [128, 4, 128], BF16, 2)
    keT_r = rot("gkeT", [128, 4, 128], BF16, 2)
    att_r = rot("gatt", [128, 4, 128], BF16, 2)
    osb_r = rot("gosb", [128, 4, 256], F32, 1)
    jg = sb("jg", [128, 4, 256], F32)
    s8_r = rot("gs8", [128, 8], F32, 2)
    og_r = rot("gog", [128, 1024], F32, 1)
    SCQ = 128.0 ** -0.5
    for t in range(NT):
        isq = t >= QT0
        smp = (t == 16)
        G = 4 if smp else 1
        qi = t - QT0
        rows = slice(t * 128, (t + 1) * 128)
        a_t, ka = a_r.next()
        k_t, kk_ = k_r.next()
        v_t, kv_ = v_r.next()
        mk.dma("sp", a_t, proj[rows, C_GA:C_NQ], writes=[ka])
        mk.dma("sp", k_t, proj[rows, C_GK:C_GV], writes=[kk_])
        mk.dma("sp", v_t, proj[rows, C_GV:C_GR], writes=[kv_])
        if isq:
            q_t, kq_ = q_r.next()
            r_t, kr_ = r_r.next()
            mk.dma("sp", q_t, proj[rows, C_GQ:C_GK], writes=[kq_])
            mk.dma("sp", r_t, proj[rows, C_GR:C_GA], writes=[kr_])
        aT, kaT = aT_r.next()
        bank, kb = psf.next()
        mk.op("pe", lambda e: e.transpose(bank[0:16, 0:128], a_t, ident), reads=[ka, "ident"], writes=[kb])
        evac(aT, bank[0:16, 0:128], kb, [kaT])
        bank, kb = psf.next()
        mk.op("pe", lambda e: e.matmul(bank, aT, wa2, start=True, stop=True), reads=[kaT, "wa2"], writes=[kb])
        lnv, kln = ln_r.next()
        mk.op("dve", lambda e: e.tensor_tensor(lnv, bank, ba2b, ALU.add), reads=["ba2b"], writes=[kb, kln])
        mk.op("act", lambda e: e.activation(lnv, lnv, AF.Exp, scale=-1.0), reads=[kln], writes=[kln])
        mk.op("act", lambda e: e.activation(lnv, lnv, AF.Ln, bias=onec[:, 0:1]), reads=[kln, "onec"], writes=[kln])
        bank, kb = psf.next()
        U = Ucs if smp else Ucp
        mk.op("pe", lambda e: e.matmul(bank, U, lnv, start=True, stop=True), reads=[kln, "Ucp", "Ucs"], writes=[kb])
        e2, ke2 = e2_r.next()
        mk.op("act", lambda e: e.activation(e2, bank, AF.Exp, scale=-1.0), writes=[kb, ke2])
        if isq:
            e1, ke1 = e1_r.next()
            mk.op("act", lambda e: e.activation(e1, bank, AF.Exp), writes=[kb, ke1])
        bank, kb = psf.next()
        uend = uends if smp else uendp

        def mmbl(e, bank=bank, lnv=lnv, uend=uend, G=G):
            for h in range(4):
                i = e.matmul(bank[:, h * G:(h + 1) * G], lnv[:, h * 128:(h + 1) * 128], uend, start=True, stop=True)
            return i
        mk.op("pe", mmbl, reads=[kln, "uendp", "uends"], writes=[kb])
        eb, keb_ = eb_r.next()
        mk.op("act", lambda e: e.activation(eb[:, 0:4 * G], bank[:, 0:4 * G], AF.Exp), writes=[kb, keb_])
        if t == 8:
            dbg("lnv", lnv, [kln]); dbg("e2", e2, [ke2]); dbg("eb", eb, [keb_]); dbg("aT", aT, [kaT]); dbg("a", a_t, [ka])
        ke, kke = ke_r.next()
        keb, kkeb = keb_r.next()
        vb, kvb = vb_r.next()
        mk.op("dve", lambda e: e.tensor_tensor(ke, k_t, e2, ALU.mult), reads=[kk_, ke2], writes=[kke])
        mk.op("pool", lambda e: e.tensor_copy(keb, ke), reads=[kke], writes=[kkeb])
        mk.op("pool", lambda e: e.tensor_copy(vb, v_t), reads=[kv_], writes=[kvb])
        if smp:
            for s in range(4):
                mk.op("dve", lambda e: e.tensor_scalar(kem[:, s, :], ke, seqm[:, s:s + 1], None, op0=ALU.mult),
                      reads=[kke, "seqm"], writes=[("kem", s)])
        if isq:
            qe, kqe = qe_r.next()
            mk.op("dve", lambda e: e.scalar_tensor_tensor(qe, q_t, SCQ, e1, op0=ALU.mult, op1=ALU.mult),
                  reads=[kq_, ke1], writes=[kqe])
            qeT, kqeT = qeT_r.next()
            keT, kkeT = keT_r.next()
            transpose_into(qeT, [qe[:, h * 128:(h + 1) * 128] for h in range(4)], [kqe], kqeT)
            transpose_into(keT, [ke[:, h * 128:(h + 1) * 128] for h in range(4)], [kke], kkeT)
            bank, kb = psf.next()

            def mmatt(e, bank=bank, keT=keT, qeT=qeT):
                for h in range(4):
                    i = e.matmul(bank[:, h * 128:(h + 1) * 128], keT[:, h, :], qeT[:, h, :], start=True, stop=True)
                return i
            mk.op("pe", mmatt, reads=[kqeT, kkeT], writes=[kb])
            att, katt = att_r.next()
            c01 = c01s if smp else c01p
            mk.op("dve", lambda e: e.tensor_tensor(att, bank.rearrange("p (h n) -> p h n", h=4),
                                                   c01.unsqueeze(1).to_broadcast([128, 4, 128]), ALU.mult),
                  reads=["c01p", "c01s"], writes=[kb, katt])
            if smp:
                for s in range(4):
                    mk.op("pool", lambda e: e.tensor_copy(qeTm[:, s, :, 32 * s:32 * s + 32], qeT[:, :, 32 * s:32 * s + 32]),
                          reads=[kqeT], writes=["qeTm"])
            osb, kosb = osb_r.next()
            for hp in range(2):
                bank, kb = psf.next()

                def mmo(e, bank=bank, hp=hp, att=att, vb=vb, qeT=qeT):
                    for h in (2 * hp, 2 * hp + 1):
                        o_ = bank[:, (h % 2) * 256:(h % 2 + 1) * 256]
                        e.matmul(o_, att[:, h, :], vb[:, h * 256:(h + 1) * 256], start=True, stop=False)
                        if smp:
                            for s in range(4):
                                i = e.matmul(o_, qeTm[:, s, h, :], Ssb[:, s, h, :], start=False, stop=(s == 3))
                        else:
                            i = e.matmul(o_, qeT[:, h, :], Spb[:, h, :], start=False, stop=True)
                    return i
                sbk = [("Sb", (1 + s if smp else 0), h) for h in (2 * hp, 2 * hp + 1) for s in range(G)]
                mk.op("pe", mmo, reads=[katt, kvb, kqeT, "qeTm"] + sbk, writes=[kb])
                evac(osb[:, 2 * hp:2 * hp + 2, :], bank.rearrange("p (h n) -> p h n", h=2), kb, [(kosb, hp)])
        for h in range(4):
            for s in range(G):
                si = 1 + s if smp else 0
                S_ = Ss[:, s, h, :] if smp else Sp[:, h, :]
                Sb_ = Ssb[:, s, h, :] if smp else Spb[:, h, :]
                lhs = kem[:, s, h * 128:(h + 1) * 128] if smp else keb[:, h * 128:(h + 1) * 128]
                bank, kb = psf.next()
                mk.op("pe", lambda e: e.matmul(bank[:, 0:256], lhs, vb[:, h * 256:(h + 1) * 256], start=True, stop=True),
                      reads=[kkeb, kvb, ("kem", s)], writes=[kb])
                ebc = eb[:, h * G + s:h * G + s + 1]
                mk.op("dve", lambda e: e.tensor_scalar(S_, S_, ebc, None, op0=ALU.mult), reads=[keb_, ("S", si, h)],
                      writes=[("S", si, h)])
                mk.op("dve", lambda e: e.scalar_tensor_tensor(S_, bank[:, 0:256], ebc, S_, op0=ALU.mult, op1=ALU.add),
                      reads=[keb_, ("S", si, h)], writes=[kb, ("S", si, h)])
                mk.op("act", lambda e: e.copy(Sb_, S_), reads=[("S", si, h)], writes=[("Sb", si, h)])
        if t == 8:
            dbg("ke", ke, [kke]); dbg("S8", Sp, [("S", 0, h) for h in range(4)]); dbg("osb", osb, [(kosb, 0), (kosb, 1)])
            dbg("qe", qe, [kqe]); dbg("e1", e1, [ke1])
        if isq:
            s8, ks8 = s8_r.next()
            og, kog = og_r.next()
            rdo = [(kosb, 0), (kosb, 1)]
            mk.op("act", lambda e: e.activation(jg, osb, AF.Square), reads=rdo, writes=["jg"])
            mk.op("dve", lambda e: e.reduce_sum(s8[:, 0:4], jg, axis=AX.X), reads=["jg"], writes=[ks8])
            mk.op("act", lambda e: e.activation(s8[:, 4:8], s8[:, 0:4], AF.Sqrt, bias=epsc[:, 0:1], scale=1.0 / 256),
                  reads=[ks8, "epsc"], writes=[ks8])
            mk.op("dve", lambda e: e.reciprocal(s8[:, 4:8], s8[:, 4:8]), reads=[ks8], writes=[ks8])
            og4 = og.rearrange("p (h n) -> p h n", h=4)
            mk.op("dve", lambda e: e.tensor_tensor(og4, osb, s8[:, 4:8].unsqueeze(2).to_broadcast([128, 4, 256]), ALU.mult),
                  reads=rdo + [ks8], writes=[kog])
            mk.op("pool", lambda e: e.tensor_tensor(og4, og4, gob.unsqueeze(1).to_broadcast([128, 4, 256]), ALU.mult),
                  reads=[kog, "gob"], writes=[kog])
            mk.op("act", lambda e: e.activation(r_t, r_t, AF.Silu), reads=[kr_], writes=[kr_])
            mk.op("dve", lambda e: e.tensor_tensor(og, og, r_t, ALU.mult), reads=[kog, kr_], writes=[kog])
            for c4 in range(2):
                transpose_into(o_glaT[:, c4 * 4:(c4 + 1) * 4, qi * 128:(qi + 1) * 128],
                               [og[:, (c4 * 4 + j) * 128:(c4 * 4 + j + 1) * 128] for j in range(4)], [kog], ("oglaT", qi))
        if t == 15:
            mk.dma("sp", gla_out[0].rearrange("h d e -> d h e"), Sp, reads=[("S", 0, h) for h in range(4)], writes=["glao0"])
    for s in range(4):
        mk.dma("sp", gla_out[1 + s].rearrange("h d e -> d h e"), Ss[:, s], reads=[("S", 1 + s, h) for h in range(4)],
               writes=[("glao", s)])
    mk.barrier()
    sb_release(m_gla)
    if stage <= 2:
        mk.barrier()
        return nc, mk

    KTn = sb("KTn", [128, 2, 2, 128], BF16)
    VVn = sb("VVn", [128, 2, 2, 128], BF16)
    kcT = sb("kcT", [128, 2, 128], BF16)
    vc = sb("vc", [128, 2, 128], F32)
    cw = {}
    for nm in ("k", "v"):
        w1 = sb("w1" + nm, [128, 32, 128], BF16)
        for r8 in range(8):
            mk.dma("pool", w1[:, r8 * 4:(r8 + 1) * 4, :],
                   CMPW[nm]["w1"][r8 * 4:(r8 + 1) * 4].rearrange("r d f -> d r f"), writes=[("w1" + nm, r8)])
        w2 = sb("w2" + nm, [128, 128], BF16)
        mk.dma("pool", w2, CMPW[nm]["w2"], writes=["w2" + nm])
        b1 = ld("b1" + nm, CMPW[nm]["b1"].rearrange("(f o) -> f o", o=1), [128, 1])
        b2b = ld("b2b" + nm, CMPW[nm]["b2"].partition_broadcast(128), [128, 128])
        pe_ = ld("pe" + nm, CMPW[nm]["pe"], [32, 128])
        peT = sb("peT" + nm, [128, 32], BF16)
        bank, kb = psf.next()
        mk.op("pe", lambda e: e.transpose(bank[:, 0:32], pe_, ident[0:32, 0:32]), reads=["pe" + nm, "ident"], writes=[kb])
        evac(peT, bank[:, 0:32], kb, ["peT" + nm])
        c1 = sb("c1" + nm, [128, 1], F32)
        bank, kb = psf.next()

        def mmc(e, bank=bank, w1=w1, peT=peT):
            for rr in range(32):
                i = e.matmul(bank[:, 0:1], w1[:, rr, :], peT[:, rr:rr + 1], start=(rr == 0), stop=(rr == 31))
            return i
        mk.op("pe", mmc, reads=[("w1" + nm, r8) for r8 in range(8)] + ["peT" + nm], writes=[kb])
        mk.op("dve", lambda e: e.tensor_tensor(c1, bank[:, 0:1], b1, ALU.add), reads=["b1" + nm], writes=[kb, "c1" + nm])
        cw[nm] = dict(w1=w1, w2=w2, b2b=b2b, c1=c1)
    gkc = ld("gkc", k_cmp_norm_g.partition_broadcast(128), [128, 128])
    gTr = rot("gTc", [128, 512], BF16, 2)
    o2r = rot("o2c", [128, 128], F32, 2)
    sc3 = rot("sc3", [128, 4], F32, 2)
    jc = sb("jc", [128, 128], F32)

    def compress(nm, rowsT, nblk, rreads, out_fn):
        W = cw[nm]
        bank, kb = psf.next()

        def mm1(e):
            for rr in range(32):
                i = e.matmul(bank[:, 0:nblk], W["w1"][:, rr, :], rowsT[:, rr:rr + 16 * (nblk - 1) + 1:16],
                             start=(rr == 0), stop=(rr == 31))
            return i
        mk.op("pe", mm1, reads=list(rreads) + [("w1" + nm, r8) for r8 in range(8)], writes=[kb])
        gT, kg = gTr.next()
        mk.op("act", lambda e: e.activation(gT[:, 0:nblk], bank[:, 0:nblk], AF.Gelu_apprx_tanh, bias=W["c1"][:, 0:1]),
              reads=["c1" + nm], writes=[kb, kg])
        for mt in range((nblk + 127) // 128):
            m = min(128, nblk - mt * 128)
            bank2, kb2 = psf.next()
            mk.op("pe", lambda e: e.matmul(bank2[0:m, 0:128], gT[:, mt * 128:mt * 128 + m], W["w2"], start=True, stop=True),
                  reads=[kg, "w2" + nm], writes=[kb2])
            o2, ko2 = o2r.next()
            mk.op("dve", lambda e: e.tensor_tensor(o2[0:m], bank2[0:m, 0:128], W["b2b"][0:m], ALU.add),
                  reads=["b2b" + nm], writes=[kb2, ko2])
            out_fn(mt, m, o2, ko2)

    def rms_rows(o2, ko2, m, gain, gkey, out, okey):
        s3, ks3 = sc3.next()
        mk.op("act", lambda e: e.activation(jc[0:m], o2[0:m], AF.Square), reads=[ko2], writes=["jc"])
        mk.op("dve", lambda e: e.reduce_sum(s3[0:m, 0:1], jc[0:m], axis=AX.X), reads=["jc"], writes=[ks3])
        mk.op("act", lambda e: e.activation(s3[0:m, 1:2], s3[0:m, 0:1], AF.Sqrt, bias=epsc[0:m, 0:1], scale=1.0 / 128),
              reads=[ks3, "epsc"], writes=[ks3])
        mk.op("dve", lambda e: e.reciprocal(s3[0:m, 1:2], s3[0:m, 1:2]), reads=[ks3], writes=[ks3])
        mk.op("dve", lambda e: e.scalar_tensor_tensor(out[0:m], o2[0:m], s3[0:m, 1:2], gain[0:m], op0=ALU.mult, op1=ALU.mult),
              reads=[ko2, ks3, gkey], writes=[okey])

    kcr = rot("kcn", [128, 128], F32, 2)
    m_smp_keep = sb_mark()
    gqb = ld("gqb", q_norm_g.partition_broadcast(128), [128, 128])
    mmap = ld("mmap", mmap_p, [128, 32])
    trim = ld("trim", tri_mask, [128, 128])
    qraw_r = rot("nq", [128, 1024], F32, 1)
    g24_r = rot("ng", [128, 24], F32, 2)
    cs2_r = rot("ncs", [128, 2, 64], F32, 2)
    s16_r = rot("ns16", [128, 16], F32, 2)
    qnT = sb("nqnT", [128, 8, 128], BF16)
    qrT = sb("nqrT", [128, 8, 128], BF16)
    pT_r = rot("npT", [128, 16, 128], BF16, 2)
    pc_r = rot("npc", [128, 128], F32, 2)
    pcT = sb("npcT", [128, 4, 128], F32)
    st_r = rot("nst", [128, 8], F32, 4)
    coef = sb("ncoef", [128, 2, 24], F32)
    scb = sb("nscb", [128, 4, 32], F32)
    m8 = sb("nm8", [128, 2, 8], F32)
    onsa = sb("nonsa", [128, 8, 128], F32)
    otmp = sb("notmp", [128, 4, 128], F32)

    m_prompt_only = sb_mark()
    KTa = sb("KTa", [128, 2, 2, 2048], BF16)
    VV = sb("VV", [128, 16, 2, 2, 128], BF16)
    m_ktc = sb_mark()
    KTc = sb("KTc", [128, 2, 2, 2048], BF16)
    m_ph3 = sb_mark()
    gkb = sb("gkb", [128, 2, 128], F32)
    mk.dma("sp", gkb[:, 0, :], k_slc_norm_g.partition_broadcast(128), writes=["gkb0"])
    mk.dma("sp", gkb[:, 1, :], k_swa_norm_g.partition_broadcast(128), writes=["gkb1"])
    kvr = rot("kv", [128, 1536], F32, 2)
    knr = rot("kn", [128, 2, 2, 128], F32, 2)
    csr = rot("cs", [128, 2, 64], F32, 2)
    j4 = sb("j4", [128, 2, 2, 128], F32)
    s4r = rot("s4", [128, 8], F32, 2)
    tmpr = rot("rt", [128, 4, 2, 2, 64], F32, 2)
    for t in range(NT):
        kv, kkv = kvr.next()
        kn, kkn = knr.next()
        cs, kcs = csr.next()
        s4, ks4 = s4r.next()
        tp, ktp = tmpr.next()
        mk.dma("sp", kv, proj[t * 128:(t + 1) * 128, C_KV:C_NG],
               reads=[("proj", t, cb) for cb in range(C_KV // 512, (C_NG - 1) // 512 + 1)], writes=[kkv])
        mk.dma("sp", cs[:, 0, :], rope_cos[t * 128:(t + 1) * 128, :], writes=[(kcs, 0)])
        mk.dma("sp", cs[:, 1, :], rope_sin[t * 128:(t + 1) * 128, :], writes=[(kcs, 1)])
        kv4 = kv.rearrange("p (a g d) -> p a g d", a=6, g=2)
        kk = kv4[:, 2:6:2]
        mk.op("act", lambda e: e.activation(j4, kk, AF.Square), reads=[kkv], writes=["j4"])
        mk.op("dve", lambda e: e.reduce_sum(s4[:, 0:4].rearrange("p (a g) -> p a g", a=2), j4, axis=AX.X),
              reads=["j4"], writes=[ks4])
        mk.op("act", lambda e: e.activation(s4[:, 4:8], s4[:, 0:4], AF.Sqrt, bias=epsc[:, 0:1], scale=1.0 / 128),
              reads=[ks4, "epsc"], writes=[ks4])
        mk.op("dve", lambda e: e.reciprocal(s4[:, 4:8], s4[:, 4:8]), reads=[ks4], writes=[ks4])
        rb = s4[:, 4:8].rearrange("p (a g) -> p a g", a=2).unsqueeze(3).to_broadcast([128, 2, 2, 128])
        mk.op("dve", lambda e: e.tensor_tensor(kn, kk, rb, ALU.mult), reads=[kkv, ks4], writes=[kkn])
        gb_ = gkb.unsqueeze(2).to_broadcast([128, 2, 2, 128])
        mk.op("pool", lambda e: e.tensor_tensor(kn, kn, gb_, ALU.mult), reads=[kkn, "gkb0", "gkb1"], writes=[kkn])
        cosb = cs[:, 0, :].unsqueeze(1).unsqueeze(1).to_broadcast([128, 2, 2, 64])
        sinb = cs[:, 1, :].unsqueeze(1).unsqueeze(1).to_broadcast([128, 2, 2, 64])
        x1, x2 = kn[:, :, :, 0:64], kn[:, :, :, 64:128]
        rd = [kkn, (kcs, 0), (kcs, 1)]
        mk.op("dve", lambda e: e.tensor_tensor(tp[:, 0], x1, cosb, ALU.mult), reads=rd, writes=[(ktp, 0)])
        mk.op("pool", lambda e: e.tensor_tensor(tp[:, 1], x2, sinb, ALU.mult), reads=rd, writes=[(ktp, 1)])
        mk.op("dve", lambda e: e.tensor_tensor(tp[:, 2], x2, cosb, ALU.mult), reads=rd, writes=[(ktp, 2)])
        mk.op("pool", lambda e: e.tensor_tensor(tp[:, 3], x1, sinb, ALU.mult), reads=rd, writes=[(ktp, 3)])
        mk.op("dve", lambda e: e.tensor_tensor(kk[:, :, :, 0:64], tp[:, 0], tp[:, 1], ALU.subtract),
              reads=[(ktp, 0), (ktp, 1)], writes=[kkv])
        mk.op("dve", lambda e: e.tensor_tensor(kk[:, :, :, 64:128], tp[:, 2], tp[:, 3], ALU.add),
              reads=[(ktp, 2), (ktp, 3)], writes=[kkv])
        mk.dma("sp", kv_out[t * 128:(t + 1) * 128, :], kv, reads=[kkv], writes=[("kvo", t)])
        bank, kb = psf.next()
        if t < 16:
            def tr1(e, bank=bank, kv4=kv4):
                for j in range(4):
                    i = e.transpose(bank[:, j * 128:(j + 1) * 128], kv4[:, j // 2, j % 2, :], ident)
                return i
            mk.op("pe", tr1, reads=[kkv, "ident"], writes=[kb])
            evac(KTc[:, :, :, t * 128:(t + 1) * 128], bank.rearrange("p (a g n) -> p a g n", a=2, g=2), kb, [("KT", 0, t)])
            bank, kb = psf.next()
        dst = KTa[:, :, :, t * 128:(t + 1) * 128] if t < 16 else KTn

        def tr2(e, bank=bank, kv4=kv4):
            for j in range(4):
                i = e.transpose(bank[:, j * 128:(j + 1) * 128], kv4[:, 2 + 2 * (j // 2), j % 2, :], ident)
            return i
        mk.op("pe", tr2, reads=[kkv, "ident"], writes=[kb])
        evac(dst, bank.rearrange("p (a g n) -> p a g n", a=2, g=2), kb, [("KT", 1, t)])
        vdst = VV[:, t] if t < 16 else VVn
        mk.op("pool", lambda e: e.tensor_copy(vdst, kv4[:, 3:6:2]), reads=[kkv], writes=[("VV", t)])
    for kv_i in range(2):
        for s in range(4):
            mk.dma("sp", swa_out[kv_i, s, 0:508, :], state_swa_c[kv_i, s, 4:512, :], writes=[("swao", kv_i, s, 0)])
            mk.dma("sp", swa_out[kv_i, s, 508:512, :], kv_out[2048 + 32 * s:2048 + 32 * s + 4, 1024 + 256 * kv_i:1280 + 256 * kv_i],
                   reads=[("kvo", 16)], writes=[("swao", kv_i, s, 1)])
    mk.barrier()
    sb_release(m_ph3)
    for g in range(2):
        def outk(mt, m, o2, ko2, g=g):
            kc_, kkc = kcr.next()
            rms_rows(o2, ko2, m, gkc, "gkc", kc_, kkc)
            bank, kb = psf.next()
            mk.op("pe", lambda e: e.transpose(bank[:, 0:m], kc_[0:m, :], ident[0:m, 0:m]), reads=[kkc, "ident"], writes=[kb])
            evac(kcT[:, g, 0:m], bank[:, 0:m], kb, [("kcT", g)])

        def outv(mt, m, o2, ko2, g=g):
            mk.op("pool", lambda e: e.tensor_copy(vc[0:m, g, :], o2[0:m]), reads=[ko2], writes=[("vc", g)])
        compress("k", KTc[:, 0, g, :], 127, [("KT", 0, t) for t in range(16)], outk)
        compress("v", KTc[:, 1, g, :], 127, [("KT", 0, t) for t in range(16)], outv)
    mk.barrier()
    sb_release(m_ktc)
    if stage <= 1:
        mk.barrier()
        return nc, mk


    psR = Rot(psf.aps[0:3], "psf")
    B_CMP, B_SLC, B_SWA = psf.aps[3], psf.aps[4], psf.aps[5]
    K_CMP, K_SLC, K_SWA = ("psf", 3), ("psf", 4), ("psf", 5)
    qn_ = sb("nqn", [128, 8, 128], F32)
    qr_ = sb("nqr", [128, 8, 128], F32)
    jq = sb("njq", [128, 8, 128], F32)
    rt4 = sb("nrt4", [128, 4, 8, 64], F32)
    tb_r = rot("ntb", [128, 2, 127], F32, 2)
    ts_r = rot("nts", [128, 2, 32], F32, 2)
    sw_r = rot("nsw", [128, 640], F32, 2)
    sS_r = rot("nsS", [128, 2048], F32, 2)
    pb_r = rot("npb", [128, 2048], BF16, 2)
    def softmax_rows(sS, ksS, n, p_out, kp, rs_col, krs):
        st, kst = st_r.next()
        mk.op("dve", lambda e: e.reduce_max(st[:, 0:1], sS[:, 0:n], axis=AX.X), reads=[ksS], writes=[kst])
        mk.op("dve", lambda e: e.tensor_scalar(st[:, 1:2], st[:, 0:1], -SC, None, op0=ALU.mult), reads=[kst], writes=[kst])
        mk.op("act", lambda e: e.activation(p_out[:, 0:n], sS[:, 0:n], AF.Exp, bias=st[:, 1:2], scale=SC),
              reads=[ksS, kst], writes=[kp])
        mk.op("dve", lambda e: e.reduce_sum(st[:, 2:3], p_out[:, 0:n], axis=AX.X), reads=[kp], writes=[kst])
        mk.op("dve", lambda e: e.reciprocal(rs_col, st[:, 2:3]), reads=[kst], writes=[krs])

    def pv_bf16(pb, kp, ntile, v_fn, vreads, o_ap, o_key):
        pT, kpT = pT_r.next()
        for k0 in range(0, ntile, 8):
            nk = min(8, ntile - k0)
            bank, kb = psb.next()

            def trp(e, bank=bank, k0=k0, nk=nk):
                for j in range(nk):
                    i = e.transpose(bank[:, j * 128:(j + 1) * 128], pb[:, (k0 + j) * 128:(k0 + j + 1) * 128], identb)
                return i
            mk.op("pe", trp, reads=[kp, "identb"], writes=[kb])
            evac(pT[:, k0:k0 + nk, :], bank[:, 0:nk * 128].rearrange("p (j n) -> p j n", j=nk), kb, [(kpT, k0)])

        def mmpv(e):
            for kt in range(ntile):
                i = e.matmul(o_ap, pT[:, kt, :], v_fn(kt), start=(kt == 0), stop=(kt == ntile - 1))
            return i
        mk.op("pe", mmpv, reads=[(kpT, k0) for k0 in range(0, ntile, 8)] + list(vreads), writes=[o_key])

    def norm_rope_q(qraw, kq, cs2, kcs):
        s16, ks16 = s16_r.next()
        q3 = qraw.rearrange("p (h d) -> p h d", h=8)
        mk.op("act", lambda e: e.activation(jq, q3, AF.Square), reads=[kq], writes=["jq"])
        mk.op("dve", lambda e: e.reduce_sum(s16[:, 0:8], jq, axis=AX.X), reads=["jq"], writes=[ks16])
        mk.op("act", lambda e: e.activation(s16[:, 8:16], s16[:, 0:8], AF.Sqrt, bias=epsc[:, 0:1], scale=1.0 / 128),
              reads=[ks16, "epsc"], writes=[ks16])
        mk.op("dve", lambda e: e.reciprocal(s16[:, 8:16], s16[:, 8:16]), reads=[ks16], writes=[ks16])
        mk.op("dve", lambda e: e.tensor_tensor(qn_, q3, s16[:, 8:16].unsqueeze(2).to_broadcast([128, 8, 128]), ALU.mult),
              reads=[kq, ks16], writes=["qn"])
        mk.op("pool", lambda e: e.tensor_tensor(qn_, qn_, gqb.unsqueeze(1).to_broadcast([128, 8, 128]), ALU.mult),
              reads=["qn", "gqb"], writes=["qn"])
        cosb = cs2[:, 0, :].unsqueeze(1).to_broadcast([128, 8, 64])
        sinb = cs2[:, 1, :].unsqueeze(1).to_broadcast([128, 8, 64])
        x1, x2 = qn_[:, :, 0:64], qn_[:, :, 64:128]
        rd = ["qn", (kcs, 0), (kcs, 1)]
        mk.op("dve", lambda e: e.tensor_tensor(rt4[:, 0], x1, cosb, ALU.mult), reads=rd, writes=[("rt4", 0)])
        mk.op("pool", lambda e: e.tensor_tensor(rt4[:, 1], x2, sinb, ALU.mult), reads=rd, writes=[("rt4", 1)])
        mk.op("dve", lambda e: e.tensor_tensor(rt4[:, 2], x2, cosb, ALU.mult), reads=rd, writes=[("rt4", 2)])
        mk.op("pool", lambda e: e.tensor_tensor(rt4[:, 3], x1, sinb, ALU.mult), reads=rd, writes=[("rt4", 3)])
        mk.op("dve", lambda e: e.tensor_tensor(qr_[:, :, 0:64], rt4[:, 0], rt4[:, 1], ALU.subtract),
              reads=[("rt4", 0), ("rt4", 1)], writes=["qr"])
        mk.op("dve", lambda e: e.tensor_tensor(qr_[:, :, 64:128], rt4[:, 2], rt4[:, 3], ALU.add),
              reads=[("rt4", 2), ("rt4", 3)], writes=["qr"])
        for c4 in range(2):
            transpose_into(qnT[:, c4 * 4:(c4 + 1) * 4, :], [qn_[:, c4 * 4 + j, :] for j in range(4)], ["qn"], ("qnT", c4))
            transpose_into(qrT[:, c4 * 4:(c4 + 1) * 4, :], [qr_[:, c4 * 4 + j, :] for j in range(4)], ["qr"], ("qrT", c4))

    def combine(g):
        cf = coef[:, 0, :]

        def cb(br):
            return cf[:, br * 8 + g * 4:br * 8 + g * 4 + 4].unsqueeze(2).to_broadcast([128, 4, 128])
        dst = onsa[:, g * 4:(g + 1) * 4, :]
        v3 = lambda b: b.rearrange("p (h n) -> p h n", h=4)
        mk.op("dve", lambda e: e.tensor_tensor(dst, v3(B_CMP), cb(0), ALU.mult), reads=["coef"], writes=[K_CMP, ("onsa", g)])
        mk.op("dve", lambda e: e.tensor_tensor(otmp, v3(B_SLC), cb(1), ALU.mult), reads=["coef"], writes=[K_SLC, "otmp"])
        mk.op("pool", lambda e: e.tensor_tensor(dst, dst, otmp, ALU.add), reads=["otmp", ("onsa", g)], writes=[("onsa", g)])
        mk.op("dve", lambda e: e.tensor_tensor(otmp, v3(B_SWA), cb(2), ALU.mult), reads=["coef"], writes=[K_SWA, "otmp"])
        mk.op("pool", lambda e: e.tensor_tensor(dst, dst, otmp, ALU.add), reads=["otmp", ("onsa", g)], writes=[("onsa", g)])

    for t in range(QT0, 16):
        qi = t - QT0
        rows = slice(t * 128, (t + 1) * 128)
        qrows = slice(qi * 128, (qi + 1) * 128)
        qraw, kq = qraw_r.next()
        g24, kg24 = g24_r.next()
        cs2, kcs = cs2_r.next()
        tb, ktb = tb_r.next()
        ts_, kts = ts_r.next()
        sw, ksw = sw_r.next()
        mk.dma("sp", qraw, proj[rows, C_NQ:C_KV], writes=[kq])
        mk.dma("sp", g24, proj[rows, C_NG:C_MG], writes=[kg24])
        mk.dma("sp", cs2[:, 0, :], rope_cos[rows, :], writes=[(kcs, 0)])
        mk.dma("sp", cs2[:, 1, :], rope_sin[rows, :], writes=[(kcs, 1)])
        mk.dma("sp", tb[:, 0, :], cmp_add[qrows, :], writes=[(ktb, 0)])
        mk.dma("sp", tb[:, 1, :], cmp_mul[qrows, :], writes=[(ktb, 1)])
        mk.dma("sp", ts_[:, 0, :], t_sel[qrows, :], writes=[(kts, 0)])
        mk.dma("sp", ts_[:, 1, :], t_inv[qrows, :], writes=[(kts, 1)])
        mk.dma("sp", sw, swa_mask[qrows, :], writes=[ksw])
        norm_rope_q(qraw, kq, cs2, kcs)
        mk.op("act", lambda e: e.activation(coef[:, 0, :], g24, AF.Sigmoid), reads=[kg24], writes=["coef"])
        mk.op("dve", lambda e: e.memset(coef[:, 1, :], 1.0), writes=["rs"])
        nkt = t + 1
        for g in range(2):
            for h4 in range(4):
                hh = g * 4 + h4
                bank, kb = psR.next()
                mk.op("pe", lambda e: e.matmul(bank[:, 0:127], qnT[:, hh, :], kcT[:, g, 0:127], start=True, stop=True),
                      reads=[("qnT", hh // 4), ("kcT", g)], writes=[kb])
                sS, ksS = sS_r.next()
                mk.op("dve", lambda e: e.tensor_tensor(sS[:, 0:127], bank[:, 0:127], tb[:, 0, :], ALU.add),
                      reads=[(ktb, 0)], writes=[kb, ksS])
                pc, kpc = pc_r.next()
                st2, kst2 = st_r.next()
                softmax_rows(sS, ksS, 127, pc, kpc, st2[:, 4:5], kst2)
                mk.op("dve", lambda e: e.scalar_tensor_tensor(pc[:, 0:127], pc[:, 0:127], st2[:, 4:5], tb[:, 1, :],
                                                              op0=ALU.mult, op1=ALU.mult),
                      reads=[kpc, kst2, (ktb, 1)], writes=[kpc])
                bank, kb = psR.next()
                mk.op("pe", lambda e: e.transpose(bank[0:127, 0:128], pc[:, 0:127], ident), reads=[kpc, "ident"], writes=[kb])
                evac(pcT[0:127, h4, :], bank[0:127, 0:128], kb, [("pcT", h4)])
                mk.op("pe", lambda e: e.matmul(B_CMP[:, h4 * 128:(h4 + 1) * 128], pcT[0:127, h4, :], vc[0:127, g, :],
                                               start=True, stop=True),
                      reads=[("pcT", h4), ("vc", g)], writes=[K_CMP])
            bank, kb = psR.next()

            def mmsc(e, bank=bank):
                for h4 in range(4):
                    i = e.matmul(bank[:, 0:32], pcT[0:127, h4, :], mmap[0:127, :], start=(h4 == 0), stop=(h4 == 3))
                return i
            mk.op("pe", mmsc, reads=[("pcT", h4) for h4 in range(4)] + ["mmap"], writes=[kb])
            mk.op("dve", lambda e: e.tensor_tensor(scb[:, 0, :], bank[:, 0:32], ts_[:, 0, :], ALU.add),
                  reads=[(kts, 0)], writes=[kb, "scb"])
            mk.op("dve", lambda e: e.max(out=m8[:, 0, :], in_=scb[:, 0, :]), reads=["scb"], writes=["m8"])
            mk.op("dve", lambda e: e.match_replace(out=scb[:, 1, :], in_to_replace=m8[:, 0, :], in_values=scb[:, 0, :],
                                                   imm_value=-1.0e9), reads=["scb", "m8"], writes=["scb"])
            mk.op("dve", lambda e: e.max(out=m8[:, 1, :], in_=scb[:, 1, :]), reads=["scb"], writes=["m8"])
            mk.op("dve", lambda e: e.tensor_scalar(scb[:, 2, :], scb[:, 0, :], m8[:, 1, 7:8], BIG, op0=ALU.is_ge, op1=ALU.mult),
                  reads=["scb", "m8"], writes=["scb"])
            mk.op("dve", lambda e: e.scalar_tensor_tensor(scb[:, 3, :], scb[:, 2, :], -BIG, ts_[:, 1, :], op0=ALU.add, op1=ALU.add),
                  reads=["scb", (kts, 1)], writes=["bb"])
            for h4 in range(4):
                hh = g * 4 + h4
                sS, ksS = sS_r.next()
                nk = nkt * 128
                for c0 in range(0, nk, 512):
                    w = min(512, nk - c0)
                    bank, kb = psR.next()
                    mk.op("pe", lambda e: e.matmul(bank[:, 0:w], qrT[:, hh, :], KTa[:, 0, g, c0:c0 + w], start=True, stop=True),
                          reads=[("qrT", hh // 4)] + [("KT", 1, kt) for kt in range(c0 // 128, (c0 + w) // 128)], writes=[kb])
                    nb = w // 64
                    mk.op("dve", lambda e: e.tensor_tensor(sS[:, c0:c0 + w].rearrange("p (b k) -> p b k", k=64),
                                                           bank[:, 0:w].rearrange("p (b k) -> p b k", k=64),
                                                           scb[:, 3, c0 // 64:c0 // 64 + nb].unsqueeze(2).to_broadcast([128, nb, 64]),
                                                           ALU.add), reads=["bb"], writes=[kb, ksS])
                mk.op("pool", lambda e: e.tensor_tensor(sS[:, t * 128:(t + 1) * 128], sS[:, t * 128:(t + 1) * 128], trim, ALU.add),
                      reads=[ksS, "trim"], writes=[ksS])
                pb, kpb = pb_r.next()
                softmax_rows(sS, ksS, nk, pb, kpb, coef[:, 1, 8 + hh:9 + hh], "rs")
                pv_bf16(pb, kpb, nkt, lambda kt: VV[:, kt, 0, g, :], [("VV", kt) for kt in range(nkt)],
                        B_SLC[:, h4 * 128:(h4 + 1) * 128], K_SLC)
                sS, ksS = sS_r.next()
                k0 = (t - 4) * 128
                for c0, w in ((0, 512), (512, 128)):
                    bank, kb = psR.next()
                    mk.op("pe", lambda e: e.matmul(bank[:, 0:w], qrT[:, hh, :], KTa[:, 1, g, k0 + c0:k0 + c0 + w], start=True, stop=True),
                          reads=[("qrT", hh // 4)] + [("KT", 1, kt) for kt in range(t - 4, t + 1)], writes=[kb])
                    mk.op("dve", lambda e: e.tensor_tensor(sS[:, c0:c0 + w], bank[:, 0:w], sw[:, c0:c0 + w], ALU.add),
                          reads=[ksw], writes=[kb, ksS])
                pb, kpb = pb_r.next()
                softmax_rows(sS, ksS, 640, pb, kpb, coef[:, 1, 16 + hh:17 + hh], "rs")
                pv_bf16(pb, kpb, 5, lambda kt: VV[:, t - 4 + kt, 1, g, :], [("VV", kt) for kt in range(t - 4, t + 1)],
                        B_SWA[:, h4 * 128:(h4 + 1) * 128], K_SWA)
            for br in range(3):
                cs_ = slice(br * 8 + g * 4, br * 8 + g * 4 + 4)
                mk.op("dve", lambda e: e.tensor_tensor(coef[:, 0, cs_], coef[:, 0, cs_], coef[:, 1, cs_], ALU.mult),
                      reads=["coef", "rs"], writes=["coef"])
            combine(g)
        for c4 in range(2):
            transpose_into(o_nsaT[:, c4 * 4:(c4 + 1) * 4, qi * 128:(qi + 1) * 128],
                           [onsa[:, c4 * 4 + j, :] for j in range(4)], [("onsa", 0), ("onsa", 1)], ("onsaT", qi))
    mk.barrier()
    while len(live) > m_prompt_only:
        live.pop().__exit__(None, None, None)
    if WITH_SAMPLE:
        m_smp = sb_mark()
        KSVS = sb("KSVS", [128, 16384], BF16)
        RT = KSVS.rearrange("p (g n) -> p g n", g=2)
        KS = KSVS[:, 0:8192]
        VS = KSVS[:, 8192:16384].rearrange("p (j d) -> p j d", d=128)
        pg_r = rot("pg", [128, 256], F32, 2)
        sS_s = sb("sS_s", [128, 8320], F32)
        jq = sS_s[:, 0:1024].rearrange("p (h d) -> p h d", h=8)
        rt4 = sS_s[:, 1024:3072].rearrange("p (a h d) -> p a h d", a=4, h=8)
        qn_ = sS_s[:, 3072:4096].rearrange("p (h d) -> p h d", h=8)
        qr_ = sS_s[:, 4096:5120].rearrange("p (h d) -> p h d", h=8)
        pbs_r = rot("pbs", [128, 2048], BF16, 1)
        kcT_s = sb("kcT_s", [128, 2, 512], BF16)
        vc_s = sb("vc_s", [128, 4, 2, 128], F32)
        pcs = sb("pcs", [128, 512], F32)
        pcT_s = sb("pcT_s", [128, 4, 128], F32)
        imp4 = sb("imp4", [128, 4, 128], F32)
        mmap_sb = ld("mmap_sb", mmap_s.rearrange("(t p) j -> p t j", p=128), [128, 4, 129])
        tsel_sb = ld("tsel_sb", t_sel_s, [128, 129])
        newm = ld("newm", newmask.rearrange("s p k -> p s k"), [128, 4, 128])
        swpm = ld("swpm", swa_past_mask, [128, 512])
        scs = sb("scs", [128, 4, 129], F32)
        qs_n = sb("qs_n", [128, 128], BF16)
        qs_r = sb("qs_r", [128, 128], BF16)
        osb3 = sb("osb3", [128, 3, 128], F32)
        ocb = sb("ocb", [128, 4, 128], F32)
        KW = sb("KW", [128, 2, 512], BF16)
        VW = sb("VW", [128, 4, 256], BF16)
        page_regs = {}

        def load_pages(ci, s, consume):
            for j in range(64):
                pg, kpg = pg_r.next()
                mk.dma("sp", pg, gath[ci, s * 64 + j], writes=[kpg])
                consume(j, pg, kpg)

        def pv_big(sS, ksS, ntile, v_fn, vreads, o_ap, o_key, rs_col, krs):
            st, kst = st_r.next()
            n = ntile * 128
            mk.op("dve", lambda e: e.reduce_max(st[:, 0:1], sS[:, 0:n], axis=AX.X), reads=[ksS], writes=[kst])
            mk.op("dve", lambda e: e.tensor_scalar(st[:, 1:2], st[:, 0:1], -SC, None, op0=ALU.mult), reads=[kst], writes=[kst])
            mk.op("dve", lambda e: e.memset(st[:, 2:3], 0.0), reads=[], writes=[kst])
            for g0 in range(0, ntile, 16):
                ng = min(16, ntile - g0)
                pb, kpb = pbs_r.next()
                mk.op("act", lambda e: e.activation(pb[:, 0:ng * 128], sS[:, g0 * 128:(g0 + ng) * 128], AF.Exp, bias=st[:, 1:2], scale=SC),
                      reads=[ksS, kst], writes=[kpb])
                mk.op("dve", lambda e: e.reduce_sum(st[:, 3:4], pb[:, 0:ng * 128], axis=AX.X), reads=[kpb], writes=[kst])
                mk.op("dve", lambda e: e.tensor_tensor(st[:, 2:3], st[:, 2:3], st[:, 3:4], ALU.add), reads=[kst], writes=[kst])
                pT, kpT = pT_r.next()
                for k0 in range(0, ng, 8):
                    nk = min(8, ng - k0)
                    bank, kb = psb.next()

                    def trp(e, bank=bank, k0=k0, nk=nk, pb=pb):
                        for j in range(nk):
                            i = e.transpose(bank[:, j * 128:(j + 1) * 128], pb[:, (k0 + j) * 128:(k0 + j + 1) * 128], identb)
                        return i
                    mk.op("pe", trp, reads=[kpb, "identb"], writes=[kb])
                    evac(pT[:, k0:k0 + nk, :], bank[:, 0:nk * 128].rearrange("p (j n) -> p j n", j=nk), kb, [(kpT, k0)])

                def mmpv(e, g0=g0, ng=ng, pT=pT):
                    for kt in range(ng):
                        i = e.matmul(o_ap, pT[:, kt, :], v_fn(g0 + kt), start=(g0 + kt == 0), stop=(g0 + kt == ntile - 1))
                    return i
                mk.op("pe", mmpv, reads=[(kpT, k0) for k0 in range(0, ng, 8)] + list(vreads), writes=[o_key])
            mk.op("dve", lambda e: e.reciprocal(rs_col, st[:, 2:3]), reads=[kst], writes=[krs])

        t = 16
        rows = slice(t * 128, (t + 1) * 128)
        qraw, kq = qraw_r.next()
        g24, kg24 = g24_r.next()
        cs2, kcs = cs2_r.next()
        mk.dma("sp", qraw, proj[rows, C_NQ:C_KV], writes=[kq])
        mk.dma("sp", g24, proj[rows, C_NG:C_MG], writes=[kg24])
        mk.dma("sp", cs2[:, 0, :], rope_cos[rows, :], writes=[(kcs, 0)])
        mk.dma("sp", cs2[:, 1, :], rope_sin[rows, :], writes=[(kcs, 1)])
        norm_rope_q(qraw, kq, cs2, kcs)
        mk.op("act", lambda e: e.activation(coef[:, 0, :], g24, AF.Sigmoid), reads=[kg24], writes=["coef"])
        mk.barrier()
        for s in range(4):
            for (nm, cache) in (("k", 0), ("v", 1)):
                def cons(j, pg, kpg):
                    bank, kb = psf.next()
                    mk.op("pe", lambda e: [e.transpose(bank[:, 0:128], pg[:, 0:128], ident),
                                           e.transpose(bank[:, 128:256], pg[:, 128:256], ident)][-1],
                          reads=[kpg, "ident"], writes=[kb])
                    evac(RT[:, :, j * 128:(j + 1) * 128], bank[:, 0:256].rearrange("p (g n) -> p g n", g=2), kb, [("KS", j), ("VS", j)])
                load_pages(cache, s, cons)
                for g in range(2):
                    if nm == "k":
                        def outk(mt, m, o2, ko2, g=g):
                            kc_, kkc = kcr.next()
                            rms_rows(o2, ko2, m, gkc, "gkc", kc_, kkc)
                            bank, kb = psf.next()
                            mk.op("pe", lambda e: e.transpose(bank[:, 0:m], kc_[0:m, :], ident[0:m, 0:m]), reads=[kkc, "ident"], writes=[kb])
                            evac(kcT_s[:, g, mt * 128:mt * 128 + m], bank[:, 0:m], kb, [("kcT_s", g, mt)])
                        compress("k", RT[:, g, :], 511, [(("KS", "VS")[g], j) for j in range(64)], outk)
                    else:
                        def outv(mt, m, o2, ko2, g=g):
                            mk.op("pool", lambda e: e.tensor_copy(vc_s[0:m, mt, g, :], o2[0:m]), reads=[ko2], writes=[("vc_s", g, mt)])
                        compress("v", RT[:, g, :], 511, [(("KS", "VS")[g], j) for j in range(64)], outv)
            for kv_i in range(2):
                for j in range(4):
                    pg, kpg = pg_r.next()
                    mk.dma("sp", pg, state_swa_c[kv_i, s, j * 128:(j + 1) * 128, :], writes=[kpg])
                    if kv_i == 0:
                        bank, kb = psf.next()
                        mk.op("pe", lambda e: [e.transpose(bank[:, 0:128], pg[:, 0:128], ident),
                                               e.transpose(bank[:, 128:256], pg[:, 128:256], ident)][-1],
                              reads=[kpg, "ident"], writes=[kb])
                        evac(KW[:, :, j * 128:(j + 1) * 128], bank[:, 0:256].rearrange("p (g n) -> p g n", g=2), kb, [("KW", j)])
                    else:
                        mk.op("pool", lambda e: e.tensor_copy(VW[:, j, :], pg), reads=[kpg], writes=[("VW", j)])
            for g in range(2):
                for h4 in range(4):
                    mk.op("pool", lambda e: e.tensor_copy(qs_n[:, h4 * 32:(h4 + 1) * 32], qnT[:, g * 4 + h4, 32 * s:32 * s + 32]),
                          reads=[("qnT", g)], writes=["qs_n"])
                    mk.op("pool", lambda e: e.tensor_copy(qs_r[:, h4 * 32:(h4 + 1) * 32], qrT[:, g * 4 + h4, 32 * s:32 * s + 32]),
                          reads=[("qrT", g)], writes=["qs_r"])
                bank, kb = psf.next()
                mk.op("pe", lambda e: e.matmul(bank[:, 0:511], qs_n, kcT_s[:, g, 0:511], start=True, stop=True),
                      reads=["qs_n"] + [("kcT_s", g, mt) for mt in range(4)], writes=[kb])
                mk.op("act", lambda e: e.copy(sS_s[:, 0:511], bank[:, 0:511]), writes=[kb, "sS_s"])
                st2, kst2 = st_r.next()
                softmax_rows(sS_s, "sS_s", 511, pcs, "pcs", st2[:, 4:5], kst2)
                mk.op("dve", lambda e: e.tensor_scalar(pcs[:, 0:511], pcs[:, 0:511], st2[:, 4:5], None, op0=ALU.mult),
                      reads=["pcs", kst2], writes=["pcs"])
                for mt in range(4):
                    m = 128 if mt < 3 else 127
                    bank, kb = psf.next()
                    mk.op("pe", lambda e: e.transpose(bank[0:m, 0:128], pcs[:, mt * 128:mt * 128 + m], ident), reads=["pcs", "ident"], writes=[kb])
                    evac(pcT_s[0:m, mt, :], bank[0:m, 0:128], kb, [("pcT_s", mt)])
                bank, kb = psf.next()

                def mmoc(e, bank=bank, g=g):
                    for mt in range(4):
                        m = 128 if mt < 3 else 127
                        i = e.matmul(bank[:, 0:128], pcT_s[0:m, mt, :], vc_s[0:m, mt, g, :], start=(mt == 0), stop=(mt == 3))
                    return i
                mk.op("pe", mmoc, reads=[("pcT_s", mt) for mt in range(4)] + [("vc_s", g, mt) for mt in range(4)], writes=[kb])
                mk.op("act", lambda e: e.copy(osb3[:, 0, :], bank[:, 0:128]), writes=[kb, ("osb3", 0)])
                p4 = pcT_s.rearrange("p t (h s) -> p t h s", h=4)
                rdp = [("pcT_s", mt) for mt in range(4)]
                mk.op("dve", lambda e: e.tensor_tensor(imp4[:, :, 0:32], p4[:, :, 0, :], p4[:, :, 1, :], ALU.add), reads=rdp, writes=["imp4"])
                mk.op("dve", lambda e: e.tensor_tensor(imp4[:, :, 0:32], imp4[:, :, 0:32], p4[:, :, 2, :], ALU.add), reads=rdp + ["imp4"], writes=["imp4"])
                mk.op("dve", lambda e: e.tensor_tensor(imp4[:, :, 0:32], imp4[:, :, 0:32], p4[:, :, 3, :], ALU.add), reads=rdp + ["imp4"], writes=["imp4"])
                for h4 in range(1, 4):
                    mk.op("pool", lambda e: e.tensor_copy(imp4[:, :, h4 * 32:(h4 + 1) * 32], imp4[:, :, 0:32]), reads=["imp4"], writes=["imp4"])
                bank, kb = psf.next()

                def mmsc2(e, bank=bank):
                    for mt in range(4):
                        m = 128 if mt < 3 else 127
                        i = e.matmul(bank[:, 0:129], imp4[0:m, mt, :], mmap_sb[0:m, mt, :], start=(mt == 0), stop=(mt == 3))
                    return i
                mk.op("pe", mmsc2, reads=["imp4", "mmap_sb"], writes=[kb])
                mk.op("dve", lambda e: e.tensor_tensor(scs[:, 0, :], bank[:, 0:129], tsel_sb, ALU.add), reads=["tsel_sb"], writes=[kb, "scs"])
                mk.op("dve", lambda e: e.max(out=m8[:, 0, :], in_=scs[:, 0, :]), reads=["scs"], writes=["m8"])
                mk.op("dve", lambda e: e.match_replace(out=scs[:, 1, :], in_to_replace=m8[:, 0, :], in_values=scs[:, 0, :], imm_value=-1.0e9),
                      reads=["scs", "m8"], writes=["scs"])
                mk.op("dve", lambda e: e.max(out=m8[:, 1, :], in_=scs[:, 1, :]), reads=["scs"], writes=["m8"])
                mk.op("dve", lambda e: e.tensor_scalar(scs[:, 2, :], scs[:, 0, :], m8[:, 1, 7:8], BIG, op0=ALU.is_ge, op1=ALU.mult),
                      reads=["scs", "m8"], writes=["scs"])
                mk.op("dve", lambda e: e.tensor_scalar(scs[:, 3, :], scs[:, 2, :], -BIG, None, op0=ALU.add), reads=["scs"], writes=["bbs"])
                def consk(j, pg, kpg, g=g):
                    bank, kb = psf.next()
                    mk.op("pe", lambda e: e.transpose(bank[:, 0:128], pg[:, g * 128:(g + 1) * 128], ident), reads=[kpg, "ident"], writes=[kb])
                    evac(KS[:, j * 128:(j + 1) * 128], bank[:, 0:128], kb, [("KS", j)])
                load_pages(2, s, consk)

                def consv(j, pg, kpg, g=g):
                    mk.op("pool", lambda e: e.tensor_copy(VS[:, j, :], pg[:, g * 128:(g + 1) * 128]), reads=[kpg], writes=[("VS", j)])
                load_pages(3, s, consv)
                for c0 in range(0, 8192, 512):
                    bank, kb = psf.next()
                    mk.op("pe", lambda e: e.matmul(bank, qs_r, KS[:, c0:c0 + 512], start=True, stop=True),
                          reads=["qs_r"] + [("KS", j) for j in range(c0 // 128, c0 // 128 + 4)], writes=[kb])
                    mk.op("dve", lambda e: e.tensor_tensor(sS_s[:, c0:c0 + 512].rearrange("p (b k) -> p b k", k=64),
                                                           bank.rearrange("p (b k) -> p b k", k=64),
                                                           scs[:, 3, c0 // 64:c0 // 64 + 8].unsqueeze(2).to_broadcast([128, 8, 64]), ALU.add),
                          reads=["bbs", "pcs"], writes=[kb, "sS_s"])
                bank, kb = psf.next()
                mk.op("pe", lambda e: e.matmul(bank[:, 0:128], qs_r, KTn[:, 0, g, :], start=True, stop=True), reads=["qs_r", ("KT", 1, 16)], writes=[kb])
                mk.op("dve", lambda e: e.scalar_tensor_tensor(sS_s[:, 8192:8320], bank[:, 0:128], scs[:, 3, 128:129], newm[:, s, :], op0=ALU.add, op1=ALU.add),
                      reads=["bbs", "newm"], writes=[kb, "sS_s"])
                bank, kb = psf.next()
                st3, kst3 = st_r.next()
                pv_big(sS_s, "sS_s", 65, lambda kt: (VS[:, kt, :] if kt < 64 else VVn[:, 0, g, :]), [("VS", j) for j in range(64)] + [("VV", 16)],
                       bank[:, 0:128], kb, st3[:, 4:5], kst3)
                mk.op("dve", lambda e: e.tensor_scalar(osb3[:, 1, :], bank[:, 0:128], st3[:, 4:5], None, op0=ALU.mult), reads=[kst3], writes=[kb, ("osb3", 1)])
                bank, kb = psf.next()
                mk.op("pe", lambda e: e.matmul(bank, qs_r, KW[:, g, :], start=True, stop=True), reads=["qs_r"] + [("KW", j) for j in range(4)], writes=[kb])
                mk.op("dve", lambda e: e.tensor_tensor(sS_s[:, 0:512], bank, swpm, ALU.add), reads=["swpm"], writes=[kb, "sS_s"])
                bank, kb = psf.next()
                mk.op("pe", lambda e: e.matmul(bank[:, 0:128], qs_r, KTn[:, 1, g, :], start=True, stop=True), reads=["qs_r", ("KT", 1, 16)], writes=[kb])
                mk.op("dve", lambda e: e.tensor_tensor(sS_s[:, 512:640], bank[:, 0:128], newm[:, s, :], ALU.add), reads=["newm"], writes=[kb, "sS_s"])
                bank, kb = psf.next()
                st4, kst4 = st_r.next()
                pv_big(sS_s, "sS_s", 5, lambda kt: (VW[:, kt, g * 128:(g + 1) * 128] if kt < 4 else VVn[:, 1, g, :]), [("VW", j) for j in range(4)] + [("VV", 16)],
                       bank[:, 0:128], kb, st4[:, 4:5], kst4)
                mk.op("dve", lambda e: e.tensor_scalar(osb3[:, 2, :], bank[:, 0:128], st4[:, 4:5], None, op0=ALU.mult), reads=[kst4], writes=[kb, ("osb3", 2)])
                mk.dma("sp", o_scr[s, g].rearrange("b r d -> r b d"), osb3, reads=[("osb3", b_) for b_ in range(3)], writes=[("oscr", s, g)])
        for g in range(2):
            cf = coef[:, 0, :]
            dst = onsa[:, g * 4:(g + 1) * 4, :]
            for b_ in range(3):
                for s in range(4):
                    mk.dma("sp", ocb[32 * s:32 * s + 32, :, :], o_scr[s, g, b_].rearrange("(h r) d -> r h d", h=4),
                           reads=[("oscr", s, g)], writes=["ocb"])
                cbk = cf[:, b_ * 8 + g * 4:b_ * 8 + g * 4 + 4].unsqueeze(2).to_broadcast([128, 4, 128])
                if b_ == 0:
                    mk.op("dve", lambda e: e.tensor_tensor(dst, ocb, cbk, ALU.mult), reads=["coef", "ocb"], writes=[("onsa", g)])
                else:
                    mk.op("dve", lambda e: e.tensor_tensor(otmp, ocb, cbk, ALU.mult), reads=["coef", "ocb"], writes=["otmp"])
                    mk.op("pool", lambda e: e.tensor_tensor(dst, dst, otmp, ALU.add), reads=["otmp", ("onsa", g)], writes=[("onsa", g)])
        for c4 in range(2):
            transpose_into(o_nsaT[:, c4 * 4:(c4 + 1) * 4, 9 * 128:10 * 128],
                           [onsa[:, c4 * 4 + j, :] for j in range(4)], [("onsa", 0), ("onsa", 1)], ("onsaT", 9))
    else:
        mk.op("pool", lambda e: e.memset(o_nsaT[:, :, 9 * 128:10 * 128], 0.0), writes=[("onsaT", 9)])
    mk.barrier()
    while len(live) > m_smp_keep:
        live.pop().__exit__(None, None, None)

    NQ = 10
    mT = sb("mT", [128, 16, NQ * 128], BF16)
    m_mrg = sb_mark()
    wg_r = rot("wg", [128, 8, 512], BF16, 2)
    wn_r = rot("wn", [128, 8, 512], BF16, 2)
    mg_r = rot("mg", [128, 2, 512], F32, 2)
    mm_r = rot("mm", [128, 2, 512], F32, 2)
    for cb in range(4):
        cs_ = slice(cb * 512, (cb + 1) * 512)
        wg, kwg = wg_r.next()
        wn, kwn = wn_r.next()
        for c in range(8):
            mk.dma("pool", wg[:, c, :], w_br_gla[c * 128:(c + 1) * 128, cs_], writes=[(kwg, c)])
            mk.dma("pool", wn[:, c, :], w_br_nsa[c * 128:(c + 1) * 128, cs_], writes=[(kwn, c)])
        for qi in range(NQ):
            t = QT0 + qi
            rows = slice(t * 128, (t + 1) * 128)
            mg, kmg = mg_r.next()
            mm_, kmm = mm_r.next()
            mk.dma("sp", mg[:, 0, :], proj[rows, C_MG + cb * 512:C_MG + (cb + 1) * 512], writes=[(kmg, 0)])
            mk.dma("sp", mg[:, 1, :], proj[rows, C_MG + 2048 + cb * 512:C_MG + 2048 + (cb + 1) * 512], writes=[(kmg, 1)])
            mk.op("act", lambda e: e.activation(mg, mg, AF.Sigmoid), reads=[(kmg, 0), (kmg, 1)], writes=[(kmg, 0), (kmg, 1)])
            bankA, kA = psf.next()
            bankB, kB = psf.next()

            def mmA(e, bank=bankA, w=wg, src=o_glaT, qi=qi):
                for c in range(8):
                    i = e.matmul(bank, src[:, c, qi * 128:(qi + 1) * 128], w[:, c, :], start=(c == 0), stop=(c == 7))
                return i

            def mmB(e, bank=bankB, w=wn, src=o_nsaT, qi=qi):
                for c in range(8):
                    i = e.matmul(bank, src[:, c, qi * 128:(qi + 1) * 128], w[:, c, :], start=(c == 0), stop=(c == 7))
                return i
            mk.op("pe", mmA, reads=[(kwg, c) for c in range(8)] + [("oglaT", qi)], writes=[kA])
            mk.op("pe", mmB, reads=[(kwn, c) for c in range(8)] + [("onsaT", qi)], writes=[kB])
            mk.op("dve", lambda e: e.tensor_tensor(mm_[:, 0, :], bankA, mg[:, 0, :], ALU.mult), reads=[(kmg, 0)], writes=[kA, (kmm, 0)])
            mk.op("dve", lambda e: e.tensor_tensor(mm_[:, 1, :], bankB, mg[:, 1, :], ALU.mult), reads=[(kmg, 1)], writes=[kB, (kmm, 1)])
            mk.op("pool", lambda e: e.tensor_tensor(mm_[:, 0, :], mm_[:, 0, :], mm_[:, 1, :], ALU.add),
                  reads=[(kmm, 0), (kmm, 1)], writes=[(kmm, 0)])
            transpose_into(mT[:, cb * 4:(cb + 1) * 4, qi * 128:(qi + 1) * 128],
                           [mm_[:, 0, j * 128:(j + 1) * 128] for j in range(4)], [(kmm, 0)], ("mT", qi, cb))
    mk.barrier()
    sb_release(m_mrg)

    m_wo = sb_mark()
    wo_r = rot("wo", [128, 16, 512], BF16, 2)
    xs_r = rot("xs", [128, 512], F32, 3)
    for cb in range(4):
        cs_ = slice(cb * 512, (cb + 1) * 512)
        wo, kwo = wo_r.next()
        for c in range(16):
            mk.dma("pool", wo[:, c, :], w_o[c * 128:(c + 1) * 128, cs_], writes=[(kwo, c)])
        for qi in range(NQ):
            t = QT0 + qi
            xs, kxs = xs_r.next()
            mk.dma("sp", xs, xbuf[t * 128:(t + 1) * 128, cs_], writes=[kxs])
            bank, kb = psf.next()

            def mmh(e, bank=bank, wo=wo, qi=qi):
                for c in range(16):
                    i = e.matmul(bank, mT[:, c, qi * 128:(qi + 1) * 128], wo[:, c, :], start=(c == 0), stop=(c == 15))
                return i
            mk.op("pe", mmh, reads=[(kwo, c) for c in range(16)], writes=[kb])
            mk.op("dve", lambda e: e.tensor_tensor(xs, bank, xs, ALU.add), reads=[kxs], writes=[kb, kxs])
            mk.dma("sp", h_scr[qi * 128:(qi + 1) * 128, cs_], xs, reads=[kxs], writes=[("h", qi, cb)])
    mk.barrier()
    while len(live) > m_keep:
        live.pop().__exit__(None, None, None)

    NF = 1042
    hnF = sb("hnF", [128, 16, NF], BF16)
    gT = sb("gT", [128, NFB, NF], BF16)
    m_n2 = sb_mark()
    g2b = ld("g2b", norm2_g.partition_broadcast(128), [128, D])
    hr = rot("ht", [128, D], F32, 2)
    hnr = rot("hn", [128, D], F32, 2)
    junk2 = sb("junk2", [128, D], F32)
    ss2r = rot("ss2", [128, 2], F32, 2)
    for qi in range(NQ):
        ht, kh = hr.next()
        hn, khn = hnr.next()
        ss, ks = ss2r.next()
        mk.dma("sp", ht, h_scr[qi * 128:(qi + 1) * 128, :], writes=[kh])
        mk.op("act", lambda e: e.activation(junk2, ht, AF.Square), reads=[kh], writes=["junk2"])
        mk.op("dve", lambda e: e.reduce_sum(ss[:, 0:1], junk2, axis=AX.X), reads=["junk2"], writes=[ks])
        mk.op("act", lambda e: e.activation(ss[:, 1:2], ss[:, 0:1], AF.Sqrt, bias=epsc[:, 0:1], scale=1.0 / D),
              reads=[ks, "epsc"], writes=[ks])
        mk.op("dve", lambda e: e.reciprocal(ss[:, 1:2], ss[:, 1:2]), reads=[ks], writes=[ks])
        mk.op("dve", lambda e: e.scalar_tensor_tensor(hn, ht, ss[:, 1:2], g2b, op0=ALU.mult, op1=ALU.mult),
              reads=[kh, ks, "g2b"], writes=[khn])
        for c4 in range(4):
            bank, kb = psf.next()

            def tr(e, c4=c4, bank=bank, hn=hn):
                for j in range(4):
                    c = c4 * 4 + j
                    i = e.transpose(bank[:, j * 128:(j + 1) * 128], hn[:, c * 128:(c + 1) * 128], ident)
                return i
            mk.op("pe", tr, reads=[khn, "ident"], writes=[kb])
            b3 = bank.rearrange("p (j n) -> p j n", j=4)
            cc = slice(c4 * 4, (c4 + 1) * 4)
            if qi == 0:
                evac(hnF[:, cc, 0:2], b3[:, :, 126:128], kb, [("hnF", qi, c4)])
            elif qi < 9:
                evac(hnF[:, cc, 2 + (qi - 1) * 128:2 + qi * 128], b3, kb, [("hnF", qi, c4)])
            else:
                for s in range(4):
                    evac(hnF[:, cc, 1026 + 4 * s:1030 + 4 * s], b3[:, :, 32 * s:32 * s + 4], kb, [("hnF", qi, c4, s)])
    mk.barrier()
    sb_release(m_n2)

    m_up = sb_mark()
    cwt = sb("cwt", [128, NFB, 4], F32)
    for j in range(3):
        mk.dma("sp", cwt[:, :, j], conv_w[j].rearrange("(fb p) -> p fb", p=128), writes=[("cwt", j)], allow_slow_non_contiguous=True)
    mk.dma("sp", cwt[:, :, 3], conv_b.rearrange("(fb p) -> p fb", p=128), writes=[("cwt", 3)], allow_slow_non_contiguous=True)
    cstT = sb("cstT", [128, NFB, 8], F32)
    for s_ in range(4):
        for j in range(2):
            mk.dma("sp", cstT[:, :, s_ * 2 + j], state_conv_c[s_, j].rearrange("(fb p) -> p fb", p=128),
                   writes=[("cstT", s_ * 2 + j)], allow_slow_non_contiguous=True)
    crow_r = rot("crow", [10, 256], F32, 2)
    wa_r = rot("wa", [128, 16, 256], BF16, 2)
    wb_r = rot("wbg", [128, 16, 256], BF16, 2)
    aT = sb("aT", [128, 2 + NF], F32)
    bT = sb("bT", [128, NF], F32)
    uu = sb("uu", [128, NF], F32)
    as6 = sb("as6", [128, 4, 6], F32)
    us4 = sb("us4", [128, 4, 4], F32)
    cc10 = sb("cc10", [128, 10], F32)
    mk.op("dve", lambda e: e.memset(aT[:, 0:2], 0.0), writes=["aTpad"])
    SEGS = ((0, 512), (512, 512), (1024, NF - 1024))
    for f2 in range(NFB // 2):
        wa, kwa = wa_r.next()
        wb_, kwb = wb_r.next()
        for c in range(16):
            mk.dma("pool", wa[:, c, :], w_up[c * 128:(c + 1) * 128, f2 * 256:(f2 + 1) * 256], writes=[(kwa, c)])
            mk.dma("pool", wb_[:, c, :], w_up[c * 128:(c + 1) * 128, DFF + f2 * 256:DFF + (f2 + 1) * 256], writes=[(kwb, c)])
        for sub in range(2):
            fb = f2 * 2 + sub
            for (c0, w) in SEGS:
                for (wt, kwt, dst, kd) in ((wa, kwa, aT[:, 2 + c0:2 + c0 + w], "aT"), (wb_, kwb, bT[:, c0:c0 + w], "bT")):
                    bank, kb = psf.next()

                    def mmu(e, bank=bank, wt=wt, c0=c0, w=w, sub=sub):
                        for c in range(16):
                            i = e.matmul(bank[:, 0:w], wt[:, c, sub * 128:(sub + 1) * 128], hnF[:, c, c0:c0 + w],
                                         start=(c == 0), stop=(c == 15))
                        return i
                    mk.op("pe", mmu, reads=[(kwt, c) for c in range(16)], writes=[kb])
                    evac(dst, bank[:, 0:w], kb, [(kd, c0)])
            ra = [("aT", c0) for (c0, w) in SEGS] + ["aTpad"] + [("cwt", j) for j in range(4)]
            w0, w1, w2, bcv = (cwt[:, fb, j:j + 1] for j in range(4))
            mk.op("dve", lambda e: e.tensor_scalar(uu, aT[:, 2:2 + NF], w2, bcv, op0=ALU.mult, op1=ALU.add), reads=ra, writes=["uu"])
            mk.op("dve", lambda e: e.scalar_tensor_tensor(uu, aT[:, 1:1 + NF], w1, uu, op0=ALU.mult, op1=ALU.add), reads=ra + ["uu"], writes=["uu"])
            mk.op("dve", lambda e: e.scalar_tensor_tensor(uu, aT[:, 0:NF], w0, uu, op0=ALU.mult, op1=ALU.add), reads=ra + ["uu"], writes=["uu"])
            a_s = aT[:, 2 + 1026:2 + 1042].rearrange("p (s t) -> p s t", s=4)
            mk.op("pool", lambda e: e.tensor_copy(as6[:, :, 0:2], cstT[:, fb, :].rearrange("p (s j) -> p s j", s=4)),
                  reads=[("cstT", i8) for i8 in range(8)], writes=["as6a"])
            mk.op("pool", lambda e: e.tensor_copy(as6[:, :, 2:6], a_s), reads=ra, writes=["as6b"])
            rs6 = ["as6a", "as6b"] + [("cwt", j) for j in range(4)]
            mk.op("dve", lambda e: e.tensor_scalar(us4, as6[:, :, 2:6], w2, bcv, op0=ALU.mult, op1=ALU.add), reads=rs6, writes=["us4"])
            mk.op("dve", lambda e: e.scalar_tensor_tensor(us4, as6[:, :, 1:5], w1, us4, op0=ALU.mult, op1=ALU.add), reads=rs6 + ["us4"], writes=["us4"])
            mk.op("dve", lambda e: e.scalar_tensor_tensor(us4, as6[:, :, 0:4], w0, us4, op0=ALU.mult, op1=ALU.add), reads=rs6 + ["us4"], writes=["us4"])
            mk.op("dve", lambda e: e.tensor_copy(uu[:, 1026:1042].rearrange("p (s t) -> p s t", s=4), us4), reads=["us4", "uu"], writes=["uu"])
            mk.op("act", lambda e: e.activation(uu, uu, AF.Gelu_apprx_tanh), reads=["uu"], writes=["uu"])
            mk.op("dve", lambda e: e.tensor_tensor(gT[:, fb, :], uu, bT, ALU.mult), reads=["uu"] + [("bT", c0) for (c0, w) in SEGS],
                  writes=[("gT", fb)])
            mk.op("pool", lambda e: e.tensor_copy(cc10[:, 0:2], aT[:, 2 + 1024:2 + 1026]), reads=ra, writes=["cc10a"])
            mk.op("pool", lambda e: e.tensor_copy(cc10[:, 2:10].rearrange("p (s t) -> p s t", s=4), a_s[:, :, 2:4]), reads=ra, writes=["cc10b"])
            bank, kb = psf.next()
            mk.op("pe", lambda e: e.transpose(bank[0:10, 0:128], cc10, ident), reads=["cc10a", "cc10b", "ident"], writes=[kb])
            if sub == 0:
                crow, kcrow = crow_r.next()
            evac(crow[:, sub * 128:(sub + 1) * 128], bank[0:10, 0:128], kb, [(kcrow, sub)])
        mk.dma("sp", conv_out[:, f2 * 256:(f2 + 1) * 256], crow, reads=[(kcrow, 0), (kcrow, 1)], writes=[("convo", f2)])
    mk.barrier()
    sb_release(m_up)

    wd_r = rot("wd", [128, NFB, 256], BF16, 2)
    hs_r = rot("hs", [128, 256], F32, 3)
    for cb in range(8):
        cs_ = slice(cb * 256, (cb + 1) * 256)
        wd, kwd = wd_r.next()
        for fb in range(NFB):
            mk.dma("pool", wd[:, fb, :], w_down[fb * 128:(fb + 1) * 128, cs_], writes=[(kwd, fb)])
        for i in range(9):
            M = 128 if i < 8 else 16
            col0 = 2 + i * 128
            hs, khs = hs_r.next()
            if i < 8:
                mk.dma("sp", hs, h_scr[(1 + i) * 128:(2 + i) * 128, cs_], writes=[khs])
            else:
                for s in range(4):
                    mk.dma("sp", hs[4 * s:4 * s + 4, :], h_scr[9 * 128 + 32 * s:9 * 128 + 32 * s + 4, cs_], writes=[(khs, s)])
            bank, kb = psf.next()

            def mmy(e, bank=bank, wd=wd, col0=col0, M=M):
                for fb in range(NFB):
                    i_ = e.matmul(bank[0:M, 0:256], gT[:, fb, col0:col0 + M], wd[:, fb, :], start=(fb == 0), stop=(fb == NFB - 1))
                return i_
            mk.op("pe", mmy, reads=[(kwd, fb) for fb in range(NFB)], writes=[kb])
            rdh = [khs] if i < 8 else [(khs, s) for s in range(4)]
            mk.op("dve", lambda e: e.tensor_tensor(hs[0:M], bank[0:M, 0:256], hs[0:M], ALU.add), reads=rdh, writes=[kb, khs])
            mk.dma("sp", y_out[i * 128:i * 128 + M, cs_], hs[0:M], reads=[khs], writes=[("y", i, cb)])
    mk.barrier()
    while live:
        live.pop().__exit__(None, None, None)
    return nc, mk


def _rope_tables(pos):
    half = 64
    inv = (10000.0 ** (-np.arange(half, dtype=np.float32) / half)).astype(np.float32)
    ang = pos.astype(np.float32)[:, None] * inv[None, :]
    return np.cos(ang).astype(np.float32), np.sin(ang).astype(np.float32)


def _sample_tables():
    f = np.float32
    n = np.arange(512)[:, None]
    j = np.arange(129)[None, :]
    mm = ((16 * n < 64 * (j + 1)) & (16 * n + 32 > 64 * j) & (n < 511)).astype(f)
    ts = np.zeros((128, 129), f)
    ts[:, [0, 127, 128]] = 1.0e4
    slot = (np.arange(128) % 32)
    newm = np.full((4, 128, 128), -BIG, f)
    for s in range(4):
        for kk in range(4):
            newm[s, (slot < 4) & (kk <= slot), 32 * s + kk] = 0.0
    swp = np.where(np.arange(512)[None, :] > slot[:, None], 0.0, -BIG).astype(f)
    return dict(mmap_s=mm, t_sel_s=ts, newmask=newm, swa_past_mask=swp)


def _const_tables(half):
    f = np.float32
    pos = lambda r: r - 1024 + 1024 * half
    n = np.arange(127)
    j = np.arange(32)
    cmp_add = np.zeros((9 * 128, 127), f)
    cmp_mul = np.zeros((9 * 128, 127), f)
    t_sel = np.zeros((9 * 128, 32), f)
    t_inv = np.zeros((9 * 128, 32), f)
    swa = np.zeros((9 * 128, 640), f)
    for qi in range(9):
        t = QT0 + qi
        qpos = pos(t * 128 + np.arange(128))[:, None]
        valid = (pos(16 * n + 31)[None, :] <= qpos) & (pos(16 * n)[None, :] >= 0)
        cmp_add[qi * 128:(qi + 1) * 128] = np.where(valid, 0.0, -BIG)
        cmp_mul[qi * 128:(qi + 1) * 128] = valid
        sp = pos(64 * j)[None, :]
        bvalid = (sp >= 0) & (sp <= qpos)
        jr = sp // 64
        cur = qpos // 64
        forced = bvalid & ((jr == 0) | (jr == cur) | (jr == cur - 1))
        t_sel[qi * 128:(qi + 1) * 128] = np.where(forced, 1.0e4, np.where(bvalid, 0.0, -1.0e4))
        t_inv[qi * 128:(qi + 1) * 128] = np.where(bvalid, 0.0, -BIG)
        kp = pos((t - 4) * 128 + np.arange(640))[None, :]
        rel = qpos - kp
        swa[qi * 128:(qi + 1) * 128] = np.where((rel >= 0) & (rel < 512) & (kp >= 0), 0.0, -BIG)
    mm = np.zeros((128, 32), f)
    mm[:127] = ((16 * n[:, None] < 64 * (j[None, :] + 1)) & (16 * n[:, None] + 32 > 64 * j[None, :]))
    i = np.arange(128)
    le = (i[:, None] <= i[None, :])
    blk = (i[:, None] // 32 == i[None, :] // 32)
    sm = (i[:, None] // 32 == np.arange(4)[None, :])
    return dict(
        cmp_add=cmp_add, cmp_mul=cmp_mul, t_sel=t_sel, t_inv=t_inv, swa_mask=swa, mmap_p=mm,
        tri_mask=np.where(i[None, :] <= i[:, None], 0.0, -BIG).astype(f),
        ucum_p=(le * (-1.0 / 16)).astype(f), ucum_s=((le & blk) * (-1.0 / 16)).astype(f),
        caus_p=le.astype(f), caus_s=(le & blk).astype(f),
        uend_p=np.full((128, 1), -1.0 / 16, f),
        uend_s=((sm & ((i % 32) < 4)[:, None]) * (-1.0 / 16)).astype(f), seqmask=sm.astype(f),
        ident=np.eye(128, dtype=f), **_sample_tables(),
    )


_SHARED = ["w_in", "norm1_g", "k_slc_norm_g", "k_swa_norm_g", "gla_w_a2", "gla_b_a2", "gla_onorm_g", "q_norm_g",
           "k_cmp_norm_g", "cmp_pe_k", "cmp_w1_k", "cmp_b1_k", "cmp_w2_k", "cmp_b2_k", "cmp_pe_v", "cmp_w1_v",
           "cmp_b1_v", "cmp_w2_v", "cmp_b2_v", "w_br_gla", "w_br_nsa", "w_o", "norm2_g", "w_up", "conv_w", "conv_b",
           "w_down"]


def make_in_maps(inputs, cores=range(8)):
    xp = np.asarray(inputs["x_prompt"], np.float32)
    xs = np.asarray(inputs["x_sample"], np.float32)
    shared = {k: np.ascontiguousarray(np.asarray(inputs[k], np.float32)) for k in _SHARED}
    ctab = [_const_tables(0), _const_tables(1)]
    maps = []
    for c in cores:
        b, half = c // 2, c % 2
        xb = np.zeros((NT * 128, D), np.float32)
        if half == 1:
            xb[0:1024] = xp[b, 0:1024]
        xb[1024:2048] = xp[b, half * 1024:(half + 1) * 1024]
        pos = np.zeros(NT * 128, np.float32)
        pos[0:2048] = np.arange(2048) - 1024 + 1024 * half
        for s in range(4):
            xb[2048 + 32 * s:2048 + 32 * s + 4] = xs[4 * c + s]
            pos[2048 + 32 * s:2048 + 32 * s + 4] = 8192 + np.arange(4)
        cos, sin = _rope_tables(pos)
        m = dict(shared)
        m.update(ctab[half])
        m["page_tab"] = np.ascontiguousarray(np.asarray(inputs["page_table"], np.int32)[4 * c:4 * c + 4].reshape(1, 256))
        if WITH_SAMPLE:
            for k in ("cache_k_cmp", "cache_v_cmp", "cache_k_slc", "cache_v_slc"):
                a = np.asarray(inputs[k], np.float32)
                m[k] = a.reshape(a.shape[0], 128, 256)
        m.update({
            "xbuf": xb, "rope_cos": cos, "rope_sin": sin,
            "state_gla_c": np.ascontiguousarray(np.asarray(inputs["state_gla"], np.float32)[4 * c:4 * c + 4]),
            "state_conv_c": np.ascontiguousarray(np.asarray(inputs["state_conv"], np.float32)[4 * c:4 * c + 4]),
            "state_swa_c": np.ascontiguousarray(np.stack([
                np.asarray(inputs["state_swa_k"], np.float32)[4 * c:4 * c + 4].reshape(4, 512, 256),
                np.asarray(inputs["state_swa_v"], np.float32)[4 * c:4 * c + 4].reshape(4, 512, 256)])),
        })
        maps.append(m)
    return maps


def assemble(results, cores=range(8)):
    f = np.float32
    y_p = np.zeros((4, 2048, D), f); y_s = np.zeros((32, 4, D), f)
    kvp = [np.zeros((4, 2048, 2, 128), f) for _ in range(4)]
    swap = [np.zeros((4, 512, 2, 128), f) for _ in range(2)]
    gla_p = np.zeros((4, 4, 128, 256), f); conv_p = np.zeros((4, 2, DFF), f)
    kvs = [np.zeros((32, 4, 2, 128), f) for _ in range(4)]
    swas = [np.zeros((32, 512, 2, 128), f) for _ in range(2)]
    gla_s = np.zeros((32, 4, 128, 256), f); conv_s = np.zeros((32, 2, DFF), f)
    for c, r in zip(cores, results):
        b, half = c // 2, c % 2
        yo = r["y_out"]
        y_p[b, half * 1024:(half + 1) * 1024] = yo[0:1024]
        y_s[4 * c:4 * c + 4] = yo[1024:1040].reshape(4, 4, D)
        kv = r["kv_out"].reshape(NT * 128, 6, 2, 128)
        for a in range(4):
            kvp[a][b, half * 1024:(half + 1) * 1024] = kv[1024:2048, a]
            kvs[a][4 * c:4 * c + 4] = kv[2048:2176, a].reshape(4, 32, 2, 128)[:, 0:4]
        if half == 1:
            swap[0][b] = kv[1536:2048, 4]
            swap[1][b] = kv[1536:2048, 5]
            gla_p[b] = r["gla_out"][0]
            conv_p[b] = r["conv_out"][0:2]
        swas[0][4 * c:4 * c + 4] = r["swa_out"][0].reshape(4, 512, 2, 128)
        swas[1][4 * c:4 * c + 4] = r["swa_out"][1].reshape(4, 512, 2, 128)
        gla_s[4 * c:4 * c + 4] = r["gla_out"][1:5]
        conv_s[4 * c:4 * c + 4] = r["conv_out"][2:10].reshape(4, 2, DFF)
    return (y_p, y_s, kvp[0], kvp[1], kvp[2], kvp[3], swap[0], swap[1], gla_p, conv_p,
            kvs[0], kvs[1], kvs[2], kvs[3], swas[0], swas[1], gla_s, conv_s)


def kernel(**inputs):
    nc, mk = build_program()
    maps = make_in_maps(inputs)
    res = run_bass_kernel_spmd(nc, maps, core_ids=list(range(8)))
    return assemble(res.results)
```

```python
import numpy as np
import concourse.bass as bass
import concourse.mybir as mybir
from concourse.bass_utils import run_bass_kernel_spmd

F32 = mybir.dt.float32
BF16 = mybir.dt.bfloat16
I32 = mybir.dt.int32
AF = mybir.ActivationFunctionType
ALU = mybir.AluOpType
AX = mybir.AxisListType

D = 2048
NCOLS = 9768
NT = 17
QT0 = 7
EPS = 1e-6
DFF = 5632
NFB = 44
C_GQ, C_GK, C_GV, C_GR, C_GA, C_NQ, C_KV, C_NG, C_MG = 0, 512, 1024, 2048, 3072, 3088, 4112, 5648, 5672
BIG = 1.0e5
WITH_SAMPLE = True


class MK:
    def __init__(self, nc, n_dma_sems=40):
        self.nc = nc
        self.eng = {"pe": nc.tensor, "act": nc.scalar, "dve": nc.vector,
                    "pool": nc.gpsimd, "sp": nc.sync}
        self._stack = []
        self.esem = {}
        for e in ("pe", "act", "dve", "pool"):
            self.esem[e] = self._sem("es_" + e)
        self.ecount = {e: 0 for e in self.esem}
        self.dsem = [self._sem("ds%d" % i) for i in range(n_dma_sems)]
        self.dtot = [0] * n_dma_sems
        self.drr = 0
        self.known = {e: {} for e in self.eng}
        self.last_w = {}
        self.readers = {}
        self.n_wait = 0
        self.n_ins = 0
        self.n_dma = 0

    def _sem(self, name):
        cm = self.nc.semaphore(name)
        s = cm.__enter__()
        self._stack.append(cm)
        return s

    def _need(self, E, reads, writes):
        need = {}

        def add(tok, same_ok):
            if tok is None:
                return
            sem, val, src = tok
            if src == E and not same_ok:
                return
            k = id(sem)
            if k not in need or need[k][1] < val:
                need[k] = (sem, val)

        for k in reads:
            add(self.last_w.get(k), E != "pe")
        for k in writes:
            add(self.last_w.get(k), False)
            for tok in self.readers.get(k, {}).values():
                add(tok, False)
        kn = self.known[E]
        eng = self.eng[E]
        for k, (sem, val) in need.items():
            if kn.get(k, 0) >= val:
                continue
            eng.wait_ge(sem, val)
            self.n_wait += 1
            kn[k] = val

    def _commit(self, tok, reads, writes):
        for k in reads:
            d = self.readers.setdefault(k, {})
            d[id(tok[0])] = tok
        for k in writes:
            self.last_w[k] = tok
            self.readers[k] = {}

    def op(self, E, fn, reads=(), writes=()):
        self._need(E, reads, writes)
        ins = fn(self.eng[E])
        self.ecount[E] += 1
        ins.then_inc(self.esem[E], 1)
        self.n_ins += 1
        tok = (self.esem[E], self.ecount[E], E)
        self._commit(tok, reads, writes)
        return tok

    def dma(self, E, out, in_, reads=(), writes=(), **kw):
        i = self.drr
        self.drr = (self.drr + 1) % len(self.dsem)
        sem = self.dsem[i]
        kn = self.known[E]
        if self.dtot[i] > 0 and kn.get(id(sem), 0) < self.dtot[i]:
            self.eng[E].wait_ge(sem, self.dtot[i])
            kn[id(sem)] = self.dtot[i]
        self._need(E, reads, writes)
        self.eng[E].dma_start(out=out, in_=in_, **kw).then_inc(sem, 16)
        self.n_dma += 1
        self.dtot[i] += 16
        tok = (sem, self.dtot[i], "dma")
        self._commit(tok, reads, writes)
        return tok

    def barrier(self):
        for E, eng in self.eng.items():
            kn = self.known[E]
            for X, sem in self.esem.items():
                if X == E or self.ecount[X] == 0:
                    continue
                if kn.get(id(sem), 0) < self.ecount[X]:
                    eng.wait_ge(sem, self.ecount[X])
                    kn[id(sem)] = self.ecount[X]
            for i, sem in enumerate(self.dsem):
                if self.dtot[i] and kn.get(id(sem), 0) < self.dtot[i]:
                    eng.wait_ge(sem, self.dtot[i])
                    kn[id(sem)] = self.dtot[i]
        self.last_w = {}
        self.readers = {}


class Rot:
    def __init__(self, aps, name):
        self.aps = aps
        self.name = name
        self.i = 0

    def next(self):
        j = self.i % len(self.aps)
        self.i += 1
        return self.aps[j], (self.name, j)


class Ctx:
    pass


def build_program(stage=99, n_pool=2560):
    N_POOL = n_pool
    nc = bass.Bass("TRN2", target_bir_lowering=False)
    mk = MK(nc)
    X = Ctx()
    X.nc, X.mk = nc, mk
    live = []

    def din(name, shape, dt=F32):
        return nc.dram_tensor(name, list(shape), dt, kind="ExternalInput").ap()

    def dout(name, shape, dt=F32):
        return nc.dram_tensor(name, list(shape), dt, kind="ExternalOutput").ap()

    def dscr(name, shape, dt=F32):
        return nc.dram_tensor(name, list(shape), dt).ap()

    def sb(name, shape, dt=F32):
        cm = nc.sbuf_tensor(name, list(shape), dt)
        t = cm.__enter__()
        live.append(cm)
        return t[:] if not hasattr(t, "shape") or True else t

    def sb_mark():
        return len(live)

    def sb_release(mark):
        while len(live) > mark:
            live.pop().__exit__(None, None, None)

    def dbg(name, ap, keys):
        if stage != 2:
            return
        d_ = dout("dbg_" + name, list(ap.shape), F32 if ap.dtype == F32 else ap.dtype)
        mk.dma("sp", d_, ap, reads=keys, writes=["dbg_" + name])

    def rot(name, shape, dt, n):
        return Rot([sb("%s%d" % (name, i), shape, dt) for i in range(n)], name)

    xbuf = din("xbuf", [NT * 128, D])
    w_in = din("w_in", [D, NCOLS])
    norm1_g = din("norm1_g", [D])
    ident_d = din("ident", [128, 128])
    rope_cos = din("rope_cos", [NT * 128, 64])
    rope_sin = din("rope_sin", [NT * 128, 64])
    k_slc_norm_g = din("k_slc_norm_g", [128])
    k_swa_norm_g = din("k_swa_norm_g", [128])
    kv_out = dout("kv_out", [NT * 128, 1536])
    proj = dscr("proj", [NT * 128, NCOLS])
    h_scr = dscr("h_scr", [10 * 128, D])
    gla_out = dout("gla_out", [5, 4, 128, 256])
    conv_out = dout("conv_out", [10, DFF])
    y_out = dout("y_out", [1040, D])
    swa_out = dout("swa_out", [2, 4, 512, 256])
    gla_w_a2 = din("gla_w_a2", [16, 512]); gla_b_a2 = din("gla_b_a2", [512]); gla_onorm_g = din("gla_onorm_g", [256])
    q_norm_g = din("q_norm_g", [128]); k_cmp_norm_g = din("k_cmp_norm_g", [128])
    CMPW = {}
    for nm in ("k", "v"):
        CMPW[nm] = dict(pe=din("cmp_pe_" + nm, [32, 128]), w1=din("cmp_w1_" + nm, [32, 128, 128]),
                        b1=din("cmp_b1_" + nm, [128]), w2=din("cmp_w2_" + nm, [128, 128]), b2=din("cmp_b2_" + nm, [128]))
    ucum_p = din("ucum_p", [128, 128]); ucum_s = din("ucum_s", [128, 128])
    caus_p = din("caus_p", [128, 128]); caus_s = din("caus_s", [128, 128])
    uend_p = din("uend_p", [128, 1]); uend_s = din("uend_s", [128, 4]); seqmask = din("seqmask", [128, 4])
    state_gla_c = din("state_gla_c", [4, 4, 128, 256])
    state_conv_c = din("state_conv_c", [4, 2, DFF])
    state_swa_c = din("state_swa_c", [2, 4, 512, 256])
    mmap_p = din("mmap_p", [128, 32]); tri_mask = din("tri_mask", [128, 128])
    cmp_add = din("cmp_add", [9 * 128, 127]); cmp_mul = din("cmp_mul", [9 * 128, 127])
    t_sel = din("t_sel", [9 * 128, 32]); t_inv = din("t_inv", [9 * 128, 32])
    swa_mask = din("swa_mask", [9 * 128, 640])
    page_tab = din("page_tab", [1, 256], I32)
    if WITH_SAMPLE:
        cache_k_cmp = din("cache_k_cmp", [N_POOL, 128, 256]); cache_v_cmp = din("cache_v_cmp", [N_POOL, 128, 256])
        cache_k_slc = din("cache_k_slc", [N_POOL, 128, 256]); cache_v_slc = din("cache_v_slc", [N_POOL, 128, 256])
    mmap_s = din("mmap_s", [512, 129]); t_sel_s = din("t_sel_s", [128, 129]); newmask = din("newmask", [4, 128, 128])
    swa_past_mask = din("swa_past_mask", [128, 512])
    o_scr = dscr("o_scr", [4, 2, 3, 128, 128])
    w_br_gla = din("w_br_gla", [1024, D]); w_br_nsa = din("w_br_nsa", [1024, D]); w_o = din("w_o", [D, D])
    norm2_g = din("norm2_g", [D]); w_up = din("w_up", [D, 2 * DFF]); conv_w = din("conv_w", [3, DFF])
    conv_b = din("conv_b", [DFF]); w_down = din("w_down", [DFF, D])

    Q1 = "act" if WITH_SAMPLE else "sp"
    if WITH_SAMPLE:
        gath = dscr("gath", [4, 256, 128, 256])
        caches = [cache_k_cmp, cache_v_cmp, cache_k_slc, cache_v_slc]
        gsem = [mk._sem("gs%d" % i) for i in range(4)]
        sp_ = nc.sync
        g_cnt = sp_.alloc_register("g_cnt")
        g_pid = sp_.alloc_register("g_pid")
        sp_.reg_mov(g_cnt, 256)
        with sp_.While(g_cnt):
            sp_.reg_sub(g_cnt, g_cnt, 1)
            g_idx = sp_.snap(g_cnt, min_val=0, max_val=255)
            sp_.reg_load(g_pid, page_tab[0:1, bass.ds(g_idx, 1)])
            g_pv = sp_.snap(g_pid, min_val=0, max_val=N_POOL - 1)
            for ci in range(4):
                sp_.dma_start(out=gath[ci, bass.ds(g_idx, 1)], in_=caches[ci][bass.ds(g_pv, 1)]).then_inc(gsem[ci], 16)
        for ci in range(4):
            sp_.wait_ge(gsem[ci], 16 * 256)
    psf = Rot([nc.alloc_psum_tensor("psf%d" % i, [128, 512], F32).ap() for i in range(6)], "psf")
    psb = Rot([nc.alloc_psum_tensor("psb%d" % i, [128, 1024], BF16).ap() for i in range(2)], "psb")

    ident = sb("identf", [128, 128], F32)
    mk.dma(Q1, ident, ident_d, writes=["ident"])
    epsc = sb("epsc", [128, 1], F32)
    mk.op("dve", lambda e: e.memset(epsc, EPS), writes=["epsc"])
    onec = sb("onec", [128, 1], F32)
    mk.op("dve", lambda e: e.memset(onec, 1.0), writes=["onec"])
    identb = sb("identb", [128, 128], BF16)
    mk.op("dve", lambda e: e.tensor_copy(identb, ident), reads=["ident"], writes=["identb"])
    cpy_i = [0]
    m_keep = sb_mark()
    o_glaT = sb("o_glaT", [128, 8, 10 * 128], BF16)
    o_nsaT = sb("o_nsaT", [128, 8, 10 * 128], BF16)

    def evac(out, in_, ps_key, writes, reads=()):
        cpy_i[0] += 1
        if cpy_i[0] % 2:
            mk.op("act", lambda e: e.copy(out, in_), reads=reads, writes=[ps_key] + list(writes))
        else:
            mk.op("dve", lambda e: e.tensor_copy(out, in_), reads=reads, writes=[ps_key] + list(writes))

    SC = 128.0 ** -0.5

    def ld(name, dram_ap, shape, dt=F32, eng="sp"):
        t_ = sb(name, shape, dt)
        mk.dma(eng, t_, dram_ap, writes=[name])
        return t_

    def transpose_into(dst, src_list, reads, wkey, idn=None, n_in=128):
        bank, kb = psf.next()

        def tr(e):
            for j, s_ in enumerate(src_list):
                i = e.transpose(bank[:, j * 128:(j + 1) * 128], s_, ident)
            return i
        mk.op("pe", tr, reads=list(reads) + ["ident"], writes=[kb])
        evac(dst, bank[:, 0:128 * len(src_list)].rearrange("p (j n) -> p j n", j=len(src_list)), kb, [wkey])


    m_ph12 = sb_mark()
    xnT = sb("xnT", [128, 16, NT * 128], BF16)
    g1b = sb("g1b", [128, D], F32)
    mk.dma(Q1, g1b, norm1_g.partition_broadcast(128), writes=["g1b"])
    xr = rot("xt", [128, D], F32, 2)
    xnr = rot("xn", [128, D], F32, 2)
    junk = sb("junk", [128, D], F32)
    ssr = rot("ss", [128, 2], F32, 2)
    for t in range(NT):
        xt, kx = xr.next()
        xn, kn_ = xnr.next()
        ss, ks = ssr.next()
        mk.dma(Q1, xt, xbuf[t * 128:(t + 1) * 128, :], writes=[kx])
        mk.op("act", lambda e: e.activation(junk, xt, AF.Square), reads=[kx], writes=["junk"])
        mk.op("dve", lambda e: e.reduce_sum(ss[:, 0:1], junk, axis=AX.X), reads=["junk"], writes=[ks])
        mk.op("act", lambda e: e.activation(ss[:, 1:2], ss[:, 0:1], AF.Sqrt, bias=epsc[:, 0:1], scale=1.0 / D),
              reads=[ks, "epsc"], writes=[ks])
        mk.op("dve", lambda e: e.reciprocal(ss[:, 1:2], ss[:, 1:2]), reads=[ks], writes=[ks])
        mk.op("dve", lambda e: e.scalar_tensor_tensor(xn, xt, ss[:, 1:2], g1b, op0=ALU.mult, op1=ALU.mult),
              reads=[kx, ks, "g1b"], writes=[kn_])
        for c4 in range(4):
            bank, kb = psf.next()

            def tr(e, c4=c4, bank=bank, xn=xn):
                for j in range(4):
                    c = c4 * 4 + j
                    i = e.transpose(bank[:, j * 128:(j + 1) * 128], xn[:, c * 128:(c + 1) * 128], ident)
                return i
            mk.op("pe", tr, reads=[kn_, "ident"], writes=[kb])
            evac(xnT[:, c4 * 4:(c4 + 1) * 4, t * 128:(t + 1) * 128],
                 bank.rearrange("p (j n) -> p j n", j=4), kb, [("xnT", t)])

    wr = rot("wb", [128, 16, 512], BF16, 2)
    str_ = rot("stg", [128, 512], F32, 4)
    ncb = (NCOLS + 511) // 512
    for cb in range(ncb):
        c0 = cb * 512
        n = min(512, NCOLS - c0)
        c1 = c0 + n
        prev_need = (c0 < C_GR and c1 > C_GK) or (c0 < C_NQ and c1 > C_GA) or (c0 < C_NG and c1 > C_KV)
        wb, kw = wr.next()
        for c in range(16):
            mk.dma("pool", wb[:, c, :n], w_in[c * 128:(c + 1) * 128, c0:c1], writes=[(kw, c)])
        for t in (range(NT) if prev_need else range(QT0, NT)):
            bank, kb = psf.next()

            def mm(e, bank=bank, wb=wb, t=t, n=n):
                for c in range(16):
                    i = e.matmul(bank[:, :n], xnT[:, c, t * 128:(t + 1) * 128], wb[:, c, :n],
                                 start=(c == 0), stop=(c == 15))
                return i
            mk.op("pe", mm, reads=[(kw, c) for c in range(16)] + [("xnT", t)], writes=[kb])
            st, kst = str_.next()
            evac(st[:, :n], bank[:, :n], kb, [kst])
            mk.dma(Q1, proj[t * 128:(t + 1) * 128, c0:c1], st[:, :n], reads=[kst], writes=[("proj", t, cb)])
    mk.barrier()
    sb_release(m_ph12)
    if stage == 0:
        while live:
            live.pop().__exit__(None, None, None)
        return nc, mk

    m_gla = sb_mark()
    wa2 = ld("wa2", gla_w_a2, [16, 512])
    ba2b = ld("ba2b", gla_b_a2.partition_broadcast(128), [128, 512])
    gob = ld("gob", gla_onorm_g.partition_broadcast(128), [128, 256])
    Ucp = ld("Ucp", ucum_p, [128, 128])
    Ucs = ld("Ucs", ucum_s, [128, 128])
    c01p = ld("c01p", caus_p, [128, 128])
    c01s = ld("c01s", caus_s, [128, 128])
    uendp = ld("uendp", uend_p, [128, 1])
    uends = ld("uends", uend_s, [128, 4])
    seqm = ld("seqm", seqmask, [128, 4])
    Sp = sb("Sp", [128, 4, 256], F32)
    Spb = sb("Spb", [128, 4, 256], BF16)
    mk.op("dve", lambda e: e.memset(Sp, 0.0), writes=[("S", 0, h) for h in range(4)])
    mk.op("pool", lambda e: e.memset(Spb, 0.0), writes=[("Sb", 0, h) for h in range(4)])
    Ss = sb("Ss", [128, 4, 4, 256], F32)
    Ssb = sb("Ssb", [128, 4, 4, 256], BF16)
    for s in range(4):
        mk.dma("sp", Ss[:, s], state_gla_c[s].rearrange("h d e -> d h e"), writes=[("S", 1 + s, h) for h in range(4)])
        mk.op("pool", lambda e: e.tensor_copy(Ssb[:, s], Ss[:, s]), reads=[("S", 1 + s, h) for h in range(4)],
              writes=[("Sb", 1 + s, h) for h in range(4)])
    qeTm = sb("qeTm", [128, 4, 4, 128], BF16)
    mk.op("pool", lambda e: e.memset(qeTm, 0.0), writes=["qeTm"])
    a_r = rot("ga", [128, 16], F32, 2)
    aT_r = rot("gaT", [16, 128], F32, 2)
    k_r = rot("gk", [128, 512], F32, 2)
    v_r = rot("gv", [128, 1024], F32, 2)
    q_r = rot("gq", [128, 512], F32, 2)
    r_r = rot("gr", [128, 1024], F32, 1)
    ln_r = rot("gln", [128, 512], F32, 2)
    e1_r = rot("ge1", [128, 512], F32, 1)
    e2_r = rot("ge2", [128, 512], F32, 2)
    eb_r = rot("geb", [128, 16], F32, 2)
    qe_r = rot("gqe", [128, 512], F32, 1)
    ke_r = rot("gke", [128, 512], F32, 2)
    keb_r = rot("gkeb", [128, 512], BF16, 2)
    kem = sb("gkem", [128, 4, 512], BF16)
    vb_r = rot("gvb", [128, 1024], BF16, 2)
    qeT_r = rot("gqeT", [128, 4, 128], BF16, 2)
    keT_r = rot("gkeT", [128, 4, 128], BF16, 2)
    att_r = rot("gatt", [128, 4, 128], BF16, 2)
    osb_r = rot("gosb", [128, 4, 256], F32, 1)
    jg = sb("jg", [128, 4, 256], F32)
    s8_r = rot("gs8", [128, 8], F32, 2)
    og_r = rot("gog", [128, 1024], F32, 1)
    SCQ = 128.0 ** -0.5
    for t in range(NT):
        isq = t >= QT0
        smp = (t == 16)
        G = 4 if smp else 1
        qi = t - QT0
        rows = slice(t * 128, (t + 1) * 128)
        a_t, ka = a_r.next()
        k_t, kk_ = k_r.next()
        v_t, kv_ = v_r.next()
        mk.dma("sp", a_t, proj[rows, C_GA:C_NQ], writes=[ka])
        mk.dma("sp", k_t, proj[rows, C_GK:C_GV], writes=[kk_])
        mk.dma("sp", v_t, proj[rows, C_GV:C_GR], writes=[kv_])
        if isq:
            q_t, kq_ = q_r.next()
            r_t, kr_ = r_r.next()
            mk.dma("sp", q_t, proj[rows, C_GQ:C_GK], writes=[kq_])
            mk.dma("sp", r_t, proj[rows, C_GR:C_GA], writes=[kr_])
        aT, kaT = aT_r.next()
        bank, kb = psf.next()
        mk.op("pe", lambda e: e.transpose(bank[0:16, 0:128], a_t, ident), reads=[ka, "ident"], writes=[kb])
        evac(aT, bank[0:16, 0:128], kb, [kaT])
        bank, kb = psf.next()
        mk.op("pe", lambda e: e.matmul(bank, aT, wa2, start=True, stop=True), reads=[kaT, "wa2"], writes=[kb])
        lnv, kln = ln_r.next()
        mk.op("dve", lambda e: e.tensor_tensor(lnv, bank, ba2b, ALU.add), reads=["ba2b"], writes=[kb, kln])
        mk.op("act", lambda e: e.activation(lnv, lnv, AF.Exp, scale=-1.0), reads=[kln], writes=[kln])
        mk.op("act", lambda e: e.activation(lnv, lnv, AF.Ln, bias=onec[:, 0:1]), reads=[kln, "onec"], writes=[kln])
        bank, kb = psf.next()
        U = Ucs if smp else Ucp
        mk.op("pe", lambda e: e.matmul(bank, U, lnv, start=True, stop=True), reads=[kln, "Ucp", "Ucs"], writes=[kb])
        e2, ke2 = e2_r.next()
        mk.op("act", lambda e: e.activation(e2, bank, AF.Exp, scale=-1.0), writes=[kb, ke2])
        if isq:
            e1, ke1 = e1_r.next()
            mk.op("act", lambda e: e.activation(e1, bank, AF.Exp), writes=[kb, ke1])
        bank, kb = psf.next()
        uend = uends if smp else uendp

        def mmbl(e, bank=bank, lnv=lnv, uend=uend, G=G):
            for h in range(4):
                i = e.matmul(bank[:, h * G:(h + 1) * G], lnv[:, h * 128:(h + 1) * 128], uend, start=True, stop=True)
            return i
        mk.op("pe", mmbl, reads=[kln, "uendp", "uends"], writes=[kb])
        eb, keb_ = eb_r.next()
        mk.op("act", lambda e: e.activation(eb[:, 0:4 * G], bank[:, 0:4 * G], AF.Exp), writes=[kb, keb_])
        if t == 8:
            dbg("lnv", lnv, [kln]); dbg("e2", e2, [ke2]); dbg("eb", eb, [keb_]); dbg("aT", aT, [kaT]); dbg("a", a_t, [ka])
        ke, kke = ke_r.next()
        keb, kkeb = keb_r.next()
        vb, kvb = vb_r.next()
        mk.op("dve", lambda e: e.tensor_tensor(ke, k_t, e2, ALU.mult), reads=[kk_, ke2], writes=[kke])
        mk.op("pool", lambda e: e.tensor_copy(keb, ke), reads=[kke], writes=[kkeb])
        mk.op("pool", lambda e: e.tensor_copy(vb, v_t), reads=[kv_], writes=[kvb])
        if smp:
            for s in range(4):
                mk.op("dve", lambda e: e.tensor_scalar(kem[:, s, :], ke, seqm[:, s:s + 1], None, op0=ALU.mult),
                      reads=[kke, "seqm"], writes=[("kem", s)])
        if isq:
            qe, kqe = qe_r.next()
            mk.op("dve", lambda e: e.scalar_tensor_tensor(qe, q_t, SCQ, e1, op0=ALU.mult, op1=ALU.mult),
                  reads=[kq_, ke1], writes=[kqe])
            qeT, kqeT = qeT_r.next()
            keT, kkeT = keT_r.next()
            transpose_into(qeT, [qe[:, h * 128:(h + 1) * 128] for h in range(4)], [kqe], kqeT)
            transpose_into(keT, [ke[:, h * 128:(h + 1) * 128] for h in range(4)], [kke], kkeT)
            bank, kb = psf.next()

            def mmatt(e, bank=bank, keT=keT, qeT=qeT):
                for h in range(4):
                    i = e.matmul(bank[:, h * 128:(h + 1) * 128], keT[:, h, :], qeT[:, h, :], start=True, stop=True)
                return i
            mk.op("pe", mmatt, reads=[kqeT, kkeT], writes=[kb])
            att, katt = att_r.next()
            c01 = c01s if smp else c01p
            mk.op("dve", lambda e: e.tensor_tensor(att, bank.rearrange("p (h n) -> p h n", h=4),
                                                   c01.unsqueeze(1).to_broadcast([128, 4, 128]), ALU.mult),
                  reads=["c01p", "c01s"], writes=[kb, katt])
            if smp:
                for s in range(4):
                    mk.op("pool", lambda e: e.tensor_copy(qeTm[:, s, :, 32 * s:32 * s + 32], qeT[:, :, 32 * s:32 * s + 32]),
                          reads=[kqeT], writes=["qeTm"])
            osb, kosb = osb_r.next()
            for hp in range(2):
                bank, kb = psf.next()

                def mmo(e, bank=bank, hp=hp, att=att, vb=vb, qeT=qeT):
                    for h in (2 * hp, 2 * hp + 1):
                        o_ = bank[:, (h % 2) * 256:(h % 2 + 1) * 256]
                        e.matmul(o_, att[:, h, :], vb[:, h * 256:(h + 1) * 256], start=True, stop=False)
                        if smp:
                            for s in range(4):
                                i = e.matmul(o_, qeTm[:, s, h, :], Ssb[:, s, h, :], start=False, stop=(s == 3))
                        else:
                            i = e.matmul(o_, qeT[:, h, :], Spb[:, h, :], start=False, stop=True)
                    return i
                sbk = [("Sb", (1 + s if smp else 0), h) for h in (2 * hp, 2 * hp + 1) for s in range(G)]
                mk.op("pe", mmo, reads=[katt, kvb, kqeT, "qeTm"] + sbk, writes=[kb])
                evac(osb[:, 2 * hp:2 * hp + 2, :], bank.rearrange("p (h n) -> p h n", h=2), kb, [(kosb, hp)])
        for h in range(4):
            for s in range(G):
                si = 1 + s if smp else 0
                S_ = Ss[:, s, h, :] if smp else Sp[:, h, :]
                Sb_ = Ssb[:, s, h, :] if smp else Spb[:, h, :]
                lhs = kem[:, s, h * 128:(h + 1) * 128] if smp else keb[:, h * 128:(h + 1) * 128]
                bank, kb = psf.next()
                mk.op("pe", lambda e: e.matmul(bank[:, 0:256], lhs, vb[:, h * 256:(h + 1) * 256], start=True, stop=True),
                      reads=[kkeb, kvb, ("kem", s)], writes=[kb])
                ebc = eb[:, h * G + s:h * G + s + 1]
                mk.op("dve", lambda e: e.tensor_scalar(S_, S_, ebc, None, op0=ALU.mult), reads=[keb_, ("S", si, h)],
                      writes=[("S", si, h)])
                mk.op("dve", lambda e: e.scalar_tensor_tensor(S_, bank[:, 0:256], ebc, S_, op0=ALU.mult, op1=ALU.add),
                      reads=[keb_, ("S", si, h)], writes=[kb, ("S", si, h)])
                mk.op("act", lambda e: e.copy(Sb_, S_), reads=[("S", si, h)], writes=[("Sb", si, h)])
        if t == 8:
            dbg("ke", ke, [kke]); dbg("S8", Sp, [("S", 0, h) for h in range(4)]); dbg("osb", osb, [(kosb, 0), (kosb, 1)])
            dbg("qe", qe, [kqe]); dbg("e1", e1, [ke1])
        if isq:
            s8, ks8 = s8_r.next()
            og, kog = og_r.next()
            rdo = [(kosb, 0), (kosb, 1)]
            mk.op("act", lambda e: e.activation(jg, osb, AF.Square), reads=rdo, writes=["jg"])
            mk.op("dve", lambda e: e.reduce_sum(s8[:, 0:4], jg, axis=AX.X), reads=["jg"], writes=[ks8])
            mk.op("act", lambda e: e.activation(s8[:, 4:8], s8[:, 0:4], AF.Sqrt, bias=epsc[:, 0:1], scale=1.0 / 256),
                  reads=[ks8, "epsc"], writes=[ks8])
            mk.op("dve", lambda e: e.reciprocal(s8[:, 4:8], s8[:, 4:8]), reads=[ks8], writes=[ks8])
            og4 = og.rearrange("p (h n) -> p h n", h=4)
            mk.op("dve", lambda e: e.tensor_tensor(og4, osb, s8[:, 4:8].unsqueeze(2).to_broadcast([128, 4, 256]), ALU.mult),
                  reads=rdo + [ks8], writes=[kog])
            mk.op("pool", lambda e: e.tensor_tensor(og4, og4, gob.unsqueeze(1).to_broadcast([128, 4, 256]), ALU.mult),
                  reads=[kog, "gob"], writes=[kog])
            mk.op("act", lambda e: e.activation(r_t, r_t, AF.Silu), reads=[kr_], writes=[kr_])
            mk.op("dve", lambda e: e.tensor_tensor(og, og, r_t, ALU.mult), reads=[kog, kr_], writes=[kog])
            for c4 in range(2):
                transpose_into(o_glaT[:, c4 * 4:(c4 + 1) * 4, qi * 128:(qi + 1) * 128],
                               [og[:, (c4 * 4 + j) * 128:(c4 * 4 + j + 1) * 128] for j in range(4)], [kog], ("oglaT", qi))
        if t == 15:
            mk.dma("sp", gla_out[0].rearrange("h d e -> d h e"), Sp, reads=[("S", 0, h) for h in range(4)], writes=["glao0"])
    for s in range(4):
        mk.dma("sp", gla_out[1 + s].rearrange("h d e -> d h e"), Ss[:, s], reads=[("S", 1 + s, h) for h in range(4)],
               writes=[("glao", s)])
    mk.barrier()
    sb_release(m_gla)
    if stage <= 2:
        mk.barrier()
        return nc, mk

    KTn = sb("KTn", [128, 2, 2, 128], BF16)
    VVn = sb("VVn", [128, 2, 2, 128], BF16)
    kcT = sb("kcT", [128, 2, 128], BF16)
    vc = sb("vc", [128, 2, 128], F32)
    cw = {}
    for nm in ("k", "v"):
        w1 = sb("w1" + nm, [128, 32, 128], BF16)
        for r8 in range(8):
            mk.dma("pool", w1[:, r8 * 4:(r8 + 1) * 4, :],
                   CMPW[nm]["w1"][r8 * 4:(r8 + 1) * 4].rearrange("r d f -> d r f"), writes=[("w1" + nm, r8)])
        w2 = sb("w2" + nm, [128, 128], BF16)
        mk.dma("pool", w2, CMPW[nm]["w2"], writes=["w2" + nm])
        b1 = ld("b1" + nm, CMPW[nm]["b1"].rearrange("(f o) -> f o", o=1), [128, 1])
        b2b = ld("b2b" + nm, CMPW[nm]["b2"].partition_broadcast(128), [128, 128])
        pe_ = ld("pe" + nm, CMPW[nm]["pe"], [32, 128])
        peT = sb("peT" + nm, [128, 32], BF16)
        bank, kb = psf.next()
        mk.op("pe", lambda e: e.transpose(bank[:, 0:32], pe_, ident[0:32, 0:32]), reads=["pe" + nm, "ident"], writes=[kb])
        evac(peT, bank[:, 0:32], kb, ["peT" + nm])
        c1 = sb("c1" + nm, [128, 1], F32)
        bank, kb = psf.next()

        def mmc(e, bank=bank, w1=w1, peT=peT):
            for rr in range(32):
                i = e.matmul(bank[:, 0:1], w1[:, rr, :], peT[:, rr:rr + 1], start=(rr == 0), stop=(rr == 31))
            return i
        mk.op("pe", mmc, reads=[("w1" + nm, r8) for r8 in range(8)] + ["peT" + nm], writes=[kb])
        mk.op("dve", lambda e: e.tensor_tensor(c1, bank[:, 0:1], b1, ALU.add), reads=["b1" + nm], writes=[kb, "c1" + nm])
        cw[nm] = dict(w1=w1, w2=w2, b2b=b2b, c1=c1)
    gkc = ld("gkc", k_cmp_norm_g.partition_broadcast(128), [128, 128])
    gTr = rot("gTc", [128, 512], BF16, 2)
    o2r = rot("o2c", [128, 128], F32, 2)
    sc3 = rot("sc3", [128, 4], F32, 2)
    jc = sb("jc", [128, 128], F32)

    def compress(nm, rowsT, nblk, rreads, out_fn):
        W = cw[nm]
        bank, kb = psf.next()

        def mm1(e):
            for rr in range(32):
                i = e.matmul(bank[:, 0:nblk], W["w1"][:, rr, :], rowsT[:, rr:rr + 16 * (nblk - 1) + 1:16],
                             start=(rr == 0), stop=(rr == 31))
            return i
        mk.op("pe", mm1, reads=list(rreads) + [("w1" + nm, r8) for r8 in range(8)], writes=[kb])
        gT, kg = gTr.next()
        mk.op("act", lambda e: e.activation(gT[:, 0:nblk], bank[:, 0:nblk], AF.Gelu_apprx_tanh, bias=W["c1"][:, 0:1]),
              reads=["c1" + nm], writes=[kb, kg])
        for mt in range((nblk + 127) // 128):
            m = min(128, nblk - mt * 128)
            bank2, kb2 = psf.next()
            mk.op("pe", lambda e: e.matmul(bank2[0:m, 0:128], gT[:, mt * 128:mt * 128 + m], W["w2"], start=True, stop=True),
                  reads=[kg, "w2" + nm], writes=[kb2])
            o2, ko2 = o2r.next()
            mk.op("dve", lambda e: e.tensor_tensor(o2[0:m], bank2[0:m, 0:128], W["b2b"][0:m], ALU.add),
                  reads=["b2b" + nm], writes=[kb2, ko2])
            out_fn(mt, m, o2, ko2)

    def rms_rows(o2, ko2, m, gain, gkey, out, okey):
        s3, ks3 = sc3.next()
        mk.op("act", lambda e: e.activation(jc[0:m], o2[0:m], AF.Square), reads=[ko2], writes=["jc"])
        mk.op("dve", lambda e: e.reduce_sum(s3[0:m, 0:1], jc[0:m], axis=AX.X), reads=["jc"], writes=[ks3])
        mk.op("act", lambda e: e.activation(s3[0:m, 1:2], s3[0:m, 0:1], AF.Sqrt, bias=epsc[0:m, 0:1], scale=1.0 / 128),
              reads=[ks3, "epsc"], writes=[ks3])
        mk.op("dve", lambda e: e.reciprocal(s3[0:m, 1:2], s3[0:m, 1:2]), reads=[ks3], writes=[ks3])
        mk.op("dve", lambda e: e.scalar_tensor_tensor(out[0:m], o2[0:m], s3[0:m, 1:2], gain[0:m], op0=ALU.mult, op1=ALU.mult),
              reads=[ko2, ks3, gkey], writes=[okey])

    kcr = rot("kcn", [128, 128], F32, 2)
    m_smp_keep = sb_mark()
    gqb = ld("gqb", q_norm_g.partition_broadcast(128), [128, 128])
    mmap = ld("mmap", mmap_p, [128, 32])
    trim = ld("trim", tri_mask, [128, 128])
    qraw_r = rot("nq", [128, 1024], F32, 1)
    g24_r = rot("ng", [128, 24], F32, 2)
    cs2_r = rot("ncs", [128, 2, 64], F32, 2)
    s16_r = rot("ns16", [128, 16], F32, 2)
    qnT = sb("nqnT", [128, 8, 128], BF16)
    qrT = sb("nqrT", [128, 8, 128], BF16)
    pT_r = rot("npT", [128, 16, 128], BF16, 2)
    pc_r = rot("npc", [128, 128], F32, 2)
    pcT = sb("npcT", [128, 4, 128], F32)
    st_r = rot("nst", [128, 8], F32, 4)
    coef = sb("ncoef", [128, 2, 24], F32)
    scb = sb("nscb", [128, 4, 32], F32)
    m8 = sb("nm8", [128, 2, 8], F32)
    onsa = sb("nonsa", [128, 8, 128], F32)
    otmp = sb("notmp", [128, 4, 128], F32)

    m_prompt_only = sb_mark()
    KTa = sb("KTa", [128, 2, 2, 2048], BF16)
    VV = sb("VV", [128, 16, 2, 2, 128], BF16)
    m_ktc = sb_mark()
    KTc = sb("KTc", [128, 2, 2, 2048], BF16)
    m_ph3 = sb_mark()
    gkb = sb("gkb", [128, 2, 128], F32)
    mk.dma("sp", gkb[:, 0, :], k_slc_norm_g.partition_broadcast(128), writes=["gkb0"])
    mk.dma("sp", gkb[:, 1, :], k_swa_norm_g.partition_broadcast(128), writes=["gkb1"])
    kvr = rot("kv", [128, 1536], F32, 2)
    knr = rot("kn", [128, 2, 2, 128], F32, 2)
    csr = rot("cs", [128, 2, 64], F32, 2)
    j4 = sb("j4", [128, 2, 2, 128], F32)
    s4r = rot("s4", [128, 8], F32, 2)
    tmpr = rot("rt", [128, 4, 2, 2, 64], F32, 2)
    for t in range(NT):
        kv, kkv = kvr.next()
        kn, kkn = knr.next()
        cs, kcs = csr.next()
        s4, ks4 = s4r.next()
        tp, ktp = tmpr.next()
        mk.dma("sp", kv, proj[t * 128:(t + 1) * 128, C_KV:C_NG],
               reads=[("proj", t, cb) for cb in range(C_KV // 512, (C_NG - 1) // 512 + 1)], writes=[kkv])
        mk.dma("sp", cs[:, 0, :], rope_cos[t * 128:(t + 1) * 128, :], writes=[(kcs, 0)])
        mk.dma("sp", cs[:, 1, :], rope_sin[t * 128:(t + 1) * 128, :], writes=[(kcs, 1)])
        kv4 = kv.rearrange("p (a g d) -> p a g d", a=6, g=2)
        kk = kv4[:, 2:6:2]
        mk.op("act", lambda e: e.activation(j4, kk, AF.Square), reads=[kkv], writes=["j4"])
        mk.op("dve", lambda e: e.reduce_sum(s4[:, 0:4].rearrange("p (a g) -> p a g", a=2), j4, axis=AX.X),
              reads=["j4"], writes=[ks4])
        mk.op("act", lambda e: e.activation(s4[:, 4:8], s4[:, 0:4], AF.Sqrt, bias=epsc[:, 0:1], scale=1.0 / 128),
              reads=[ks4, "epsc"], writes=[ks4])
        mk.op("dve", lambda e: e.reciprocal(s4[:, 4:8], s4[:, 4:8]), reads=[ks4], writes=[ks4])
        rb = s4[:, 4:8].rearrange("p (a g) -> p a g", a=2).unsqueeze(3).to_broadcast([128, 2, 2, 128])
        mk.op("dve", lambda e: e.tensor_tensor(kn, kk, rb, ALU.mult), reads=[kkv, ks4], writes=[kkn])
        gb_ = gkb.unsqueeze(2).to_broadcast([128, 2, 2, 128])
        mk.op("pool", lambda e: e.tensor_tensor(kn, kn, gb_, ALU.mult), reads=[kkn, "gkb0", "gkb1"], writes=[kkn])
        cosb = cs[:, 0, :].unsqueeze(1).unsqueeze(1).to_broadcast([128, 2, 2, 64])
        sinb = cs[:, 1, :].unsqueeze(1).unsqueeze(1).to_broadcast([128, 2, 2, 64])
        x1, x2 = kn[:, :, :, 0:64], kn[:, :, :, 64:128]
        rd = [kkn, (kcs, 0), (kcs, 1)]
        mk.op("dve", lambda e: e.tensor_tensor(tp[:, 0], x1, cosb, ALU.mult), reads=rd, writes=[(ktp, 0)])
        mk.op("pool", lambda e: e.tensor_tensor(tp[:, 1], x2, sinb, ALU.mult), reads=rd, writes=[(ktp, 1)])
        mk.op("dve", lambda e: e.tensor_tensor(tp[:, 2], x2, cosb, ALU.mult), reads=rd, writes=[(ktp, 2)])
        mk.op("pool", lambda e: e.tensor_tensor(tp[:, 3], x1, sinb, ALU.mult), reads=rd, writes=[(ktp, 3)])
        mk.op("dve", lambda e: e.tensor_tensor(kk[:, :, :, 0:64], tp[:, 0], tp[:, 1], ALU.subtract),
              reads=[(ktp, 0), (ktp, 1)], writes=[kkv])
        mk.op("dve", lambda e: e.tensor_tensor(kk[:, :, :, 64:128], tp[:, 2], tp[:, 3], ALU.add),
              reads=[(ktp, 2), (ktp, 3)], writes=[kkv])
        mk.dma("sp", kv_out[t * 128:(t + 1) * 128, :], kv, reads=[kkv], writes=[("kvo", t)])
        bank, kb = psf.next()
        if t < 16:
            def tr1(e, bank=bank, kv4=kv4):
                for j in range(4):
                    i = e.transpose(bank[:, j * 128:(j + 1) * 128], kv4[:, j // 2, j % 2, :], ident)
                return i
            mk.op("pe", tr1, reads=[kkv, "ident"], writes=[kb])
            evac(KTc[:, :, :, t * 128:(t + 1) * 128], bank.rearrange("p (a g n) -> p a g n", a=2, g=2), kb, [("KT", 0, t)])
            bank, kb = psf.next()
        dst = KTa[:, :, :, t * 128:(t + 1) * 128] if t < 16 else KTn

        def tr2(e, bank=bank, kv4=kv4):
            for j in range(4):
                i = e.transpose(bank[:, j * 128:(j + 1) * 128], kv4[:, 2 + 2 * (j // 2), j % 2, :], ident)
            return i
        mk.op("pe", tr2, reads=[kkv, "ident"], writes=[kb])
        evac(dst, bank.rearrange("p (a g n) -> p a g n", a=2, g=2), kb, [("KT", 1, t)])
        vdst = VV[:, t] if t < 16 else VVn
        mk.op("pool", lambda e: e.tensor_copy(vdst, kv4[:, 3:6:2]), reads=[kkv], writes=[("VV", t)])
    for kv_i in range(2):
        for s in range(4):
            mk.dma("sp", swa_out[kv_i, s, 0:508, :], state_swa_c[kv_i, s, 4:512, :], writes=[("swao", kv_i, s, 0)])
            mk.dma("sp", swa_out[kv_i, s, 508:512, :], kv_out[2048 + 32 * s:2048 + 32 * s + 4, 1024 + 256 * kv_i:1280 + 256 * kv_i],
                   reads=[("kvo", 16)], writes=[("swao", kv_i, s, 1)])
    mk.barrier()
    sb_release(m_ph3)
    for g in range(2):
        def outk(mt, m, o2, ko2, g=g):
            kc_, kkc = kcr.next()
            rms_rows(o2, ko2, m, gkc, "gkc", kc_, kkc)
            bank, kb = psf.next()
            mk.op("pe", lambda e: e.transpose(bank[:, 0:m], kc_[0:m, :], ident[0:m, 0:m]), reads=[kkc, "ident"], writes=[kb])
            evac(kcT[:, g, 0:m], bank[:, 0:m], kb, [("kcT", g)])

        def outv(mt, m, o2, ko2, g=g):
            mk.op("pool", lambda e: e.tensor_copy(vc[0:m, g, :], o2[0:m]), reads=[ko2], writes=[("vc", g)])
        compress("k", KTc[:, 0, g, :], 127, [("KT", 0, t) for t in range(16)], outk)
        compress("v", KTc[:, 1, g, :], 127, [("KT", 0, t) for t in range(16)], outv)
    mk.barrier()
    sb_release(m_ktc)
    if stage <= 1:
        mk.barrier()
        return nc, mk


    psR = Rot(psf.aps[0:3], "psf")
    B_CMP, B_SLC, B_SWA = psf.aps[3], psf.aps[4], psf.aps[5]
    K_CMP, K_SLC, K_SWA = ("psf", 3), ("psf", 4), ("psf", 5)
    qn_ = sb("nqn", [128, 8, 128], F32)
    qr_ = sb("nqr", [128, 8, 128], F32)
    jq = sb("njq", [128, 8, 128], F32)
    rt4 = sb("nrt4", [128, 4, 8, 64], F32)
    tb_r = rot("ntb", [128, 2, 127], F32, 2)
    ts_r = rot("nts", [128, 2, 32], F32, 2)
    sw_r = rot("nsw", [128, 640], F32, 2)
    sS_r = rot("nsS", [128, 2048], F32, 3)
    pb_r = rot("npb", [128, 2048], BF16, 3)
    def softmax_rows(sS, ksS, n, p_out, kp, rs_col, krs):
        st, kst = st_r.next()
        mk.op("dve", lambda e: e.reduce_max(st[:, 0:1], sS[:, 0:n], axis=AX.X), reads=[ksS], writes=[kst])
        mk.op("dve", lambda e: e.tensor_scalar(st[:, 1:2], st[:, 0:1], -SC, None, op0=ALU.mult), reads=[kst], writes=[kst])
        mk.op("act", lambda e: e.activation(p_out[:, 0:n], sS[:, 0:n], AF.Exp, bias=st[:, 1:2], scale=SC),
              reads=[ksS, kst], writes=[kp])
        mk.op("dve", lambda e: e.reduce_sum(st[:, 2:3], p_out[:, 0:n], axis=AX.X), reads=[kp], writes=[kst])
        mk.op("dve", lambda e: e.reciprocal(rs_col, st[:, 2:3]), reads=[kst], writes=[krs])

    def pv_bf16(pb, kp, ntile, v_fn, vreads, o_ap, o_key):
        pT, kpT = pT_r.next()
        for k0 in range(0, ntile, 8):
            nk = min(8, ntile - k0)
            bank, kb = psb.next()

            def trp(e, bank=bank, k0=k0, nk=nk):
                for j in range(nk):
                    i = e.transpose(bank[:, j * 128:(j + 1) * 128], pb[:, (k0 + j) * 128:(k0 + j + 1) * 128], identb)
                return i
            mk.op("pe", trp, reads=[kp, "identb"], writes=[kb])
            evac(pT[:, k0:k0 + nk, :], bank[:, 0:nk * 128].rearrange("p (j n) -> p j n", j=nk), kb, [(kpT, k0)])

        def mmpv(e):
            for kt in range(ntile):
                i = e.matmul(o_ap, pT[:, kt, :], v_fn(kt), start=(kt == 0), stop=(kt == ntile - 1))
            return i
        mk.op("pe", mmpv, reads=[(kpT, k0) for k0 in range(0, ntile, 8)] + list(vreads), writes=[o_key])

    def norm_rope_q(qraw, kq, cs2, kcs):
        s16, ks16 = s16_r.next()
        q3 = qraw.rearrange("p (h d) -> p h d", h=8)
        mk.op("act", lambda e: e.activation(jq, q3, AF.Square), reads=[kq], writes=["jq"])
        mk.op("dve", lambda e: e.reduce_sum(s16[:, 0:8], jq, axis=AX.X), reads=["jq"], writes=[ks16])
        mk.op("act", lambda e: e.activation(s16[:, 8:16], s16[:, 0:8], AF.Sqrt, bias=epsc[:, 0:1], scale=1.0 / 128),
              reads=[ks16, "epsc"], writes=[ks16])
        mk.op("dve", lambda e: e.reciprocal(s16[:, 8:16], s16[:, 8:16]), reads=[ks16], writes=[ks16])
        mk.op("dve", lambda e: e.tensor_tensor(qn_, q3, s16[:, 8:16].unsqueeze(2).to_broadcast([128, 8, 128]), ALU.mult),
              reads=[kq, ks16], writes=["qn"])
        mk.op("pool", lambda e: e.tensor_tensor(qn_, qn_, gqb.unsqueeze(1).to_broadcast([128, 8, 128]), ALU.mult),
              reads=["qn", "gqb"], writes=["qn"])
        cosb = cs2[:, 0, :].unsqueeze(1).to_broadcast([128, 8, 64])
        sinb = cs2[:, 1, :].unsqueeze(1).to_broadcast([128, 8, 64])
        x1, x2 = qn_[:, :, 0:64], qn_[:, :, 64:128]
        rd = ["qn", (kcs, 0), (kcs, 1)]
        mk.op("dve", lambda e: e.tensor_tensor(rt4[:, 0], x1, cosb, ALU.mult), reads=rd, writes=[("rt4", 0)])
        mk.op("pool", lambda e: e.tensor_tensor(rt4[:, 1], x2, sinb, ALU.mult), reads=rd, writes=[("rt4", 1)])
        mk.op("dve", lambda e: e.tensor_tensor(rt4[:, 2], x2, cosb, ALU.mult), reads=rd, writes=[("rt4", 2)])
        mk.op("pool", lambda e: e.tensor_tensor(rt4[:, 3], x1, sinb, ALU.mult), reads=rd, writes=[("rt4", 3)])
        mk.op("dve", lambda e: e.tensor_tensor(qr_[:, :, 0:64], rt4[:, 0], rt4[:, 1], ALU.subtract),
              reads=[("rt4", 0), ("rt4", 1)], writes=["qr"])
        mk.op("dve", lambda e: e.tensor_tensor(qr_[:, :, 64:128], rt4[:, 2], rt4[:, 3], ALU.add),
              reads=[("rt4", 2), ("rt4", 3)], writes=["qr"])
        for c4 in range(2):
            transpose_into(qnT[:, c4 * 4:(c4 + 1) * 4, :], [qn_[:, c4 * 4 + j, :] for j in range(4)], ["qn"], ("qnT", c4))
            transpose_into(qrT[:, c4 * 4:(c4 + 1) * 4, :], [qr_[:, c4 * 4 + j, :] for j in range(4)], ["qr"], ("qrT", c4))

    def combine(g):
        cf = coef[:, 0, :]

        def cb(br):
            return cf[:, br * 8 + g * 4:br * 8 + g * 4 + 4].unsqueeze(2).to_broadcast([128, 4, 128])
        dst = onsa[:, g * 4:(g + 1) * 4, :]
        v3 = lambda b: b.rearrange("p (h n) -> p h n", h=4)
        mk.op("dve", lambda e: e.tensor_tensor(dst, v3(B_CMP), cb(0), ALU.mult), reads=["coef"], writes=[K_CMP, ("onsa", g)])
        mk.op("dve", lambda e: e.tensor_tensor(otmp, v3(B_SLC), cb(1), ALU.mult), reads=["coef"], writes=[K_SLC, "otmp"])
        mk.op("pool", lambda e: e.tensor_tensor(dst, dst, otmp, ALU.add), reads=["otmp", ("onsa", g)], writes=[("onsa", g)])
        mk.op("dve", lambda e: e.tensor_tensor(otmp, v3(B_SWA), cb(2), ALU.mult), reads=["coef"], writes=[K_SWA, "otmp"])
        mk.op("pool", lambda e: e.tensor_tensor(dst, dst, otmp, ALU.add), reads=["otmp", ("onsa", g)], writes=[("onsa", g)])

    for t in range(QT0, 16):
        qi = t - QT0
        rows = slice(t * 128, (t + 1) * 128)
        qrows = slice(qi * 128, (qi + 1) * 128)
        qraw, kq = qraw_r.next()
        g24, kg24 = g24_r.next()
        cs2, kcs = cs2_r.next()
        tb, ktb = tb_r.next()
        ts_, kts = ts_r.next()
        sw, ksw = sw_r.next()
        mk.dma("sp", qraw, proj[rows, C_NQ:C_KV], writes=[kq])
        mk.dma("sp", g24, proj[rows, C_NG:C_MG], writes=[kg24])
        mk.dma("sp", cs2[:, 0, :], rope_cos[rows, :], writes=[(kcs, 0)])
        mk.dma("sp", cs2[:, 1, :], rope_sin[rows, :], writes=[(kcs, 1)])
        mk.dma("sp", tb[:, 0, :], cmp_add[qrows, :], writes=[(ktb, 0)])
        mk.dma("sp", tb[:, 1, :], cmp_mul[qrows, :], writes=[(ktb, 1)])
        mk.dma("sp", ts_[:, 0, :], t_sel[qrows, :], writes=[(kts, 0)])
        mk.dma("sp", ts_[:, 1, :], t_inv[qrows, :], writes=[(kts, 1)])
        mk.dma("sp", sw, swa_mask[qrows, :], writes=[ksw])
        norm_rope_q(qraw, kq, cs2, kcs)
        mk.op("act", lambda e: e.activation(coef[:, 0, :], g24, AF.Sigmoid), reads=[kg24], writes=["coef"])
        mk.op("dve", lambda e: e.memset(coef[:, 1, :], 1.0), writes=["rs"])
        nkt = t + 1
        for g in range(2):
            for h4 in range(4):
                hh = g * 4 + h4
                bank, kb = psR.next()
                mk.op("pe", lambda e: e.matmul(bank[:, 0:127], qnT[:, hh, :], kcT[:, g, 0:127], start=True, stop=True),
                      reads=[("qnT", hh // 4), ("kcT", g)], writes=[kb])
                sS, ksS = sS_r.next()
                mk.op("dve", lambda e: e.tensor_tensor(sS[:, 0:127], bank[:, 0:127], tb[:, 0, :], ALU.add),
                      reads=[(ktb, 0)], writes=[kb, ksS])
                pc, kpc = pc_r.next()
                st2, kst2 = st_r.next()
                softmax_rows(sS, ksS, 127, pc, kpc, st2[:, 4:5], kst2)
                mk.op("dve", lambda e: e.scalar_tensor_tensor(pc[:, 0:127], pc[:, 0:127], st2[:, 4:5], tb[:, 1, :],
                                                              op0=ALU.mult, op1=ALU.mult),
                      reads=[kpc, kst2, (ktb, 1)], writes=[kpc])
                bank, kb = psR.next()
                mk.op("pe", lambda e: e.transpose(bank[0:127, 0:128], pc[:, 0:127], ident), reads=[kpc, "ident"], writes=[kb])
                evac(pcT[0:127, h4, :], bank[0:127, 0:128], kb, [("pcT", h4)])
                mk.op("pe", lambda e: e.matmul(B_CMP[:, h4 * 128:(h4 + 1) * 128], pcT[0:127, h4, :], vc[0:127, g, :],
                                               start=True, stop=True),
                      reads=[("pcT", h4), ("vc", g)], writes=[K_CMP])
            bank, kb = psR.next()

            def mmsc(e, bank=bank):
                for h4 in range(4):
                    i = e.matmul(bank[:, 0:32], pcT[0:127, h4, :], mmap[0:127, :], start=(h4 == 0), stop=(h4 == 3))
                return i
            mk.op("pe", mmsc, reads=[("pcT", h4) for h4 in range(4)] + ["mmap"], writes=[kb])
            mk.op("dve", lambda e: e.tensor_tensor(scb[:, 0, :], bank[:, 0:32], ts_[:, 0, :], ALU.add),
                  reads=[(kts, 0)], writes=[kb, "scb"])
            mk.op("dve", lambda e: e.max(out=m8[:, 0, :], in_=scb[:, 0, :]), reads=["scb"], writes=["m8"])
            mk.op("dve", lambda e: e.match_replace(out=scb[:, 1, :], in_to_replace=m8[:, 0, :], in_values=scb[:, 0, :],
                                                   imm_value=-1.0e9), reads=["scb", "m8"], writes=["scb"])
            mk.op("dve", lambda e: e.max(out=m8[:, 1, :], in_=scb[:, 1, :]), reads=["scb"], writes=["m8"])
            mk.op("dve", lambda e: e.tensor_scalar(scb[:, 2, :], scb[:, 0, :], m8[:, 1, 7:8], BIG, op0=ALU.is_ge, op1=ALU.mult),
                  reads=["scb", "m8"], writes=["scb"])
            mk.op("dve", lambda e: e.scalar_tensor_tensor(scb[:, 3, :], scb[:, 2, :], -BIG, ts_[:, 1, :], op0=ALU.add, op1=ALU.add),
                  reads=["scb", (kts, 1)], writes=["bb"])
            for h4 in range(4):
                hh = g * 4 + h4
                sS, ksS = sS_r.next()
                nk = nkt * 128
                for c0 in range(0, nk, 512):
                    w = min(512, nk - c0)
                    bank, kb = psR.next()
                    mk.op("pe", lambda e: e.matmul(bank[:, 0:w], qrT[:, hh, :], KTa[:, 0, g, c0:c0 + w], start=True, stop=True),
                          reads=[("qrT", hh // 4)] + [("KT", 1, kt) for kt in range(c0 // 128, (c0 + w) // 128)], writes=[kb])
                    nb = w // 64
                    mk.op("dve", lambda e: e.tensor_tensor(sS[:, c0:c0 + w].rearrange("p (b k) -> p b k", k=64),
                                                           bank[:, 0:w].rearrange("p (b k) -> p b k", k=64),
                                                           scb[:, 3, c0 // 64:c0 // 64 + nb].unsqueeze(2).to_broadcast([128, nb, 64]),
                                                           ALU.add), reads=["bb"], writes=[kb, ksS])
                mk.op("pool", lambda e: e.tensor_tensor(sS[:, t * 128:(t + 1) * 128], sS[:, t * 128:(t + 1) * 128], trim, ALU.add),
                      reads=[ksS, "trim"], writes=[ksS])
                pb, kpb = pb_r.next()
                softmax_rows(sS, ksS, nk, pb, kpb, coef[:, 1, 8 + hh:9 + hh], "rs")
                pv_bf16(pb, kpb, nkt, lambda kt: VV[:, kt, 0, g, :], [("VV", kt) for kt in range(nkt)],
                        B_SLC[:, h4 * 128:(h4 + 1) * 128], K_SLC)
                sS, ksS = sS_r.next()
                k0 = (t - 4) * 128
                for c0, w in ((0, 512), (512, 128)):
                    bank, kb = psR.next()
                    mk.op("pe", lambda e: e.matmul(bank[:, 0:w], qrT[:, hh, :], KTa[:, 1, g, k0 + c0:k0 + c0 + w], start=True, stop=True),
                          reads=[("qrT", hh // 4)] + [("KT", 1, kt) for kt in range(t - 4, t + 1)], writes=[kb])
                    mk.op("dve", lambda e: e.tensor_tensor(sS[:, c0:c0 + w], bank[:, 0:w], sw[:, c0:c0 + w], ALU.add),
                          reads=[ksw], writes=[kb, ksS])
                pb, kpb = pb_r.next()
                softmax_rows(sS, ksS, 640, pb, kpb, coef[:, 1, 16 + hh:17 + hh], "rs")
                pv_bf16(pb, kpb, 5, lambda kt: VV[:, t - 4 + kt, 1, g, :], [("VV", kt) for kt in range(t - 4, t + 1)],
                        B_SWA[:, h4 * 128:(h4 + 1) * 128], K_SWA)
            for br in range(3):
                cs_ = slice(br * 8 + g * 4, br * 8 + g * 4 + 4)
                mk.op("dve", lambda e: e.tensor_tensor(coef[:, 0, cs_], coef[:, 0, cs_], coef[:, 1, cs_], ALU.mult),
                      reads=["coef", "rs"], writes=["coef"])
            combine(g)
        for c4 in range(2):
            transpose_into(o_nsaT[:, c4 * 4:(c4 + 1) * 4, qi * 128:(qi + 1) * 128],
                           [onsa[:, c4 * 4 + j, :] for j in range(4)], [("onsa", 0), ("onsa", 1)], ("onsaT", qi))
    mk.barrier()
    while len(live) > m_prompt_only:
        live.pop().__exit__(None, None, None)
    if WITH_SAMPLE:
        m_smp = sb_mark()
        KSVS = sb("KSVS", [128, 16384], BF16)
        RT = KSVS.rearrange("p (g n) -> p g n", g=2)
        KS = KSVS[:, 0:8192]
        VS = KSVS[:, 8192:16384].rearrange("p (j d) -> p j d", d=128)
        pg_r = rot("pg", [128, 4, 256], F32, 2)
        sS_s = sb("sS_s", [128, 8320], F32)
        jq = sS_s[:, 0:1024].rearrange("p (h d) -> p h d", h=8)
        rt4 = sS_s[:, 1024:3072].rearrange("p (a h d) -> p a h d", a=4, h=8)
        qn_ = sS_s[:, 3072:4096].rearrange("p (h d) -> p h d", h=8)
        qr_ = sS_s[:, 4096:5120].rearrange("p (h d) -> p h d", h=8)
        pbs_r = rot("pbs", [128, 2048], BF16, 1)
        kcT_s = sb("kcT_s", [128, 2, 512], BF16)
        vc_s = sb("vc_s", [128, 4, 2, 128], F32)
        pcs = sb("pcs", [128, 512], F32)
        pcT_s = sb("pcT_s", [128, 4, 128], F32)
        imp4 = sb("imp4", [128, 4, 128], F32)
        mmap_sb = ld("mmap_sb", mmap_s.rearrange("(t p) j -> p t j", p=128), [128, 4, 129])
        tsel_sb = ld("tsel_sb", t_sel_s, [128, 129])
        newm = ld("newm", newmask.rearrange("s p k -> p s k"), [128, 4, 128])
        swpm = ld("swpm", swa_past_mask, [128, 512])
        scs = sb("scs", [128, 4, 129], F32)
        qs_n = sb("qs_n", [128, 128], BF16)
        qs_r = sb("qs_r", [128, 128], BF16)
        osb3 = sb("osb3", [128, 3, 128], F32)
        ocb = sb("ocb", [128, 4, 128], F32)
        KW = sb("KW", [128, 2, 512], BF16)
        VW = sb("VW", [128, 4, 256], BF16)
        page_regs = {}

        def load_pages(ci, s, consume):
            for j4 in range(16):
                pg, kpg = pg_r.next()
                mk.dma("sp", pg, gath[ci, s * 64 + 4 * j4:s * 64 + 4 * j4 + 4].rearrange("j p n -> p j n"), writes=[kpg])
                consume(j4, pg, kpg)

        def pv_big(sS, ksS, ntile, v_fn, vreads, o_ap, o_key, rs_col, krs):
            st, kst = st_r.next()
            n = ntile * 128
            mk.op("dve", lambda e: e.reduce_max(st[:, 0:1], sS[:, 0:n], axis=AX.X), reads=[ksS], writes=[kst])
            mk.op("dve", lambda e: e.tensor_scalar(st[:, 1:2], st[:, 0:1], -SC, None, op0=ALU.mult), reads=[kst], writes=[kst])
            mk.op("dve", lambda e: e.memset(st[:, 2:3], 0.0), reads=[], writes=[kst])
            for g0 in range(0, ntile, 16):
                ng = min(16, ntile - g0)
                pb, kpb = pbs_r.next()
                mk.op("act", lambda e: e.activation(pb[:, 0:ng * 128], sS[:, g0 * 128:(g0 + ng) * 128], AF.Exp, bias=st[:, 1:2], scale=SC),
                      reads=[ksS, kst], writes=[kpb])
                mk.op("dve", lambda e: e.reduce_sum(st[:, 3:4], pb[:, 0:ng * 128], axis=AX.X), reads=[kpb], writes=[kst])
                mk.op("dve", lambda e: e.tensor_tensor(st[:, 2:3], st[:, 2:3], st[:, 3:4], ALU.add), reads=[kst], writes=[kst])
                pT, kpT = pT_r.next()
                for k0 in range(0, ng, 8):
                    nk = min(8, ng - k0)
                    bank, kb = psb.next()

                    def trp(e, bank=bank, k0=k0, nk=nk, pb=pb):
                        for j in range(nk):
                            i = e.transpose(bank[:, j * 128:(j + 1) * 128], pb[:, (k0 + j) * 128:(k0 + j + 1) * 128], identb)
                        return i
                    mk.op("pe", trp, reads=[kpb, "identb"], writes=[kb])
                    evac(pT[:, k0:k0 + nk, :], bank[:, 0:nk * 128].rearrange("p (j n) -> p j n", j=nk), kb, [(kpT, k0)])

                def mmpv(e, g0=g0, ng=ng, pT=pT):
                    for kt in range(ng):
                        i = e.matmul(o_ap, pT[:, kt, :], v_fn(g0 + kt), start=(g0 + kt == 0), stop=(g0 + kt == ntile - 1))
                    return i
                mk.op("pe", mmpv, reads=[(kpT, k0) for k0 in range(0, ng, 8)] + list(vreads), writes=[o_key])
            mk.op("dve", lambda e: e.reciprocal(rs_col, st[:, 2:3]), reads=[kst], writes=[krs])

        t = 16
        rows = slice(t * 128, (t + 1) * 128)
        qraw, kq = qraw_r.next()
        g24, kg24 = g24_r.next()
        cs2, kcs = cs2_r.next()
        mk.dma("sp", qraw, proj[rows, C_NQ:C_KV], writes=[kq])
        mk.dma("sp", g24, proj[rows, C_NG:C_MG], writes=[kg24])
        mk.dma("sp", cs2[:, 0, :], rope_cos[rows, :], writes=[(kcs, 0)])
        mk.dma("sp", cs2[:, 1, :], rope_sin[rows, :], writes=[(kcs, 1)])
        norm_rope_q(qraw, kq, cs2, kcs)
        mk.op("act", lambda e: e.activation(coef[:, 0, :], g24, AF.Sigmoid), reads=[kg24], writes=["coef"])
        mk.barrier()
        for s in range(4):
            for (nm, cache) in (("k", 0), ("v", 1)):
                def cons(j4, pg, kpg):
                    for g_ in range(2):
                        bank, kb = psf.next()

                        def trA(e, bank=bank, pg=pg, g_=g_):
                            for p_ in range(4):
                                i = e.transpose(bank[:, p_ * 128:(p_ + 1) * 128], pg[:, p_, g_ * 128:(g_ + 1) * 128], ident)
                            return i
                        mk.op("pe", trA, reads=[kpg, "ident"], writes=[kb])
                        evac(RT[:, g_, j4 * 512:(j4 + 1) * 512], bank, kb, [(("KS", "VS")[g_], j4)])
                load_pages(cache, s, cons)
                for g in range(2):
                    if nm == "k":
                        def outk(mt, m, o2, ko2, g=g):
                            kc_, kkc = kcr.next()
                            rms_rows(o2, ko2, m, gkc, "gkc", kc_, kkc)
                            bank, kb = psf.next()
                            mk.op("pe", lambda e: e.transpose(bank[:, 0:m], kc_[0:m, :], ident[0:m, 0:m]), reads=[kkc, "ident"], writes=[kb])
                            evac(kcT_s[:, g, mt * 128:mt * 128 + m], bank[:, 0:m], kb, [("kcT_s", g, mt)])
                        compress("k", RT[:, g, :], 511, [(("KS", "VS")[g], j) for j in range(16)], outk)
                    else:
                        def outv(mt, m, o2, ko2, g=g):
                            mk.op("pool", lambda e: e.tensor_copy(vc_s[0:m, mt, g, :], o2[0:m]), reads=[ko2], writes=[("vc_s", g, mt)])
                        compress("v", RT[:, g, :], 511, [(("KS", "VS")[g], j) for j in range(16)], outv)
            for kv_i in range(2):
                pg, kpg = pg_r.next()
                mk.dma("sp", pg, state_swa_c[kv_i, s].rearrange("(j p) n -> p j n", p=128), writes=[kpg])
                if kv_i == 0:
                    for g_ in range(2):
                        bank, kb = psf.next()

                        def trW(e, bank=bank, pg=pg, g_=g_):
                            for p_ in range(4):
                                i = e.transpose(bank[:, p_ * 128:(p_ + 1) * 128], pg[:, p_, g_ * 128:(g_ + 1) * 128], ident)
                            return i
                        mk.op("pe", trW, reads=[kpg, "ident"], writes=[kb])
                        evac(KW[:, g_, :], bank, kb, [("KW", g_)])
                else:
                    mk.op("pool", lambda e: e.tensor_copy(VW, pg), reads=[kpg], writes=["VW"])
            for g in range(2):
                for h4 in range(4):
                    mk.op("pool", lambda e: e.tensor_copy(qs_n[:, h4 * 32:(h4 + 1) * 32], qnT[:, g * 4 + h4, 32 * s:32 * s + 32]),
                          reads=[("qnT", g)], writes=["qs_n"])
                    mk.op("pool", lambda e: e.tensor_copy(qs_r[:, h4 * 32:(h4 + 1) * 32], qrT[:, g * 4 + h4, 32 * s:32 * s + 32]),
                          reads=[("qrT", g)], writes=["qs_r"])
                bank, kb = psf.next()
                mk.op("pe", lambda e: e.matmul(bank[:, 0:511], qs_n, kcT_s[:, g, 0:511], start=True, stop=True),
                      reads=["qs_n"] + [("kcT_s", g, mt) for mt in range(4)], writes=[kb])
                mk.op("act", lambda e: e.copy(sS_s[:, 0:511], bank[:, 0:511]), writes=[kb, "sS_s"])
                st2, kst2 = st_r.next()
                softmax_rows(sS_s, "sS_s", 511, pcs, "pcs", st2[:, 4:5], kst2)
                mk.op("dve", lambda e: e.tensor_scalar(pcs[:, 0:511], pcs[:, 0:511], st2[:, 4:5], None, op0=ALU.mult),
                      reads=["pcs", kst2], writes=["pcs"])
                for mt in range(4):
                    m = 128 if mt < 3 else 127
                    bank, kb = psf.next()
                    mk.op("pe", lambda e: e.transpose(bank[0:m, 0:128], pcs[:, mt * 128:mt * 128 + m], ident), reads=["pcs", "ident"], writes=[kb])
                    evac(pcT_s[0:m, mt, :], bank[0:m, 0:128], kb, [("pcT_s", mt)])
                bank, kb = psf.next()

                def mmoc(e, bank=bank, g=g):
                    for mt in range(4):
                        m = 128 if mt < 3 else 127
                        i = e.matmul(bank[:, 0:128], pcT_s[0:m, mt, :], vc_s[0:m, mt, g, :], start=(mt == 0), stop=(mt == 3))
                    return i
                mk.op("pe", mmoc, reads=[("pcT_s", mt) for mt in range(4)] + [("vc_s", g, mt) for mt in range(4)], writes=[kb])
                mk.op("act", lambda e: e.copy(osb3[:, 0, :], bank[:, 0:128]), writes=[kb, ("osb3", 0)])
                p4 = pcT_s.rearrange("p t (h s) -> p t h s", h=4)
                rdp = [("pcT_s", mt) for mt in range(4)]
                mk.op("dve", lambda e: e.tensor_tensor(imp4[:, :, 0:32], p4[:, :, 0, :], p4[:, :, 1, :], ALU.add), reads=rdp, writes=["imp4"])
                mk.op("dve", lambda e: e.tensor_tensor(imp4[:, :, 0:32], imp4[:, :, 0:32], p4[:, :, 2, :], ALU.add), reads=rdp + ["imp4"], writes=["imp4"])
                mk.op("dve", lambda e: e.tensor_tensor(imp4[:, :, 0:32], imp4[:, :, 0:32], p4[:, :, 3, :], ALU.add), reads=rdp + ["imp4"], writes=["imp4"])
                for h4 in range(1, 4):
                    mk.op("pool", lambda e: e.tensor_copy(imp4[:, :, h4 * 32:(h4 + 1) * 32], imp4[:, :, 0:32]), reads=["imp4"], writes=["imp4"])
                bank, kb = psf.next()

                def mmsc2(e, bank=bank):
                    for mt in range(4):
                        m = 128 if mt < 3 else 127
                        i = e.matmul(bank[:, 0:129], imp4[0:m, mt, :], mmap_sb[0:m, mt, :], start=(mt == 0), stop=(mt == 3))
                    return i
                mk.op("pe", mmsc2, reads=["imp4", "mmap_sb"], writes=[kb])
                mk.op("dve", lambda e: e.tensor_tensor(scs[:, 0, :], bank[:, 0:129], tsel_sb, ALU.add), reads=["tsel_sb"], writes=[kb, "scs"])
                mk.op("dve", lambda e: e.max(out=m8[:, 0, :], in_=scs[:, 0, :]), reads=["scs"], writes=["m8"])
                mk.op("dve", lambda e: e.match_replace(out=scs[:, 1, :], in_to_replace=m8[:, 0, :], in_values=scs[:, 0, :], imm_value=-1.0e9),
                      reads=["scs", "m8"], writes=["scs"])
                mk.op("dve", lambda e: e.max(out=m8[:, 1, :], in_=scs[:, 1, :]), reads=["scs"], writes=["m8"])
                mk.op("dve", lambda e: e.tensor_scalar(scs[:, 2, :], scs[:, 0, :], m8[:, 1, 7:8], BIG, op0=ALU.is_ge, op1=ALU.mult),
                      reads=["scs", "m8"], writes=["scs"])
                mk.op("dve", lambda e: e.tensor_scalar(scs[:, 3, :], scs[:, 2, :], -BIG, None, op0=ALU.add), reads=["scs"], writes=["bbs"])
                def consk(j4, pg, kpg, g=g):
                    bank, kb = psf.next()

                    def trK(e, bank=bank, pg=pg):
                        for p_ in range(4):
                            i = e.transpose(bank[:, p_ * 128:(p_ + 1) * 128], pg[:, p_, g * 128:(g + 1) * 128], ident)
                        return i
                    mk.op("pe", trK, reads=[kpg, "ident"], writes=[kb])
                    evac(KS[:, j4 * 512:(j4 + 1) * 512], bank, kb, [("KS", j4)])
                load_pages(2, s, consk)

                def consv(j4, pg, kpg, g=g):
                    mk.op("pool", lambda e: e.tensor_copy(VS[:, 4 * j4:4 * j4 + 4, :], pg[:, :, g * 128:(g + 1) * 128]), reads=[kpg], writes=[("VS", j4)])
                load_pages(3, s, consv)
                for c0 in range(0, 8192, 512):
                    bank, kb = psf.next()
                    mk.op("pe", lambda e: e.matmul(bank, qs_r, KS[:, c0:c0 + 512], start=True, stop=True),
                          reads=["qs_r", ("KS", c0 // 512)], writes=[kb])
                    mk.op("dve", lambda e: e.tensor_tensor(sS_s[:, c0:c0 + 512].rearrange("p (b k) -> p b k", k=64),
                                                           bank.rearrange("p (b k) -> p b k", k=64),
                                                           scs[:, 3, c0 // 64:c0 // 64 + 8].unsqueeze(2).to_broadcast([128, 8, 64]), ALU.add),
                          reads=["bbs", "pcs"], writes=[kb, "sS_s"])
                bank, kb = psf.next()
                mk.op("pe", lambda e: e.matmul(bank[:, 0:128], qs_r, KTn[:, 0, g, :], start=True, stop=True), reads=["qs_r", ("KT", 1, 16)], writes=[kb])
                mk.op("dve", lambda e: e.scalar_tensor_tensor(sS_s[:, 8192:8320], bank[:, 0:128], scs[:, 3, 128:129], newm[:, s, :], op0=ALU.add, op1=ALU.add),
                      reads=["bbs", "newm"], writes=[kb, "sS_s"])
                bank, kb = psf.next()
                st3, kst3 = st_r.next()
                pv_big(sS_s, "sS_s", 65, lambda kt: (VS[:, kt, :] if kt < 64 else VVn[:, 0, g, :]), [("VS", j) for j in range(16)] + [("VV", 16)],
                       bank[:, 0:128], kb, st3[:, 4:5], kst3)
                mk.op("dve", lambda e: e.tensor_scalar(osb3[:, 1, :], bank[:, 0:128], st3[:, 4:5], None, op0=ALU.mult), reads=[kst3], writes=[kb, ("osb3", 1)])
                bank, kb = psf.next()
                mk.op("pe", lambda e: e.matmul(bank, qs_r, KW[:, g, :], start=True, stop=True), reads=["qs_r", ("KW", g)], writes=[kb])
                mk.op("dve", lambda e: e.tensor_tensor(sS_s[:, 0:512], bank, swpm, ALU.add), reads=["swpm"], writes=[kb, "sS_s"])
                bank, kb = psf.next()
                mk.op("pe", lambda e: e.matmul(bank[:, 0:128], qs_r, KTn[:, 1, g, :], start=True, stop=True), reads=["qs_r", ("KT", 1, 16)], writes=[kb])
                mk.op("dve", lambda e: e.tensor_tensor(sS_s[:, 512:640], bank[:, 0:128], newm[:, s, :], ALU.add), reads=["newm"], writes=[kb, "sS_s"])
                bank, kb = psf.next()
                st4, kst4 = st_r.next()
                pv_big(sS_s, "sS_s", 5, lambda kt: (VW[:, kt, g * 128:(g + 1) * 128] if kt < 4 else VVn[:, 1, g, :]), ["VW", ("VV", 16)],
                       bank[:, 0:128], kb, st4[:, 4:5], kst4)
                mk.op("dve", lambda e: e.tensor_scalar(osb3[:, 2, :], bank[:, 0:128], st4[:, 4:5], None, op0=ALU.mult), reads=[kst4], writes=[kb, ("osb3", 2)])
                mk.dma("sp", o_scr[s, g].rearrange("b r d -> r b d"), osb3, reads=[("osb3", b_) for b_ in range(3)], writes=[("oscr", s, g)])
        for g in range(2):
            cf = coef[:, 0, :]
            dst = onsa[:, g * 4:(g + 1) * 4, :]
            for b_ in range(3):
                for s in range(4):
                    mk.dma("sp", ocb[32 * s:32 * s + 32, :, :], o_scr[s, g, b_].rearrange("(h r) d -> r h d", h=4),
                           reads=[("oscr", s, g)], writes=["ocb"])
                cbk = cf[:, b_ * 8 + g * 4:b_ * 8 + g * 4 + 4].unsqueeze(2).to_broadcast([128, 4, 128])
                if b_ == 0:
                    mk.op("dve", lambda e: e.tensor_tensor(dst, ocb, cbk, ALU.mult), reads=["coef", "ocb"], writes=[("onsa", g)])
                else:
                    mk.op("dve", lambda e: e.tensor_tensor(otmp, ocb, cbk, ALU.mult), reads=["coef", "ocb"], writes=["otmp"])
                    mk.op("pool", lambda e: e.tensor_tensor(dst, dst, otmp, ALU.add), reads=["otmp", ("onsa", g)], writes=[("onsa", g)])
        for c4 in range(2):
            transpose_into(o_nsaT[:, c4 * 4:(c4 + 1) * 4, 9 * 128:10 * 128],
                           [onsa[:, c4 * 4 + j, :] for j in range(4)], [("onsa", 0), ("onsa", 1)], ("onsaT", 9))
    else:
        mk.op("pool", lambda e: e.memset(o_nsaT[:, :, 9 * 128:10 * 128], 0.0), writes=[("onsaT", 9)])
    mk.barrier()
    while len(live) > m_smp_keep:
        live.pop().__exit__(None, None, None)

    NQ = 10
    mT = sb("mT", [128, 16, NQ * 128], BF16)
    m_mrg = sb_mark()
    wg_r = rot("wg", [128, 8, 512], BF16, 2)
    wn_r = rot("wn", [128, 8, 512], BF16, 2)
    mg_r = rot("mg", [128, 2, 512], F32, 2)
    mm_r = rot("mm", [128, 2, 512], F32, 2)
    for cb in range(4):
        cs_ = slice(cb * 512, (cb + 1) * 512)
        wg, kwg = wg_r.next()
        wn, kwn = wn_r.next()
        for c in range(8):
            mk.dma("pool", wg[:, c, :], w_br_gla[c * 128:(c + 1) * 128, cs_], writes=[(kwg, c)])
            mk.dma("pool", wn[:, c, :], w_br_nsa[c * 128:(c + 1) * 128, cs_], writes=[(kwn, c)])
        for qi in range(NQ):
            t = QT0 + qi
            rows = slice(t * 128, (t + 1) * 128)
            mg, kmg = mg_r.next()
            mm_, kmm = mm_r.next()
            mk.dma("sp", mg[:, 0, :], proj[rows, C_MG + cb * 512:C_MG + (cb + 1) * 512], writes=[(kmg, 0)])
            mk.dma("sp", mg[:, 1, :], proj[rows, C_MG + 2048 + cb * 512:C_MG + 2048 + (cb + 1) * 512], writes=[(kmg, 1)])
            mk.op("act", lambda e: e.activation(mg, mg, AF.Sigmoid), reads=[(kmg, 0), (kmg, 1)], writes=[(kmg, 0), (kmg, 1)])
            bankA, kA = psf.next()
            bankB, kB = psf.next()

            def mmA(e, bank=bankA, w=wg, src=o_glaT, qi=qi):
                for c in range(8):
                    i = e.matmul(bank, src[:, c, qi * 128:(qi + 1) * 128], w[:, c, :], start=(c == 0), stop=(c == 7))
                return i

            def mmB(e, bank=bankB, w=wn, src=o_nsaT, qi=qi):
                for c in range(8):
                    i = e.matmul(bank, src[:, c, qi * 128:(qi + 1) * 128], w[:, c, :], start=(c == 0), stop=(c == 7))
                return i
            mk.op("pe", mmA, reads=[(kwg, c) for c in range(8)] + [("oglaT", qi)], writes=[kA])
            mk.op("pe", mmB, reads=[(kwn, c) for c in range(8)] + [("onsaT", qi)], writes=[kB])
            mk.op("dve", lambda e: e.tensor_tensor(mm_[:, 0, :], bankA, mg[:, 0, :], ALU.mult), reads=[(kmg, 0)], writes=[kA, (kmm, 0)])
            mk.op("dve", lambda e: e.tensor_tensor(mm_[:, 1, :], bankB, mg[:, 1, :], ALU.mult), reads=[(kmg, 1)], writes=[kB, (kmm, 1)])
            mk.op("pool", lambda e: e.tensor_tensor(mm_[:, 0, :], mm_[:, 0, :], mm_[:, 1, :], ALU.add),
                  reads=[(kmm, 0), (kmm, 1)], writes=[(kmm, 0)])
            transpose_into(mT[:, cb * 4:(cb + 1) * 4, qi * 128:(qi + 1) * 128],
                           [mm_[:, 0, j * 128:(j + 1) * 128] for j in range(4)], [(kmm, 0)], ("mT", qi, cb))
    mk.barrier()
    sb_release(m_mrg)

    m_wo = sb_mark()
    wo_r = rot("wo", [128, 16, 512], BF16, 2)
    xs_r = rot("xs", [128, 512], F32, 3)
    for cb in range(4):
        cs_ = slice(cb * 512, (cb + 1) * 512)
        wo, kwo = wo_r.next()
        for c in range(16):
            mk.dma("pool", wo[:, c, :], w_o[c * 128:(c + 1) * 128, cs_], writes=[(kwo, c)])
        for qi in range(NQ):
            t = QT0 + qi
            xs, kxs = xs_r.next()
            mk.dma("sp", xs, xbuf[t * 128:(t + 1) * 128, cs_], writes=[kxs])
            bank, kb = psf.next()

            def mmh(e, bank=bank, wo=wo, qi=qi):
                for c in range(16):
                    i = e.matmul(bank, mT[:, c, qi * 128:(qi + 1) * 128], wo[:, c, :], start=(c == 0), stop=(c == 15))
                return i
            mk.op("pe", mmh, reads=[(kwo, c) for c in range(16)], writes=[kb])
            mk.op("dve", lambda e: e.tensor_tensor(xs, bank, xs, ALU.add), reads=[kxs], writes=[kb, kxs])
            mk.dma("sp", h_scr[qi * 128:(qi + 1) * 128, cs_], xs, reads=[kxs], writes=[("h", qi, cb)])
    mk.barrier()
    while len(live) > m_keep:
        live.pop().__exit__(None, None, None)

    NF = 1042
    hnF = sb("hnF", [128, 16, NF], BF16)
    gT = sb("gT", [128, NFB, NF], BF16)
    m_n2 = sb_mark()
    g2b = ld("g2b", norm2_g.partition_broadcast(128), [128, D])
    hr = rot("ht", [128, D], F32, 2)
    hnr = rot("hn", [128, D], F32, 2)
    junk2 = sb("junk2", [128, D], F32)
    ss2r = rot("ss2", [128, 2], F32, 2)
    for qi in range(NQ):
        ht, kh = hr.next()
        hn, khn = hnr.next()
        ss, ks = ss2r.next()
        mk.dma("sp", ht, h_scr[qi * 128:(qi + 1) * 128, :], writes=[kh])
        mk.op("act", lambda e: e.activation(junk2, ht, AF.Square), reads=[kh], writes=["junk2"])
        mk.op("dve", lambda e: e.reduce_sum(ss[:, 0:1], junk2, axis=AX.X), reads=["junk2"], writes=[ks])
        mk.op("act", lambda e: e.activation(ss[:, 1:2], ss[:, 0:1], AF.Sqrt, bias=epsc[:, 0:1], scale=1.0 / D),
              reads=[ks, "epsc"], writes=[ks])
        mk.op("dve", lambda e: e.reciprocal(ss[:, 1:2], ss[:, 1:2]), reads=[ks], writes=[ks])
        mk.op("dve", lambda e: e.scalar_tensor_tensor(hn, ht, ss[:, 1:2], g2b, op0=ALU.mult, op1=ALU.mult),
              reads=[kh, ks, "g2b"], writes=[khn])
        for c4 in range(4):
            bank, kb = psf.next()

            def tr(e, c4=c4, bank=bank, hn=hn):
                for j in range(4):
                    c = c4 * 4 + j
                    i = e.transpose(bank[:, j * 128:(j + 1) * 128], hn[:, c * 128:(c + 1) * 128], ident)
                return i
            mk.op("pe", tr, reads=[khn, "ident"], writes=[kb])
            b3 = bank.rearrange("p (j n) -> p j n", j=4)
            cc = slice(c4 * 4, (c4 + 1) * 4)
            if qi == 0:
                evac(hnF[:, cc, 0:2], b3[:, :, 126:128], kb, [("hnF", qi, c4)])
            elif qi < 9:
                evac(hnF[:, cc, 2 + (qi - 1) * 128:2 + qi * 128], b3, kb, [("hnF", qi, c4)])
            else:
                for s in range(4):
                    evac(hnF[:, cc, 1026 + 4 * s:1030 + 4 * s], b3[:, :, 32 * s:32 * s + 4], kb, [("hnF", qi, c4, s)])
    mk.barrier()
    sb_release(m_n2)

    m_up = sb_mark()
    cwt = sb("cwt", [128, NFB, 4], F32)
    for j in range(3):
        mk.dma("sp", cwt[:, :, j], conv_w[j].rearrange("(fb p) -> p fb", p=128), writes=[("cwt", j)], allow_slow_non_contiguous=True)
    mk.dma("sp", cwt[:, :, 3], conv_b.rearrange("(fb p) -> p fb", p=128), writes=[("cwt", 3)], allow_slow_non_contiguous=True)
    cstT = sb("cstT", [128, NFB, 8], F32)
    for s_ in range(4):
        for j in range(2):
            mk.dma("sp", cstT[:, :, s_ * 2 + j], state_conv_c[s_, j].rearrange("(fb p) -> p fb", p=128),
                   writes=[("cstT", s_ * 2 + j)], allow_slow_non_contiguous=True)
    crow_r = rot("crow", [10, 256], F32, 2)
    wa_r = rot("wa", [128, 16, 256], BF16, 2)
    wb_r = rot("wbg", [128, 16, 256], BF16, 2)
    aT = sb("aT", [128, 2 + NF], F32)
    bT = sb("bT", [128, NF], F32)
    uu = sb("uu", [128, NF], F32)
    as6 = sb("as6", [128, 4, 6], F32)
    us4 = sb("us4", [128, 4, 4], F32)
    cc10 = sb("cc10", [128, 10], F32)
    mk.op("dve", lambda e: e.memset(aT[:, 0:2], 0.0), writes=["aTpad"])
    SEGS = ((0, 512), (512, 512), (1024, NF - 1024))
    for f2 in range(NFB // 2):
        wa, kwa = wa_r.next()
        wb_, kwb = wb_r.next()
        for c in range(16):
            mk.dma("pool", wa[:, c, :], w_up[c * 128:(c + 1) * 128, f2 * 256:(f2 + 1) * 256], writes=[(kwa, c)])
            mk.dma("pool", wb_[:, c, :], w_up[c * 128:(c + 1) * 128, DFF + f2 * 256:DFF + (f2 + 1) * 256], writes=[(kwb, c)])
        for sub in range(2):
            fb = f2 * 2 + sub
            for (c0, w) in SEGS:
                for (wt, kwt, dst, kd) in ((wa, kwa, aT[:, 2 + c0:2 + c0 + w], "aT"), (wb_, kwb, bT[:, c0:c0 + w], "bT")):
                    bank, kb = psf.next()

                    def mmu(e, bank=bank, wt=wt, c0=c0, w=w, sub=sub):
                        for c in range(16):
                            i = e.matmul(bank[:, 0:w], wt[:, c, sub * 128:(sub + 1) * 128], hnF[:, c, c0:c0 + w],
                                         start=(c == 0), stop=(c == 15))
                        return i
                    mk.op("pe", mmu, reads=[(kwt, c) for c in range(16)], writes=[kb])
                    evac(dst, bank[:, 0:w], kb, [(kd, c0)])
            ra = [("aT", c0) for (c0, w) in SEGS] + ["aTpad"] + [("cwt", j) for j in range(4)]
            w0, w1, w2, bcv = (cwt[:, fb, j:j + 1] for j in range(4))
            mk.op("dve", lambda e: e.tensor_scalar(uu, aT[:, 2:2 + NF], w2, bcv, op0=ALU.mult, op1=ALU.add), reads=ra, writes=["uu"])
            mk.op("dve", lambda e: e.scalar_tensor_tensor(uu, aT[:, 1:1 + NF], w1, uu, op0=ALU.mult, op1=ALU.add), reads=ra + ["uu"], writes=["uu"])
            mk.op("dve", lambda e: e.scalar_tensor_tensor(uu, aT[:, 0:NF], w0, uu, op0=ALU.mult, op1=ALU.add), reads=ra + ["uu"], writes=["uu"])
            a_s = aT[:, 2 + 1026:2 + 1042].rearrange("p (s t) -> p s t", s=4)
            mk.op("pool", lambda e: e.tensor_copy(as6[:, :, 0:2], cstT[:, fb, :].rearrange("p (s j) -> p s j", s=4)),
                  reads=[("cstT", i8) for i8 in range(8)], writes=["as6a"])
            mk.op("pool", lambda e: e.tensor_copy(as6[:, :, 2:6], a_s), reads=ra, writes=["as6b"])
            rs6 = ["as6a", "as6b"] + [("cwt", j) for j in range(4)]
            mk.op("dve", lambda e: e.tensor_scalar(us4, as6[:, :, 2:6], w2, bcv, op0=ALU.mult, op1=ALU.add), reads=rs6, writes=["us4"])
            mk.op("dve", lambda e: e.scalar_tensor_tensor(us4, as6[:, :, 1:5], w1, us4, op0=ALU.mult, op1=ALU.add), reads=rs6 + ["us4"], writes=["us4"])
            mk.op("dve", lambda e: e.scalar_tensor_tensor(us4, as6[:, :, 0:4], w0, us4, op0=ALU.mult, op1=ALU.add), reads=rs6 + ["us4"], writes=["us4"])
            mk.op("dve", lambda e: e.tensor_copy(uu[:, 1026:1042].rearrange("p (s t) -> p s t", s=4), us4), reads=["us4", "uu"], writes=["uu"])
            mk.op("act", lambda e: e.activation(uu, uu, AF.Gelu_apprx_tanh), reads=["uu"], writes=["uu"])
            mk.op("dve", lambda e: e.tensor_tensor(gT[:, fb, :], uu, bT, ALU.mult), reads=["uu"] + [("bT", c0) for (c0, w) in SEGS],
                  writes=[("gT", fb)])
            mk.op("pool", lambda e: e.tensor_copy(cc10[:, 0:2], aT[:, 2 + 1024:2 + 1026]), reads=ra, writes=["cc10a"])
            mk.op("pool", lambda e: e.tensor_copy(cc10[:, 2:10].rearrange("p (s t) -> p s t", s=4), a_s[:, :, 2:4]), reads=ra, writes=["cc10b"])
            bank, kb = psf.next()
            mk.op("pe", lambda e: e.transpose(bank[0:10, 0:128], cc10, ident), reads=["cc10a", "cc10b", "ident"], writes=[kb])
            if sub == 0:
                crow, kcrow = crow_r.next()
            evac(crow[:, sub * 128:(sub + 1) * 128], bank[0:10, 0:128], kb, [(kcrow, sub)])
        mk.dma("sp", conv_out[:, f2 * 256:(f2 + 1) * 256], crow, reads=[(kcrow, 0), (kcrow, 1)], writes=[("convo", f2)])
    mk.barrier()
    sb_release(m_up)

    wd_r = rot("wd", [128, NFB, 256], BF16, 2)
    hs_r = rot("hs", [128, 256], F32, 3)
    for cb in range(8):
        cs_ = slice(cb * 256, (cb + 1) * 256)
        wd, kwd = wd_r.next()
        for fb in range(NFB):
            mk.dma("pool", wd[:, fb, :], w_down[fb * 128:(fb + 1) * 128, cs_], writes=[(kwd, fb)])
        for i in range(9):
            M = 128 if i < 8 else 16
            col0 = 2 + i * 128
            hs, khs = hs_r.next()
            if i < 8:
                mk.dma("sp", hs, h_scr[(1 + i) * 128:(2 + i) * 128, cs_], writes=[khs])
            else:
                for s in range(4):
                    mk.dma("sp", hs[4 * s:4 * s + 4, :], h_scr[9 * 128 + 32 * s:9 * 128 + 32 * s + 4, cs_], writes=[(khs, s)])
            bank, kb = psf.next()

            def mmy(e, bank=bank, wd=wd, col0=col0, M=M):
                for fb in range(NFB):
                    i_ = e.matmul(bank[0:M, 0:256], gT[:, fb, col0:col0 + M], wd[:, fb, :], start=(fb == 0), stop=(fb == NFB - 1))
                return i_
            mk.op("pe", mmy, reads=[(kwd, fb) for fb in range(NFB)], writes=[kb])
            rdh = [khs] if i < 8 else [(khs, s) for s in range(4)]
            mk.op("dve", lambda e: e.tensor_tensor(hs[0:M], bank[0:M, 0:256], hs[0:M], ALU.add), reads=rdh, writes=[kb, khs])
            mk.dma("sp", y_out[i * 128:i * 128 + M, cs_], hs[0:M], reads=[khs], writes=[("y", i, cb)])
    mk.barrier()
    while live:
        live.pop().__exit__(None, None, None)
    return nc, mk


def _rope_tables(pos):
    half = 64
    inv = (10000.0 ** (-np.arange(half, dtype=np.float32) / half)).astype(np.float32)
    ang = pos.astype(np.float32)[:, None] * inv[None, :]
    return np.cos(ang).astype(np.float32), np.sin(ang).astype(np.float32)


def _sample_tables():
    f = np.float32
    n = np.arange(512)[:, None]
    j = np.arange(129)[None, :]
    mm = ((16 * n < 64 * (j + 1)) & (16 * n + 32 > 64 * j) & (n < 511)).astype(f)
    ts = np.zeros((128, 129), f)
    ts[:, [0, 127, 128]] = 1.0e4
    slot = (np.arange(128) % 32)
    newm = np.full((4, 128, 128), -BIG, f)
    for s in range(4):
        for kk in range(4):
            newm[s, (slot < 4) & (kk <= slot), 32 * s + kk] = 0.0
    swp = np.where(np.arange(512)[None, :] > slot[:, None], 0.0, -BIG).astype(f)
    return dict(mmap_s=mm, t_sel_s=ts, newmask=newm, swa_past_mask=swp)


def _const_tables(half):
    f = np.float32
    pos = lambda r: r - 1024 + 1024 * half
    n = np.arange(127)
    j = np.arange(32)
    cmp_add = np.zeros((9 * 128, 127), f)
    cmp_mul = np.zeros((9 * 128, 127), f)
    t_sel = np.zeros((9 * 128, 32), f)
    t_inv = np.zeros((9 * 128, 32), f)
    swa = np.zeros((9 * 128, 640), f)
    for qi in range(9):
        t = QT0 + qi
        qpos = pos(t * 128 + np.arange(128))[:, None]
        valid = (pos(16 * n + 31)[None, :] <= qpos) & (pos(16 * n)[None, :] >= 0)
        cmp_add[qi * 128:(qi + 1) * 128] = np.where(valid, 0.0, -BIG)
        cmp_mul[qi * 128:(qi + 1) * 128] = valid
        sp = pos(64 * j)[None, :]
        bvalid = (sp >= 0) & (sp <= qpos)
        jr = sp // 64
        cur = qpos // 64
        forced = bvalid & ((jr == 0) | (jr == cur) | (jr == cur - 1))
        t_sel[qi * 128:(qi + 1) * 128] = np.where(forced, 1.0e4, np.where(bvalid, 0.0, -1.0e4))
        t_inv[qi * 128:(qi + 1) * 128] = np.where(bvalid, 0.0, -BIG)
        kp = pos((t - 4) * 128 + np.arange(640))[None, :]
        rel = qpos - kp
        swa[qi * 128:(qi + 1) * 128] = np.where((rel >= 0) & (rel < 512) & (kp >= 0), 0.0, -BIG)
    mm = np.zeros((128, 32), f)
    mm[:127] = ((16 * n[:, None] < 64 * (j[None, :] + 1)) & (16 * n[:, None] + 32 > 64 * j[None, :]))
    i = np.arange(128)
    le = (i[:, None] <= i[None, :])
    blk = (i[:, None] // 32 == i[None, :] // 32)
    sm = (i[:, None] // 32 == np.arange(4)[None, :])
    return dict(
        cmp_add=cmp_add, cmp_mul=cmp_mul, t_sel=t_sel, t_inv=t_inv, swa_mask=swa, mmap_p=mm,
        tri_mask=np.where(i[None, :] <= i[:, None], 0.0, -BIG).astype(f),
        ucum_p=(le * (-1.0 / 16)).astype(f), ucum_s=((le & blk) * (-1.0 / 16)).astype(f),
        caus_p=le.astype(f), caus_s=(le & blk).astype(f),
        uend_p=np.full((128, 1), -1.0 / 16, f),
        uend_s=((sm & ((i % 32) < 4)[:, None]) * (-1.0 / 16)).astype(f), seqmask=sm.astype(f),
        ident=np.eye(128, dtype=f), **_sample_tables(),
    )


_SHARED = ["w_in", "norm1_g", "k_slc_norm_g", "k_swa_norm_g", "gla_w_a2", "gla_b_a2", "gla_onorm_g", "q_norm_g",
           "k_cmp_norm_g", "cmp_pe_k", "cmp_w1_k", "cmp_b1_k", "cmp_w2_k", "cmp_b2_k", "cmp_pe_v", "cmp_w1_v",
           "cmp_b1_v", "cmp_w2_v", "cmp_b2_v", "w_br_gla", "w_br_nsa", "w_o", "norm2_g", "w_up", "conv_w", "conv_b",
           "w_down"]


def make_in_maps(inputs, cores=range(8)):
    xp = np.asarray(inputs["x_prompt"], np.float32)
    xs = np.asarray(inputs["x_sample"], np.float32)
    shared = {k: np.ascontiguousarray(np.asarray(inputs[k], np.float32)) for k in _SHARED}
    ctab = [_const_tables(0), _const_tables(1)]
    maps = []
    for c in cores:
        b, half = c // 2, c % 2
        xb = np.zeros((NT * 128, D), np.float32)
        if half == 1:
            xb[0:1024] = xp[b, 0:1024]
        xb[1024:2048] = xp[b, half * 1024:(half + 1) * 1024]
        pos = np.zeros(NT * 128, np.float32)
        pos[0:2048] = np.arange(2048) - 1024 + 1024 * half
        for s in range(4):
            xb[2048 + 32 * s:2048 + 32 * s + 4] = xs[4 * c + s]
            pos[2048 + 32 * s:2048 + 32 * s + 4] = 8192 + np.arange(4)
        cos, sin = _rope_tables(pos)
        m = dict(shared)
        m.update(ctab[half])
        m["page_tab"] = np.ascontiguousarray(np.asarray(inputs["page_table"], np.int32)[4 * c:4 * c + 4].reshape(1, 256))
        if WITH_SAMPLE:
            for k in ("cache_k_cmp", "cache_v_cmp", "cache_k_slc", "cache_v_slc"):
                a = np.asarray(inputs[k], np.float32)
                m[k] = a.reshape(a.shape[0], 128, 256)
        m.update({
            "xbuf": xb, "rope_cos": cos, "rope_sin": sin,
            "state_gla_c": np.ascontiguousarray(np.asarray(inputs["state_gla"], np.float32)[4 * c:4 * c + 4]),
            "state_conv_c": np.ascontiguousarray(np.asarray(inputs["state_conv"], np.float32)[4 * c:4 * c + 4]),
            "state_swa_c": np.ascontiguousarray(np.stack([
                np.asarray(inputs["state_swa_k"], np.float32)[4 * c:4 * c + 4].reshape(4, 512, 256),
                np.asarray(inputs["state_swa_v"], np.float32)[4 * c:4 * c + 4].reshape(4, 512, 256)])),
        })
        maps.append(m)
    return maps


def assemble(results, cores=range(8)):
    f = np.float32
    y_p = np.zeros((4, 2048, D), f); y_s = np.zeros((32, 4, D), f)
    kvp = [np.zeros((4, 2048, 2, 128), f) for _ in range(4)]
    swap = [np.zeros((4, 512, 2, 128), f) for _ in range(2)]
    gla_p = np.zeros((4, 4, 128, 256), f); conv_p = np.zeros((4, 2, DFF), f)
    kvs = [np.zeros((32, 4, 2, 128), f) for _ in range(4)]
    swas = [np.zeros((32, 512, 2, 128), f) for _ in range(2)]
    gla_s = np.zeros((32, 4, 128, 256), f); conv_s = np.zeros((32, 2, DFF), f)
    for c, r in zip(cores, results):
        b, half = c // 2, c % 2
        yo = r["y_out"]
        y_p[b, half * 1024:(half + 1) * 1024] = yo[0:1024]
        y_s[4 * c:4 * c + 4] = yo[1024:1040].reshape(4, 4, D)
        kv = r["kv_out"].reshape(NT * 128, 6, 2, 128)
        for a in range(4):
            kvp[a][b, half * 1024:(half + 1) * 1024] = kv[1024:2048, a]
            kvs[a][4 * c:4 * c + 4] = kv[2048:2176, a].reshape(4, 32, 2, 128)[:, 0:4]
        if half == 1:
            swap[0][b] = kv[1536:2048, 4]
            swap[1][b] = kv[1536:2048, 5]
            gla_p[b] = r["gla_out"][0]
            conv_p[b] = r["conv_out"][0:2]
        swas[0][4 * c:4 * c + 4] = r["swa_out"][0].reshape(4, 512, 2, 128)
        swas[1][4 * c:4 * c + 4] = r["swa_out"][1].reshape(4, 512, 2, 128)
        gla_s[4 * c:4 * c + 4] = r["gla_out"][1:5]
        conv_s[4 * c:4 * c + 4] = r["conv_out"][2:10].reshape(4, 2, DFF)
    return (y_p, y_s, kvp[0], kvp[1], kvp[2], kvp[3], swap[0], swap[1], gla_p, conv_p,
            kvs[0], kvs[1], kvs[2], kvs[3], swas[0], swas[1], gla_s, conv_s)


def kernel(**inputs):
    nc, mk = build_program()
    maps = make_in_maps(inputs)
    res = run_bass_kernel_spmd(nc, maps, core_ids=list(range(8)))
    return assemble(res.results)
```

```python
import numpy as np
import concourse.bass as bass
import concourse.mybir as mybir
from concourse.bass_utils import run_bass_kernel_spmd

F32 = mybir.dt.float32
BF16 = mybir.dt.bfloat16
I32 = mybir.dt.int32
AF = mybir.ActivationFunctionType
ALU = mybir.AluOpType
AX = mybir.AxisListType

D = 2048
NCOLS = 9768
NT = 17
QT0 = 7
EPS = 1e-6
DFF = 5632
NFB = 44
C_GQ, C_GK, C_GV, C_GR, C_GA, C_NQ, C_KV, C_NG, C_MG = 0, 512, 1024, 2048, 3072, 3088, 4112, 5648, 5672
BIG = 1.0e5
WITH_SAMPLE = True


class MK:
    def __init__(self, nc, n_dma_sems=40):
        self.nc = nc
        self.eng = {"pe": nc.tensor, "act": nc.scalar, "dve": nc.vector,
                    "pool": nc.gpsimd, "sp": nc.sync}
        self._stack = []
        self.esem = {}
        for e in ("pe", "act", "dve", "pool"):
            self.esem[e] = self._sem("es_" + e)
        self.ecount = {e: 0 for e in self.esem}
        self.dsem = [self._sem("ds%d" % i) for i in range(n_dma_sems)]
        self.dtot = [0] * n_dma_sems
        self.drr = 0
        self.known = {e: {} for e in self.eng}
        self.last_w = {}
        self.readers = {}
        self.n_wait = 0
        self.n_ins = 0
        self.n_dma = 0

    def _sem(self, name):
        cm = self.nc.semaphore(name)
        s = cm.__enter__()
        self._stack.append(cm)
        return s

    def _need(self, E, reads, writes):
        need = {}

        def add(tok, same_ok):
            if tok is None:
                return
            sem, val, src = tok
            if src == E and not same_ok:
                return
            k = id(sem)
            if k not in need or need[k][1] < val:
                need[k] = (sem, val)

        for k in reads:
            add(self.last_w.get(k), E != "pe")
        for k in writes:
            add(self.last_w.get(k), False)
            for tok in self.readers.get(k, {}).values():
                add(tok, False)
        kn = self.known[E]
        eng = self.eng[E]
        for k, (sem, val) in need.items():
            if kn.get(k, 0) >= val:
                continue
            eng.wait_ge(sem, val)
            self.n_wait += 1
            kn[k] = val

    def _commit(self, tok, reads, writes):
        for k in reads:
            d = self.readers.setdefault(k, {})
            d[id(tok[0])] = tok
        for k in writes:
            self.last_w[k] = tok
            self.readers[k] = {}

    def op(self, E, fn, reads=(), writes=()):
        self._need(E, reads, writes)
        ins = fn(self.eng[E])
        self.ecount[E] += 1
        ins.then_inc(self.esem[E], 1)
        self.n_ins += 1
        tok = (self.esem[E], self.ecount[E], E)
        self._commit(tok, reads, writes)
        return tok

    def dma(self, E, out, in_, reads=(), writes=(), **kw):
        i = self.drr
        self.drr = (self.drr + 1) % len(self.dsem)
        sem = self.dsem[i]
        kn = self.known[E]
        if self.dtot[i] > 0 and kn.get(id(sem), 0) < self.dtot[i]:
            self.eng[E].wait_ge(sem, self.dtot[i])
            kn[id(sem)] = self.dtot[i]
        self._need(E, reads, writes)
        self.eng[E].dma_start(out=out, in_=in_, **kw).then_inc(sem, 16)
        self.n_dma += 1
        self.dtot[i] += 16
        tok = (sem, self.dtot[i], "dma")
        self._commit(tok, reads, writes)
        return tok

    def barrier(self):
        for E, eng in self.eng.items():
            kn = self.known[E]
            for X, sem in self.esem.items():
                if X == E or self.ecount[X] == 0:
                    continue
                if kn.get(id(sem), 0) < self.ecount[X]:
                    eng.wait_ge(sem, self.ecount[X])
                    kn[id(sem)] = self.ecount[X]
            for i, sem in enumerate(self.dsem):
                if self.dtot[i] and kn.get(id(sem), 0) < self.dtot[i]:
                    eng.wait_ge(sem, self.dtot[i])
                    kn[id(sem)] = self.dtot[i]
        self.last_w = {}
        self.readers = {}


class Rot:
    def __init__(self, aps, name):
        self.aps = aps
        self.name = name
        self.i = 0

    def next(self):
        j = self.i % len(self.aps)
        self.i += 1
        return self.aps[j], (self.name, j)


class Ctx:
    pass


def build_program(stage=99, n_pool=2560):
    N_POOL = n_pool
    nc = bass.Bass("TRN2", target_bir_lowering=False)
    mk = MK(nc)
    X = Ctx()
    X.nc, X.mk = nc, mk
    live = []

    def din(name, shape, dt=F32):
        return nc.dram_tensor(name, list(shape), dt, kind="ExternalInput").ap()

    def dout(name, shape, dt=F32):
        return nc.dram_tensor(name, list(shape), dt, kind="ExternalOutput").ap()

    def dscr(name, shape, dt=F32):
        return nc.dram_tensor(name, list(shape), dt).ap()

    def sb(name, shape, dt=F32):
        cm = nc.sbuf_tensor(name, list(shape), dt)
        t = cm.__enter__()
        live.append(cm)
        return t[:] if not hasattr(t, "shape") or True else t

    def sb_mark():
        return len(live)

    def sb_release(mark):
        while len(live) > mark:
            live.pop().__exit__(None, None, None)

    def dbg(name, ap, keys):
        if stage != 2:
            return
        d_ = dout("dbg_" + name, list(ap.shape), F32 if ap.dtype == F32 else ap.dtype)
        mk.dma("sp", d_, ap, reads=keys, writes=["dbg_" + name])

    def rot(name, shape, dt, n):
        return Rot([sb("%s%d" % (name, i), shape, dt) for i in range(n)], name)

    xbuf = din("xbuf", [NT * 128, D])
    w_in = din("w_in", [D, NCOLS])
    norm1_g = din("norm1_g", [D])
    ident_d = din("ident", [128, 128])
    rope_cos = din("rope_cos", [NT * 128, 64])
    rope_sin = din("rope_sin", [NT * 128, 64])
    k_slc_norm_g = din("k_slc_norm_g", [128])
    k_swa_norm_g = din("k_swa_norm_g", [128])
    kv_out = dout("kv_out", [NT * 128, 1536])
    proj = dscr("proj", [NT * 128, NCOLS])
    h_scr = dscr("h_scr", [10 * 128, D])
    gla_out = dout("gla_out", [5, 4, 128, 256])
    conv_out = dout("conv_out", [10, DFF])
    y_out = dout("y_out", [1040, D])
    swa_out = dout("swa_out", [2, 4, 512, 256])
    gla_w_a2 = din("gla_w_a2", [16, 512]); gla_b_a2 = din("gla_b_a2", [512]); gla_onorm_g = din("gla_onorm_g", [256])
    q_norm_g = din("q_norm_g", [128]); k_cmp_norm_g = din("k_cmp_norm_g", [128])
    CMPW = {}
    for nm in ("k", "v"):
        CMPW[nm] = dict(pe=din("cmp_pe_" + nm, [32, 128]), w1=din("cmp_w1_" + nm, [32, 128, 128]),
                        b1=din("cmp_b1_" + nm, [128]), w2=din("cmp_w2_" + nm, [128, 128]), b2=din("cmp_b2_" + nm, [128]))
    ucum_p = din("ucum_p", [128, 128]); ucum_s = din("ucum_s", [128, 128])
    caus_p = din("caus_p", [128, 128]); caus_s = din("caus_s", [128, 128])
    uend_p = din("uend_p", [128, 1]); uend_s = din("uend_s", [128, 4]); seqmask = din("seqmask", [128, 4])
    state_gla_c = din("state_gla_c", [4, 4, 128, 256])
    state_conv_c = din("state_conv_c", [4, 2, DFF])
    state_swa_c = din("state_swa_c", [2, 4, 512, 256])
    mmap_p = din("mmap_p", [128, 32]); tri_mask = din("tri_mask", [128, 128])
    cmp_add = din("cmp_add", [9 * 128, 127]); cmp_mul = din("cmp_mul", [9 * 128, 127])
    t_sel = din("t_sel", [9 * 128, 32]); t_inv = din("t_inv", [9 * 128, 32])
    swa_mask = din("swa_mask", [9 * 128, 640])
    page_tab = din("page_tab", [1, 256], I32)
    if WITH_SAMPLE:
        cache_k_cmp = din("cache_k_cmp", [N_POOL, 128, 256]); cache_v_cmp = din("cache_v_cmp", [N_POOL, 128, 256])
        cache_k_slc = din("cache_k_slc", [N_POOL, 128, 256]); cache_v_slc = din("cache_v_slc", [N_POOL, 128, 256])
    mmap_s = din("mmap_s", [512, 129]); t_sel_s = din("t_sel_s", [128, 129]); newmask = din("newmask", [4, 128, 128])
    swa_past_mask = din("swa_past_mask", [128, 512])
    o_scr = dscr("o_scr", [4, 2, 3, 128, 128])
    w_br_gla = din("w_br_gla", [1024, D]); w_br_nsa = din("w_br_nsa", [1024, D]); w_o = din("w_o", [D, D])
    norm2_g = din("norm2_g", [D]); w_up = din("w_up", [D, 2 * DFF]); conv_w = din("conv_w", [3, DFF])
    conv_b = din("conv_b", [DFF]); w_down = din("w_down", [DFF, D])

    Q1 = "act" if WITH_SAMPLE else "sp"
    if WITH_SAMPLE:
        gath = dscr("gath", [4, 256, 128, 256])
        caches = [cache_k_cmp, cache_v_cmp, cache_k_slc, cache_v_slc]
        gsem = [mk._sem("gs%d" % i) for i in range(4)]
        sp_ = nc.sync
        g_cnt = sp_.alloc_register("g_cnt")
        g_pid = sp_.alloc_register("g_pid")
        sp_.reg_mov(g_cnt, 256)
        with sp_.While(g_cnt):
            sp_.reg_sub(g_cnt, g_cnt, 1)
            g_idx = sp_.snap(g_cnt, min_val=0, max_val=255)
            sp_.reg_load(g_pid, page_tab[0:1, bass.ds(g_idx, 1)])
            g_pv = sp_.snap(g_pid, min_val=0, max_val=N_POOL - 1)
            for ci in range(4):
                sp_.dma_start(out=gath[ci, bass.ds(g_idx, 1)], in_=caches[ci][bass.ds(g_pv, 1)]).then_inc(gsem[ci], 16)
        for ci in range(4):
            sp_.wait_ge(gsem[ci], 16 * 256)
    psf = Rot([nc.alloc_psum_tensor("psf%d" % i, [128, 512], F32).ap() for i in range(6)], "psf")
    psb = Rot([nc.alloc_psum_tensor("psb%d" % i, [128, 1024], BF16).ap() for i in range(2)], "psb")

    ident = sb("identf", [128, 128], F32)
    mk.dma(Q1, ident, ident_d, writes=["ident"])
    epsc = sb("epsc", [128, 1], F32)
    mk.op("dve", lambda e: e.memset(epsc, EPS), writes=["epsc"])
    onec = sb("onec", [128, 1], F32)
    mk.op("dve", lambda e: e.memset(onec, 1.0), writes=["onec"])
    identb = sb("identb", [128, 128], BF16)
    mk.op("dve", lambda e: e.tensor_copy(identb, ident), reads=["ident"], writes=["identb"])
    cpy_i = [0]
    LDQ = ["sp"]
    m_keep = sb_mark()
    o_glaT = sb("o_glaT", [128, 8, 10 * 128], BF16)
    o_nsaT = sb("o_nsaT", [128, 8, 10 * 128], BF16)

    def evac(out, in_, ps_key, writes, reads=()):
        cpy_i[0] += 1
        if cpy_i[0] % 2:
            mk.op("act", lambda e: e.copy(out, in_), reads=reads, writes=[ps_key] + list(writes))
        else:
            mk.op("dve", lambda e: e.tensor_copy(out, in_), reads=reads, writes=[ps_key] + list(writes))

    SC = 128.0 ** -0.5

    def ld(name, dram_ap, shape, dt=F32, eng=None):
        t_ = sb(name, shape, dt)
        mk.dma(eng or LDQ[0], t_, dram_ap, writes=[name])
        return t_

    def transpose_into(dst, src_list, reads, wkey, idn=None, n_in=128):
        bank, kb = psf.next()

        def tr(e):
            for j, s_ in enumerate(src_list):
                i = e.transpose(bank[:, j * 128:(j + 1) * 128], s_, ident)
            return i
        mk.op("pe", tr, reads=list(reads) + ["ident"], writes=[kb])
        evac(dst, bank[:, 0:128 * len(src_list)].rearrange("p (j n) -> p j n", j=len(src_list)), kb, [wkey])


    m_ph12 = sb_mark()
    xnT = sb("xnT", [128, 16, NT * 128], BF16)
    g1b = sb("g1b", [128, D], F32)
    mk.dma(Q1, g1b, norm1_g.partition_broadcast(128), writes=["g1b"])
    xr = rot("xt", [128, D], F32, 2)
    xnr = rot("xn", [128, D], F32, 2)
    junk = sb("junk", [128, D], F32)
    ssr = rot("ss", [128, 2], F32, 2)
    for t in range(NT):
        xt, kx = xr.next()
        xn, kn_ = xnr.next()
        ss, ks = ssr.next()
        mk.dma(Q1, xt, xbuf[t * 128:(t + 1) * 128, :], writes=[kx])
        mk.op("act", lambda e: e.activation(junk, xt, AF.Square), reads=[kx], writes=["junk"])
        mk.op("dve", lambda e: e.reduce_sum(ss[:, 0:1], junk, axis=AX.X), reads=["junk"], writes=[ks])
        mk.op("act", lambda e: e.activation(ss[:, 1:2], ss[:, 0:1], AF.Sqrt, bias=epsc[:, 0:1], scale=1.0 / D),
              reads=[ks, "epsc"], writes=[ks])
        mk.op("dve", lambda e: e.reciprocal(ss[:, 1:2], ss[:, 1:2]), reads=[ks], writes=[ks])
        mk.op("dve", lambda e: e.scalar_tensor_tensor(xn, xt, ss[:, 1:2], g1b, op0=ALU.mult, op1=ALU.mult),
              reads=[kx, ks, "g1b"], writes=[kn_])
        for c4 in range(4):
            bank, kb = psf.next()

            def tr(e, c4=c4, bank=bank, xn=xn):
                for j in range(4):
                    c = c4 * 4 + j
                    i = e.transpose(bank[:, j * 128:(j + 1) * 128], xn[:, c * 128:(c + 1) * 128], ident)
                return i
            mk.op("pe", tr, reads=[kn_, "ident"], writes=[kb])
            evac(xnT[:, c4 * 4:(c4 + 1) * 4, t * 128:(t + 1) * 128],
                 bank.rearrange("p (j n) -> p j n", j=4), kb, [("xnT", t)])

    wr = rot("wb", [128, 16, 512], BF16, 2)
    str_ = rot("stg", [128, 512], F32, 4)
    ncb = (NCOLS + 511) // 512
    for cb in range(ncb):
        c0 = cb * 512
        n = min(512, NCOLS - c0)
        c1 = c0 + n
        prev_need = (c0 < C_GR and c1 > C_GK) or (c0 < C_NQ and c1 > C_GA) or (c0 < C_NG and c1 > C_KV)
        wb, kw = wr.next()
        for c in range(16):
            mk.dma("pool", wb[:, c, :n], w_in[c * 128:(c + 1) * 128, c0:c1], writes=[(kw, c)])
        for t in (range(NT) if prev_need else range(QT0, NT)):
            bank, kb = psf.next()

            def mm(e, bank=bank, wb=wb, t=t, n=n):
                for c in range(16):
                    i = e.matmul(bank[:, :n], xnT[:, c, t * 128:(t + 1) * 128], wb[:, c, :n],
                                 start=(c == 0), stop=(c == 15))
                return i
            mk.op("pe", mm, reads=[(kw, c) for c in range(16)] + [("xnT", t)], writes=[kb])
            st, kst = str_.next()
            evac(st[:, :n], bank[:, :n], kb, [kst])
            mk.dma(Q1, proj[t * 128:(t + 1) * 128, c0:c1], st[:, :n], reads=[kst], writes=[("proj", t, cb)])
    mk.barrier()
    sb_release(m_ph12)
    if stage == 0:
        while live:
            live.pop().__exit__(None, None, None)
        return nc, mk

    LDQ[0] = Q1
    m_gla = sb_mark()
    wa2 = ld("wa2", gla_w_a2, [16, 512])
    ba2b = ld("ba2b", gla_b_a2.partition_broadcast(128), [128, 512])
    gob = ld("gob", gla_onorm_g.partition_broadcast(128), [128, 256])
    Ucp = ld("Ucp", ucum_p, [128, 128])
    Ucs = ld("Ucs", ucum_s, [128, 128])
    c01p = ld("c01p", caus_p, [128, 128])
    c01s = ld("c01s", caus_s, [128, 128])
    uendp = ld("uendp", uend_p, [128, 1])
    uends = ld("uends", uend_s, [128, 4])
    seqm = ld("seqm", seqmask, [128, 4])
    Sp = sb("Sp", [128, 4, 256], F32)
    Spb = sb("Spb", [128, 4, 256], BF16)
    mk.op("dve", lambda e: e.memset(Sp, 0.0), writes=[("S", 0, h) for h in range(4)])
    mk.op("pool", lambda e: e.memset(Spb, 0.0), writes=[("Sb", 0, h) for h in range(4)])
    Ss = sb("Ss", [128, 4, 4, 256], F32)
    Ssb = sb("Ssb", [128, 4, 4, 256], BF16)
    for s in range(4):
        mk.dma(Q1, Ss[:, s], state_gla_c[s].rearrange("h d e -> d h e"), writes=[("S", 1 + s, h) for h in range(4)])
        mk.op("pool", lambda e: e.tensor_copy(Ssb[:, s], Ss[:, s]), reads=[("S", 1 + s, h) for h in range(4)],
              writes=[("Sb", 1 + s, h) for h in range(4)])
    qeTm = sb("qeTm", [128, 4, 4, 128], BF16)
    mk.op("pool", lambda e: e.memset(qeTm, 0.0), writes=["qeTm"])
    a_r = rot("ga", [128, 16], F32, 2)
    aT_r = rot("gaT", [16, 128], F32, 2)
    k_r = rot("gk", [128, 512], F32, 2)
    v_r = rot("gv", [128, 1024], F32, 2)
    q_r = rot("gq", [128, 512], F32, 2)
    r_r = rot("gr", [128, 1024], F32, 1)
    ln_r = rot("gln", [128, 512], F32, 2)
    e1_r = rot("ge1", [128, 512], F32, 1)
    e2_r = rot("ge2", [128, 512], F32, 2)
    eb_r = rot("geb", [128, 16], F32, 2)
    qe_r = rot("gqe", [128, 512], F32, 1)
    ke_r = rot("gke", [128, 512], F32, 2)
    keb_r = rot("gkeb", [128, 512], BF16, 2)
    kem = sb("gkem", [128, 4, 512], BF16)
    vb_r = rot("gvb", [128, 1024], BF16, 2)
    qeT_r = rot("gqeT", [128, 4, 128], BF16, 2)
    keT_r = rot("gkeT", [128, 4, 128], BF16, 2)
    att_r = rot("gatt", [128, 4, 128], BF16, 2)
    osb_r = rot("gosb", [128, 4, 256], F32, 1)
    jg = sb("jg", [128, 4, 256], F32)
    s8_r = rot("gs8", [128, 8], F32, 2)
    og_r = rot("gog", [128, 1024], F32, 1)
    SCQ = 128.0 ** -0.5
    for t in range(NT):
        isq = t >= QT0
        smp = (t == 16)
        G = 4 if smp else 1
        qi = t - QT0
        rows = slice(t * 128, (t + 1) * 128)
        a_t, ka = a_r.next()
        k_t, kk_ = k_r.next()
        v_t, kv_ = v_r.next()
        mk.dma(Q1, a_t, proj[rows, C_GA:C_NQ], writes=[ka])
        mk.dma(Q1, k_t, proj[rows, C_GK:C_GV], writes=[kk_])
        mk.dma(Q1, v_t, proj[rows, C_GV:C_GR], writes=[kv_])
        if isq:
            q_t, kq_ = q_r.next()
            r_t, kr_ = r_r.next()
            mk.dma(Q1, q_t, proj[rows, C_GQ:C_GK], writes=[kq_])
            mk.dma(Q1, r_t, proj[rows, C_GR:C_GA], writes=[kr_])
        aT, kaT = aT_r.next()
        bank, kb = psf.next()
        mk.op("pe", lambda e: e.transpose(bank[0:16, 0:128], a_t, ident), reads=[ka, "ident"], writes=[kb])
        evac(aT, bank[0:16, 0:128], kb, [kaT])
        bank, kb = psf.next()
        mk.op("pe", lambda e: e.matmul(bank, aT, wa2, start=True, stop=True), reads=[kaT, "wa2"], writes=[kb])
        lnv, kln = ln_r.next()
        mk.op("dve", lambda e: e.tensor_tensor(lnv, bank, ba2b, ALU.add), reads=["ba2b"], writes=[kb, kln])
        mk.op("act", lambda e: e.activation(lnv, lnv, AF.Exp, scale=-1.0), reads=[kln], writes=[kln])
        mk.op("act", lambda e: e.activation(lnv, lnv, AF.Ln, bias=onec[:, 0:1]), reads=[kln, "onec"], writes=[kln])
        bank, kb = psf.next()
        U = Ucs if smp else Ucp
        mk.op("pe", lambda e: e.matmul(bank, U, lnv, start=True, stop=True), reads=[kln, "Ucp", "Ucs"], writes=[kb])
        e2, ke2 = e2_r.next()
        mk.op("act", lambda e: e.activation(e2, bank, AF.Exp, scale=-1.0), writes=[kb, ke2])
        if isq:
            e1, ke1 = e1_r.next()
            mk.op("act", lambda e: e.activation(e1, bank, AF.Exp), writes=[kb, ke1])
        bank, kb = psf.next()
        uend = uends if smp else uendp

        def mmbl(e, bank=bank, lnv=lnv, uend=uend, G=G):
            for h in range(4):
                i = e.matmul(bank[:, h * G:(h + 1) * G], lnv[:, h * 128:(h + 1) * 128], uend, start=True, stop=True)
            return i
        mk.op("pe", mmbl, reads=[kln, "uendp", "uends"], writes=[kb])
        eb, keb_ = eb_r.next()
        mk.op("act", lambda e: e.activation(eb[:, 0:4 * G], bank[:, 0:4 * G], AF.Exp), writes=[kb, keb_])
        if t == 8:
            dbg("lnv", lnv, [kln]); dbg("e2", e2, [ke2]); dbg("eb", eb, [keb_]); dbg("aT", aT, [kaT]); dbg("a", a_t, [ka])
        ke, kke = ke_r.next()
        keb, kkeb = keb_r.next()
        vb, kvb = vb_r.next()
        mk.op("dve", lambda e: e.tensor_tensor(ke, k_t, e2, ALU.mult), reads=[kk_, ke2], writes=[kke])
        mk.op("pool", lambda e: e.tensor_copy(keb, ke), reads=[kke], writes=[kkeb])
        mk.op("pool", lambda e: e.tensor_copy(vb, v_t), reads=[kv_], writes=[kvb])
        if smp:
            for s in range(4):
                mk.op("dve", lambda e: e.tensor_scalar(kem[:, s, :], ke, seqm[:, s:s + 1], None, op0=ALU.mult),
                      reads=[kke, "seqm"], writes=[("kem", s)])
        if isq:
            qe, kqe = qe_r.next()
            mk.op("dve", lambda e: e.scalar_tensor_tensor(qe, q_t, SCQ, e1, op0=ALU.mult, op1=ALU.mult),
                  reads=[kq_, ke1], writes=[kqe])
            qeT, kqeT = qeT_r.next()
            keT, kkeT = keT_r.next()
            transpose_into(qeT, [qe[:, h * 128:(h + 1) * 128] for h in range(4)], [kqe], kqeT)
            transpose_into(keT, [ke[:, h * 128:(h + 1) * 128] for h in range(4)], [kke], kkeT)
            bank, kb = psf.next()

            def mmatt(e, bank=bank, keT=keT, qeT=qeT):
                for h in range(4):
                    i = e.matmul(bank[:, h * 128:(h + 1) * 128], keT[:, h, :], qeT[:, h, :], start=True, stop=True)
                return i
            mk.op("pe", mmatt, reads=[kqeT, kkeT], writes=[kb])
            att, katt = att_r.next()
            c01 = c01s if smp else c01p
            mk.op("dve", lambda e: e.tensor_tensor(att, bank.rearrange("p (h n) -> p h n", h=4),
                                                   c01.unsqueeze(1).to_broadcast([128, 4, 128]), ALU.mult),
                  reads=["c01p", "c01s"], writes=[kb, katt])
            if smp:
                for s in range(4):
                    mk.op("pool", lambda e: e.tensor_copy(qeTm[:, s, :, 32 * s:32 * s + 32], qeT[:, :, 32 * s:32 * s + 32]),
                          reads=[kqeT], writes=["qeTm"])
            osb, kosb = osb_r.next()
            for hp in range(2):
                bank, kb = psf.next()

                def mmo(e, bank=bank, hp=hp, att=att, vb=vb, qeT=qeT):
                    for h in (2 * hp, 2 * hp + 1):
                        o_ = bank[:, (h % 2) * 256:(h % 2 + 1) * 256]
                        e.matmul(o_, att[:, h, :], vb[:, h * 256:(h + 1) * 256], start=True, stop=False)
                        if smp:
                            for s in range(4):
                                i = e.matmul(o_, qeTm[:, s, h, :], Ssb[:, s, h, :], start=False, stop=(s == 3))
                        else:
                            i = e.matmul(o_, qeT[:, h, :], Spb[:, h, :], start=False, stop=True)
                    return i
                sbk = [("Sb", (1 + s if smp else 0), h) for h in (2 * hp, 2 * hp + 1) for s in range(G)]
                mk.op("pe", mmo, reads=[katt, kvb, kqeT, "qeTm"] + sbk, writes=[kb])
                evac(osb[:, 2 * hp:2 * hp + 2, :], bank.rearrange("p (h n) -> p h n", h=2), kb, [(kosb, hp)])
        for h in range(4):
            for s in range(G):
                si = 1 + s if smp else 0
                S_ = Ss[:, s, h, :] if smp else Sp[:, h, :]
                Sb_ = Ssb[:, s, h, :] if smp else Spb[:, h, :]
                lhs = kem[:, s, h * 128:(h + 1) * 128] if smp else keb[:, h * 128:(h + 1) * 128]
                bank, kb = psf.next()
                mk.op("pe", lambda e: e.matmul(bank[:, 0:256], lhs, vb[:, h * 256:(h + 1) * 256], start=True, stop=True),
                      reads=[kkeb, kvb, ("kem", s)], writes=[kb])
                ebc = eb[:, h * G + s:h * G + s + 1]
                mk.op("dve", lambda e: e.tensor_scalar(S_, S_, ebc, None, op0=ALU.mult), reads=[keb_, ("S", si, h)],
                      writes=[("S", si, h)])
                mk.op("dve", lambda e: e.scalar_tensor_tensor(S_, bank[:, 0:256], ebc, S_, op0=ALU.mult, op1=ALU.add),
                      reads=[keb_, ("S", si, h)], writes=[kb, ("S", si, h)])
                mk.op("act", lambda e: e.copy(Sb_, S_), reads=[("S", si, h)], writes=[("Sb", si, h)])
        if t == 8:
            dbg("ke", ke, [kke]); dbg("S8", Sp, [("S", 0, h) for h in range(4)]); dbg("osb", osb, [(kosb, 0), (kosb, 1)])
            dbg("qe", qe, [kqe]); dbg("e1", e1, [ke1])
        if isq:
            s8, ks8 = s8_r.next()
            og, kog = og_r.next()
            rdo = [(kosb, 0), (kosb, 1)]
            mk.op("act", lambda e: e.activation(jg, osb, AF.Square), reads=rdo, writes=["jg"])
            mk.op("dve", lambda e: e.reduce_sum(s8[:, 0:4], jg, axis=AX.X), reads=["jg"], writes=[ks8])
            mk.op("act", lambda e: e.activation(s8[:, 4:8], s8[:, 0:4], AF.Sqrt, bias=epsc[:, 0:1], scale=1.0 / 256),
                  reads=[ks8, "epsc"], writes=[ks8])
            mk.op("dve", lambda e: e.reciprocal(s8[:, 4:8], s8[:, 4:8]), reads=[ks8], writes=[ks8])
            og4 = og.rearrange("p (h n) -> p h n", h=4)
            mk.op("dve", lambda e: e.tensor_tensor(og4, osb, s8[:, 4:8].unsqueeze(2).to_broadcast([128, 4, 256]), ALU.mult),
                  reads=rdo + [ks8], writes=[kog])
            mk.op("pool", lambda e: e.tensor_tensor(og4, og4, gob.unsqueeze(1).to_broadcast([128, 4, 256]), ALU.mult),
                  reads=[kog, "gob"], writes=[kog])
            mk.op("act", lambda e: e.activation(r_t, r_t, AF.Silu), reads=[kr_], writes=[kr_])
            mk.op("dve", lambda e: e.tensor_tensor(og, og, r_t, ALU.mult), reads=[kog, kr_], writes=[kog])
            for c4 in range(2):
                transpose_into(o_glaT[:, c4 * 4:(c4 + 1) * 4, qi * 128:(qi + 1) * 128],
                               [og[:, (c4 * 4 + j) * 128:(c4 * 4 + j + 1) * 128] for j in range(4)], [kog], ("oglaT", qi))
        if t == 15:
            mk.dma(Q1, gla_out[0].rearrange("h d e -> d h e"), Sp, reads=[("S", 0, h) for h in range(4)], writes=["glao0"])
    for s in range(4):
        mk.dma(Q1, gla_out[1 + s].rearrange("h d e -> d h e"), Ss[:, s], reads=[("S", 1 + s, h) for h in range(4)],
               writes=[("glao", s)])
    mk.barrier()
    sb_release(m_gla)
    if stage <= 2:
        mk.barrier()
        return nc, mk

    KTn = sb("KTn", [128, 2, 2, 128], BF16)
    VVn = sb("VVn", [128, 2, 2, 128], BF16)
    kcT = sb("kcT", [128, 2, 128], BF16)
    vc = sb("vc", [128, 2, 128], F32)
    cw = {}
    for nm in ("k", "v"):
        w1 = sb("w1" + nm, [128, 32, 128], BF16)
        for r8 in range(8):
            mk.dma("pool", w1[:, r8 * 4:(r8 + 1) * 4, :],
                   CMPW[nm]["w1"][r8 * 4:(r8 + 1) * 4].rearrange("r d f -> d r f"), writes=[("w1" + nm, r8)])
        w2 = sb("w2" + nm, [128, 128], BF16)
        mk.dma("pool", w2, CMPW[nm]["w2"], writes=["w2" + nm])
        b1 = ld("b1" + nm, CMPW[nm]["b1"].rearrange("(f o) -> f o", o=1), [128, 1])
        b2b = ld("b2b" + nm, CMPW[nm]["b2"].partition_broadcast(128), [128, 128])
        pe_ = ld("pe" + nm, CMPW[nm]["pe"], [32, 128])
        peT = sb("peT" + nm, [128, 32], BF16)
        bank, kb = psf.next()
        mk.op("pe", lambda e: e.transpose(bank[:, 0:32], pe_, ident[0:32, 0:32]), reads=["pe" + nm, "ident"], writes=[kb])
        evac(peT, bank[:, 0:32], kb, ["peT" + nm])
        c1 = sb("c1" + nm, [128, 1], F32)
        bank, kb = psf.next()

        def mmc(e, bank=bank, w1=w1, peT=peT):
            for rr in range(32):
                i = e.matmul(bank[:, 0:1], w1[:, rr, :], peT[:, rr:rr + 1], start=(rr == 0), stop=(rr == 31))
            return i
        mk.op("pe", mmc, reads=[("w1" + nm, r8) for r8 in range(8)] + ["peT" + nm], writes=[kb])
        mk.op("dve", lambda e: e.tensor_tensor(c1, bank[:, 0:1], b1, ALU.add), reads=["b1" + nm], writes=[kb, "c1" + nm])
        cw[nm] = dict(w1=w1, w2=w2, b2b=b2b, c1=c1)
    gkc = ld("gkc", k_cmp_norm_g.partition_broadcast(128), [128, 128])
    gTr = rot("gTc", [128, 512], BF16, 2)
    o2r = rot("o2c", [128, 128], F32, 2)
    sc3 = rot("sc3", [128, 4], F32, 2)
    jc = sb("jc", [128, 128], F32)

    def compress(nm, rowsT, nblk, rreads, out_fn):
        W = cw[nm]
        bank, kb = psf.next()

        def mm1(e):
            for rr in range(32):
                i = e.matmul(bank[:, 0:nblk], W["w1"][:, rr, :], rowsT[:, rr:rr + 16 * (nblk - 1) + 1:16],
                             start=(rr == 0), stop=(rr == 31))
            return i
        mk.op("pe", mm1, reads=list(rreads) + [("w1" + nm, r8) for r8 in range(8)], writes=[kb])
        gT, kg = gTr.next()
        mk.op("act", lambda e: e.activation(gT[:, 0:nblk], bank[:, 0:nblk], AF.Gelu_apprx_tanh, bias=W["c1"][:, 0:1]),
              reads=["c1" + nm], writes=[kb, kg])
        for mt in range((nblk + 127) // 128):
            m = min(128, nblk - mt * 128)
            bank2, kb2 = psf.next()
            mk.op("pe", lambda e: e.matmul(bank2[0:m, 0:128], gT[:, mt * 128:mt * 128 + m], W["w2"], start=True, stop=True),
                  reads=[kg, "w2" + nm], writes=[kb2])
            o2, ko2 = o2r.next()
            mk.op("dve", lambda e: e.tensor_tensor(o2[0:m], bank2[0:m, 0:128], W["b2b"][0:m], ALU.add),
                  reads=["b2b" + nm], writes=[kb2, ko2])
            out_fn(mt, m, o2, ko2)

    def rms_rows(o2, ko2, m, gain, gkey, out, okey):
        s3, ks3 = sc3.next()
        mk.op("act", lambda e: e.activation(jc[0:m], o2[0:m], AF.Square), reads=[ko2], writes=["jc"])
        mk.op("dve", lambda e: e.reduce_sum(s3[0:m, 0:1], jc[0:m], axis=AX.X), reads=["jc"], writes=[ks3])
        mk.op("act", lambda e: e.activation(s3[0:m, 1:2], s3[0:m, 0:1], AF.Sqrt, bias=epsc[0:m, 0:1], scale=1.0 / 128),
              reads=[ks3, "epsc"], writes=[ks3])
        mk.op("dve", lambda e: e.reciprocal(s3[0:m, 1:2], s3[0:m, 1:2]), reads=[ks3], writes=[ks3])
        mk.op("dve", lambda e: e.scalar_tensor_tensor(out[0:m], o2[0:m], s3[0:m, 1:2], gain[0:m], op0=ALU.mult, op1=ALU.mult),
              reads=[ko2, ks3, gkey], writes=[okey])

    kcr = rot("kcn", [128, 128], F32, 2)
    m_smp_keep = sb_mark()
    gqb = ld("gqb", q_norm_g.partition_broadcast(128), [128, 128])
    mmap = ld("mmap", mmap_p, [128, 32])
    trim = ld("trim", tri_mask, [128, 128])
    qraw_r = rot("nq", [128, 1024], F32, 1)
    g24_r = rot("ng", [128, 24], F32, 2)
    cs2_r = rot("ncs", [128, 2, 64], F32, 2)
    s16_r = rot("ns16", [128, 16], F32, 2)
    qnT = sb("nqnT", [128, 8, 128], BF16)
    qrT = sb("nqrT", [128, 8, 128], BF16)
    pT_r = rot("npT", [128, 16, 128], BF16, 2)
    pc_r = rot("npc", [128, 128], F32, 2)
    pcT = sb("npcT", [128, 4, 128], F32)
    st_r = rot("nst", [128, 8], F32, 4)
    coef = sb("ncoef", [128, 2, 24], F32)
    scb = sb("nscb", [128, 4, 32], F32)
    m8 = sb("nm8", [128, 2, 8], F32)
    onsa = sb("nonsa", [128, 8, 128], F32)
    otmp = sb("notmp", [128, 4, 128], F32)

    m_prompt_only = sb_mark()
    KTa = sb("KTa", [128, 2, 2, 2048], BF16)
    VV = sb("VV", [128, 16, 2, 2, 128], BF16)
    m_ktc = sb_mark()
    KTc = sb("KTc", [128, 2, 2, 2048], BF16)
    m_ph3 = sb_mark()
    gkb = sb("gkb", [128, 2, 128], F32)
    mk.dma(Q1, gkb[:, 0, :], k_slc_norm_g.partition_broadcast(128), writes=["gkb0"])
    mk.dma(Q1, gkb[:, 1, :], k_swa_norm_g.partition_broadcast(128), writes=["gkb1"])
    kvr = rot("kv", [128, 1536], F32, 2)
    knr = rot("kn", [128, 2, 2, 128], F32, 2)
    csr = rot("cs", [128, 2, 64], F32, 2)
    j4 = sb("j4", [128, 2, 2, 128], F32)
    s4r = rot("s4", [128, 8], F32, 2)
    tmpr = rot("rt", [128, 4, 2, 2, 64], F32, 2)
    for t in range(NT):
        kv, kkv = kvr.next()
        kn, kkn = knr.next()
        cs, kcs = csr.next()
        s4, ks4 = s4r.next()
        tp, ktp = tmpr.next()
        mk.dma(Q1, kv, proj[t * 128:(t + 1) * 128, C_KV:C_NG],
               reads=[("proj", t, cb) for cb in range(C_KV // 512, (C_NG - 1) // 512 + 1)], writes=[kkv])
        mk.dma(Q1, cs[:, 0, :], rope_cos[t * 128:(t + 1) * 128, :], writes=[(kcs, 0)])
        mk.dma(Q1, cs[:, 1, :], rope_sin[t * 128:(t + 1) * 128, :], writes=[(kcs, 1)])
        kv4 = kv.rearrange("p (a g d) -> p a g d", a=6, g=2)
        kk = kv4[:, 2:6:2]
        mk.op("act", lambda e: e.activation(j4, kk, AF.Square), reads=[kkv], writes=["j4"])
        mk.op("dve", lambda e: e.reduce_sum(s4[:, 0:4].rearrange("p (a g) -> p a g", a=2), j4, axis=AX.X),
              reads=["j4"], writes=[ks4])
        mk.op("act", lambda e: e.activation(s4[:, 4:8], s4[:, 0:4], AF.Sqrt, bias=epsc[:, 0:1], scale=1.0 / 128),
              reads=[ks4, "epsc"], writes=[ks4])
        mk.op("dve", lambda e: e.reciprocal(s4[:, 4:8], s4[:, 4:8]), reads=[ks4], writes=[ks4])
        rb = s4[:, 4:8].rearrange("p (a g) -> p a g", a=2).unsqueeze(3).to_broadcast([128, 2, 2, 128])
        mk.op("dve", lambda e: e.tensor_tensor(kn, kk, rb, ALU.mult), reads=[kkv, ks4], writes=[kkn])
        gb_ = gkb.unsqueeze(2).to_broadcast([128, 2, 2, 128])
        mk.op("pool", lambda e: e.tensor_tensor(kn, kn, gb_, ALU.mult), reads=[kkn, "gkb0", "gkb1"], writes=[kkn])
        cosb = cs[:, 0, :].unsqueeze(1).unsqueeze(1).to_broadcast([128, 2, 2, 64])
        sinb = cs[:, 1, :].unsqueeze(1).unsqueeze(1).to_broadcast([128, 2, 2, 64])
        x1, x2 = kn[:, :, :, 0:64], kn[:, :, :, 64:128]
        rd = [kkn, (kcs, 0), (kcs, 1)]
        mk.op("dve", lambda e: e.tensor_tensor(tp[:, 0], x1, cosb, ALU.mult), reads=rd, writes=[(ktp, 0)])
        mk.op("pool", lambda e: e.tensor_tensor(tp[:, 1], x2, sinb, ALU.mult), reads=rd, writes=[(ktp, 1)])
        mk.op("dve", lambda e: e.tensor_tensor(tp[:, 2], x2, cosb, ALU.mult), reads=rd, writes=[(ktp, 2)])
        mk.op("pool", lambda e: e.tensor_tensor(tp[:, 3], x1, sinb, ALU.mult), reads=rd, writes=[(ktp, 3)])
        mk.op("dve", lambda e: e.tensor_tensor(kk[:, :, :, 0:64], tp[:, 0], tp[:, 1], ALU.subtract),
              reads=[(ktp, 0), (ktp, 1)], writes=[kkv])
        mk.op("dve", lambda e: e.tensor_tensor(kk[:, :, :, 64:128], tp[:, 2], tp[:, 3], ALU.add),
              reads=[(ktp, 2), (ktp, 3)], writes=[kkv])
        mk.dma(Q1, kv_out[t * 128:(t + 1) * 128, :], kv, reads=[kkv], writes=[("kvo", t)])
        bank, kb = psf.next()
        if t < 16:
            def tr1(e, bank=bank, kv4=kv4):
                for j in range(4):
                    i = e.transpose(bank[:, j * 128:(j + 1) * 128], kv4[:, j // 2, j % 2, :], ident)
                return i
            mk.op("pe", tr1, reads=[kkv, "ident"], writes=[kb])
            evac(KTc[:, :, :, t * 128:(t + 1) * 128], bank.rearrange("p (a g n) -> p a g n", a=2, g=2), kb, [("KT", 0, t)])
            bank, kb = psf.next()
        dst = KTa[:, :, :, t * 128:(t + 1) * 128] if t < 16 else KTn

        def tr2(e, bank=bank, kv4=kv4):
            for j in range(4):
                i = e.transpose(bank[:, j * 128:(j + 1) * 128], kv4[:, 2 + 2 * (j // 2), j % 2, :], ident)
            return i
        mk.op("pe", tr2, reads=[kkv, "ident"], writes=[kb])
        evac(dst, bank.rearrange("p (a g n) -> p a g n", a=2, g=2), kb, [("KT", 1, t)])
        vdst = VV[:, t] if t < 16 else VVn
        mk.op("pool", lambda e: e.tensor_copy(vdst, kv4[:, 3:6:2]), reads=[kkv], writes=[("VV", t)])
    for kv_i in range(2):
        for s in range(4):
            mk.dma(Q1, swa_out[kv_i, s, 0:508, :], state_swa_c[kv_i, s, 4:512, :], writes=[("swao", kv_i, s, 0)])
            mk.dma(Q1, swa_out[kv_i, s, 508:512, :], kv_out[2048 + 32 * s:2048 + 32 * s + 4, 1024 + 256 * kv_i:1280 + 256 * kv_i],
                   reads=[("kvo", 16)], writes=[("swao", kv_i, s, 1)])
    mk.barrier()
    sb_release(m_ph3)
    LDQ[0] = "sp"
    for g in range(2):
        def outk(mt, m, o2, ko2, g=g):
            kc_, kkc = kcr.next()
            rms_rows(o2, ko2, m, gkc, "gkc", kc_, kkc)
            bank, kb = psf.next()
            mk.op("pe", lambda e: e.transpose(bank[:, 0:m], kc_[0:m, :], ident[0:m, 0:m]), reads=[kkc, "ident"], writes=[kb])
            evac(kcT[:, g, 0:m], bank[:, 0:m], kb, [("kcT", g)])

        def outv(mt, m, o2, ko2, g=g):
            mk.op("pool", lambda e: e.tensor_copy(vc[0:m, g, :], o2[0:m]), reads=[ko2], writes=[("vc", g)])
        compress("k", KTc[:, 0, g, :], 127, [("KT", 0, t) for t in range(16)], outk)
        compress("v", KTc[:, 1, g, :], 127, [("KT", 0, t) for t in range(16)], outv)
    mk.barrier()
    sb_release(m_ktc)
    if stage <= 1:
        mk.barrier()
        return nc, mk


    psR = Rot(psf.aps[0:3], "psf")
    B_CMP, B_SLC, B_SWA = psf.aps[3], psf.aps[4], psf.aps[5]
    K_CMP, K_SLC, K_SWA = ("psf", 3), ("psf", 4), ("psf", 5)
    qn_ = sb("nqn", [128, 8, 128], F32)
    qr_ = sb("nqr", [128, 8, 128], F32)
    jq = sb("njq", [128, 8, 128], F32)
    rt4 = sb("nrt4", [128, 4, 8, 64], F32)
    tb_r = rot("ntb", [128, 2, 127], F32, 2)
    ts_r = rot("nts", [128, 2, 32], F32, 2)
    sw_r = rot("nsw", [128, 640], F32, 2)
    sS_r = rot("nsS", [128, 2048], F32, 3)
    pb_r = rot("npb", [128, 2048], BF16, 3)
    def softmax_rows(sS, ksS, n, p_out, kp, rs_col, krs):
        st, kst = st_r.next()
        mk.op("dve", lambda e: e.reduce_max(st[:, 0:1], sS[:, 0:n], axis=AX.X), reads=[ksS], writes=[kst])
        mk.op("dve", lambda e: e.tensor_scalar(st[:, 1:2], st[:, 0:1], -SC, None, op0=ALU.mult), reads=[kst], writes=[kst])
        mk.op("act", lambda e: e.activation(p_out[:, 0:n], sS[:, 0:n], AF.Exp, bias=st[:, 1:2], scale=SC),
              reads=[ksS, kst], writes=[kp])
        mk.op("dve", lambda e: e.reduce_sum(st[:, 2:3], p_out[:, 0:n], axis=AX.X), reads=[kp], writes=[kst])
        mk.op("dve", lambda e: e.reciprocal(rs_col, st[:, 2:3]), reads=[kst], writes=[krs])

    def pv_bf16(pb, kp, ntile, v_fn, vreads, o_ap, o_key):
        pT, kpT = pT_r.next()
        for k0 in range(0, ntile, 8):
            nk = min(8, ntile - k0)
            bank, kb = psb.next()

            def trp(e, bank=bank, k0=k0, nk=nk):
                for j in range(nk):
                    i = e.transpose(bank[:, j * 128:(j + 1) * 128], pb[:, (k0 + j) * 128:(k0 + j + 1) * 128], identb)
                return i
            mk.op("pe", trp, reads=[kp, "identb"], writes=[kb])
            evac(pT[:, k0:k0 + nk, :], bank[:, 0:nk * 128].rearrange("p (j n) -> p j n", j=nk), kb, [(kpT, k0)])

        def mmpv(e):
            for kt in range(ntile):
                i = e.matmul(o_ap, pT[:, kt, :], v_fn(kt), start=(kt == 0), stop=(kt == ntile - 1))
            return i
        mk.op("pe", mmpv, reads=[(kpT, k0) for k0 in range(0, ntile, 8)] + list(vreads), writes=[o_key])

    def norm_rope_q(qraw, kq, cs2, kcs):
        s16, ks16 = s16_r.next()
        q3 = qraw.rearrange("p (h d) -> p h d", h=8)
        mk.op("act", lambda e: e.activation(jq, q3, AF.Square), reads=[kq], writes=["jq"])
        mk.op("dve", lambda e: e.reduce_sum(s16[:, 0:8], jq, axis=AX.X), reads=["jq"], writes=[ks16])
        mk.op("act", lambda e: e.activation(s16[:, 8:16], s16[:, 0:8], AF.Sqrt, bias=epsc[:, 0:1], scale=1.0 / 128),
              reads=[ks16, "epsc"], writes=[ks16])
        mk.op("dve", lambda e: e.reciprocal(s16[:, 8:16], s16[:, 8:16]), reads=[ks16], writes=[ks16])
        mk.op("dve", lambda e: e.tensor_tensor(qn_, q3, s16[:, 8:16].unsqueeze(2).to_broadcast([128, 8, 128]), ALU.mult),
              reads=[kq, ks16], writes=["qn"])
        mk.op("pool", lambda e: e.tensor_tensor(qn_, qn_, gqb.unsqueeze(1).to_broadcast([128, 8, 128]), ALU.mult),
              reads=["qn", "gqb"], writes=["qn"])
        cosb = cs2[:, 0, :].unsqueeze(1).to_broadcast([128, 8, 64])
        sinb = cs2[:, 1, :].unsqueeze(1).to_broadcast([128, 8, 64])
        x1, x2 = qn_[:, :, 0:64], qn_[:, :, 64:128]
        rd = ["qn", (kcs, 0), (kcs, 1)]
        mk.op("dve", lambda e: e.tensor_tensor(rt4[:, 0], x1, cosb, ALU.mult), reads=rd, writes=[("rt4", 0)])
        mk.op("pool", lambda e: e.tensor_tensor(rt4[:, 1], x2, sinb, ALU.mult), reads=rd, writes=[("rt4", 1)])
        mk.op("dve", lambda e: e.tensor_tensor(rt4[:, 2], x2, cosb, ALU.mult), reads=rd, writes=[("rt4", 2)])
        mk.op("pool", lambda e: e.tensor_tensor(rt4[:, 3], x1, sinb, ALU.mult), reads=rd, writes=[("rt4", 3)])
        mk.op("dve", lambda e: e.tensor_tensor(qr_[:, :, 0:64], rt4[:, 0], rt4[:, 1], ALU.subtract),
              reads=[("rt4", 0), ("rt4", 1)], writes=["qr"])
        mk.op("dve", lambda e: e.tensor_tensor(qr_[:, :, 64:128], rt4[:, 2], rt4[:, 3], ALU.add),
              reads=[("rt4", 2), ("rt4", 3)], writes=["qr"])
        for c4 in range(2):
            transpose_into(qnT[:, c4 * 4:(c4 + 1) * 4, :], [qn_[:, c4 * 4 + j, :] for j in range(4)], ["qn"], ("qnT", c4))
            transpose_into(qrT[:, c4 * 4:(c4 + 1) * 4, :], [qr_[:, c4 * 4 + j, :] for j in range(4)], ["qr"], ("qrT", c4))

    def combine(g):
        cf = coef[:, 0, :]

        def cb(br):
            return cf[:, br * 8 + g * 4:br * 8 + g * 4 + 4].unsqueeze(2).to_broadcast([128, 4, 128])
        dst = onsa[:, g * 4:(g + 1) * 4, :]
        v3 = lambda b: b.rearrange("p (h n) -> p h n", h=4)
        mk.op("dve", lambda e: e.tensor_tensor(dst, v3(B_CMP), cb(0), ALU.mult), reads=["coef"], writes=[K_CMP, ("onsa", g)])
        mk.op("dve", lambda e: e.tensor_tensor(otmp, v3(B_SLC), cb(1), ALU.mult), reads=["coef"], writes=[K_SLC, "otmp"])
        mk.op("pool", lambda e: e.tensor_tensor(dst, dst, otmp, ALU.add), reads=["otmp", ("onsa", g)], writes=[("onsa", g)])
        mk.op("dve", lambda e: e.tensor_tensor(otmp, v3(B_SWA), cb(2), ALU.mult), reads=["coef"], writes=[K_SWA, "otmp"])
        mk.op("pool", lambda e: e.tensor_tensor(dst, dst, otmp, ALU.add), reads=["otmp", ("onsa", g)], writes=[("onsa", g)])

    for t in range(QT0, 16):
        qi = t - QT0
        rows = slice(t * 128, (t + 1) * 128)
        qrows = slice(qi * 128, (qi + 1) * 128)
        qraw, kq = qraw_r.next()
        g24, kg24 = g24_r.next()
        cs2, kcs = cs2_r.next()
        tb, ktb = tb_r.next()
        ts_, kts = ts_r.next()
        sw, ksw = sw_r.next()
        mk.dma("sp", qraw, proj[rows, C_NQ:C_KV], writes=[kq])
        mk.dma("sp", g24, proj[rows, C_NG:C_MG], writes=[kg24])
        mk.dma("sp", cs2[:, 0, :], rope_cos[rows, :], writes=[(kcs, 0)])
        mk.dma("sp", cs2[:, 1, :], rope_sin[rows, :], writes=[(kcs, 1)])
        mk.dma("sp", tb[:, 0, :], cmp_add[qrows, :], writes=[(ktb, 0)])
        mk.dma("sp", tb[:, 1, :], cmp_mul[qrows, :], writes=[(ktb, 1)])
        mk.dma("sp", ts_[:, 0, :], t_sel[qrows, :], writes=[(kts, 0)])
        mk.dma("sp", ts_[:, 1, :], t_inv[qrows, :], writes=[(kts, 1)])
        mk.dma("sp", sw, swa_mask[qrows, :], writes=[ksw])
        norm_rope_q(qraw, kq, cs2, kcs)
        mk.op("act", lambda e: e.activation(coef[:, 0, :], g24, AF.Sigmoid), reads=[kg24], writes=["coef"])
        mk.op("dve", lambda e: e.memset(coef[:, 1, :], 1.0), writes=["rs"])
        nkt = t + 1
        for g in range(2):
            for h4 in range(4):
                hh = g * 4 + h4
                bank, kb = psR.next()
                mk.op("pe", lambda e: e.matmul(bank[:, 0:127], qnT[:, hh, :], kcT[:, g, 0:127], start=True, stop=True),
                      reads=[("qnT", hh // 4), ("kcT", g)], writes=[kb])
                sS, ksS = sS_r.next()
                mk.op("dve", lambda e: e.tensor_tensor(sS[:, 0:127], bank[:, 0:127], tb[:, 0, :], ALU.add),
                      reads=[(ktb, 0)], writes=[kb, ksS])
                pc, kpc = pc_r.next()
                st2, kst2 = st_r.next()
                softmax_rows(sS, ksS, 127, pc, kpc, st2[:, 4:5], kst2)
                mk.op("dve", lambda e: e.scalar_tensor_tensor(pc[:, 0:127], pc[:, 0:127], st2[:, 4:5], tb[:, 1, :],
                                                              op0=ALU.mult, op1=ALU.mult),
                      reads=[kpc, kst2, (ktb, 1)], writes=[kpc])
                bank, kb = psR.next()
                mk.op("pe", lambda e: e.transpose(bank[0:127, 0:128], pc[:, 0:127], ident), reads=[kpc, "ident"], writes=[kb])
                evac(pcT[0:127, h4, :], bank[0:127, 0:128], kb, [("pcT", h4)])
                mk.op("pe", lambda e: e.matmul(B_CMP[:, h4 * 128:(h4 + 1) * 128], pcT[0:127, h4, :], vc[0:127, g, :],
                                               start=True, stop=True),
                      reads=[("pcT", h4), ("vc", g)], writes=[K_CMP])
            bank, kb = psR.next()

            def mmsc(e, bank=bank):
                for h4 in range(4):
                    i = e.matmul(bank[:, 0:32], pcT[0:127, h4, :], mmap[0:127, :], start=(h4 == 0), stop=(h4 == 3))
                return i
            mk.op("pe", mmsc, reads=[("pcT", h4) for h4 in range(4)] + ["mmap"], writes=[kb])
            mk.op("dve", lambda e: e.tensor_tensor(scb[:, 0, :], bank[:, 0:32], ts_[:, 0, :], ALU.add),
                  reads=[(kts, 0)], writes=[kb, "scb"])
            mk.op("dve", lambda e: e.max(out=m8[:, 0, :], in_=scb[:, 0, :]), reads=["scb"], writes=["m8"])
            mk.op("dve", lambda e: e.match_replace(out=scb[:, 1, :], in_to_replace=m8[:, 0, :], in_values=scb[:, 0, :],
                                                   imm_value=-1.0e9), reads=["scb", "m8"], writes=["scb"])
            mk.op("dve", lambda e: e.max(out=m8[:, 1, :], in_=scb[:, 1, :]), reads=["scb"], writes=["m8"])
            mk.op("dve", lambda e: e.tensor_scalar(scb[:, 2, :], scb[:, 0, :], m8[:, 1, 7:8], BIG, op0=ALU.is_ge, op1=ALU.mult),
                  reads=["scb", "m8"], writes=["scb"])
            mk.op("dve", lambda e: e.scalar_tensor_tensor(scb[:, 3, :], scb[:, 2, :], -BIG, ts_[:, 1, :], op0=ALU.add, op1=ALU.add),
                  reads=["scb", (kts, 1)], writes=["bb"])
            for h4 in range(4):
                hh = g * 4 + h4
                sS, ksS = sS_r.next()
                nk = nkt * 128
                for c0 in range(0, nk, 512):
                    w = min(512, nk - c0)
                    bank, kb = psR.next()
                    mk.op("pe", lambda e: e.matmul(bank[:, 0:w], qrT[:, hh, :], KTa[:, 0, g, c0:c0 + w], start=True, stop=True),
                          reads=[("qrT", hh // 4)] + [("KT", 1, kt) for kt in range(c0 // 128, (c0 + w) // 128)], writes=[kb])
                    nb = w // 64
                    mk.op("dve", lambda e: e.tensor_tensor(sS[:, c0:c0 + w].rearrange("p (b k) -> p b k", k=64),
                                                           bank[:, 0:w].rearrange("p (b k) -> p b k", k=64),
                                                           scb[:, 3, c0 // 64:c0 // 64 + nb].unsqueeze(2).to_broadcast([128, nb, 64]),
                                                           ALU.add), reads=["bb"], writes=[kb, ksS])
                mk.op("pool", lambda e: e.tensor_tensor(sS[:, t * 128:(t + 1) * 128], sS[:, t * 128:(t + 1) * 128], trim, ALU.add),
                      reads=[ksS, "trim"], writes=[ksS])
                pb, kpb = pb_r.next()
                softmax_rows(sS, ksS, nk, pb, kpb, coef[:, 1, 8 + hh:9 + hh], "rs")
                pv_bf16(pb, kpb, nkt, lambda kt: VV[:, kt, 0, g, :], [("VV", kt) for kt in range(nkt)],
                        B_SLC[:, h4 * 128:(h4 + 1) * 128], K_SLC)
                sS, ksS = sS_r.next()
                k0 = (t - 4) * 128
                for c0, w in ((0, 512), (512, 128)):
                    bank, kb = psR.next()
                    mk.op("pe", lambda e: e.matmul(bank[:, 0:w], qrT[:, hh, :], KTa[:, 1, g, k0 + c0:k0 + c0 + w], start=True, stop=True),
                          reads=[("qrT", hh // 4)] + [("KT", 1, kt) for kt in range(t - 4, t + 1)], writes=[kb])
                    mk.op("dve", lambda e: e.tensor_tensor(sS[:, c0:c0 + w], bank[:, 0:w], sw[:, c0:c0 + w], ALU.add),
                          reads=[ksw], writes=[kb, ksS])
                pb, kpb = pb_r.next()
                softmax_rows(sS, ksS, 640, pb, kpb, coef[:, 1, 16 + hh:17 + hh], "rs")
                pv_bf16(pb, kpb, 5, lambda kt: VV[:, t - 4 + kt, 1, g, :], [("VV", kt) for kt in range(t - 4, t + 1)],
                        B_SWA[:, h4 * 128:(h4 + 1) * 128], K_SWA)
            for br in range(3):
                cs_ = slice(br * 8 + g * 4, br * 8 + g * 4 + 4)
                mk.op("dve", lambda e: e.tensor_tensor(coef[:, 0, cs_], coef[:, 0, cs_], coef[:, 1, cs_], ALU.mult),
                      reads=["coef", "rs"], writes=["coef"])
            combine(g)
        for c4 in range(2):
            transpose_into(o_nsaT[:, c4 * 4:(c4 + 1) * 4, qi * 128:(qi + 1) * 128],
                           [onsa[:, c4 * 4 + j, :] for j in range(4)], [("onsa", 0), ("onsa", 1)], ("onsaT", qi))
    mk.barrier()
    while len(live) > m_prompt_only:
        live.pop().__exit__(None, None, None)
    if WITH_SAMPLE:
        m_smp = sb_mark()
        KSVS = sb("KSVS", [128, 16384], BF16)
        RT = KSVS.rearrange("p (g n) -> p g n", g=2)
        KS = KSVS[:, 0:8192]
        VS = KSVS[:, 8192:16384].rearrange("p (j d) -> p j d", d=128)
        pg_r = rot("pg", [128, 4, 256], F32, 2)
        sS_s = sb("sS_s", [128, 8320], F32)
        jq = sS_s[:, 0:1024].rearrange("p (h d) -> p h d", h=8)
        rt4 = sS_s[:, 1024:3072].rearrange("p (a h d) -> p a h d", a=4, h=8)
        qn_ = sS_s[:, 3072:4096].rearrange("p (h d) -> p h d", h=8)
        qr_ = sS_s[:, 4096:5120].rearrange("p (h d) -> p h d", h=8)
        pbs_r = rot("pbs", [128, 2048], BF16, 1)
        kcT_s = sb("kcT_s", [128, 2, 512], BF16)
        vc_s = sb("vc_s", [128, 4, 2, 128], F32)
        pcs = sb("pcs", [128, 512], F32)
        pcT_s = sb("pcT_s", [128, 4, 128], F32)
        imp4 = sb("imp4", [128, 4, 128], F32)
        mmap_sb = ld("mmap_sb", mmap_s.rearrange("(t p) j -> p t j", p=128), [128, 4, 129])
        tsel_sb = ld("tsel_sb", t_sel_s, [128, 129])
        newm = ld("newm", newmask.rearrange("s p k -> p s k"), [128, 4, 128])
        swpm = ld("swpm", swa_past_mask, [128, 512])
        scs = sb("scs", [128, 4, 129], F32)
        qs_n = sb("qs_n", [128, 128], BF16)
        qs_r = sb("qs_r", [128, 128], BF16)
        osb3 = sb("osb3", [128, 3, 128], F32)
        ocb = sb("ocb", [128, 4, 128], F32)
        KW = sb("KW", [128, 2, 512], BF16)
        VW = sb("VW", [128, 4, 256], BF16)
        page_regs = {}

        def load_pages(ci, s, consume):
            for j4 in range(16):
                pg, kpg = pg_r.next()
                mk.dma("sp", pg, gath[ci, s * 64 + 4 * j4:s * 64 + 4 * j4 + 4].rearrange("j p n -> p j n"), writes=[kpg])
                consume(j4, pg, kpg)

        def pv_big(sS, ksS, ntile, v_fn, vreads, o_ap, o_key, rs_col, krs):
            st, kst = st_r.next()
            n = ntile * 128
            mk.op("dve", lambda e: e.reduce_max(st[:, 0:1], sS[:, 0:n], axis=AX.X), reads=[ksS], writes=[kst])
            mk.op("dve", lambda e: e.tensor_scalar(st[:, 1:2], st[:, 0:1], -SC, None, op0=ALU.mult), reads=[kst], writes=[kst])
            mk.op("dve", lambda e: e.memset(st[:, 2:3], 0.0), reads=[], writes=[kst])
            for g0 in range(0, ntile, 16):
                ng = min(16, ntile - g0)
                pb, kpb = pbs_r.next()
                mk.op("act", lambda e: e.activation(pb[:, 0:ng * 128], sS[:, g0 * 128:(g0 + ng) * 128], AF.Exp, bias=st[:, 1:2], scale=SC),
                      reads=[ksS, kst], writes=[kpb])
                mk.op("dve", lambda e: e.reduce_sum(st[:, 3:4], pb[:, 0:ng * 128], axis=AX.X), reads=[kpb], writes=[kst])
                mk.op("dve", lambda e: e.tensor_tensor(st[:, 2:3], st[:, 2:3], st[:, 3:4], ALU.add), reads=[kst], writes=[kst])
                pT, kpT = pT_r.next()
                for k0 in range(0, ng, 8):
                    nk = min(8, ng - k0)
                    bank, kb = psb.next()

                    def trp(e, bank=bank, k0=k0, nk=nk, pb=pb):
                        for j in range(nk):
                            i = e.transpose(bank[:, j * 128:(j + 1) * 128], pb[:, (k0 + j) * 128:(k0 + j + 1) * 128], identb)
                        return i
                    mk.op("pe", trp, reads=[kpb, "identb"], writes=[kb])
                    evac(pT[:, k0:k0 + nk, :], bank[:, 0:nk * 128].rearrange("p (j n) -> p j n", j=nk), kb, [(kpT, k0)])

                def mmpv(e, g0=g0, ng=ng, pT=pT):
                    for kt in range(ng):
                        i = e.matmul(o_ap, pT[:, kt, :], v_fn(g0 + kt), start=(g0 + kt == 0), stop=(g0 + kt == ntile - 1))
                    return i
                mk.op("pe", mmpv, reads=[(kpT, k0) for k0 in range(0, ng, 8)] + list(vreads), writes=[o_key])
            mk.op("dve", lambda e: e.reciprocal(rs_col, st[:, 2:3]), reads=[kst], writes=[krs])

        t = 16
        rows = slice(t * 128, (t + 1) * 128)
        qraw, kq = qraw_r.next()
        g24, kg24 = g24_r.next()
        cs2, kcs = cs2_r.next()
        mk.dma("sp", qraw, proj[rows, C_NQ:C_KV], writes=[kq])
        mk.dma("sp", g24, proj[rows, C_NG:C_MG], writes=[kg24])
        mk.dma("sp", cs2[:, 0, :], rope_cos[rows, :], writes=[(kcs, 0)])
        mk.dma("sp", cs2[:, 1, :], rope_sin[rows, :], writes=[(kcs, 1)])
        norm_rope_q(qraw, kq, cs2, kcs)
        mk.op("act", lambda e: e.activation(coef[:, 0, :], g24, AF.Sigmoid), reads=[kg24], writes=["coef"])
        mk.barrier()
        for s in range(4):
            for (nm, cache) in (("k", 0), ("v", 1)):
                def cons(j4, pg, kpg):
                    for g_ in range(2):
                        bank, kb = psf.next()

                        def trA(e, bank=bank, pg=pg, g_=g_):
                            for p_ in range(4):
                                i = e.transpose(bank[:, p_ * 128:(p_ + 1) * 128], pg[:, p_, g_ * 128:(g_ + 1) * 128], ident)
                            return i
                        mk.op("pe", trA, reads=[kpg, "ident"], writes=[kb])
                        evac(RT[:, g_, j4 * 512:(j4 + 1) * 512], bank, kb, [(("KS", "VS")[g_], j4)])
                load_pages(cache, s, cons)
                for g in range(2):
                    if nm == "k":
                        def outk(mt, m, o2, ko2, g=g):
                            kc_, kkc = kcr.next()
                            rms_rows(o2, ko2, m, gkc, "gkc", kc_, kkc)
                            bank, kb = psf.next()
                            mk.op("pe", lambda e: e.transpose(bank[:, 0:m], kc_[0:m, :], ident[0:m, 0:m]), reads=[kkc, "ident"], writes=[kb])
                            evac(kcT_s[:, g, mt * 128:mt * 128 + m], bank[:, 0:m], kb, [("kcT_s", g, mt)])
                        compress("k", RT[:, g, :], 511, [(("KS", "VS")[g], j) for j in range(16)], outk)
                    else:
                        def outv(mt, m, o2, ko2, g=g):
                            mk.op("pool", lambda e: e.tensor_copy(vc_s[0:m, mt, g, :], o2[0:m]), reads=[ko2], writes=[("vc_s", g, mt)])
                        compress("v", RT[:, g, :], 511, [(("KS", "VS")[g], j) for j in range(16)], outv)
            for kv_i in range(2):
                pg, kpg = pg_r.next()
                mk.dma("sp", pg, state_swa_c[kv_i, s].rearrange("(j p) n -> p j n", p=128), writes=[kpg])
                if kv_i == 0:
                    for g_ in range(2):
                        bank, kb = psf.next()

                        def trW(e, bank=bank, pg=pg, g_=g_):
                            for p_ in range(4):
                                i = e.transpose(bank[:, p_ * 128:(p_ + 1) * 128], pg[:, p_, g_ * 128:(g_ + 1) * 128], ident)
                            return i
                        mk.op("pe", trW, reads=[kpg, "ident"], writes=[kb])
                        evac(KW[:, g_, :], bank, kb, [("KW", g_)])
                else:
                    mk.op("pool", lambda e: e.tensor_copy(VW, pg), reads=[kpg], writes=["VW"])
            for g in range(2):
                for h4 in range(4):
                    mk.op("pool", lambda e: e.tensor_copy(qs_n[:, h4 * 32:(h4 + 1) * 32], qnT[:, g * 4 + h4, 32 * s:32 * s + 32]),
                          reads=[("qnT", g)], writes=["qs_n"])
                    mk.op("pool", lambda e: e.tensor_copy(qs_r[:, h4 * 32:(h4 + 1) * 32], qrT[:, g * 4 + h4, 32 * s:32 * s + 32]),
                          reads=[("qrT", g)], writes=["qs_r"])
                bank, kb = psf.next()
                mk.op("pe", lambda e: e.matmul(bank[:, 0:511], qs_n, kcT_s[:, g, 0:511], start=True, stop=True),
                      reads=["qs_n"] + [("kcT_s", g, mt) for mt in range(4)], writes=[kb])
                mk.op("act", lambda e: e.copy(sS_s[:, 0:511], bank[:, 0:511]), writes=[kb, "sS_s"])
                st2, kst2 = st_r.next()
                softmax_rows(sS_s, "sS_s", 511, pcs, "pcs", st2[:, 4:5], kst2)
                mk.op("dve", lambda e: e.tensor_scalar(pcs[:, 0:511], pcs[:, 0:511], st2[:, 4:5], None, op0=ALU.mult),
                      reads=["pcs", kst2], writes=["pcs"])
                for mt in range(4):
                    m = 128 if mt < 3 else 127
                    bank, kb = psf.next()
                    mk.op("pe", lambda e: e.transpose(bank[0:m, 0:128], pcs[:, mt * 128:mt * 128 + m], ident), reads=["pcs", "ident"], writes=[kb])
                    evac(pcT_s[0:m, mt, :], bank[0:m, 0:128], kb, [("pcT_s", mt)])
                bank, kb = psf.next()

                def mmoc(e, bank=bank, g=g):
                    for mt in range(4):
                        m = 128 if mt < 3 else 127
                        i = e.matmul(bank[:, 0:128], pcT_s[0:m, mt, :], vc_s[0:m, mt, g, :], start=(mt == 0), stop=(mt == 3))
                    return i
                mk.op("pe", mmoc, reads=[("pcT_s", mt) for mt in range(4)] + [("vc_s", g, mt) for mt in range(4)], writes=[kb])
                mk.op("act", lambda e: e.copy(osb3[:, 0, :], bank[:, 0:128]), writes=[kb, ("osb3", 0)])
                p4 = pcT_s.rearrange("p t (h s) -> p t h s", h=4)
                rdp = [("pcT_s", mt) for mt in range(4)]
                mk.op("dve", lambda e: e.tensor_tensor(imp4[:, :, 0:32], p4[:, :, 0, :], p4[:, :, 1, :], ALU.add), reads=rdp, writes=["imp4"])
                mk.op("dve", lambda e: e.tensor_tensor(imp4[:, :, 0:32], imp4[:, :, 0:32], p4[:, :, 2, :], ALU.add), reads=rdp + ["imp4"], writes=["imp4"])
                mk.op("dve", lambda e: e.tensor_tensor(imp4[:, :, 0:32], imp4[:, :, 0:32], p4[:, :, 3, :], ALU.add), reads=rdp + ["imp4"], writes=["imp4"])
                for h4 in range(1, 4):
                    mk.op("pool", lambda e: e.tensor_copy(imp4[:, :, h4 * 32:(h4 + 1) * 32], imp4[:, :, 0:32]), reads=["imp4"], writes=["imp4"])
                bank, kb = psf.next()

                def mmsc2(e, bank=bank):
                    for mt in range(4):
                        m = 128 if mt < 3 else 127
                        i = e.matmul(bank[:, 0:129], imp4[0:m, mt, :], mmap_sb[0:m, mt, :], start=(mt == 0), stop=(mt == 3))
                    return i
                mk.op("pe", mmsc2, reads=["imp4", "mmap_sb"], writes=[kb])
                mk.op("dve", lambda e: e.tensor_tensor(scs[:, 0, :], bank[:, 0:129], tsel_sb, ALU.add), reads=["tsel_sb"], writes=[kb, "scs"])
                mk.op("dve", lambda e: e.max(out=m8[:, 0, :], in_=scs[:, 0, :]), reads=["scs"], writes=["m8"])
                mk.op("dve", lambda e: e.match_replace(out=scs[:, 1, :], in_to_replace=m8[:, 0, :], in_values=scs[:, 0, :], imm_value=-1.0e9),
                      reads=["scs", "m8"], writes=["scs"])
                mk.op("dve", lambda e: e.max(out=m8[:, 1, :], in_=scs[:, 1, :]), reads=["scs"], writes=["m8"])
                mk.op("dve", lambda e: e.tensor_scalar(scs[:, 2, :], scs[:, 0, :], m8[:, 1, 7:8], BIG, op0=ALU.is_ge, op1=ALU.mult),
                      reads=["scs", "m8"], writes=["scs"])
                mk.op("dve", lambda e: e.tensor_scalar(scs[:, 3, :], scs[:, 2, :], -BIG, None, op0=ALU.add), reads=["scs"], writes=["bbs"])
                def consk(j4, pg, kpg, g=g):
                    bank, kb = psf.next()

                    def trK(e, bank=bank, pg=pg):
                        for p_ in range(4):
                            i = e.transpose(bank[:, p_ * 128:(p_ + 1) * 128], pg[:, p_, g * 128:(g + 1) * 128], ident)
                        return i
                    mk.op("pe", trK, reads=[kpg, "ident"], writes=[kb])
                    evac(KS[:, j4 * 512:(j4 + 1) * 512], bank, kb, [("KS", j4)])
                load_pages(2, s, consk)

                def consv(j4, pg, kpg, g=g):
                    mk.op("pool", lambda e: e.tensor_copy(VS[:, 4 * j4:4 * j4 + 4, :], pg[:, :, g * 128:(g + 1) * 128]), reads=[kpg], writes=[("VS", j4)])
                load_pages(3, s, consv)
                for c0 in range(0, 8192, 512):
                    bank, kb = psf.next()
                    mk.op("pe", lambda e: e.matmul(bank, qs_r, KS[:, c0:c0 + 512], start=True, stop=True),
                          reads=["qs_r", ("KS", c0 // 512)], writes=[kb])
                    mk.op("dve", lambda e: e.tensor_tensor(sS_s[:, c0:c0 + 512].rearrange("p (b k) -> p b k", k=64),
                                                           bank.rearrange("p (b k) -> p b k", k=64),
                                                           scs[:, 3, c0 // 64:c0 // 64 + 8].unsqueeze(2).to_broadcast([128, 8, 64]), ALU.add),
                          reads=["bbs", "pcs"], writes=[kb, "sS_s"])
                bank, kb = psf.next()
                mk.op("pe", lambda e: e.matmul(bank[:, 0:128], qs_r, KTn[:, 0, g, :], start=True, stop=True), reads=["qs_r", ("KT", 1, 16)], writes=[kb])
                mk.op("dve", lambda e: e.scalar_tensor_tensor(sS_s[:, 8192:8320], bank[:, 0:128], scs[:, 3, 128:129], newm[:, s, :], op0=ALU.add, op1=ALU.add),
                      reads=["bbs", "newm"], writes=[kb, "sS_s"])
                bank, kb = psf.next()
                st3, kst3 = st_r.next()
                pv_big(sS_s, "sS_s", 65, lambda kt: (VS[:, kt, :] if kt < 64 else VVn[:, 0, g, :]), [("VS", j) for j in range(16)] + [("VV", 16)],
                       bank[:, 0:128], kb, st3[:, 4:5], kst3)
                mk.op("dve", lambda e: e.tensor_scalar(osb3[:, 1, :], bank[:, 0:128], st3[:, 4:5], None, op0=ALU.mult), reads=[kst3], writes=[kb, ("osb3", 1)])
                bank, kb = psf.next()
                mk.op("pe", lambda e: e.matmul(bank, qs_r, KW[:, g, :], start=True, stop=True), reads=["qs_r", ("KW", g)], writes=[kb])
                mk.op("dve", lambda e: e.tensor_tensor(sS_s[:, 0:512], bank, swpm, ALU.add), reads=["swpm"], writes=[kb, "sS_s"])
                bank, kb = psf.next()
                mk.op("pe", lambda e: e.matmul(bank[:, 0:128], qs_r, KTn[:, 1, g, :], start=True, stop=True), reads=["qs_r", ("KT", 1, 16)], writes=[kb])
                mk.op("dve", lambda e: e.tensor_tensor(sS_s[:, 512:640], bank[:, 0:128], newm[:, s, :], ALU.add), reads=["newm"], writes=[kb, "sS_s"])
                bank, kb = psf.next()
                st4, kst4 = st_r.next()
                pv_big(sS_s, "sS_s", 5, lambda kt: (VW[:, kt, g * 128:(g + 1) * 128] if kt < 4 else VVn[:, 1, g, :]), ["VW", ("VV", 16)],
                       bank[:, 0:128], kb, st4[:, 4:5], kst4)
                mk.op("dve", lambda e: e.tensor_scalar(osb3[:, 2, :], bank[:, 0:128], st4[:, 4:5], None, op0=ALU.mult), reads=[kst4], writes=[kb, ("osb3", 2)])
                mk.dma("sp", o_scr[s, g].rearrange("b r d -> r b d"), osb3, reads=[("osb3", b_) for b_ in range(3)], writes=[("oscr", s, g)])
        for g in range(2):
            cf = coef[:, 0, :]
            dst = onsa[:, g * 4:(g + 1) * 4, :]
            for b_ in range(3):
                for s in range(4):
                    mk.dma("sp", ocb[32 * s:32 * s + 32, :, :], o_scr[s, g, b_].rearrange("(h r) d -> r h d", h=4),
                           reads=[("oscr", s, g)], writes=["ocb"])
                cbk = cf[:, b_ * 8 + g * 4:b_ * 8 + g * 4 + 4].unsqueeze(2).to_broadcast([128, 4, 128])
                if b_ == 0:
                    mk.op("dve", lambda e: e.tensor_tensor(dst, ocb, cbk, ALU.mult), reads=["coef", "ocb"], writes=[("onsa", g)])
                else:
                    mk.op("dve", lambda e: e.tensor_tensor(otmp, ocb, cbk, ALU.mult), reads=["coef", "ocb"], writes=["otmp"])
                    mk.op("pool", lambda e: e.tensor_tensor(dst, dst, otmp, ALU.add), reads=["otmp", ("onsa", g)], writes=[("onsa", g)])
        for c4 in range(2):
            transpose_into(o_nsaT[:, c4 * 4:(c4 + 1) * 4, 9 * 128:10 * 128],
                           [onsa[:, c4 * 4 + j, :] for j in range(4)], [("onsa", 0), ("onsa", 1)], ("onsaT", 9))
    else:
        mk.op("pool", lambda e: e.memset(o_nsaT[:, :, 9 * 128:10 * 128], 0.0), writes=[("onsaT", 9)])
    mk.barrier()
    while len(live) > m_smp_keep:
        live.pop().__exit__(None, None, None)

    NQ = 10
    mT = sb("mT", [128, 16, NQ * 128], BF16)
    m_mrg = sb_mark()
    wg_r = rot("wg", [128, 8, 512], BF16, 2)
    wn_r = rot("wn", [128, 8, 512], BF16, 2)
    mg_r = rot("mg", [128, 2, 512], F32, 2)
    mm_r = rot("mm", [128, 2, 512], F32, 2)
    for cb in range(4):
        cs_ = slice(cb * 512, (cb + 1) * 512)
        wg, kwg = wg_r.next()
        wn, kwn = wn_r.next()
        for c in range(8):
            mk.dma("pool", wg[:, c, :], w_br_gla[c * 128:(c + 1) * 128, cs_], writes=[(kwg, c)])
            mk.dma("pool", wn[:, c, :], w_br_nsa[c * 128:(c + 1) * 128, cs_], writes=[(kwn, c)])
        for qi in range(NQ):
            t = QT0 + qi
            rows = slice(t * 128, (t + 1) * 128)
            mg, kmg = mg_r.next()
            mm_, kmm = mm_r.next()
            mk.dma("sp", mg[:, 0, :], proj[rows, C_MG + cb * 512:C_MG + (cb + 1) * 512], writes=[(kmg, 0)])
            mk.dma("sp", mg[:, 1, :], proj[rows, C_MG + 2048 + cb * 512:C_MG + 2048 + (cb + 1) * 512], writes=[(kmg, 1)])
            mk.op("act", lambda e: e.activation(mg, mg, AF.Sigmoid), reads=[(kmg, 0), (kmg, 1)], writes=[(kmg, 0), (kmg, 1)])
            bankA, kA = psf.next()
            bankB, kB = psf.next()

            def mmA(e, bank=bankA, w=wg, src=o_glaT, qi=qi):
                for c in range(8):
                    i = e.matmul(bank, src[:, c, qi * 128:(qi + 1) * 128], w[:, c, :], start=(c == 0), stop=(c == 7))
                return i

            def mmB(e, bank=bankB, w=wn, src=o_nsaT, qi=qi):
                for c in range(8):
                    i = e.matmul(bank, src[:, c, qi * 128:(qi + 1) * 128], w[:, c, :], start=(c == 0), stop=(c == 7))
                return i
            mk.op("pe", mmA, reads=[(kwg, c) for c in range(8)] + [("oglaT", qi)], writes=[kA])
            mk.op("pe", mmB, reads=[(kwn, c) for c in range(8)] + [("onsaT", qi)], writes=[kB])
            mk.op("dve", lambda e: e.tensor_tensor(mm_[:, 0, :], bankA, mg[:, 0, :], ALU.mult), reads=[(kmg, 0)], writes=[kA, (kmm, 0)])
            mk.op("dve", lambda e: e.tensor_tensor(mm_[:, 1, :], bankB, mg[:, 1, :], ALU.mult), reads=[(kmg, 1)], writes=[kB, (kmm, 1)])
            mk.op("pool", lambda e: e.tensor_tensor(mm_[:, 0, :], mm_[:, 0, :], mm_[:, 1, :], ALU.add),
                  reads=[(kmm, 0), (kmm, 1)], writes=[(kmm, 0)])
            transpose_into(mT[:, cb * 4:(cb + 1) * 4, qi * 128:(qi + 1) * 128],
                           [mm_[:, 0, j * 128:(j + 1) * 128] for j in range(4)], [(kmm, 0)], ("mT", qi, cb))
    mk.barrier()
    sb_release(m_mrg)

    m_wo = sb_mark()
    wo_r = rot("wo", [128, 16, 512], BF16, 2)
    xs_r = rot("xs", [128, 512], F32, 3)
    for cb in range(4):
        cs_ = slice(cb * 512, (cb + 1) * 512)
        wo, kwo = wo_r.next()
        for c in range(16):
            mk.dma("pool", wo[:, c, :], w_o[c * 128:(c + 1) * 128, cs_], writes=[(kwo, c)])
        for qi in range(NQ):
            t = QT0 + qi
            xs, kxs = xs_r.next()
            mk.dma("sp", xs, xbuf[t * 128:(t + 1) * 128, cs_], writes=[kxs])
            bank, kb = psf.next()

            def mmh(e, bank=bank, wo=wo, qi=qi):
                for c in range(16):
                    i = e.matmul(bank, mT[:, c, qi * 128:(qi + 1) * 128], wo[:, c, :], start=(c == 0), stop=(c == 15))
                return i
            mk.op("pe", mmh, reads=[(kwo, c) for c in range(16)], writes=[kb])
            mk.op("dve", lambda e: e.tensor_tensor(xs, bank, xs, ALU.add), reads=[kxs], writes=[kb, kxs])
            mk.dma("sp", h_scr[qi * 128:(qi + 1) * 128, cs_], xs, reads=[kxs], writes=[("h", qi, cb)])
    mk.barrier()
    while len(live) > m_keep:
        live.pop().__exit__(None, None, None)

    NF = 1042
    hnF = sb("hnF", [128, 16, NF], BF16)
    gT = sb("gT", [128, NFB, NF], BF16)
    m_n2 = sb_mark()
    g2b = ld("g2b", norm2_g.partition_broadcast(128), [128, D])
    hr = rot("ht", [128, D], F32, 2)
    hnr = rot("hn", [128, D], F32, 2)
    junk2 = sb("junk2", [128, D], F32)
    ss2r = rot("ss2", [128, 2], F32, 2)
    for qi in range(NQ):
        ht, kh = hr.next()
        hn, khn = hnr.next()
        ss, ks = ss2r.next()
        mk.dma("sp", ht, h_scr[qi * 128:(qi + 1) * 128, :], writes=[kh])
        mk.op("act", lambda e: e.activation(junk2, ht, AF.Square), reads=[kh], writes=["junk2"])
        mk.op("dve", lambda e: e.reduce_sum(ss[:, 0:1], junk2, axis=AX.X), reads=["junk2"], writes=[ks])
        mk.op("act", lambda e: e.activation(ss[:, 1:2], ss[:, 0:1], AF.Sqrt, bias=epsc[:, 0:1], scale=1.0 / D),
              reads=[ks, "epsc"], writes=[ks])
        mk.op("dve", lambda e: e.reciprocal(ss[:, 1:2], ss[:, 1:2]), reads=[ks], writes=[ks])
        mk.op("dve", lambda e: e.scalar_tensor_tensor(hn, ht, ss[:, 1:2], g2b, op0=ALU.mult, op1=ALU.mult),
              reads=[kh, ks, "g2b"], writes=[khn])
        for c4 in range(4):
            bank, kb = psf.next()

            def tr(e, c4=c4, bank=bank, hn=hn):
                for j in range(4):
                    c = c4 * 4 + j
                    i = e.transpose(bank[:, j * 128:(j + 1) * 128], hn[:, c * 128:(c + 1) * 128], ident)
                return i
            mk.op("pe", tr, reads=[khn, "ident"], writes=[kb])
            b3 = bank.rearrange("p (j n) -> p j n", j=4)
            cc = slice(c4 * 4, (c4 + 1) * 4)
            if qi == 0:
                evac(hnF[:, cc, 0:2], b3[:, :, 126:128], kb, [("hnF", qi, c4)])
            elif qi < 9:
                evac(hnF[:, cc, 2 + (qi - 1) * 128:2 + qi * 128], b3, kb, [("hnF", qi, c4)])
            else:
                for s in range(4):
                    evac(hnF[:, cc, 1026 + 4 * s:1030 + 4 * s], b3[:, :, 32 * s:32 * s + 4], kb, [("hnF", qi, c4, s)])
    mk.barrier()
    sb_release(m_n2)

    m_up = sb_mark()
    cwt = sb("cwt", [128, NFB, 4], F32)
    for j in range(3):
        mk.dma("sp", cwt[:, :, j], conv_w[j].rearrange("(fb p) -> p fb", p=128), writes=[("cwt", j)], allow_slow_non_contiguous=True)
    mk.dma("sp", cwt[:, :, 3], conv_b.rearrange("(fb p) -> p fb", p=128), writes=[("cwt", 3)], allow_slow_non_contiguous=True)
    cstT = sb("cstT", [128, NFB, 8], F32)
    for s_ in range(4):
        for j in range(2):
            mk.dma("sp", cstT[:, :, s_ * 2 + j], state_conv_c[s_, j].rearrange("(fb p) -> p fb", p=128),
                   writes=[("cstT", s_ * 2 + j)], allow_slow_non_contiguous=True)
    crow_r = rot("crow", [10, 256], F32, 2)
    wa_r = rot("wa", [128, 16, 256], BF16, 2)
    wb_r = rot("wbg", [128, 16, 256], BF16, 2)
    aT = sb("aT", [128, 2 + NF], F32)
    bT = sb("bT", [128, NF], F32)
    uu = sb("uu", [128, NF], F32)
    as6 = sb("as6", [128, 4, 6], F32)
    us4 = sb("us4", [128, 4, 4], F32)
    cc10 = sb("cc10", [128, 10], F32)
    mk.op("dve", lambda e: e.memset(aT[:, 0:2], 0.0), writes=["aTpad"])
    SEGS = ((0, 512), (512, 512), (1024, NF - 1024))
    for f2 in range(NFB // 2):
        wa, kwa = wa_r.next()
        wb_, kwb = wb_r.next()
        for c in range(16):
            mk.dma("pool", wa[:, c, :], w_up[c * 128:(c + 1) * 128, f2 * 256:(f2 + 1) * 256], writes=[(kwa, c)])
            mk.dma("pool", wb_[:, c, :], w_up[c * 128:(c + 1) * 128, DFF + f2 * 256:DFF + (f2 + 1) * 256], writes=[(kwb, c)])
        for sub in range(2):
            fb = f2 * 2 + sub
            for (c0, w) in SEGS:
                for (wt, kwt, dst, kd) in ((wa, kwa, aT[:, 2 + c0:2 + c0 + w], "aT"), (wb_, kwb, bT[:, c0:c0 + w], "bT")):
                    bank, kb = psf.next()

                    def mmu(e, bank=bank, wt=wt, c0=c0, w=w, sub=sub):
                        for c in range(16):
                            i = e.matmul(bank[:, 0:w], wt[:, c, sub * 128:(sub + 1) * 128], hnF[:, c, c0:c0 + w],
                                         start=(c == 0), stop=(c == 15))
                        return i
                    mk.op("pe", mmu, reads=[(kwt, c) for c in range(16)], writes=[kb])
                    evac(dst, bank[:, 0:w], kb, [(kd, c0)])
            ra = [("aT", c0) for (c0, w) in SEGS] + ["aTpad"] + [("cwt", j) for j in range(4)]
            w0, w1, w2, bcv = (cwt[:, fb, j:j + 1] for j in range(4))
            mk.op("dve", lambda e: e.tensor_scalar(uu, aT[:, 2:2 + NF], w2, bcv, op0=ALU.mult, op1=ALU.add), reads=ra, writes=["uu"])
            mk.op("dve", lambda e: e.scalar_tensor_tensor(uu, aT[:, 1:1 + NF], w1, uu, op0=ALU.mult, op1=ALU.add), reads=ra + ["uu"], writes=["uu"])
            mk.op("dve", lambda e: e.scalar_tensor_tensor(uu, aT[:, 0:NF], w0, uu, op0=ALU.mult, op1=ALU.add), reads=ra + ["uu"], writes=["uu"])
            a_s = aT[:, 2 + 1026:2 + 1042].rearrange("p (s t) -> p s t", s=4)
            mk.op("pool", lambda e: e.tensor_copy(as6[:, :, 0:2], cstT[:, fb, :].rearrange("p (s j) -> p s j", s=4)),
                  reads=[("cstT", i8) for i8 in range(8)], writes=["as6a"])
            mk.op("pool", lambda e: e.tensor_copy(as6[:, :, 2:6], a_s), reads=ra, writes=["as6b"])
            rs6 = ["as6a", "as6b"] + [("cwt", j) for j in range(4)]
            mk.op("dve", lambda e: e.tensor_scalar(us4, as6[:, :, 2:6], w2, bcv, op0=ALU.mult, op1=ALU.add), reads=rs6, writes=["us4"])
            mk.op("dve", lambda e: e.scalar_tensor_tensor(us4, as6[:, :, 1:5], w1, us4, op0=ALU.mult, op1=ALU.add), reads=rs6 + ["us4"], writes=["us4"])
            mk.op("dve", lambda e: e.scalar_tensor_tensor(us4, as6[:, :, 0:4], w0, us4, op0=ALU.mult, op1=ALU.add), reads=rs6 + ["us4"], writes=["us4"])
            mk.op("dve", lambda e: e.tensor_copy(uu[:, 1026:1042].rearrange("p (s t) -> p s t", s=4), us4), reads=["us4", "uu"], writes=["uu"])
            mk.op("act", lambda e: e.activation(uu, uu, AF.Gelu_apprx_tanh), reads=["uu"], writes=["uu"])
            mk.op("dve", lambda e: e.tensor_tensor(gT[:, fb, :], uu, bT, ALU.mult), reads=["uu"] + [("bT", c0) for (c0, w) in SEGS],
                  writes=[("gT", fb)])
            mk.op("pool", lambda e: e.tensor_copy(cc10[:, 0:2], aT[:, 2 + 1024:2 + 1026]), reads=ra, writes=["cc10a"])
            mk.op("pool", lambda e: e.tensor_copy(cc10[:, 2:10].rearrange("p (s t) -> p s t", s=4), a_s[:, :, 2:4]), reads=ra, writes=["cc10b"])
            bank, kb = psf.next()
            mk.op("pe", lambda e: e.transpose(bank[0:10, 0:128], cc10, ident), reads=["cc10a", "cc10b", "ident"], writes=[kb])
            if sub == 0:
                crow, kcrow = crow_r.next()
            evac(crow[:, sub * 128:(sub + 1) * 128], bank[0:10, 0:128], kb, [(kcrow, sub)])
        mk.dma("sp", conv_out[:, f2 * 256:(f2 + 1) * 256], crow, reads=[(kcrow, 0), (kcrow, 1)], writes=[("convo", f2)])
    mk.barrier()
    sb_release(m_up)

    wd_r = rot("wd", [128, NFB, 256], BF16, 2)
    hs_r = rot("hs", [128, 256], F32, 3)
    for cb in range(8):
        cs_ = slice(cb * 256, (cb + 1) * 256)
        wd, kwd = wd_r.next()
        for fb in range(NFB):
            mk.dma("pool", wd[:, fb, :], w_down[fb * 128:(fb + 1) * 128, cs_], writes=[(kwd, fb)])
        for i in range(9):
            M = 128 if i < 8 else 16
            col0 = 2 + i * 128
            hs, khs = hs_r.next()
            if i < 8:
                mk.dma("sp", hs, h_scr[(1 + i) * 128:(2 + i) * 128, cs_], writes=[khs])
            else:
                for s in range(4):
                    mk.dma("sp", hs[4 * s:4 * s + 4, :], h_scr[9 * 128 + 32 * s:9 * 128 + 32 * s + 4, cs_], writes=[(khs, s)])
            bank, kb = psf.next()

            def mmy(e, bank=bank, wd=wd, col0=col0, M=M):
                for fb in range(NFB):
                    i_ = e.matmul(bank[0:M, 0:256], gT[:, fb, col0:col0 + M], wd[:, fb, :], start=(fb == 0), stop=(fb == NFB - 1))
                return i_
            mk.op("pe", mmy, reads=[(kwd, fb) for fb in range(NFB)], writes=[kb])
            rdh = [khs] if i < 8 else [(khs, s) for s in range(4)]
            mk.op("dve", lambda e: e.tensor_tensor(hs[0:M], bank[0:M, 0:256], hs[0:M], ALU.add), reads=rdh, writes=[kb, khs])
            mk.dma("sp", y_out[i * 128:i * 128 + M, cs_], hs[0:M], reads=[khs], writes=[("y", i, cb)])
    mk.barrier()
    while live:
        live.pop().__exit__(None, None, None)
    return nc, mk


def _rope_tables(pos):
    half = 64
    inv = (10000.0 ** (-np.arange(half, dtype=np.float32) / half)).astype(np.float32)
    ang = pos.astype(np.float32)[:, None] * inv[None, :]
    return np.cos(ang).astype(np.float32), np.sin(ang).astype(np.float32)


def _sample_tables():
    f = np.float32
    n = np.arange(512)[:, None]
    j = np.arange(129)[None, :]
    mm = ((16 * n < 64 * (j + 1)) & (16 * n + 32 > 64 * j) & (n < 511)).astype(f)
    ts = np.zeros((128, 129), f)
    ts[:, [0, 127, 128]] = 1.0e4
    slot = (np.arange(128) % 32)
    newm = np.full((4, 128, 128), -BIG, f)
    for s in range(4):
        for kk in range(4):
            newm[s, (slot < 4) & (kk <= slot), 32 * s + kk] = 0.0
    swp = np.where(np.arange(512)[None, :] > slot[:, None], 0.0, -BIG).astype(f)
    return dict(mmap_s=mm, t_sel_s=ts, newmask=newm, swa_past_mask=swp)


def _const_tables(half):
    f = np.float32
    pos = lambda r: r - 1024 + 1024 * half
    n = np.arange(127)
    j = np.arange(32)
    cmp_add = np.zeros((9 * 128, 127), f)
    cmp_mul = np.zeros((9 * 128, 127), f)
    t_sel = np.zeros((9 * 128, 32), f)
    t_inv = np.zeros((9 * 128, 32), f)
    swa = np.zeros((9 * 128, 640), f)
    for qi in range(9):
        t = QT0 + qi
        qpos = pos(t * 128 + np.arange(128))[:, None]
        valid = (pos(16 * n + 31)[None, :] <= qpos) & (pos(16 * n)[None, :] >= 0)
        cmp_add[qi * 128:(qi + 1) * 128] = np.where(valid, 0.0, -BIG)
        cmp_mul[qi * 128:(qi + 1) * 128] = valid
        sp = pos(64 * j)[None, :]
        bvalid = (sp >= 0) & (sp <= qpos)
        jr = sp // 64
        cur = qpos // 64
        forced = bvalid & ((jr == 0) | (jr == cur) | (jr == cur - 1))
        t_sel[qi * 128:(qi + 1) * 128] = np.where(forced, 1.0e4, np.where(bvalid, 0.0, -1.0e4))
        t_inv[qi * 128:(qi + 1) * 128] = np.where(bvalid, 0.0, -BIG)
        kp = pos((t - 4) * 128 + np.arange(640))[None, :]
        rel = qpos - kp
        swa[qi * 128:(qi + 1) * 128] = np.where((rel >= 0) & (rel < 512) & (kp >= 0), 0.0, -BIG)
    mm = np.zeros((128, 32), f)
    mm[:127] = ((16 * n[:, None] < 64 * (j[None, :] + 1)) & (16 * n[:, None] + 32 > 64 * j[None, :]))
    i = np.arange(128)
    le = (i[:, None] <= i[None, :])
    blk = (i[:, None] // 32 == i[None, :] // 32)
    sm = (i[:, None] // 32 == np.arange(4)[None, :])
    return dict(
        cmp_add=cmp_add, cmp_mul=cmp_mul, t_sel=t_sel, t_inv=t_inv, swa_mask=swa, mmap_p=mm,
        tri_mask=np.where(i[None, :] <= i[:, None], 0.0, -BIG).astype(f),
        ucum_p=(le * (-1.0 / 16)).astype(f), ucum_s=((le & blk) * (-1.0 / 16)).astype(f),
        caus_p=le.astype(f), caus_s=(le & blk).astype(f),
        uend_p=np.full((128, 1), -1.0 / 16, f),
        uend_s=((sm & ((i % 32) < 4)[:, None]) * (-1.0 / 16)).astype(f), seqmask=sm.astype(f),
        ident=np.eye(128, dtype=f), **_sample_tables(),
    )


_SHARED = ["w_in", "norm1_g", "k_slc_norm_g", "k_swa_norm_g", "gla_w_a2", "gla_b_a2", "gla_onorm_g", "q_norm_g",
           "k_cmp_norm_g", "cmp_pe_k", "cmp_w1_k", "cmp_b1_k", "cmp_w2_k", "cmp_b2_k", "cmp_pe_v", "cmp_w1_v",
           "cmp_b1_v", "cmp_w2_v", "cmp_b2_v", "w_br_gla", "w_br_nsa", "w_o", "norm2_g", "w_up", "conv_w", "conv_b",
           "w_down"]


def make_in_maps(inputs, cores=range(8)):
    xp = np.asarray(inputs["x_prompt"], np.float32)
    xs = np.asarray(inputs["x_sample"], np.float32)
    shared = {k: np.ascontiguousarray(np.asarray(inputs[k], np.float32)) for k in _SHARED}
    ctab = [_const_tables(0), _const_tables(1)]
    maps = []
    for c in cores:
        b, half = c // 2, c % 2
        xb = np.zeros((NT * 128, D), np.float32)
        if half == 1:
            xb[0:1024] = xp[b, 0:1024]
        xb[1024:2048] = xp[b, half * 1024:(half + 1) * 1024]
        pos = np.zeros(NT * 128, np.float32)
        pos[0:2048] = np.arange(2048) - 1024 + 1024 * half
        for s in range(4):
            xb[2048 + 32 * s:2048 + 32 * s + 4] = xs[4 * c + s]
            pos[2048 + 32 * s:2048 + 32 * s + 4] = 8192 + np.arange(4)
        cos, sin = _rope_tables(pos)
        m = dict(shared)
        m.update(ctab[half])
        m["page_tab"] = np.ascontiguousarray(np.asarray(inputs["page_table"], np.int32)[4 * c:4 * c + 4].reshape(1, 256))
        if WITH_SAMPLE:
            for k in ("cache_k_cmp", "cache_v_cmp", "cache_k_slc", "cache_v_slc"):
                a = np.asarray(inputs[k], np.float32)
                m[k] = a.reshape(a.shape[0], 128, 256)
        m.update({
            "xbuf": xb, "rope_cos": cos, "rope_sin": sin,
            "state_gla_c": np.ascontiguousarray(np.asarray(inputs["state_gla"], np.float32)[4 * c:4 * c + 4]),
            "state_conv_c": np.ascontiguousarray(np.asarray(inputs["state_conv"], np.float32)[4 * c:4 * c + 4]),
            "state_swa_c": np.ascontiguousarray(np.stack([
                np.asarray(inputs["state_swa_k"], np.float32)[4 * c:4 * c + 4].reshape(4, 512, 256),
                np.asarray(inputs["state_swa_v"], np.float32)[4 * c:4 * c + 4].reshape(4, 512, 256)])),
        })
        maps.append(m)
    return maps


def assemble(results, cores=range(8)):
    f = np.float32
    y_p = np.zeros((4, 2048, D), f); y_s = np.zeros((32, 4, D), f)
    kvp = [np.zeros((4, 2048, 2, 128), f) for _ in range(4)]
    swap = [np.zeros((4, 512, 2, 128), f) for _ in range(2)]
    gla_p = np.zeros((4, 4, 128, 256), f); conv_p = np.zeros((4, 2, DFF), f)
    kvs = [np.zeros((32, 4, 2, 128), f) for _ in range(4)]
    swas = [np.zeros((32, 512, 2, 128), f) for _ in range(2)]
    gla_s = np.zeros((32, 4, 128, 256), f); conv_s = np.zeros((32, 2, DFF), f)
    for c, r in zip(cores, results):
        b, half = c // 2, c % 2
        yo = r["y_out"]
        y_p[b, half * 1024:(half + 1) * 1024] = yo[0:1024]
        y_s[4 * c:4 * c + 4] = yo[1024:1040].reshape(4, 4, D)
        kv = r["kv_out"].reshape(NT * 128, 6, 2, 128)
        for a in range(4):
            kvp[a][b, half * 1024:(half + 1) * 1024] = kv[1024:2048, a]
            kvs[a][4 * c:4 * c + 4] = kv[2048:2176, a].reshape(4, 32, 2, 128)[:, 0:4]
        if half == 1:
            swap[0][b] = kv[1536:2048, 4]
            swap[1][b] = kv[1536:2048, 5]
            gla_p[b] = r["gla_out"][0]
            conv_p[b] = r["conv_out"][0:2]
        swas[0][4 * c:4 * c + 4] = r["swa_out"][0].reshape(4, 512, 2, 128)
        swas[1][4 * c:4 * c + 4] = r["swa_out"][1].reshape(4, 512, 2, 128)
        gla_s[4 * c:4 * c + 4] = r["gla_out"][1:5]
        conv_s[4 * c:4 * c + 4] = r["conv_out"][2:10].reshape(4, 2, DFF)
    return (y_p, y_s, kvp[0], kvp[1], kvp[2], kvp[3], swap[0], swap[1], gla_p, conv_p,
            kvs[0], kvs[1], kvs[2], kvs[3], swas[0], swas[1], gla_s, conv_s)


def kernel(**inputs):
    nc, mk = build_program()
    maps = make_in_maps(inputs)
    res = run_bass_kernel_spmd(nc, maps, core_ids=list(range(8)))
    return assemble(res.results)
```

```python
import numpy as np
import concourse.bass as bass
import concourse.mybir as mybir
from concourse.bass_utils import run_bass_kernel_spmd

F32 = mybir.dt.float32
BF16 = mybir.dt.bfloat16
I32 = mybir.dt.int32
AF = mybir.ActivationFunctionType
ALU = mybir.AluOpType
AX = mybir.AxisListType

D = 2048
NCOLS = 9768
NT = 17
QT0 = 7
EPS = 1e-6
DFF = 5632
NFB = 44
C_GQ, C_GK, C_GV, C_GR, C_GA, C_NQ, C_KV, C_NG, C_MG = 0, 512, 1024, 2048, 3072, 3088, 4112, 5648, 5672
BIG = 1.0e5
WITH_SAMPLE = True


class MK:
    def __init__(self, nc, n_dma_sems=40):
        self.nc = nc
        self.eng = {"pe": nc.tensor, "act": nc.scalar, "dve": nc.vector,
                    "pool": nc.gpsimd, "sp": nc.sync}
        self._stack = []
        self.esem = {}
        for e in ("pe", "act", "dve", "pool"):
            self.esem[e] = self._sem("es_" + e)
        self.ecount = {e: 0 for e in self.esem}
        self.dsem = [self._sem("ds%d" % i) for i in range(n_dma_sems)]
        self.dtot = [0] * n_dma_sems
        self.drr = 0
        self.known = {e: {} for e in self.eng}
        self.last_w = {}
        self.readers = {}
        self.n_wait = 0
        self.n_ins = 0
        self.n_dma = 0

    def _sem(self, name):
        cm = self.nc.semaphore(name)
        s = cm.__enter__()
        self._stack.append(cm)
        return s

    def _need(self, E, reads, writes):
        need = {}

        def add(tok, same_ok):
            if tok is None:
                return
            sem, val, src = tok
            if src == E and not same_ok:
                return
            k = id(sem)
            if k not in need or need[k][1] < val:
                need[k] = (sem, val)

        for k in reads:
            add(self.last_w.get(k), E != "pe")
        for k in writes:
            add(self.last_w.get(k), False)
            for tok in self.readers.get(k, {}).values():
                add(tok, False)
        kn = self.known[E]
        eng = self.eng[E]
        for k, (sem, val) in need.items():
            if kn.get(k, 0) >= val:
                continue
            eng.wait_ge(sem, val)
            self.n_wait += 1
            kn[k] = val

    def _commit(self, tok, reads, writes):
        for k in reads:
            d = self.readers.setdefault(k, {})
            d[id(tok[0])] = tok
        for k in writes:
            self.last_w[k] = tok
            self.readers[k] = {}

    def op(self, E, fn, reads=(), writes=()):
        self._need(E, reads, writes)
        ins = fn(self.eng[E])
        self.ecount[E] += 1
        ins.then_inc(self.esem[E], 1)
        self.n_ins += 1
        tok = (self.esem[E], self.ecount[E], E)
        self._commit(tok, reads, writes)
        return tok

    def dma(self, E, out, in_, reads=(), writes=(), **kw):
        i = self.drr
        self.drr = (self.drr + 1) % len(self.dsem)
        sem = self.dsem[i]
        kn = self.known[E]
        if self.dtot[i] > 0 and kn.get(id(sem), 0) < self.dtot[i]:
            self.eng[E].wait_ge(sem, self.dtot[i])
            kn[id(sem)] = self.dtot[i]
        self._need(E, reads, writes)
        self.eng[E].dma_start(out=out, in_=in_, **kw).then_inc(sem, 16)
        self.n_dma += 1
        self.dtot[i] += 16
        tok = (sem, self.dtot[i], "dma")
        self._commit(tok, reads, writes)
        return tok

    def barrier(self):
        for E, eng in self.eng.items():
            kn = self.known[E]
            for X, sem in self.esem.items():
                if X == E or self.ecount[X] == 0:
                    continue
                if kn.get(id(sem), 0) < self.ecount[X]:
                    eng.wait_ge(sem, self.ecount[X])
                    kn[id(sem)] = self.ecount[X]
            for i, sem in enumerate(self.dsem):
                if self.dtot[i] and kn.get(id(sem), 0) < self.dtot[i]:
                    eng.wait_ge(sem, self.dtot[i])
                    kn[id(sem)] = self.dtot[i]
        self.last_w = {}
        self.readers = {}


class Rot:
    def __init__(self, aps, name):
        self.aps = aps
        self.name = name
        self.i = 0

    def next(self):
        j = self.i % len(self.aps)
        self.i += 1
        return self.aps[j], (self.name, j)


class Ctx:
    pass


def build_program(stage=99, n_pool=2560):
    N_POOL = n_pool
    nc = bass.Bass("TRN2", target_bir_lowering=False)
    mk = MK(nc)
    X = Ctx()
    X.nc, X.mk = nc, mk
    live = []

    def din(name, shape, dt=F32):
        return nc.dram_tensor(name, list(shape), dt, kind="ExternalInput").ap()

    def dout(name, shape, dt=F32):
        return nc.dram_tensor(name, list(shape), dt, kind="ExternalOutput").ap()

    def dscr(name, shape, dt=F32):
        return nc.dram_tensor(name, list(shape), dt).ap()

    def sb(name, shape, dt=F32):
        cm = nc.sbuf_tensor(name, list(shape), dt)
        t = cm.__enter__()
        live.append(cm)
        return t[:] if not hasattr(t, "shape") or True else t

    def sb_mark():
        return len(live)

    def sb_release(mark):
        while len(live) > mark:
            live.pop().__exit__(None, None, None)

    def dbg(name, ap, keys):
        if stage != 2:
            return
        d_ = dout("dbg_" + name, list(ap.shape), F32 if ap.dtype == F32 else ap.dtype)
        mk.dma("sp", d_, ap, reads=keys, writes=["dbg_" + name])

    def rot(name, shape, dt, n):
        return Rot([sb("%s%d" % (name, i), shape, dt) for i in range(n)], name)

    xbuf = din("xbuf", [NT * 128, D])
    w_in = din("w_in", [D, NCOLS])
    norm1_g = din("norm1_g", [D])
    ident_d = din("ident", [128, 128])
    rope_cos = din("rope_cos", [NT * 128, 64])
    rope_sin = din("rope_sin", [NT * 128, 64])
    k_slc_norm_g = din("k_slc_norm_g", [128])
    k_swa_norm_g = din("k_swa_norm_g", [128])
    kv_out = dout("kv_out", [NT * 128, 1536])
    proj = dscr("proj", [NT * 128, NCOLS])
    h_scr = dscr("h_scr", [10 * 128, D])
    gla_out = dout("gla_out", [5, 4, 128, 256])
    conv_out = dout("conv_out", [10, DFF])
    y_out = dout("y_out", [1040, D])
    swa_out = dout("swa_out", [2, 4, 512, 256])
    gla_w_a2 = din("gla_w_a2", [16, 512]); gla_b_a2 = din("gla_b_a2", [512]); gla_onorm_g = din("gla_onorm_g", [256])
    q_norm_g = din("q_norm_g", [128]); k_cmp_norm_g = din("k_cmp_norm_g", [128])
    CMPW = {}
    for nm in ("k", "v"):
        CMPW[nm] = dict(pe=din("cmp_pe_" + nm, [32, 128]), w1=din("cmp_w1_" + nm, [32, 128, 128]),
                        b1=din("cmp_b1_" + nm, [128]), w2=din("cmp_w2_" + nm, [128, 128]), b2=din("cmp_b2_" + nm, [128]))
    ucum_p = din("ucum_p", [128, 128]); ucum_s = din("ucum_s", [128, 128])
    caus_p = din("caus_p", [128, 128]); caus_s = din("caus_s", [128, 128])
    uend_p = din("uend_p", [128, 1]); uend_s = din("uend_s", [128, 4]); seqmask = din("seqmask", [128, 4])
    state_gla_c = din("state_gla_c", [4, 4, 128, 256])
    state_conv_c = din("state_conv_c", [4, 2, DFF])
    state_swa_c = din("state_swa_c", [2, 4, 512, 256])
    mmap_p = din("mmap_p", [128, 32]); tri_mask = din("tri_mask", [128, 128])
    cmp_add = din("cmp_add", [9 * 128, 127]); cmp_mul = din("cmp_mul", [9 * 128, 127])
    t_sel = din("t_sel", [9 * 128, 32]); t_inv = din("t_inv", [9 * 128, 32])
    swa_mask = din("swa_mask", [9 * 128, 640])
    page_tab = din("page_tab", [1, 256], I32)
    if WITH_SAMPLE:
        cache_k_cmp = din("cache_k_cmp", [N_POOL, 128, 256]); cache_v_cmp = din("cache_v_cmp", [N_POOL, 128, 256])
        cache_k_slc = din("cache_k_slc", [N_POOL, 128, 256]); cache_v_slc = din("cache_v_slc", [N_POOL, 128, 256])
    mmap_s = din("mmap_s", [512, 129]); t_sel_s = din("t_sel_s", [128, 129]); newmask = din("newmask", [4, 128, 128])
    swa_past_mask = din("swa_past_mask", [128, 512])
    o_scr = dscr("o_scr", [4, 2, 3, 128, 128])
    w_br_gla = din("w_br_gla", [1024, D]); w_br_nsa = din("w_br_nsa", [1024, D]); w_o = din("w_o", [D, D])
    norm2_g = din("norm2_g", [D]); w_up = din("w_up", [D, 2 * DFF]); conv_w = din("conv_w", [3, DFF])
    conv_b = din("conv_b", [DFF]); w_down = din("w_down", [DFF, D])

    Q1 = "act" if WITH_SAMPLE else "sp"
    if WITH_SAMPLE:
        gath = dscr("gath", [4, 256, 128, 256])
        caches = [cache_k_cmp, cache_v_cmp, cache_k_slc, cache_v_slc]
        gsem = [mk._sem("gs%d" % i) for i in range(4)]
        sp_ = nc.sync
        g_cnt = sp_.alloc_register("g_cnt")
        g_pid = sp_.alloc_register("g_pid")
        sp_.reg_mov(g_cnt, 256)
        with sp_.While(g_cnt):
            sp_.reg_sub(g_cnt, g_cnt, 1)
            g_idx = sp_.snap(g_cnt, min_val=0, max_val=255)
            sp_.reg_load(g_pid, page_tab[0:1, bass.ds(g_idx, 1)])
            g_pv = sp_.snap(g_pid, min_val=0, max_val=N_POOL - 1)
            for ci in range(4):
                sp_.dma_start(out=gath[ci, bass.ds(g_idx, 1)], in_=caches[ci][bass.ds(g_pv, 1)]).then_inc(gsem[ci], 16)
        for ci in range(4):
            sp_.wait_ge(gsem[ci], 16 * 256)
    psf = Rot([nc.alloc_psum_tensor("psf%d" % i, [128, 512], F32).ap() for i in range(6)], "psf")
    psb = Rot([nc.alloc_psum_tensor("psb%d" % i, [128, 1024], BF16).ap() for i in range(2)], "psb")

    ident = sb("identf", [128, 128], F32)
    mk.dma(Q1, ident, ident_d, writes=["ident"])
    epsc = sb("epsc", [128, 1], F32)
    mk.op("dve", lambda e: e.memset(epsc, EPS), writes=["epsc"])
    onec = sb("onec", [128, 1], F32)
    mk.op("dve", lambda e: e.memset(onec, 1.0), writes=["onec"])
    identb = sb("identb", [128, 128], BF16)
    mk.op("dve", lambda e: e.tensor_copy(identb, ident), reads=["ident"], writes=["identb"])
    cpy_i = [0]
    LDQ = ["sp"]
    m_keep = sb_mark()
    o_glaT = sb("o_glaT", [128, 8, 10 * 128], BF16)
    o_nsaT = sb("o_nsaT", [128, 8, 10 * 128], BF16)

    def evac(out, in_, ps_key, writes, reads=()):
        cpy_i[0] += 1
        if cpy_i[0] % 2:
            mk.op("act", lambda e: e.copy(out, in_), reads=reads, writes=[ps_key] + list(writes))
        else:
            mk.op("dve", lambda e: e.tensor_copy(out, in_), reads=reads, writes=[ps_key] + list(writes))

    SC = 128.0 ** -0.5

    def ld(name, dram_ap, shape, dt=F32, eng=None):
        t_ = sb(name, shape, dt)
        mk.dma(eng or LDQ[0], t_, dram_ap, writes=[name])
        return t_

    def transpose_into(dst, src_list, reads, wkey, idn=None, n_in=128):
        bank, kb = psf.next()

        def tr(e):
            for j, s_ in enumerate(src_list):
                i = e.transpose(bank[:, j * 128:(j + 1) * 128], s_, ident)
            return i
        mk.op("pe", tr, reads=list(reads) + ["ident"], writes=[kb])
        evac(dst, bank[:, 0:128 * len(src_list)].rearrange("p (j n) -> p j n", j=len(src_list)), kb, [wkey])


    m_ph12 = sb_mark()
    xnT = sb("xnT", [128, 16, NT * 128], BF16)
    g1b = sb("g1b", [128, D], F32)
    mk.dma(Q1, g1b, norm1_g.partition_broadcast(128), writes=["g1b"])
    xr = rot("xt", [128, D], F32, 2)
    xnr = rot("xn", [128, D], F32, 2)
    junk = sb("junk", [128, D], F32)
    ssr = rot("ss", [128, 2], F32, 2)
    for t in range(NT):
        xt, kx = xr.next()
        xn, kn_ = xnr.next()
        ss, ks = ssr.next()
        mk.dma(Q1, xt, xbuf[t * 128:(t + 1) * 128, :], writes=[kx])
        mk.op("act", lambda e: e.activation(junk, xt, AF.Square), reads=[kx], writes=["junk"])
        mk.op("dve", lambda e: e.reduce_sum(ss[:, 0:1], junk, axis=AX.X), reads=["junk"], writes=[ks])
        mk.op("act", lambda e: e.activation(ss[:, 1:2], ss[:, 0:1], AF.Sqrt, bias=epsc[:, 0:1], scale=1.0 / D),
              reads=[ks, "epsc"], writes=[ks])
        mk.op("dve", lambda e: e.reciprocal(ss[:, 1:2], ss[:, 1:2]), reads=[ks], writes=[ks])
        mk.op("dve", lambda e: e.scalar_tensor_tensor(xn, xt, ss[:, 1:2], g1b, op0=ALU.mult, op1=ALU.mult),
              reads=[kx, ks, "g1b"], writes=[kn_])
        for c4 in range(4):
            bank, kb = psf.next()

            def tr(e, c4=c4, bank=bank, xn=xn):
                for j in range(4):
                    c = c4 * 4 + j
                    i = e.transpose(bank[:, j * 128:(j + 1) * 128], xn[:, c * 128:(c + 1) * 128], ident)
                return i
            mk.op("pe", tr, reads=[kn_, "ident"], writes=[kb])
            evac(xnT[:, c4 * 4:(c4 + 1) * 4, t * 128:(t + 1) * 128],
                 bank.rearrange("p (j n) -> p j n", j=4), kb, [("xnT", t)])

    wr = rot("wb", [128, 16, 512], BF16, 2)
    str_ = rot("stg", [128, 512], F32, 4)
    ncb = (NCOLS + 511) // 512
    for cb in range(ncb):
        c0 = cb * 512
        n = min(512, NCOLS - c0)
        c1 = c0 + n
        prev_need = (c0 < C_GR and c1 > C_GK) or (c0 < C_NQ and c1 > C_GA) or (c0 < C_NG and c1 > C_KV)
        wb, kw = wr.next()
        for c in range(16):
            mk.dma("pool", wb[:, c, :n], w_in[c * 128:(c + 1) * 128, c0:c1], writes=[(kw, c)])
        for t in (range(NT) if prev_need else range(QT0, NT)):
            bank, kb = psf.next()

            def mm(e, bank=bank, wb=wb, t=t, n=n):
                for c in range(16):
                    i = e.matmul(bank[:, :n], xnT[:, c, t * 128:(t + 1) * 128], wb[:, c, :n],
                                 start=(c == 0), stop=(c == 15))
                return i
            mk.op("pe", mm, reads=[(kw, c) for c in range(16)] + [("xnT", t)], writes=[kb])
            st, kst = str_.next()
            evac(st[:, :n], bank[:, :n], kb, [kst])
            mk.dma(Q1, proj[t * 128:(t + 1) * 128, c0:c1], st[:, :n], reads=[kst], writes=[("proj", t, cb)])
    mk.barrier()
    sb_release(m_ph12)
    if stage == 0:
        while live:
            live.pop().__exit__(None, None, None)
        return nc, mk

    LDQ[0] = Q1
    m_gla = sb_mark()
    wa2 = ld("wa2", gla_w_a2, [16, 512])
    ba2b = ld("ba2b", gla_b_a2.partition_broadcast(128), [128, 512])
    gob = ld("gob", gla_onorm_g.partition_broadcast(128), [128, 256])
    Ucp = ld("Ucp", ucum_p, [128, 128])
    Ucs = ld("Ucs", ucum_s, [128, 128])
    c01p = ld("c01p", caus_p, [128, 128])
    c01s = ld("c01s", caus_s, [128, 128])
    uendp = ld("uendp", uend_p, [128, 1])
    uends = ld("uends", uend_s, [128, 4])
    seqm = ld("seqm", seqmask, [128, 4])
    Sp = sb("Sp", [128, 4, 256], F32)
    Spb = sb("Spb", [128, 4, 256], BF16)
    mk.op("dve", lambda e: e.memset(Sp, 0.0), writes=[("S", 0, h) for h in range(4)])
    mk.op("pool", lambda e: e.memset(Spb, 0.0), writes=[("Sb", 0, h) for h in range(4)])
    Ss = sb("Ss", [128, 4, 4, 256], F32)
    Ssb = sb("Ssb", [128, 4, 4, 256], BF16)
    for s in range(4):
        mk.dma(Q1, Ss[:, s], state_gla_c[s].rearrange("h d e -> d h e"), writes=[("S", 1 + s, h) for h in range(4)])
        mk.op("pool", lambda e: e.tensor_copy(Ssb[:, s], Ss[:, s]), reads=[("S", 1 + s, h) for h in range(4)],
              writes=[("Sb", 1 + s, h) for h in range(4)])
    qeTm = sb("qeTm", [128, 4, 4, 128], BF16)
    mk.op("pool", lambda e: e.memset(qeTm, 0.0), writes=["qeTm"])
    a_r = rot("ga", [128, 16], F32, 2)
    aT_r = rot("gaT", [16, 128], F32, 2)
    k_r = rot("gk", [128, 512], F32, 2)
    v_r = rot("gv", [128, 1024], F32, 2)
    q_r = rot("gq", [128, 512], F32, 2)
    r_r = rot("gr", [128, 1024], F32, 1)
    ln_r = rot("gln", [128, 512], F32, 2)
    e1_r = rot("ge1", [128, 512], F32, 1)
    e2_r = rot("ge2", [128, 512], F32, 2)
    eb_r = rot("geb", [128, 16], F32, 2)
    qe_r = rot("gqe", [128, 512], F32, 1)
    ke_r = rot("gke", [128, 512], F32, 2)
    keb_r = rot("gkeb", [128, 512], BF16, 2)
    kem = sb("gkem", [128, 4, 512], BF16)
    vb_r = rot("gvb", [128, 1024], BF16, 2)
    qeT_r = rot("gqeT", [128, 4, 128], BF16, 2)
    keT_r = rot("gkeT", [128, 4, 128], BF16, 2)
    att_r = rot("gatt", [128, 4, 128], BF16, 2)
    osb_r = rot("gosb", [128, 4, 256], F32, 1)
    jg = sb("jg", [128, 4, 256], F32)
    s8_r = rot("gs8", [128, 8], F32, 2)
    og_r = rot("gog", [128, 1024], F32, 1)
    SCQ = 128.0 ** -0.5
    for t in range(NT):
        isq = t >= QT0
        smp = (t == 16)
        G = 4 if smp else 1
        qi = t - QT0
        rows = slice(t * 128, (t + 1) * 128)
        a_t, ka = a_r.next()
        k_t, kk_ = k_r.next()
        v_t, kv_ = v_r.next()
        mk.dma(Q1, a_t, proj[rows, C_GA:C_NQ], writes=[ka])
        mk.dma(Q1, k_t, proj[rows, C_GK:C_GV], writes=[kk_])
        mk.dma(Q1, v_t, proj[rows, C_GV:C_GR], writes=[kv_])
        if isq:
            q_t, kq_ = q_r.next()
            r_t, kr_ = r_r.next()
            mk.dma(Q1, q_t, proj[rows, C_GQ:C_GK], writes=[kq_])
            mk.dma(Q1, r_t, proj[rows, C_GR:C_GA], writes=[kr_])
        aT, kaT = aT_r.next()
        bank, kb = psf.next()
        mk.op("pe", lambda e: e.transpose(bank[0:16, 0:128], a_t, ident), reads=[ka, "ident"], writes=[kb])
        evac(aT, bank[0:16, 0:128], kb, [kaT])
        bank, kb = psf.next()
        mk.op("pe", lambda e: e.matmul(bank, aT, wa2, start=True, stop=True), reads=[kaT, "wa2"], writes=[kb])
        lnv, kln = ln_r.next()
        mk.op("dve", lambda e: e.tensor_tensor(lnv, bank, ba2b, ALU.add), reads=["ba2b"], writes=[kb, kln])
        mk.op("act", lambda e: e.activation(lnv, lnv, AF.Exp, scale=-1.0), reads=[kln], writes=[kln])
        mk.op("act", lambda e: e.activation(lnv, lnv, AF.Ln, bias=onec[:, 0:1]), reads=[kln, "onec"], writes=[kln])
        bank, kb = psf.next()
        U = Ucs if smp else Ucp
        mk.op("pe", lambda e: e.matmul(bank, U, lnv, start=True, stop=True), reads=[kln, "Ucp", "Ucs"], writes=[kb])
        e2, ke2 = e2_r.next()
        mk.op("act", lambda e: e.activation(e2, bank, AF.Exp, scale=-1.0), writes=[kb, ke2])
        if isq:
            e1, ke1 = e1_r.next()
            mk.op("act", lambda e: e.activation(e1, bank, AF.Exp), writes=[kb, ke1])
        bank, kb = psf.next()
        uend = uends if smp else uendp

        def mmbl(e, bank=bank, lnv=lnv, uend=uend, G=G):
            for h in range(4):
                i = e.matmul(bank[:, h * G:(h + 1) * G], lnv[:, h * 128:(h + 1) * 128], uend, start=True, stop=True)
            return i
        mk.op("pe", mmbl, reads=[kln, "uendp", "uends"], writes=[kb])
        eb, keb_ = eb_r.next()
        mk.op("act", lambda e: e.activation(eb[:, 0:4 * G], bank[:, 0:4 * G], AF.Exp), writes=[kb, keb_])
        if t == 8:
            dbg("lnv", lnv, [kln]); dbg("e2", e2, [ke2]); dbg("eb", eb, [keb_]); dbg("aT", aT, [kaT]); dbg("a", a_t, [ka])
        ke, kke = ke_r.next()
        keb, kkeb = keb_r.next()
        vb, kvb = vb_r.next()
        mk.op("dve", lambda e: e.tensor_tensor(ke, k_t, e2, ALU.mult), reads=[kk_, ke2], writes=[kke])
        mk.op("pool", lambda e: e.tensor_copy(keb, ke), reads=[kke], writes=[kkeb])
        mk.op("pool", lambda e: e.tensor_copy(vb, v_t), reads=[kv_], writes=[kvb])
        if smp:
            for s in range(4):
                mk.op("dve", lambda e: e.tensor_scalar(kem[:, s, :], ke, seqm[:, s:s + 1], None, op0=ALU.mult),
                      reads=[kke, "seqm"], writes=[("kem", s)])
        if isq:
            qe, kqe = qe_r.next()
            mk.op("dve", lambda e: e.scalar_tensor_tensor(qe, q_t, SCQ, e1, op0=ALU.mult, op1=ALU.mult),
                  reads=[kq_, ke1], writes=[kqe])
            qeT, kqeT = qeT_r.next()
            keT, kkeT = keT_r.next()
            transpose_into(qeT, [qe[:, h * 128:(h + 1) * 128] for h in range(4)], [kqe], kqeT)
            transpose_into(keT, [ke[:, h * 128:(h + 1) * 128] for h in range(4)], [kke], kkeT)
            bank, kb = psf.next()

            def mmatt(e, bank=bank, keT=keT, qeT=qeT):
                for h in range(4):
                    i = e.matmul(bank[:, h * 128:(h + 1) * 128], keT[:, h, :], qeT[:, h, :], start=True, stop=True)
                return i
            mk.op("pe", mmatt, reads=[kqeT, kkeT], writes=[kb])
            att, katt = att_r.next()
            c01 = c01s if smp else c01p
            mk.op("dve", lambda e: e.tensor_tensor(att, bank.rearrange("p (h n) -> p h n", h=4),
                                                   c01.unsqueeze(1).to_broadcast([128, 4, 128]), ALU.mult),
                  reads=["c01p", "c01s"], writes=[kb, katt])
            if smp:
                for s in range(4):
                    mk.op("pool", lambda e: e.tensor_copy(qeTm[:, s, :, 32 * s:32 * s + 32], qeT[:, :, 32 * s:32 * s + 32]),
                          reads=[kqeT], writes=["qeTm"])
            osb, kosb = osb_r.next()
            for hp in range(2):
                bank, kb = psf.next()

                def mmo(e, bank=bank, hp=hp, att=att, vb=vb, qeT=qeT):
                    for h in (2 * hp, 2 * hp + 1):
                        o_ = bank[:, (h % 2) * 256:(h % 2 + 1) * 256]
                        e.matmul(o_, att[:, h, :], vb[:, h * 256:(h + 1) * 256], start=True, stop=False)
                        if smp:
                            for s in range(4):
                                i = e.matmul(o_, qeTm[:, s, h, :], Ssb[:, s, h, :], start=False, stop=(s == 3))
                        else:
                            i = e.matmul(o_, qeT[:, h, :], Spb[:, h, :], start=False, stop=True)
                    return i
                sbk = [("Sb", (1 + s if smp else 0), h) for h in (2 * hp, 2 * hp + 1) for s in range(G)]
                mk.op("pe", mmo, reads=[katt, kvb, kqeT, "qeTm"] + sbk, writes=[kb])
                evac(osb[:, 2 * hp:2 * hp + 2, :], bank.rearrange("p (h n) -> p h n", h=2), kb, [(kosb, hp)])
        for h in range(4):
            for s in range(G):
                si = 1 + s if smp else 0
                S_ = Ss[:, s, h, :] if smp else Sp[:, h, :]
                Sb_ = Ssb[:, s, h, :] if smp else Spb[:, h, :]
                lhs = kem[:, s, h * 128:(h + 1) * 128] if smp else keb[:, h * 128:(h + 1) * 128]
                bank, kb = psf.next()
                mk.op("pe", lambda e: e.matmul(bank[:, 0:256], lhs, vb[:, h * 256:(h + 1) * 256], start=True, stop=True),
                      reads=[kkeb, kvb, ("kem", s)], writes=[kb])
                ebc = eb[:, h * G + s:h * G + s + 1]
                mk.op("dve", lambda e: e.tensor_scalar(S_, S_, ebc, None, op0=ALU.mult), reads=[keb_, ("S", si, h)],
                      writes=[("S", si, h)])
                mk.op("dve", lambda e: e.scalar_tensor_tensor(S_, bank[:, 0:256], ebc, S_, op0=ALU.mult, op1=ALU.add),
                      reads=[keb_, ("S", si, h)], writes=[kb, ("S", si, h)])
                mk.op("act", lambda e: e.copy(Sb_, S_), reads=[("S", si, h)], writes=[("Sb", si, h)])
        if t == 8:
            dbg("ke", ke, [kke]); dbg("S8", Sp, [("S", 0, h) for h in range(4)]); dbg("osb", osb, [(kosb, 0), (kosb, 1)])
            dbg("qe", qe, [kqe]); dbg("e1", e1, [ke1])
        if isq:
            s8, ks8 = s8_r.next()
            og, kog = og_r.next()
            rdo = [(kosb, 0), (kosb, 1)]
            mk.op("act", lambda e: e.activation(jg, osb, AF.Square), reads=rdo, writes=["jg"])
            mk.op("dve", lambda e: e.reduce_sum(s8[:, 0:4], jg, axis=AX.X), reads=["jg"], writes=[ks8])
            mk.op("act", lambda e: e.activation(s8[:, 4:8], s8[:, 0:4], AF.Sqrt, bias=epsc[:, 0:1], scale=1.0 / 256),
                  reads=[ks8, "epsc"], writes=[ks8])
            mk.op("dve", lambda e: e.reciprocal(s8[:, 4:8], s8[:, 4:8]), reads=[ks8], writes=[ks8])
            og4 = og.rearrange("p (h n) -> p h n", h=4)
            mk.op("dve", lambda e: e.tensor_tensor(og4, osb, s8[:, 4:8].unsqueeze(2).to_broadcast([128, 4, 256]), ALU.mult),
                  reads=rdo + [ks8], writes=[kog])
            mk.op("pool", lambda e: e.tensor_tensor(og4, og4, gob.unsqueeze(1).to_broadcast([128, 4, 256]), ALU.mult),
                  reads=[kog, "gob"], writes=[kog])
            mk.op("act", lambda e: e.activation(r_t, r_t, AF.Silu), reads=[kr_], writes=[kr_])
            mk.op("dve", lambda e: e.tensor_tensor(og, og, r_t, ALU.mult), reads=[kog, kr_], writes=[kog])
            for c4 in range(2):
                transpose_into(o_glaT[:, c4 * 4:(c4 + 1) * 4, qi * 128:(qi + 1) * 128],
                               [og[:, (c4 * 4 + j) * 128:(c4 * 4 + j + 1) * 128] for j in range(4)], [kog], ("oglaT", qi))
        if t == 15:
            mk.dma(Q1, gla_out[0].rearrange("h d e -> d h e"), Sp, reads=[("S", 0, h) for h in range(4)], writes=["glao0"])
    for s in range(4):
        mk.dma(Q1, gla_out[1 + s].rearrange("h d e -> d h e"), Ss[:, s], reads=[("S", 1 + s, h) for h in range(4)],
               writes=[("glao", s)])
    mk.barrier()
    sb_release(m_gla)
    if stage <= 2:
        mk.barrier()
        return nc, mk

    KTn = sb("KTn", [128, 2, 2, 128], BF16)
    VVn = sb("VVn", [128, 2, 2, 128], BF16)
    kcT = sb("kcT", [128, 2, 128], BF16)
    vc = sb("vc", [128, 2, 128], F32)
    cw = {}
    for nm in ("k", "v"):
        w1 = sb("w1" + nm, [128, 32, 128], BF16)
        for r8 in range(8):
            mk.dma("pool", w1[:, r8 * 4:(r8 + 1) * 4, :],
                   CMPW[nm]["w1"][r8 * 4:(r8 + 1) * 4].rearrange("r d f -> d r f"), writes=[("w1" + nm, r8)])
        w2 = sb("w2" + nm, [128, 128], BF16)
        mk.dma("pool", w2, CMPW[nm]["w2"], writes=["w2" + nm])
        b1 = ld("b1" + nm, CMPW[nm]["b1"].rearrange("(f o) -> f o", o=1), [128, 1])
        b2b = ld("b2b" + nm, CMPW[nm]["b2"].partition_broadcast(128), [128, 128])
        pe_ = ld("pe" + nm, CMPW[nm]["pe"], [32, 128])
        peT = sb("peT" + nm, [128, 32], BF16)
        bank, kb = psf.next()
        mk.op("pe", lambda e: e.transpose(bank[:, 0:32], pe_, ident[0:32, 0:32]), reads=["pe" + nm, "ident"], writes=[kb])
        evac(peT, bank[:, 0:32], kb, ["peT" + nm])
        c1 = sb("c1" + nm, [128, 1], F32)
        bank, kb = psf.next()

        def mmc(e, bank=bank, w1=w1, peT=peT):
            for rr in range(32):
                i = e.matmul(bank[:, 0:1], w1[:, rr, :], peT[:, rr:rr + 1], start=(rr == 0), stop=(rr == 31))
            return i
        mk.op("pe", mmc, reads=[("w1" + nm, r8) for r8 in range(8)] + ["peT" + nm], writes=[kb])
        mk.op("dve", lambda e: e.tensor_tensor(c1, bank[:, 0:1], b1, ALU.add), reads=["b1" + nm], writes=[kb, "c1" + nm])
        cw[nm] = dict(w1=w1, w2=w2, b2b=b2b, c1=c1)
    gkc = ld("gkc", k_cmp_norm_g.partition_broadcast(128), [128, 128])
    gTr = rot("gTc", [128, 512], BF16, 2)
    o2r = rot("o2c", [128, 128], F32, 2)
    sc3 = rot("sc3", [128, 4], F32, 2)
    jc = sb("jc", [128, 128], F32)

    def compress(nm, rowsT, nblk, rreads, out_fn):
        W = cw[nm]
        bank, kb = psf.next()

        def mm1(e):
            for rr in range(32):
                i = e.matmul(bank[:, 0:nblk], W["w1"][:, rr, :], rowsT[:, rr:rr + 16 * (nblk - 1) + 1:16],
                             start=(rr == 0), stop=(rr == 31))
            return i
        mk.op("pe", mm1, reads=list(rreads) + [("w1" + nm, r8) for r8 in range(8)], writes=[kb])
        gT, kg = gTr.next()
        mk.op("act", lambda e: e.activation(gT[:, 0:nblk], bank[:, 0:nblk], AF.Gelu_apprx_tanh, bias=W["c1"][:, 0:1]),
              reads=["c1" + nm], writes=[kb, kg])
        for mt in range((nblk + 127) // 128):
            m = min(128, nblk - mt * 128)
            bank2, kb2 = psf.next()
            mk.op("pe", lambda e: e.matmul(bank2[0:m, 0:128], gT[:, mt * 128:mt * 128 + m], W["w2"], start=True, stop=True),
                  reads=[kg, "w2" + nm], writes=[kb2])
            o2, ko2 = o2r.next()
            mk.op("dve", lambda e: e.tensor_tensor(o2[0:m], bank2[0:m, 0:128], W["b2b"][0:m], ALU.add),
                  reads=["b2b" + nm], writes=[kb2, ko2])
            out_fn(mt, m, o2, ko2)

    def rms_rows(o2, ko2, m, gain, gkey, out, okey):
        s3, ks3 = sc3.next()
        mk.op("act", lambda e: e.activation(jc[0:m], o2[0:m], AF.Square), reads=[ko2], writes=["jc"])
        mk.op("dve", lambda e: e.reduce_sum(s3[0:m, 0:1], jc[0:m], axis=AX.X), reads=["jc"], writes=[ks3])
        mk.op("act", lambda e: e.activation(s3[0:m, 1:2], s3[0:m, 0:1], AF.Sqrt, bias=epsc[0:m, 0:1], scale=1.0 / 128),
              reads=[ks3, "epsc"], writes=[ks3])
        mk.op("dve", lambda e: e.reciprocal(s3[0:m, 1:2], s3[0:m, 1:2]), reads=[ks3], writes=[ks3])
        mk.op("dve", lambda e: e.scalar_tensor_tensor(out[0:m], o2[0:m], s3[0:m, 1:2], gain[0:m], op0=ALU.mult, op1=ALU.mult),
              reads=[ko2, ks3, gkey], writes=[okey])

    kcr = rot("kcn", [128, 128], F32, 2)
    m_smp_keep = sb_mark()
    gqb = ld("gqb", q_norm_g.partition_broadcast(128), [128, 128])
    mmap = ld("mmap", mmap_p, [128, 32])
    trim = ld("trim", tri_mask, [128, 128])
    qraw_r = rot("nq", [128, 1024], F32, 1)
    g24_r = rot("ng", [128, 24], F32, 2)
    cs2_r = rot("ncs", [128, 2, 64], F32, 2)
    s16_r = rot("ns16", [128, 16], F32, 2)
    qnT = sb("nqnT", [128, 8, 128], BF16)
    qrT = sb("nqrT", [128, 8, 128], BF16)
    pT_r = rot("npT", [128, 16, 128], BF16, 2)
    pc_r = rot("npc", [128, 128], F32, 2)
    pcT = sb("npcT", [128, 4, 128], F32)
    st_r = rot("nst", [128, 8], F32, 4)
    coef = sb("ncoef", [128, 2, 24], F32)
    scb = sb("nscb", [128, 4, 32], F32)
    m8 = sb("nm8", [128, 2, 8], F32)
    onsa = sb("nonsa", [128, 8, 128], F32)
    otmp = sb("notmp", [128, 4, 128], F32)

    m_prompt_only = sb_mark()
    KTa = sb("KTa", [128, 2, 2, 2048], BF16)
    VV = sb("VV", [128, 16, 2, 2, 128], BF16)
    m_ktc = sb_mark()
    KTc = sb("KTc", [128, 2, 2, 2048], BF16)
    m_ph3 = sb_mark()
    gkb = sb("gkb", [128, 2, 128], F32)
    mk.dma(Q1, gkb[:, 0, :], k_slc_norm_g.partition_broadcast(128), writes=["gkb0"])
    mk.dma(Q1, gkb[:, 1, :], k_swa_norm_g.partition_broadcast(128), writes=["gkb1"])
    kvr = rot("kv", [128, 1536], F32, 2)
    knr = rot("kn", [128, 2, 2, 128], F32, 2)
    csr = rot("cs", [128, 2, 64], F32, 2)
    j4 = sb("j4", [128, 2, 2, 128], F32)
    s4r = rot("s4", [128, 8], F32, 2)
    tmpr = rot("rt", [128, 4, 2, 2, 64], F32, 2)
    for t in range(NT):
        kv, kkv = kvr.next()
        kn, kkn = knr.next()
        cs, kcs = csr.next()
        s4, ks4 = s4r.next()
        tp, ktp = tmpr.next()
        mk.dma(Q1, kv, proj[t * 128:(t + 1) * 128, C_KV:C_NG],
               reads=[("proj", t, cb) for cb in range(C_KV // 512, (C_NG - 1) // 512 + 1)], writes=[kkv])
        mk.dma(Q1, cs[:, 0, :], rope_cos[t * 128:(t + 1) * 128, :], writes=[(kcs, 0)])
        mk.dma(Q1, cs[:, 1, :], rope_sin[t * 128:(t + 1) * 128, :], writes=[(kcs, 1)])
        kv4 = kv.rearrange("p (a g d) -> p a g d", a=6, g=2)
        kk = kv4[:, 2:6:2]
        mk.op("act", lambda e: e.activation(j4, kk, AF.Square), reads=[kkv], writes=["j4"])
        mk.op("dve", lambda e: e.reduce_sum(s4[:, 0:4].rearrange("p (a g) -> p a g", a=2), j4, axis=AX.X),
              reads=["j4"], writes=[ks4])
        mk.op("act", lambda e: e.activation(s4[:, 4:8], s4[:, 0:4], AF.Sqrt, bias=epsc[:, 0:1], scale=1.0 / 128),
              reads=[ks4, "epsc"], writes=[ks4])
        mk.op("dve", lambda e: e.reciprocal(s4[:, 4:8], s4[:, 4:8]), reads=[ks4], writes=[ks4])
        rb = s4[:, 4:8].rearrange("p (a g) -> p a g", a=2).unsqueeze(3).to_broadcast([128, 2, 2, 128])
        mk.op("dve", lambda e: e.tensor_tensor(kn, kk, rb, ALU.mult), reads=[kkv, ks4], writes=[kkn])
        gb_ = gkb.unsqueeze(2).to_broadcast([128, 2, 2, 128])
        mk.op("pool", lambda e: e.tensor_tensor(kn, kn, gb_, ALU.mult), reads=[kkn, "gkb0", "gkb1"], writes=[kkn])
        cosb = cs[:, 0, :].unsqueeze(1).unsqueeze(1).to_broadcast([128, 2, 2, 64])
        sinb = cs[:, 1, :].unsqueeze(1).unsqueeze(1).to_broadcast([128, 2, 2, 64])
        x1, x2 = kn[:, :, :, 0:64], kn[:, :, :, 64:128]
        rd = [kkn, (kcs, 0), (kcs, 1)]
        mk.op("dve", lambda e: e.tensor_tensor(tp[:, 0], x1, cosb, ALU.mult), reads=rd, writes=[(ktp, 0)])
        mk.op("pool", lambda e: e.tensor_tensor(tp[:, 1], x2, sinb, ALU.mult), reads=rd, writes=[(ktp, 1)])
        mk.op("dve", lambda e: e.tensor_tensor(tp[:, 2], x2, cosb, ALU.mult), reads=rd, writes=[(ktp, 2)])
        mk.op("pool", lambda e: e.tensor_tensor(tp[:, 3], x1, sinb, ALU.mult), reads=rd, writes=[(ktp, 3)])
        mk.op("dve", lambda e: e.tensor_tensor(kk[:, :, :, 0:64], tp[:, 0], tp[:, 1], ALU.subtract),
              reads=[(ktp, 0), (ktp, 1)], writes=[kkv])
        mk.op("dve", lambda e: e.tensor_tensor(kk[:, :, :, 64:128], tp[:, 2], tp[:, 3], ALU.add),
              reads=[(ktp, 2), (ktp, 3)], writes=[kkv])
        mk.dma(Q1, kv_out[t * 128:(t + 1) * 128, :], kv, reads=[kkv], writes=[("kvo", t)])
        bank, kb = psf.next()
        if t < 16:
            def tr1(e, bank=bank, kv4=kv4):
                for j in range(4):
                    i = e.transpose(bank[:, j * 128:(j + 1) * 128], kv4[:, j // 2, j % 2, :], ident)
                return i
            mk.op("pe", tr1, reads=[kkv, "ident"], writes=[kb])
            evac(KTc[:, :, :, t * 128:(t + 1) * 128], bank.rearrange("p (a g n) -> p a g n", a=2, g=2), kb, [("KT", 0, t)])
            bank, kb = psf.next()
        dst = KTa[:, :, :, t * 128:(t + 1) * 128] if t < 16 else KTn

        def tr2(e, bank=bank, kv4=kv4):
            for j in range(4):
                i = e.transpose(bank[:, j * 128:(j + 1) * 128], kv4[:, 2 + 2 * (j // 2), j % 2, :], ident)
            return i
        mk.op("pe", tr2, reads=[kkv, "ident"], writes=[kb])
        evac(dst, bank.rearrange("p (a g n) -> p a g n", a=2, g=2), kb, [("KT", 1, t)])
        vdst = VV[:, t] if t < 16 else VVn
        mk.op("pool", lambda e: e.tensor_copy(vdst, kv4[:, 3:6:2]), reads=[kkv], writes=[("VV", t)])
    for kv_i in range(2):
        for s in range(4):
            mk.dma(Q1, swa_out[kv_i, s, 0:508, :], state_swa_c[kv_i, s, 4:512, :], writes=[("swao", kv_i, s, 0)])
            mk.dma(Q1, swa_out[kv_i, s, 508:512, :], kv_out[2048 + 32 * s:2048 + 32 * s + 4, 1024 + 256 * kv_i:1280 + 256 * kv_i],
                   reads=[("kvo", 16)], writes=[("swao", kv_i, s, 1)])
    mk.barrier()
    sb_release(m_ph3)
    LDQ[0] = "sp"
    for g in range(2):
        def outk(mt, m, o2, ko2, g=g):
            kc_, kkc = kcr.next()
            rms_rows(o2, ko2, m, gkc, "gkc", kc_, kkc)
            bank, kb = psf.next()
            mk.op("pe", lambda e: e.transpose(bank[:, 0:m], kc_[0:m, :], ident[0:m, 0:m]), reads=[kkc, "ident"], writes=[kb])
            evac(kcT[:, g, 0:m], bank[:, 0:m], kb, [("kcT", g)])

        def outv(mt, m, o2, ko2, g=g):
            mk.op("pool", lambda e: e.tensor_copy(vc[0:m, g, :], o2[0:m]), reads=[ko2], writes=[("vc", g)])
        compress("k", KTc[:, 0, g, :], 127, [("KT", 0, t) for t in range(16)], outk)
        compress("v", KTc[:, 1, g, :], 127, [("KT", 0, t) for t in range(16)], outv)
    mk.barrier()
    sb_release(m_ktc)
    if stage <= 1:
        mk.barrier()
        return nc, mk


    psR = Rot(psf.aps[0:3], "psf")
    B_CMP, B_SLC, B_SWA = psf.aps[3], psf.aps[4], psf.aps[5]
    K_CMP, K_SLC, K_SWA = ("psf", 3), ("psf", 4), ("psf", 5)
    qn_ = sb("nqn", [128, 8, 128], F32)
    qr_ = sb("nqr", [128, 8, 128], F32)
    jq = sb("njq", [128, 8, 128], F32)
    rt4 = sb("nrt4", [128, 4, 8, 64], F32)
    tb_r = rot("ntb", [128, 2, 127], F32, 2)
    ts_r = rot("nts", [128, 2, 32], F32, 2)
    sw_r = rot("nsw", [128, 640], F32, 2)
    sS_r = rot("nsS", [128, 2048], F32, 3)
    pb_r = rot("npb", [128, 2048], BF16, 3)
    def softmax_rows(sS, ksS, n, p_out, kp, rs_col, krs):
        st, kst = st_r.next()
        mk.op("dve", lambda e: e.reduce_max(st[:, 0:1], sS[:, 0:n], axis=AX.X), reads=[ksS], writes=[kst])
        mk.op("dve", lambda e: e.tensor_scalar(st[:, 1:2], st[:, 0:1], -SC, None, op0=ALU.mult), reads=[kst], writes=[kst])
        mk.op("act", lambda e: e.activation(p_out[:, 0:n], sS[:, 0:n], AF.Exp, bias=st[:, 1:2], scale=SC),
              reads=[ksS, kst], writes=[kp])
        mk.op("dve", lambda e: e.reduce_sum(st[:, 2:3], p_out[:, 0:n], axis=AX.X), reads=[kp], writes=[kst])
        mk.op("dve", lambda e: e.reciprocal(rs_col, st[:, 2:3]), reads=[kst], writes=[krs])

    def pv_bf16(pb, kp, ntile, v_fn, vreads, o_ap, o_key):
        pT, kpT = pT_r.next()
        for k0 in range(0, ntile, 8):
            nk = min(8, ntile - k0)
            bank, kb = psb.next()

            def trp(e, bank=bank, k0=k0, nk=nk):
                for j in range(nk):
                    i = e.transpose(bank[:, j * 128:(j + 1) * 128], pb[:, (k0 + j) * 128:(k0 + j + 1) * 128], identb)
                return i
            mk.op("pe", trp, reads=[kp, "identb"], writes=[kb])
            evac(pT[:, k0:k0 + nk, :], bank[:, 0:nk * 128].rearrange("p (j n) -> p j n", j=nk), kb, [(kpT, k0)])

        def mmpv(e):
            for kt in range(ntile):
                i = e.matmul(o_ap, pT[:, kt, :], v_fn(kt), start=(kt == 0), stop=(kt == ntile - 1))
            return i
        mk.op("pe", mmpv, reads=[(kpT, k0) for k0 in range(0, ntile, 8)] + list(vreads), writes=[o_key])

    def norm_rope_q(qraw, kq, cs2, kcs):
        s16, ks16 = s16_r.next()
        q3 = qraw.rearrange("p (h d) -> p h d", h=8)
        mk.op("act", lambda e: e.activation(jq, q3, AF.Square), reads=[kq], writes=["jq"])
        mk.op("dve", lambda e: e.reduce_sum(s16[:, 0:8], jq, axis=AX.X), reads=["jq"], writes=[ks16])
        mk.op("act", lambda e: e.activation(s16[:, 8:16], s16[:, 0:8], AF.Sqrt, bias=epsc[:, 0:1], scale=1.0 / 128),
              reads=[ks16, "epsc"], writes=[ks16])
        mk.op("dve", lambda e: e.reciprocal(s16[:, 8:16], s16[:, 8:16]), reads=[ks16], writes=[ks16])
        mk.op("dve", lambda e: e.tensor_tensor(qn_, q3, s16[:, 8:16].unsqueeze(2).to_broadcast([128, 8, 128]), ALU.mult),
              reads=[kq, ks16], writes=["qn"])
        mk.op("pool", lambda e: e.tensor_tensor(qn_, qn_, gqb.unsqueeze(1).to_broadcast([128, 8, 128]), ALU.mult),
              reads=["qn", "gqb"], writes=["qn"])
        cosb = cs2[:, 0, :].unsqueeze(1).to_broadcast([128, 8, 64])
        sinb = cs2[:, 1, :].unsqueeze(1).to_broadcast([128, 8, 64])
        x1, x2 = qn_[:, :, 0:64], qn_[:, :, 64:128]
        rd = ["qn", (kcs, 0), (kcs, 1)]
        mk.op("dve", lambda e: e.tensor_tensor(rt4[:, 0], x1, cosb, ALU.mult), reads=rd, writes=[("rt4", 0)])
        mk.op("pool", lambda e: e.tensor_tensor(rt4[:, 1], x2, sinb, ALU.mult), reads=rd, writes=[("rt4", 1)])
        mk.op("dve", lambda e: e.tensor_tensor(rt4[:, 2], x2, cosb, ALU.mult), reads=rd, writes=[("rt4", 2)])
        mk.op("pool", lambda e: e.tensor_tensor(rt4[:, 3], x1, sinb, ALU.mult), reads=rd, writes=[("rt4", 3)])
        mk.op("dve", lambda e: e.tensor_tensor(qr_[:, :, 0:64], rt4[:, 0], rt4[:, 1], ALU.subtract),
              reads=[("rt4", 0), ("rt4", 1)], writes=["qr"])
        mk.op("dve", lambda e: e.tensor_tensor(qr_[:, :, 64:128], rt4[:, 2], rt4[:, 3], ALU.add),
              reads=[("rt4", 2), ("rt4", 3)], writes=["qr"])
        for c4 in range(2):
            transpose_into(qnT[:, c4 * 4:(c4 + 1) * 4, :], [qn_[:, c4 * 4 + j, :] for j in range(4)], ["qn"], ("qnT", c4))
            transpose_into(qrT[:, c4 * 4:(c4 + 1) * 4, :], [qr_[:, c4 * 4 + j, :] for j in range(4)], ["qr"], ("qrT", c4))

    def combine(g):
        cf = coef[:, 0, :]

        def cb(br):
            return cf[:, br * 8 + g * 4:br * 8 + g * 4 + 4].unsqueeze(2).to_broadcast([128, 4, 128])
        dst = onsa[:, g * 4:(g + 1) * 4, :]
        v3 = lambda b: b.rearrange("p (h n) -> p h n", h=4)
        mk.op("dve", lambda e: e.tensor_tensor(dst, v3(B_CMP), cb(0), ALU.mult), reads=["coef"], writes=[K_CMP, ("onsa", g)])
        mk.op("dve", lambda e: e.tensor_tensor(otmp, v3(B_SLC), cb(1), ALU.mult), reads=["coef"], writes=[K_SLC, "otmp"])
        mk.op("pool", lambda e: e.tensor_tensor(dst, dst, otmp, ALU.add), reads=["otmp", ("onsa", g)], writes=[("onsa", g)])
        mk.op("dve", lambda e: e.tensor_tensor(otmp, v3(B_SWA), cb(2), ALU.mult), reads=["coef"], writes=[K_SWA, "otmp"])
        mk.op("pool", lambda e: e.tensor_tensor(dst, dst, otmp, ALU.add), reads=["otmp", ("onsa", g)], writes=[("onsa", g)])

    for t in range(QT0, 16):
        qi = t - QT0
        rows = slice(t * 128, (t + 1) * 128)
        qrows = slice(qi * 128, (qi + 1) * 128)
        qraw, kq = qraw_r.next()
        g24, kg24 = g24_r.next()
        cs2, kcs = cs2_r.next()
        tb, ktb = tb_r.next()
        ts_, kts = ts_r.next()
        sw, ksw = sw_r.next()
        mk.dma("sp", qraw, proj[rows, C_NQ:C_KV], writes=[kq])
        mk.dma("sp", g24, proj[rows, C_NG:C_MG], writes=[kg24])
        mk.dma("sp", cs2[:, 0, :], rope_cos[rows, :], writes=[(kcs, 0)])
        mk.dma("sp", cs2[:, 1, :], rope_sin[rows, :], writes=[(kcs, 1)])
        mk.dma("sp", tb[:, 0, :], cmp_add[qrows, :], writes=[(ktb, 0)])
        mk.dma("sp", tb[:, 1, :], cmp_mul[qrows, :], writes=[(ktb, 1)])
        mk.dma("sp", ts_[:, 0, :], t_sel[qrows, :], writes=[(kts, 0)])
        mk.dma("sp", ts_[:, 1, :], t_inv[qrows, :], writes=[(kts, 1)])
        mk.dma("sp", sw, swa_mask[qrows, :], writes=[ksw])
        norm_rope_q(qraw, kq, cs2, kcs)
        mk.op("act", lambda e: e.activation(coef[:, 0, :], g24, AF.Sigmoid), reads=[kg24], writes=["coef"])
        mk.op("dve", lambda e: e.memset(coef[:, 1, :], 1.0), writes=["rs"])
        nkt = t + 1
        for g in range(2):
            for h4 in range(4):
                hh = g * 4 + h4
                bank, kb = psR.next()
                mk.op("pe", lambda e: e.matmul(bank[:, 0:127], qnT[:, hh, :], kcT[:, g, 0:127], start=True, stop=True),
                      reads=[("qnT", hh // 4), ("kcT", g)], writes=[kb])
                sS, ksS = sS_r.next()
                mk.op("dve", lambda e: e.tensor_tensor(sS[:, 0:127], bank[:, 0:127], tb[:, 0, :], ALU.add),
                      reads=[(ktb, 0)], writes=[kb, ksS])
                pc, kpc = pc_r.next()
                st2, kst2 = st_r.next()
                softmax_rows(sS, ksS, 127, pc, kpc, st2[:, 4:5], kst2)
                mk.op("dve", lambda e: e.scalar_tensor_tensor(pc[:, 0:127], pc[:, 0:127], st2[:, 4:5], tb[:, 1, :],
                                                              op0=ALU.mult, op1=ALU.mult),
                      reads=[kpc, kst2, (ktb, 1)], writes=[kpc])
                bank, kb = psR.next()
                mk.op("pe", lambda e: e.transpose(bank[0:127, 0:128], pc[:, 0:127], ident), reads=[kpc, "ident"], writes=[kb])
                evac(pcT[0:127, h4, :], bank[0:127, 0:128], kb, [("pcT", h4)])
                mk.op("pe", lambda e: e.matmul(B_CMP[:, h4 * 128:(h4 + 1) * 128], pcT[0:127, h4, :], vc[0:127, g, :],
                                               start=True, stop=True),
                      reads=[("pcT", h4), ("vc", g)], writes=[K_CMP])
            bank, kb = psR.next()

            def mmsc(e, bank=bank):
                for h4 in range(4):
                    i = e.matmul(bank[:, 0:32], pcT[0:127, h4, :], mmap[0:127, :], start=(h4 == 0), stop=(h4 == 3))
                return i
            mk.op("pe", mmsc, reads=[("pcT", h4) for h4 in range(4)] + ["mmap"], writes=[kb])
            mk.op("dve", lambda e: e.tensor_tensor(scb[:, 0, :], bank[:, 0:32], ts_[:, 0, :], ALU.add),
                  reads=[(kts, 0)], writes=[kb, "scb"])
            mk.op("dve", lambda e: e.max(out=m8[:, 0, :], in_=scb[:, 0, :]), reads=["scb"], writes=["m8"])
            mk.op("dve", lambda e: e.match_replace(out=scb[:, 1, :], in_to_replace=m8[:, 0, :], in_values=scb[:, 0, :],
                                                   imm_value=-1.0e9), reads=["scb", "m8"], writes=["scb"])
            mk.op("dve", lambda e: e.max(out=m8[:, 1, :], in_=scb[:, 1, :]), reads=["scb"], writes=["m8"])
            mk.op("dve", lambda e: e.tensor_scalar(scb[:, 2, :], scb[:, 0, :], m8[:, 1, 7:8], BIG, op0=ALU.is_ge, op1=ALU.mult),
                  reads=["scb", "m8"], writes=["scb"])
            mk.op("dve", lambda e: e.scalar_tensor_tensor(scb[:, 3, :], scb[:, 2, :], -BIG, ts_[:, 1, :], op0=ALU.add, op1=ALU.add),
                  reads=["scb", (kts, 1)], writes=["bb"])
            for h4 in range(4):
                hh = g * 4 + h4
                sS, ksS = sS_r.next()
                nk = nkt * 128
                for c0 in range(0, nk, 512):
                    w = min(512, nk - c0)
                    bank, kb = psR.next()
                    mk.op("pe", lambda e: e.matmul(bank[:, 0:w], qrT[:, hh, :], KTa[:, 0, g, c0:c0 + w], start=True, stop=True),
                          reads=[("qrT", hh // 4)] + [("KT", 1, kt) for kt in range(c0 // 128, (c0 + w) // 128)], writes=[kb])
                    nb = w // 64
                    mk.op("dve", lambda e: e.tensor_tensor(sS[:, c0:c0 + w].rearrange("p (b k) -> p b k", k=64),
                                                           bank[:, 0:w].rearrange("p (b k) -> p b k", k=64),
                                                           scb[:, 3, c0 // 64:c0 // 64 + nb].unsqueeze(2).to_broadcast([128, nb, 64]),
                                                           ALU.add), reads=["bb"], writes=[kb, ksS])
                mk.op("pool", lambda e: e.tensor_tensor(sS[:, t * 128:(t + 1) * 128], sS[:, t * 128:(t + 1) * 128], trim, ALU.add),
                      reads=[ksS, "trim"], writes=[ksS])
                pb, kpb = pb_r.next()
                softmax_rows(sS, ksS, nk, pb, kpb, coef[:, 1, 8 + hh:9 + hh], "rs")
                pv_bf16(pb, kpb, nkt, lambda kt: VV[:, kt, 0, g, :], [("VV", kt) for kt in range(nkt)],
                        B_SLC[:, h4 * 128:(h4 + 1) * 128], K_SLC)
                sS, ksS = sS_r.next()
                k0 = (t - 4) * 128
                for c0, w in ((0, 512), (512, 128)):
                    bank, kb = psR.next()
                    mk.op("pe", lambda e: e.matmul(bank[:, 0:w], qrT[:, hh, :], KTa[:, 1, g, k0 + c0:k0 + c0 + w], start=True, stop=True),
                          reads=[("qrT", hh // 4)] + [("KT", 1, kt) for kt in range(t - 4, t + 1)], writes=[kb])
                    mk.op("dve", lambda e: e.tensor_tensor(sS[:, c0:c0 + w], bank[:, 0:w], sw[:, c0:c0 + w], ALU.add),
                          reads=[ksw], writes=[kb, ksS])
                pb, kpb = pb_r.next()
                softmax_rows(sS, ksS, 640, pb, kpb, coef[:, 1, 16 + hh:17 + hh], "rs")
                pv_bf16(pb, kpb, 5, lambda kt: VV[:, t - 4 + kt, 1, g, :], [("VV", kt) for kt in range(t - 4, t + 1)],
                        B_SWA[:, h4 * 128:(h4 + 1) * 128], K_SWA)
            for br in range(3):
                cs_ = slice(br * 8 + g * 4, br * 8 + g * 4 + 4)
                mk.op("dve", lambda e: e.tensor_tensor(coef[:, 0, cs_], coef[:, 0, cs_], coef[:, 1, cs_], ALU.mult),
                      reads=["coef", "rs"], writes=["coef"])
            combine(g)
        for c4 in range(2):
            transpose_into(o_nsaT[:, c4 * 4:(c4 + 1) * 4, qi * 128:(qi + 1) * 128],
                           [onsa[:, c4 * 4 + j, :] for j in range(4)], [("onsa", 0), ("onsa", 1)], ("onsaT", qi))
    mk.barrier()
    while len(live) > m_prompt_only:
        live.pop().__exit__(None, None, None)
    if WITH_SAMPLE:
        m_smp = sb_mark()
        KSVS = sb("KSVS", [128, 16384], BF16)
        RT = KSVS.rearrange("p (g n) -> p g n", g=2)
        KS = KSVS[:, 0:8192]
        VS = KSVS[:, 8192:16384].rearrange("p (j d) -> p j d", d=128)
        pg_r = rot("pg", [128, 4, 256], BF16, 3)
        sS_s = sb("sS_s", [128, 8320], F32)
        jq = sS_s[:, 0:1024].rearrange("p (h d) -> p h d", h=8)
        rt4 = sS_s[:, 1024:3072].rearrange("p (a h d) -> p a h d", a=4, h=8)
        qn_ = sS_s[:, 3072:4096].rearrange("p (h d) -> p h d", h=8)
        qr_ = sS_s[:, 4096:5120].rearrange("p (h d) -> p h d", h=8)
        pbs_r = rot("pbs", [128, 2048], BF16, 1)
        kcT_s = sb("kcT_s", [128, 2, 512], BF16)
        vc_s = sb("vc_s", [128, 4, 2, 128], F32)
        pcs = sb("pcs", [128, 512], F32)
        pcT_s = sb("pcT_s", [128, 4, 128], F32)
        imp4 = sb("imp4", [128, 4, 128], F32)
        mmap_sb = ld("mmap_sb", mmap_s.rearrange("(t p) j -> p t j", p=128), [128, 4, 129])
        tsel_sb = ld("tsel_sb", t_sel_s, [128, 129])
        newm = ld("newm", newmask.rearrange("s p k -> p s k"), [128, 4, 128])
        swpm = ld("swpm", swa_past_mask, [128, 512])
        scs = sb("scs", [128, 4, 129], F32)
        qs_n = sb("qs_n", [128, 128], BF16)
        qs_r = sb("qs_r", [128, 128], BF16)
        osb3 = sb("osb3", [128, 3, 128], F32)
        ocb = sb("ocb", [128, 4, 128], F32)
        KW = sb("KW", [128, 2, 512], BF16)
        VW = sb("VW", [128, 4, 256], BF16)
        page_regs = {}

        def load_pages(ci, s, consume):
            for j4 in range(16):
                pg, kpg = pg_r.next()
                mk.dma("pool", pg, gath[ci, s * 64 + 4 * j4:s * 64 + 4 * j4 + 4].rearrange("j p n -> p j n"), writes=[kpg])
                consume(j4, pg, kpg)

        def pv_big(sS, ksS, ntile, v_fn, vreads, o_ap, o_key, rs_col, krs):
            st, kst = st_r.next()
            n = ntile * 128
            mk.op("dve", lambda e: e.reduce_max(st[:, 0:1], sS[:, 0:n], axis=AX.X), reads=[ksS], writes=[kst])
            mk.op("dve", lambda e: e.tensor_scalar(st[:, 1:2], st[:, 0:1], -SC, None, op0=ALU.mult), reads=[kst], writes=[kst])
            mk.op("dve", lambda e: e.memset(st[:, 2:3], 0.0), reads=[], writes=[kst])
            for g0 in range(0, ntile, 16):
                ng = min(16, ntile - g0)
                pb, kpb = pbs_r.next()
                mk.op("act", lambda e: e.activation(pb[:, 0:ng * 128], sS[:, g0 * 128:(g0 + ng) * 128], AF.Exp, bias=st[:, 1:2], scale=SC),
                      reads=[ksS, kst], writes=[kpb])
                mk.op("dve", lambda e: e.reduce_sum(st[:, 3:4], pb[:, 0:ng * 128], axis=AX.X), reads=[kpb], writes=[kst])
                mk.op("dve", lambda e: e.tensor_tensor(st[:, 2:3], st[:, 2:3], st[:, 3:4], ALU.add), reads=[kst], writes=[kst])
                pT, kpT = pT_r.next()
                for k0 in range(0, ng, 8):
                    nk = min(8, ng - k0)
                    bank, kb = psb.next()

                    def trp(e, bank=bank, k0=k0, nk=nk, pb=pb):
                        for j in range(nk):
                            i = e.transpose(bank[:, j * 128:(j + 1) * 128], pb[:, (k0 + j) * 128:(k0 + j + 1) * 128], identb)
                        return i
                    mk.op("pe", trp, reads=[kpb, "identb"], writes=[kb])
                    evac(pT[:, k0:k0 + nk, :], bank[:, 0:nk * 128].rearrange("p (j n) -> p j n", j=nk), kb, [(kpT, k0)])

                def mmpv(e, g0=g0, ng=ng, pT=pT):
                    for kt in range(ng):
                        i = e.matmul(o_ap, pT[:, kt, :], v_fn(g0 + kt), start=(g0 + kt == 0), stop=(g0 + kt == ntile - 1))
                    return i
                mk.op("pe", mmpv, reads=[(kpT, k0) for k0 in range(0, ng, 8)] + list(vreads), writes=[o_key])
            mk.op("dve", lambda e: e.reciprocal(rs_col, st[:, 2:3]), reads=[kst], writes=[krs])

        t = 16
        rows = slice(t * 128, (t + 1) * 128)
        qraw, kq = qraw_r.next()
        g24, kg24 = g24_r.next()
        cs2, kcs = cs2_r.next()
        mk.dma("sp", qraw, proj[rows, C_NQ:C_KV], writes=[kq])
        mk.dma("sp", g24, proj[rows, C_NG:C_MG], writes=[kg24])
        mk.dma("sp", cs2[:, 0, :], rope_cos[rows, :], writes=[(kcs, 0)])
        mk.dma("sp", cs2[:, 1, :], rope_sin[rows, :], writes=[(kcs, 1)])
        norm_rope_q(qraw, kq, cs2, kcs)
        mk.op("act", lambda e: e.activation(coef[:, 0, :], g24, AF.Sigmoid), reads=[kg24], writes=["coef"])
        mk.barrier()
        for s in range(4):
            for (nm, cache) in (("k", 0), ("v", 1)):
                def cons(j4, pg, kpg):
                    bank, kb = psb.next()

                    def trA(e, bank=bank, pg=pg):
                        for g_ in range(2):
                            for p_ in range(4):
                                i = e.transpose(bank[:, (g_ * 4 + p_) * 128:(g_ * 4 + p_ + 1) * 128], pg[:, p_, g_ * 128:(g_ + 1) * 128], identb)
                        return i
                    mk.op("pe", trA, reads=[kpg, "identb"], writes=[kb])
                    evac(RT[:, :, j4 * 512:(j4 + 1) * 512], bank.rearrange("q (g n) -> q g n", g=2), kb, [("KS", j4), ("VS", j4)])
                load_pages(cache, s, cons)
                for g in range(2):
                    if nm == "k":
                        def outk(mt, m, o2, ko2, g=g):
                            kc_, kkc = kcr.next()
                            rms_rows(o2, ko2, m, gkc, "gkc", kc_, kkc)
                            bank, kb = psf.next()
                            mk.op("pe", lambda e: e.transpose(bank[:, 0:m], kc_[0:m, :], ident[0:m, 0:m]), reads=[kkc, "ident"], writes=[kb])
                            evac(kcT_s[:, g, mt * 128:mt * 128 + m], bank[:, 0:m], kb, [("kcT_s", g, mt)])
                        compress("k", RT[:, g, :], 511, [(("KS", "VS")[g], j) for j in range(16)], outk)
                    else:
                        def outv(mt, m, o2, ko2, g=g):
                            mk.op("pool", lambda e: e.tensor_copy(vc_s[0:m, mt, g, :], o2[0:m]), reads=[ko2], writes=[("vc_s", g, mt)])
                        compress("v", RT[:, g, :], 511, [(("KS", "VS")[g], j) for j in range(16)], outv)
            pg, kpg = pg_r.next()
            mk.dma("pool", pg, state_swa_c[0, s].rearrange("(j p) n -> p j n", p=128), writes=[kpg])
            bank, kb = psb.next()

            def trW(e, bank=bank, pg=pg):
                for g_ in range(2):
                    for p_ in range(4):
                        i = e.transpose(bank[:, (g_ * 4 + p_) * 128:(g_ * 4 + p_ + 1) * 128], pg[:, p_, g_ * 128:(g_ + 1) * 128], identb)
                return i
            mk.op("pe", trW, reads=[kpg, "identb"], writes=[kb])
            evac(KW, bank.rearrange("q (g n) -> q g n", g=2), kb, [("KW", 0), ("KW", 1)])
            mk.dma("pool", VW, state_swa_c[1, s].rearrange("(j p) n -> p j n", p=128), writes=["VW"])
            for g in range(2):
                for h4 in range(4):
                    mk.op("pool", lambda e: e.tensor_copy(qs_n[:, h4 * 32:(h4 + 1) * 32], qnT[:, g * 4 + h4, 32 * s:32 * s + 32]),
                          reads=[("qnT", g)], writes=["qs_n"])
                    mk.op("pool", lambda e: e.tensor_copy(qs_r[:, h4 * 32:(h4 + 1) * 32], qrT[:, g * 4 + h4, 32 * s:32 * s + 32]),
                          reads=[("qrT", g)], writes=["qs_r"])
                bank, kb = psf.next()
                mk.op("pe", lambda e: e.matmul(bank[:, 0:511], qs_n, kcT_s[:, g, 0:511], start=True, stop=True),
                      reads=["qs_n"] + [("kcT_s", g, mt) for mt in range(4)], writes=[kb])
                mk.op("act", lambda e: e.copy(sS_s[:, 0:511], bank[:, 0:511]), writes=[kb, "sS_s"])
                st2, kst2 = st_r.next()
                softmax_rows(sS_s, "sS_s", 511, pcs, "pcs", st2[:, 4:5], kst2)
                mk.op("dve", lambda e: e.tensor_scalar(pcs[:, 0:511], pcs[:, 0:511], st2[:, 4:5], None, op0=ALU.mult),
                      reads=["pcs", kst2], writes=["pcs"])
                for mt in range(4):
                    m = 128 if mt < 3 else 127
                    bank, kb = psf.next()
                    mk.op("pe", lambda e: e.transpose(bank[0:m, 0:128], pcs[:, mt * 128:mt * 128 + m], ident), reads=["pcs", "ident"], writes=[kb])
                    evac(pcT_s[0:m, mt, :], bank[0:m, 0:128], kb, [("pcT_s", mt)])
                bank, kb = psf.next()

                def mmoc(e, bank=bank, g=g):
                    for mt in range(4):
                        m = 128 if mt < 3 else 127
                        i = e.matmul(bank[:, 0:128], pcT_s[0:m, mt, :], vc_s[0:m, mt, g, :], start=(mt == 0), stop=(mt == 3))
                    return i
                mk.op("pe", mmoc, reads=[("pcT_s", mt) for mt in range(4)] + [("vc_s", g, mt) for mt in range(4)], writes=[kb])
                mk.op("act", lambda e: e.copy(osb3[:, 0, :], bank[:, 0:128]), writes=[kb, ("osb3", 0)])
                p4 = pcT_s.rearrange("p t (h s) -> p t h s", h=4)
                rdp = [("pcT_s", mt) for mt in range(4)]
                mk.op("dve", lambda e: e.tensor_tensor(imp4[:, :, 0:32], p4[:, :, 0, :], p4[:, :, 1, :], ALU.add), reads=rdp, writes=["imp4"])
                mk.op("dve", lambda e: e.tensor_tensor(imp4[:, :, 0:32], imp4[:, :, 0:32], p4[:, :, 2, :], ALU.add), reads=rdp + ["imp4"], writes=["imp4"])
                mk.op("dve", lambda e: e.tensor_tensor(imp4[:, :, 0:32], imp4[:, :, 0:32], p4[:, :, 3, :], ALU.add), reads=rdp + ["imp4"], writes=["imp4"])
                for h4 in range(1, 4):
                    mk.op("pool", lambda e: e.tensor_copy(imp4[:, :, h4 * 32:(h4 + 1) * 32], imp4[:, :, 0:32]), reads=["imp4"], writes=["imp4"])
                bank, kb = psf.next()

                def mmsc2(e, bank=bank):
                    for mt in range(4):
                        m = 128 if mt < 3 else 127
                        i = e.matmul(bank[:, 0:129], imp4[0:m, mt, :], mmap_sb[0:m, mt, :], start=(mt == 0), stop=(mt == 3))
                    return i
                mk.op("pe", mmsc2, reads=["imp4", "mmap_sb"], writes=[kb])
                mk.op("dve", lambda e: e.tensor_tensor(scs[:, 0, :], bank[:, 0:129], tsel_sb, ALU.add), reads=["tsel_sb"], writes=[kb, "scs"])
                mk.op("dve", lambda e: e.max(out=m8[:, 0, :], in_=scs[:, 0, :]), reads=["scs"], writes=["m8"])
                mk.op("dve", lambda e: e.match_replace(out=scs[:, 1, :], in_to_replace=m8[:, 0, :], in_values=scs[:, 0, :], imm_value=-1.0e9),
                      reads=["scs", "m8"], writes=["scs"])
                mk.op("dve", lambda e: e.max(out=m8[:, 1, :], in_=scs[:, 1, :]), reads=["scs"], writes=["m8"])
                mk.op("dve", lambda e: e.tensor_scalar(scs[:, 2, :], scs[:, 0, :], m8[:, 1, 7:8], BIG, op0=ALU.is_ge, op1=ALU.mult),
                      reads=["scs", "m8"], writes=["scs"])
                mk.op("dve", lambda e: e.tensor_scalar(scs[:, 3, :], scs[:, 2, :], -BIG, None, op0=ALU.add), reads=["scs"], writes=["bbs"])
                def consk(j4, pg, kpg, g=g):
                    bank, kb = psb.next()

                    def trK(e, bank=bank, pg=pg):
                        for p_ in range(4):
                            i = e.transpose(bank[:, p_ * 128:(p_ + 1) * 128], pg[:, p_, g * 128:(g + 1) * 128], identb)
                        return i
                    mk.op("pe", trK, reads=[kpg, "identb"], writes=[kb])
                    evac(KS[:, j4 * 512:(j4 + 1) * 512], bank[:, 0:512], kb, [("KS", j4)])
                load_pages(2, s, consk)
                for j4 in range(16):
                    mk.dma("pool", VS[:, 4 * j4:4 * j4 + 4, :],
                           gath[3, s * 64 + 4 * j4:s * 64 + 4 * j4 + 4, :, g * 128:(g + 1) * 128].rearrange("j p n -> p j n"),
                           writes=[("VS", j4)])
                for c0 in range(0, 8192, 512):
                    bank, kb = psf.next()
                    mk.op("pe", lambda e: e.matmul(bank, qs_r, KS[:, c0:c0 + 512], start=True, stop=True),
                          reads=["qs_r", ("KS", c0 // 512)], writes=[kb])
                    mk.op("dve", lambda e: e.tensor_tensor(sS_s[:, c0:c0 + 512].rearrange("p (b k) -> p b k", k=64),
                                                           bank.rearrange("p (b k) -> p b k", k=64),
                                                           scs[:, 3, c0 // 64:c0 // 64 + 8].unsqueeze(2).to_broadcast([128, 8, 64]), ALU.add),
                          reads=["bbs", "pcs"], writes=[kb, "sS_s"])
                bank, kb = psf.next()
                mk.op("pe", lambda e: e.matmul(bank[:, 0:128], qs_r, KTn[:, 0, g, :], start=True, stop=True), reads=["qs_r", ("KT", 1, 16)], writes=[kb])
                mk.op("dve", lambda e: e.scalar_tensor_tensor(sS_s[:, 8192:8320], bank[:, 0:128], scs[:, 3, 128:129], newm[:, s, :], op0=ALU.add, op1=ALU.add),
                      reads=["bbs", "newm"], writes=[kb, "sS_s"])
                bank, kb = psf.next()
                st3, kst3 = st_r.next()
                pv_big(sS_s, "sS_s", 65, lambda kt: (VS[:, kt, :] if kt < 64 else VVn[:, 0, g, :]), [("VS", j) for j in range(16)] + [("VV", 16)],
                       bank[:, 0:128], kb, st3[:, 4:5], kst3)
                mk.op("dve", lambda e: e.tensor_scalar(osb3[:, 1, :], bank[:, 0:128], st3[:, 4:5], None, op0=ALU.mult), reads=[kst3], writes=[kb, ("osb3", 1)])
                bank, kb = psf.next()
                mk.op("pe", lambda e: e.matmul(bank, qs_r, KW[:, g, :], start=True, stop=True), reads=["qs_r", ("KW", g)], writes=[kb])
                mk.op("dve", lambda e: e.tensor_tensor(sS_s[:, 0:512], bank, swpm, ALU.add), reads=["swpm"], writes=[kb, "sS_s"])
                bank, kb = psf.next()
                mk.op("pe", lambda e: e.matmul(bank[:, 0:128], qs_r, KTn[:, 1, g, :], start=True, stop=True), reads=["qs_r", ("KT", 1, 16)], writes=[kb])
                mk.op("dve", lambda e: e.tensor_tensor(sS_s[:, 512:640], bank[:, 0:128], newm[:, s, :], ALU.add), reads=["newm"], writes=[kb, "sS_s"])
                bank, kb = psf.next()
                st4, kst4 = st_r.next()
                pv_big(sS_s, "sS_s", 5, lambda kt: (VW[:, kt, g * 128:(g + 1) * 128] if kt < 4 else VVn[:, 1, g, :]), ["VW", ("VV", 16)],
                       bank[:, 0:128], kb, st4[:, 4:5], kst4)
                mk.op("dve", lambda e: e.tensor_scalar(osb3[:, 2, :], bank[:, 0:128], st4[:, 4:5], None, op0=ALU.mult), reads=[kst4], writes=[kb, ("osb3", 2)])
                mk.dma("sp", o_scr[s, g].rearrange("b r d -> r b d"), osb3, reads=[("osb3", b_) for b_ in range(3)], writes=[("oscr", s, g)])
        for g in range(2):
            cf = coef[:, 0, :]
            dst = onsa[:, g * 4:(g + 1) * 4, :]
            for b_ in range(3):
                for s in range(4):
                    mk.dma("sp", ocb[32 * s:32 * s + 32, :, :], o_scr[s, g, b_].rearrange("(h r) d -> r h d", h=4),
                           reads=[("oscr", s, g)], writes=["ocb"])
                cbk = cf[:, b_ * 8 + g * 4:b_ * 8 + g * 4 + 4].unsqueeze(2).to_broadcast([128, 4, 128])
                if b_ == 0:
                    mk.op("dve", lambda e: e.tensor_tensor(dst, ocb, cbk, ALU.mult), reads=["coef", "ocb"], writes=[("onsa", g)])
                else:
                    mk.op("dve", lambda e: e.tensor_tensor(otmp, ocb, cbk, ALU.mult), reads=["coef", "ocb"], writes=["otmp"])
                    mk.op("pool", lambda e: e.tensor_tensor(dst, dst, otmp, ALU.add), reads=["otmp", ("onsa", g)], writes=[("onsa", g)])
        for c4 in range(2):
            transpose_into(o_nsaT[:, c4 * 4:(c4 + 1) * 4, 9 * 128:10 * 128],
                           [onsa[:, c4 * 4 + j, :] for j in range(4)], [("onsa", 0), ("onsa", 1)], ("onsaT", 9))
    else:
        mk.op("pool", lambda e: e.memset(o_nsaT[:, :, 9 * 128:10 * 128], 0.0), writes=[("onsaT", 9)])
    mk.barrier()
    while len(live) > m_smp_keep:
        live.pop().__exit__(None, None, None)

    NQ = 10
    mT = sb("mT", [128, 16, NQ * 128], BF16)
    m_mrg = sb_mark()
    wg_r = rot("wg", [128, 8, 512], BF16, 2)
    wn_r = rot("wn", [128, 8, 512], BF16, 2)
    mg_r = rot("mg", [128, 2, 512], F32, 2)
    mm_r = rot("mm", [128, 2, 512], F32, 2)
    for cb in range(4):
        cs_ = slice(cb * 512, (cb + 1) * 512)
        wg, kwg = wg_r.next()
        wn, kwn = wn_r.next()
        for c in range(8):
            mk.dma("pool", wg[:, c, :], w_br_gla[c * 128:(c + 1) * 128, cs_], writes=[(kwg, c)])
            mk.dma("pool", wn[:, c, :], w_br_nsa[c * 128:(c + 1) * 128, cs_], writes=[(kwn, c)])
        for qi in range(NQ):
            t = QT0 + qi
            rows = slice(t * 128, (t + 1) * 128)
            mg, kmg = mg_r.next()
            mm_, kmm = mm_r.next()
            mk.dma("sp", mg[:, 0, :], proj[rows, C_MG + cb * 512:C_MG + (cb + 1) * 512], writes=[(kmg, 0)])
            mk.dma("sp", mg[:, 1, :], proj[rows, C_MG + 2048 + cb * 512:C_MG + 2048 + (cb + 1) * 512], writes=[(kmg, 1)])
            mk.op("act", lambda e: e.activation(mg, mg, AF.Sigmoid), reads=[(kmg, 0), (kmg, 1)], writes=[(kmg, 0), (kmg, 1)])
            bankA, kA = psf.next()
            bankB, kB = psf.next()

            def mmA(e, bank=bankA, w=wg, src=o_glaT, qi=qi):
                for c in range(8):
                    i = e.matmul(bank, src[:, c, qi * 128:(qi + 1) * 128], w[:, c, :], start=(c == 0), stop=(c == 7))
                return i

            def mmB(e, bank=bankB, w=wn, src=o_nsaT, qi=qi):
                for c in range(8):
                    i = e.matmul(bank, src[:, c, qi * 128:(qi + 1) * 128], w[:, c, :], start=(c == 0), stop=(c == 7))
                return i
            mk.op("pe", mmA, reads=[(kwg, c) for c in range(8)] + [("oglaT", qi)], writes=[kA])
            mk.op("pe", mmB, reads=[(kwn, c) for c in range(8)] + [("onsaT", qi)], writes=[kB])
            mk.op("dve", lambda e: e.tensor_tensor(mm_[:, 0, :], bankA, mg[:, 0, :], ALU.mult), reads=[(kmg, 0)], writes=[kA, (kmm, 0)])
            mk.op("dve", lambda e: e.tensor_tensor(mm_[:, 1, :], bankB, mg[:, 1, :], ALU.mult), reads=[(kmg, 1)], writes=[kB, (kmm, 1)])
            mk.op("pool", lambda e: e.tensor_tensor(mm_[:, 0, :], mm_[:, 0, :], mm_[:, 1, :], ALU.add),
                  reads=[(kmm, 0), (kmm, 1)], writes=[(kmm, 0)])
            transpose_into(mT[:, cb * 4:(cb + 1) * 4, qi * 128:(qi + 1) * 128],
                           [mm_[:, 0, j * 128:(j + 1) * 128] for j in range(4)], [(kmm, 0)], ("mT", qi, cb))
    mk.barrier()
    sb_release(m_mrg)

    m_wo = sb_mark()
    wo_r = rot("wo", [128, 16, 512], BF16, 2)
    xs_r = rot("xs", [128, 512], F32, 3)
    for cb in range(4):
        cs_ = slice(cb * 512, (cb + 1) * 512)
        wo, kwo = wo_r.next()
        for c in range(16):
            mk.dma("pool", wo[:, c, :], w_o[c * 128:(c + 1) * 128, cs_], writes=[(kwo, c)])
        for qi in range(NQ):
            t = QT0 + qi
            xs, kxs = xs_r.next()
            mk.dma("sp", xs, xbuf[t * 128:(t + 1) * 128, cs_], writes=[kxs])
            bank, kb = psf.next()

            def mmh(e, bank=bank, wo=wo, qi=qi):
                for c in range(16):
                    i = e.matmul(bank, mT[:, c, qi * 128:(qi + 1) * 128], wo[:, c, :], start=(c == 0), stop=(c == 15))
                return i
            mk.op("pe", mmh, reads=[(kwo, c) for c in range(16)], writes=[kb])
            mk.op("dve", lambda e: e.tensor_tensor(xs, bank, xs, ALU.add), reads=[kxs], writes=[kb, kxs])
            mk.dma("sp", h_scr[qi * 128:(qi + 1) * 128, cs_], xs, reads=[kxs], writes=[("h", qi, cb)])
    mk.barrier()
    while len(live) > m_keep:
        live.pop().__exit__(None, None, None)

    NF = 1042
    gT = sb("gT", [128, NFB, NF], BF16)
    m_hn = sb_mark()
    hnF = sb("hnF", [128, 16, NF], BF16)
    m_n2 = sb_mark()
    g2b = ld("g2b", norm2_g.partition_broadcast(128), [128, D])
    hr = rot("ht", [128, D], F32, 2)
    hnr = rot("hn", [128, D], F32, 2)
    junk2 = sb("junk2", [128, D], F32)
    ss2r = rot("ss2", [128, 2], F32, 2)
    for qi in range(NQ):
        ht, kh = hr.next()
        hn, khn = hnr.next()
        ss, ks = ss2r.next()
        mk.dma("sp", ht, h_scr[qi * 128:(qi + 1) * 128, :], writes=[kh])
        mk.op("act", lambda e: e.activation(junk2, ht, AF.Square), reads=[kh], writes=["junk2"])
        mk.op("dve", lambda e: e.reduce_sum(ss[:, 0:1], junk2, axis=AX.X), reads=["junk2"], writes=[ks])
        mk.op("act", lambda e: e.activation(ss[:, 1:2], ss[:, 0:1], AF.Sqrt, bias=epsc[:, 0:1], scale=1.0 / D),
              reads=[ks, "epsc"], writes=[ks])
        mk.op("dve", lambda e: e.reciprocal(ss[:, 1:2], ss[:, 1:2]), reads=[ks], writes=[ks])
        mk.op("dve", lambda e: e.scalar_tensor_tensor(hn, ht, ss[:, 1:2], g2b, op0=ALU.mult, op1=ALU.mult),
              reads=[kh, ks, "g2b"], writes=[khn])
        for c4 in range(4):
            bank, kb = psf.next()

            def tr(e, c4=c4, bank=bank, hn=hn):
                for j in range(4):
                    c = c4 * 4 + j
                    i = e.transpose(bank[:, j * 128:(j + 1) * 128], hn[:, c * 128:(c + 1) * 128], ident)
                return i
            mk.op("pe", tr, reads=[khn, "ident"], writes=[kb])
            b3 = bank.rearrange("p (j n) -> p j n", j=4)
            cc = slice(c4 * 4, (c4 + 1) * 4)
            if qi == 0:
                evac(hnF[:, cc, 0:2], b3[:, :, 126:128], kb, [("hnF", qi, c4)])
            elif qi < 9:
                evac(hnF[:, cc, 2 + (qi - 1) * 128:2 + qi * 128], b3, kb, [("hnF", qi, c4)])
            else:
                for s in range(4):
                    evac(hnF[:, cc, 1026 + 4 * s:1030 + 4 * s], b3[:, :, 32 * s:32 * s + 4], kb, [("hnF", qi, c4, s)])
    mk.barrier()
    sb_release(m_n2)

    m_up = sb_mark()
    cwt = sb("cwt", [128, NFB, 4], F32)
    for j in range(3):
        mk.dma("sp", cwt[:, :, j], conv_w[j].rearrange("(fb p) -> p fb", p=128), writes=[("cwt", j)], allow_slow_non_contiguous=True)
    mk.dma("sp", cwt[:, :, 3], conv_b.rearrange("(fb p) -> p fb", p=128), writes=[("cwt", 3)], allow_slow_non_contiguous=True)
    cstT = sb("cstT", [128, NFB, 8], F32)
    for s_ in range(4):
        for j in range(2):
            mk.dma("sp", cstT[:, :, s_ * 2 + j], state_conv_c[s_, j].rearrange("(fb p) -> p fb", p=128),
                   writes=[("cstT", s_ * 2 + j)], allow_slow_non_contiguous=True)
    crow_r = rot("crow", [10, 512], F32, 2)
    wa_r = rot("wa", [128, 16, 512], BF16, 2)
    wb_r = rot("wbg", [128, 16, 512], BF16, 2)
    aT = sb("aT", [128, 2 + NF], F32)
    bT = sb("bT", [128, NF], F32)
    uu = sb("uu", [128, NF], F32)
    as6 = sb("as6", [128, 4, 6], F32)
    us4 = sb("us4", [128, 4, 4], F32)
    cc10 = sb("cc10", [128, 10], F32)
    mk.op("dve", lambda e: e.memset(aT[:, 0:2], 0.0), writes=["aTpad"])
    SEGS = ((0, 512), (512, 512), (1024, NF - 1024))
    for f2 in range(NFB // 4):
        wa, kwa = wa_r.next()
        wb_, kwb = wb_r.next()
        for c in range(16):
            mk.dma("pool", wa[:, c, :], w_up[c * 128:(c + 1) * 128, f2 * 512:(f2 + 1) * 512], writes=[(kwa, c)])
            mk.dma("pool", wb_[:, c, :], w_up[c * 128:(c + 1) * 128, DFF + f2 * 512:DFF + (f2 + 1) * 512], writes=[(kwb, c)])
        for sub in range(4):
            fb = f2 * 4 + sub
            for (c0, w) in SEGS:
                for (wt, kwt, dst, kd) in ((wa, kwa, aT[:, 2 + c0:2 + c0 + w], "aT"), (wb_, kwb, bT[:, c0:c0 + w], "bT")):
                    bank, kb = psf.next()

                    def mmu(e, bank=bank, wt=wt, c0=c0, w=w, sub=sub):
                        for c in range(16):
                            i = e.matmul(bank[:, 0:w], wt[:, c, sub * 128:(sub + 1) * 128], hnF[:, c, c0:c0 + w],
                                         start=(c == 0), stop=(c == 15))
                        return i
                    mk.op("pe", mmu, reads=[(kwt, c) for c in range(16)], writes=[kb])
                    evac(dst, bank[:, 0:w], kb, [(kd, c0)])
            ra = [("aT", c0) for (c0, w) in SEGS] + ["aTpad"] + [("cwt", j) for j in range(4)]
            w0, w1, w2, bcv = (cwt[:, fb, j:j + 1] for j in range(4))
            mk.op("dve", lambda e: e.tensor_scalar(uu, aT[:, 2:2 + NF], w2, bcv, op0=ALU.mult, op1=ALU.add), reads=ra, writes=["uu"])
            mk.op("dve", lambda e: e.scalar_tensor_tensor(uu, aT[:, 1:1 + NF], w1, uu, op0=ALU.mult, op1=ALU.add), reads=ra + ["uu"], writes=["uu"])
            mk.op("dve", lambda e: e.scalar_tensor_tensor(uu, aT[:, 0:NF], w0, uu, op0=ALU.mult, op1=ALU.add), reads=ra + ["uu"], writes=["uu"])
            a_s = aT[:, 2 + 1026:2 + 1042].rearrange("p (s t) -> p s t", s=4)
            mk.op("pool", lambda e: e.tensor_copy(as6[:, :, 0:2], cstT[:, fb, :].rearrange("p (s j) -> p s j", s=4)),
                  reads=[("cstT", i8) for i8 in range(8)], writes=["as6a"])
            mk.op("pool", lambda e: e.tensor_copy(as6[:, :, 2:6], a_s), reads=ra, writes=["as6b"])
            rs6 = ["as6a", "as6b"] + [("cwt", j) for j in range(4)]
            mk.op("dve", lambda e: e.tensor_scalar(us4, as6[:, :, 2:6], w2, bcv, op0=ALU.mult, op1=ALU.add), reads=rs6, writes=["us4"])
            mk.op("dve", lambda e: e.scalar_tensor_tensor(us4, as6[:, :, 1:5], w1, us4, op0=ALU.mult, op1=ALU.add), reads=rs6 + ["us4"], writes=["us4"])
            mk.op("dve", lambda e: e.scalar_tensor_tensor(us4, as6[:, :, 0:4], w0, us4, op0=ALU.mult, op1=ALU.add), reads=rs6 + ["us4"], writes=["us4"])
            mk.op("dve", lambda e: e.tensor_copy(uu[:, 1026:1042].rearrange("p (s t) -> p s t", s=4), us4), reads=["us4", "uu"], writes=["uu"])
            mk.op("act", lambda e: e.activation(uu, uu, AF.Gelu_apprx_tanh), reads=["uu"], writes=["uu"])
            mk.op("dve", lambda e: e.tensor_tensor(gT[:, fb, :], uu, bT, ALU.mult), reads=["uu"] + [("bT", c0) for (c0, w) in SEGS],
                  writes=[("gT", fb)])
            mk.op("pool", lambda e: e.tensor_copy(cc10[:, 0:2], aT[:, 2 + 1024:2 + 1026]), reads=ra, writes=["cc10a"])
            mk.op("pool", lambda e: e.tensor_copy(cc10[:, 2:10].rearrange("p (s t) -> p s t", s=4), a_s[:, :, 2:4]), reads=ra, writes=["cc10b"])
            bank, kb = psf.next()
            mk.op("pe", lambda e: e.transpose(bank[0:10, 0:128], cc10, ident), reads=["cc10a", "cc10b", "ident"], writes=[kb])
            if sub == 0:
                crow, kcrow = crow_r.next()
            evac(crow[:, sub * 128:(sub + 1) * 128], bank[0:10, 0:128], kb, [(kcrow, sub)])
        mk.dma("sp", conv_out[:, f2 * 512:(f2 + 1) * 512], crow, reads=[(kcrow, i_) for i_ in range(4)], writes=[("convo", f2)])
    mk.barrier()
    sb_release(m_hn)

    wd_r = rot("wd", [128, NFB, 512], BF16, 2)
    hs_r = rot("hs", [128, 512], F32, 3)
    for cb in range(4):
        cs_ = slice(cb * 512, (cb + 1) * 512)
        wd, kwd = wd_r.next()
        for fb in range(NFB):
            mk.dma("pool", wd[:, fb, :], w_down[fb * 128:(fb + 1) * 128, cs_], writes=[(kwd, fb)])
        for i in range(9):
            M = 128 if i < 8 else 16
            col0 = 2 + i * 128
            hs, khs = hs_r.next()
            if i < 8:
                mk.dma("sp", hs, h_scr[(1 + i) * 128:(2 + i) * 128, cs_], writes=[khs])
            else:
                for s in range(4):
                    mk.dma("sp", hs[4 * s:4 * s + 4, :], h_scr[9 * 128 + 32 * s:9 * 128 + 32 * s + 4, cs_], writes=[(khs, s)])
            bank, kb = psf.next()

            def mmy(e, bank=bank, wd=wd, col0=col0, M=M):
                for fb in range(NFB):
                    i_ = e.matmul(bank[0:M, 0:512], gT[:, fb, col0:col0 + M], wd[:, fb, :], start=(fb == 0), stop=(fb == NFB - 1))
                return i_
            mk.op("pe", mmy, reads=[(kwd, fb) for fb in range(NFB)], writes=[kb])
            rdh = [khs] if i < 8 else [(khs, s) for s in range(4)]
            mk.op("dve", lambda e: e.tensor_tensor(hs[0:M], bank[0:M, 0:512], hs[0:M], ALU.add), reads=rdh, writes=[kb, khs])
            mk.dma("sp", y_out[i * 128:i * 128 + M, cs_], hs[0:M], reads=[khs], writes=[("y", i, cb)])
    mk.barrier()
    while live:
        live.pop().__exit__(None, None, None)
    return nc, mk


def _rope_tables(pos):
    half = 64
    inv = (10000.0 ** (-np.arange(half, dtype=np.float32) / half)).astype(np.float32)
    ang = pos.astype(np.float32)[:, None] * inv[None, :]
    return np.cos(ang).astype(np.float32), np.sin(ang).astype(np.float32)


def _sample_tables():
    f = np.float32
    n = np.arange(512)[:, None]
    j = np.arange(129)[None, :]
    mm = ((16 * n < 64 * (j + 1)) & (16 * n + 32 > 64 * j) & (n < 511)).astype(f)
    ts = np.zeros((128, 129), f)
    ts[:, [0, 127, 128]] = 1.0e4
    slot = (np.arange(128) % 32)
    newm = np.full((4, 128, 128), -BIG, f)
    for s in range(4):
        for kk in range(4):
            newm[s, (slot < 4) & (kk <= slot), 32 * s + kk] = 0.0
    swp = np.where(np.arange(512)[None, :] > slot[:, None], 0.0, -BIG).astype(f)
    return dict(mmap_s=mm, t_sel_s=ts, newmask=newm, swa_past_mask=swp)


def _const_tables(half):
    f = np.float32
    pos = lambda r: r - 1024 + 1024 * half
    n = np.arange(127)
    j = np.arange(32)
    cmp_add = np.zeros((9 * 128, 127), f)
    cmp_mul = np.zeros((9 * 128, 127), f)
    t_sel = np.zeros((9 * 128, 32), f)
    t_inv = np.zeros((9 * 128, 32), f)
    swa = np.zeros((9 * 128, 640), f)
    for qi in range(9):
        t = QT0 + qi
        qpos = pos(t * 128 + np.arange(128))[:, None]
        valid = (pos(16 * n + 31)[None, :] <= qpos) & (pos(16 * n)[None, :] >= 0)
        cmp_add[qi * 128:(qi + 1) * 128] = np.where(valid, 0.0, -BIG)
        cmp_mul[qi * 128:(qi + 1) * 128] = valid
        sp = pos(64 * j)[None, :]
        bvalid = (sp >= 0) & (sp <= qpos)
        jr = sp // 64
        cur = qpos // 64
        forced = bvalid & ((jr == 0) | (jr == cur) | (jr == cur - 1))
        t_sel[qi * 128:(qi + 1) * 128] = np.where(forced, 1.0e4, np.where(bvalid, 0.0, -1.0e4))
        t_inv[qi * 128:(qi + 1) * 128] = np.where(bvalid, 0.0, -BIG)
        kp = pos((t - 4) * 128 + np.arange(640))[None, :]
        rel = qpos - kp
        swa[qi * 128:(qi + 1) * 128] = np.where((rel >= 0) & (rel < 512) & (kp >= 0), 0.0, -BIG)
    mm = np.zeros((128, 32), f)
    mm[:127] = ((16 * n[:, None] < 64 * (j[None, :] + 1)) & (16 * n[:, None] + 32 > 64 * j[None, :]))
    i = np.arange(128)
    le = (i[:, None] <= i[None, :])
    blk = (i[:, None] // 32 == i[None, :] // 32)
    sm = (i[:, None] // 32 == np.arange(4)[None, :])
    return dict(
        cmp_add=cmp_add, cmp_mul=cmp_mul, t_sel=t_sel, t_inv=t_inv, swa_mask=swa, mmap_p=mm,
        tri_mask=np.where(i[None, :] <= i[:, None], 0.0, -BIG).astype(f),
        ucum_p=(le * (-1.0 / 16)).astype(f), ucum_s=((le & blk) * (-1.0 / 16)).astype(f),
        caus_p=le.astype(f), caus_s=(le & blk).astype(f),
        uend_p=np.full((128, 1), -1.0 / 16, f),
        uend_s=((sm & ((i % 32) < 4)[:, None]) * (-1.0 / 16)).astype(f), seqmask=sm.astype(f),
        ident=np.eye(128, dtype=f), **_sample_tables(),
    )


_SHARED = ["w_in", "norm1_g", "k_slc_norm_g", "k_swa_norm_g", "gla_w_a2", "gla_b_a2", "gla_onorm_g", "q_norm_g",
           "k_cmp_norm_g", "cmp_pe_k", "cmp_w1_k", "cmp_b1_k", "cmp_w2_k", "cmp_b2_k", "cmp_pe_v", "cmp_w1_v",
           "cmp_b1_v", "cmp_w2_v", "cmp_b2_v", "w_br_gla", "w_br_nsa", "w_o", "norm2_g", "w_up", "conv_w", "conv_b",
           "w_down"]


def make_in_maps(inputs, cores=range(8)):
    xp = np.asarray(inputs["x_prompt"], np.float32)
    xs = np.asarray(inputs["x_sample"], np.float32)
    shared = {k: np.ascontiguousarray(np.asarray(inputs[k], np.float32)) for k in _SHARED}
    ctab = [_const_tables(0), _const_tables(1)]
    maps = []
    for c in cores:
        b, half = c // 2, c % 2
        xb = np.zeros((NT * 128, D), np.float32)
        if half == 1:
            xb[0:1024] = xp[b, 0:1024]
        xb[1024:2048] = xp[b, half * 1024:(half + 1) * 1024]
        pos = np.zeros(NT * 128, np.float32)
        pos[0:2048] = np.arange(2048) - 1024 + 1024 * half
        for s in range(4):
            xb[2048 + 32 * s:2048 + 32 * s + 4] = xs[4 * c + s]
            pos[2048 + 32 * s:2048 + 32 * s + 4] = 8192 + np.arange(4)
        cos, sin = _rope_tables(pos)
        m = dict(shared)
        m.update(ctab[half])
        m["page_tab"] = np.ascontiguousarray(np.asarray(inputs["page_table"], np.int32)[4 * c:4 * c + 4].reshape(1, 256))
        if WITH_SAMPLE:
            for k in ("cache_k_cmp", "cache_v_cmp", "cache_k_slc", "cache_v_slc"):
                a = np.asarray(inputs[k], np.float32)
                m[k] = a.reshape(a.shape[0], 128, 256)
        m.update({
            "xbuf": xb, "rope_cos": cos, "rope_sin": sin,
            "state_gla_c": np.ascontiguousarray(np.asarray(inputs["state_gla"], np.float32)[4 * c:4 * c + 4]),
            "state_conv_c": np.ascontiguousarray(np.asarray(inputs["state_conv"], np.float32)[4 * c:4 * c + 4]),
            "state_swa_c": np.ascontiguousarray(np.stack([
                np.asarray(inputs["state_swa_k"], np.float32)[4 * c:4 * c + 4].reshape(4, 512, 256),
                np.asarray(inputs["state_swa_v"], np.float32)[4 * c:4 * c + 4].reshape(4, 512, 256)])),
        })
        maps.append(m)
    return maps


def assemble(results, cores=range(8)):
    f = np.float32
    y_p = np.zeros((4, 2048, D), f); y_s = np.zeros((32, 4, D), f)
    kvp = [np.zeros((4, 2048, 2, 128), f) for _ in range(4)]
    swap = [np.zeros((4, 512, 2, 128), f) for _ in range(2)]
    gla_p = np.zeros((4, 4, 128, 256), f); conv_p = np.zeros((4, 2, DFF), f)
    kvs = [np.zeros((32, 4, 2, 128), f) for _ in range(4)]
    swas = [np.zeros((32, 512, 2, 128), f) for _ in range(2)]
    gla_s = np.zeros((32, 4, 128, 256), f); conv_s = np.zeros((32, 2, DFF), f)
    for c, r in zip(cores, results):
        b, half = c // 2, c % 2
        yo = r["y_out"]
        y_p[b, half * 1024:(half + 1) * 1024] = yo[0:1024]
        y_s[4 * c:4 * c + 4] = yo[1024:1040].reshape(4, 4, D)
        kv = r["kv_out"].reshape(NT * 128, 6, 2, 128)
        for a in range(4):
            kvp[a][b, half * 1024:(half + 1) * 1024] = kv[1024:2048, a]
            kvs[a][4 * c:4 * c + 4] = kv[2048:2176, a].reshape(4, 32, 2, 128)[:, 0:4]
        if half == 1:
            swap[0][b] = kv[1536:2048, 4]
            swap[1][b] = kv[1536:2048, 5]
            gla_p[b] = r["gla_out"][0]
            conv_p[b] = r["conv_out"][0:2]
        swas[0][4 * c:4 * c + 4] = r["swa_out"][0].reshape(4, 512, 2, 128)
        swas[1][4 * c:4 * c + 4] = r["swa_out"][1].reshape(4, 512, 2, 128)
        gla_s[4 * c:4 * c + 4] = r["gla_out"][1:5]
        conv_s[4 * c:4 * c + 4] = r["conv_out"][2:10].reshape(4, 2, DFF)
    return (y_p, y_s, kvp[0], kvp[1], kvp[2], kvp[3], swap[0], swap[1], gla_p, conv_p,
            kvs[0], kvs[1], kvs[2], kvs[3], swas[0], swas[1], gla_s, conv_s)


def kernel(**inputs):
    nc, mk = build_program()
    maps = make_in_maps(inputs)
    res = run_bass_kernel_spmd(nc, maps, core_ids=list(range(8)))
    return assemble(res.results)
```
